# Optimizing a Trainium2 kernel written in Bass

```python
import math
import jax
import jax.numpy as jnp
from jax import lax
import numpy as np

D_MODEL = 1024
BATCH = 8
SEQ = 4096
DEPTH = 2

N_MIXERS = 2
N_SSD_LAYERS = (DEPTH + N_MIXERS - 1) // N_MIXERS
N_MLA_LAYERS = DEPTH // N_MIXERS
N_SUBLAYERS = 3

D_FF = 2816
FFN_RES_WEIGHT = 0.5
NORM_EPS = 1e-6

SSD_EXPAND = 2
D_INNER = SSD_EXPAND * D_MODEL
SSD_HEAD_DIM = 64
SSD_HEADS = D_INNER // SSD_HEAD_DIM
SSD_GROUPS = 8
SSD_HEADS_PER_GROUP = SSD_HEADS // SSD_GROUPS
SSD_STATE = 128
SSD_CONV = 5
SSD_CHUNK = 128
SSD_CONV_CH = D_INNER + 2 * SSD_GROUPS * SSD_STATE
SSD_IN = D_INNER + SSD_CONV_CH + 2 * SSD_HEADS
SSD_NORM_GROUP = D_INNER // SSD_GROUPS

MLA_HEADS = 16
MLA_Q_LORA = 384
MLA_KV_LORA = 256
MLA_NOPE = 64
MLA_ROPE = 32
MLA_V = 64
MLA_QK = MLA_NOPE + MLA_ROPE
MLA_IN = MLA_Q_LORA + MLA_KV_LORA + MLA_ROPE
ROPE_THETA = 10000.0
ATTN_Q_BLOCK = 128
MAX_POS_OFFSET = 1024

kernel_name = "hybrid_ssd_mla_macaron_adaln_encoder"


def rms_norm(x, g):
    xf = x.astype(jnp.float32)
    y = xf * lax.rsqrt(jnp.mean(xf * xf, axis=-1, keepdims=True) + NORM_EPS)
    return (y * g.astype(jnp.float32)).astype(x.dtype)


def modulate(x, g, shift, scale):
    return rms_norm(x, g) * (1.0 + scale[:, None, :]) + shift[:, None, :]


def swiglu(h, w_gate, w_up, w_down):
    return (jax.nn.silu(h @ w_gate) * (h @ w_up)) @ w_down


def depthwise_conv_centred(x, w, bias):
    ch = x.shape[-1]
    pad = SSD_CONV // 2
    out = lax.conv_general_dilated(
        x, w[:, None, :], window_strides=(1,), padding=[(pad, pad)],
        dimension_numbers=("NWC", "WIO", "NWC"), feature_group_count=ch)
    return out + bias


def ssd_chunked(x, dt, A, Bm, Cm):
    b, s, g, r, p = x.shape
    n = Bm.shape[-1]
    nc = s // SSD_CHUNK
    q = SSD_CHUNK
    x = x.reshape(b, nc, q, g, r, p)
    dt = dt.reshape(b, nc, q, g, r)
    Bm = Bm.reshape(b, nc, q, g, n)
    Cm = Cm.reshape(b, nc, q, g, n)
    a_cum = jnp.cumsum(dt * A, axis=2)
    xdt = x * dt[..., None]

    seg = a_cum[:, :, :, None] - a_cum[:, :, None, :]
    mask = jnp.tril(jnp.ones((q, q), dtype=bool))[:, :, None, None]
    decay = jnp.exp(jnp.where(mask, seg, -jnp.inf))
    cb = jnp.einsum("bcign,bcjgn->bcijg", Cm, Bm)
    y_diag = jnp.einsum("bcijgr,bcjgrp->bcigrp", cb[..., None] * decay, xdt)

    decay_to_end = jnp.exp(a_cum[:, :, -1:] - a_cum)
    states = jnp.einsum("bcjgn,bcjgrp->bcgrpn", Bm, xdt * decay_to_end[..., None])
    chunk_decay = jnp.exp(a_cum[:, :, -1])

    def step(h, inp):
        st, dec = inp
        return h * dec[..., None, None] + st, h

    h0 = jnp.zeros((b, g, r, p, n), dtype=states.dtype)
    _, h_prev = lax.scan(step, h0, (jnp.moveaxis(states, 1, 0), jnp.moveaxis(chunk_decay, 1, 0)))
    h_prev = jnp.moveaxis(h_prev, 0, 1)

    y_off = jnp.einsum("bcign,bcgrpn->bcigrp", Cm, h_prev) * jnp.exp(a_cum)[..., None]
    return (y_diag + y_off).reshape(b, s, g, r, p)


def ssd_mixer(h, w_in, conv_w, conv_b, dt_bias, a_log, d_skip, norm_g, w_out):
    b, s, _ = h.shape
    g, r = SSD_GROUPS, SSD_HEADS_PER_GROUP
    proj = h @ w_in
    z, xbc, dt_raw = jnp.split(proj, [D_INNER, D_INNER + SSD_CONV_CH], axis=-1)
    xbc = jax.nn.silu(depthwise_conv_centred(xbc, conv_w, conv_b))
    xs, Bm, Cm = jnp.split(xbc, [D_INNER, D_INNER + SSD_GROUPS * SSD_STATE], axis=-1)
    xs = xs.reshape(b, s, g, r, SSD_HEAD_DIM)
    Bm = Bm.reshape(b, s, g, SSD_STATE)
    Cm = Cm.reshape(b, s, g, SSD_STATE)
    dt = jax.nn.softplus(dt_raw.reshape(b, s, 2, g, r) + dt_bias.reshape(2, g, r))
    A = -jnp.exp(a_log).reshape(2, g, r)

    flip = lambda t: jnp.flip(t, axis=1)
    y_fwd = ssd_chunked(xs, dt[:, :, 0], A[0], Bm, Cm)
    y_bwd = flip(ssd_chunked(flip(xs), flip(dt[:, :, 1]), A[1], flip(Bm), flip(Cm)))
    y = y_fwd + y_bwd + xs * d_skip.reshape(g, r)[..., None]

    y = y.reshape(b, s, D_INNER) * jax.nn.silu(z)
    y = rms_norm(y.reshape(b, s, g, SSD_NORM_GROUP), norm_g.reshape(g, SSD_NORM_GROUP))
    return y.reshape(b, s, D_INNER) @ w_out


def rope_tables(positions):
    inv_freq = ROPE_THETA ** (-jnp.arange(0, MLA_ROPE, 2, dtype=jnp.float32) / MLA_ROPE)
    ang = positions.astype(jnp.float32)[..., None] * inv_freq
    return jnp.cos(ang)[:, :, None, :], jnp.sin(ang)[:, :, None, :]


def apply_rope_tail(t, cos, sin):
    t_nope, t_pe = t[..., :MLA_NOPE], t[..., MLA_NOPE:]
    x1, x2 = jnp.split(t_pe.astype(jnp.float32), 2, axis=-1)
    rot = jnp.concatenate([x1 * cos - x2 * sin, x2 * cos + x1 * sin], axis=-1).astype(t.dtype)
    return jnp.concatenate([t_nope, rot], axis=-1)


def blocked_attention(q, k, v):
    b, s, h, dq = q.shape
    nq = s // ATTN_Q_BLOCK
    scale = dq ** -0.5
    qb = jnp.moveaxis(q.reshape(b, nq, ATTN_Q_BLOCK, h, dq), 1, 0)

    def one_block(q_blk):
        logits = jnp.einsum("bqhd,bkhd->bhqk", q_blk, k).astype(jnp.float32) * scale
        p = jax.nn.softmax(logits, axis=-1).astype(v.dtype)
        return jnp.einsum("bhqk,bkhv->bqhv", p, v)

    o = lax.map(one_block, qb)
    return jnp.moveaxis(o, 0, 1).reshape(b, s, h, v.shape[-1])


def mla_mixer(h, positions, w_in, q_norm_g, kv_norm_g, w_uq, w_ukv, q_head_g, k_head_g, w_out):
    b, s, _ = h.shape
    q_lat, kv_lat, k_pe = jnp.split(h @ w_in, [MLA_Q_LORA, MLA_Q_LORA + MLA_KV_LORA], axis=-1)
    q = (rms_norm(q_lat, q_norm_g) @ w_uq).reshape(b, s, MLA_HEADS, MLA_QK)
    kv = (rms_norm(kv_lat, kv_norm_g) @ w_ukv).reshape(b, s, MLA_HEADS, MLA_NOPE + MLA_V)
    k_nope, v = jnp.split(kv, [MLA_NOPE], axis=-1)
    k_pe = jnp.broadcast_to(k_pe[:, :, None, :], (b, s, MLA_HEADS, MLA_ROPE))
    k = jnp.concatenate([k_nope, k_pe], axis=-1)
    q = rms_norm(q, q_head_g)
    k = rms_norm(k, k_head_g)
    cos, sin = rope_tables(positions)
    q = apply_rope_tail(q, cos, sin)
    k = apply_rope_tail(k, cos, sin)
    o = blocked_attention(q, k, v)
    return o.reshape(b, s, MLA_HEADS * MLA_V) @ w_out


def setup_inputs(seed: int = 0) -> dict:
    key = jax.random.key(seed)
    ks = iter(jax.random.split(key, 40))

    def normal(shape, scale):
        return scale * jax.random.normal(next(ks), shape, jnp.float32)

    def gain(shape):
        return 1.0 + 0.05 * jax.random.normal(next(ks), shape, jnp.float32)

    na, nb = N_SSD_LAYERS, N_MLA_LAYERS
    x = normal((BATCH, SEQ, D_MODEL), 1.0)
    c = normal((BATCH, D_MODEL), 1.0)
    positions = (jax.random.randint(next(ks), (BATCH, 1), 0, MAX_POS_OFFSET, dtype=jnp.int32)
                 + jnp.arange(SEQ, dtype=jnp.int32)[None, :])
    norm_g = gain((DEPTH, N_SUBLAYERS, D_MODEL))
    w_mod = normal((DEPTH, D_MODEL, N_SUBLAYERS * 3 * D_MODEL), D_MODEL ** -0.5)
    b_mod = normal((DEPTH, N_SUBLAYERS * 3 * D_MODEL), 0.02)
    ffn_w_gate = normal((DEPTH, 2, D_MODEL, D_FF), D_MODEL ** -0.5)
    ffn_w_up = normal((DEPTH, 2, D_MODEL, D_FF), D_MODEL ** -0.5)
    ffn_w_down = normal((DEPTH, 2, D_FF, D_MODEL), D_FF ** -0.5)

    ssd_w_in = normal((na, D_MODEL, SSD_IN), D_MODEL ** -0.5)
    ssd_conv_w = normal((na, SSD_CONV, SSD_CONV_CH), SSD_CONV ** -0.5)
    ssd_conv_b = normal((na, SSD_CONV_CH), 0.02)
    dt0 = jnp.exp(jax.random.uniform(next(ks), (na, 2, SSD_HEADS), jnp.float32,
                                     math.log(1e-3), math.log(1e-1)))
    ssd_dt_bias = dt0 + jnp.log(-jnp.expm1(-dt0))
    ssd_a_log = jnp.log(jax.random.uniform(next(ks), (na, 2, SSD_HEADS), jnp.float32, 1.0, 16.0))
    ssd_d = gain((na, SSD_HEADS))
    ssd_norm_g = gain((na, D_INNER))
    ssd_w_out = normal((na, D_INNER, D_MODEL), D_INNER ** -0.5)

    mla_w_in = normal((nb, D_MODEL, MLA_IN), D_MODEL ** -0.5)
    mla_q_norm_g = gain((nb, MLA_Q_LORA))
    mla_kv_norm_g = gain((nb, MLA_KV_LORA))
    mla_w_uq = normal((nb, MLA_Q_LORA, MLA_HEADS * MLA_QK), MLA_Q_LORA ** -0.5)
    mla_w_ukv = normal((nb, MLA_KV_LORA, MLA_HEADS * (MLA_NOPE + MLA_V)), MLA_KV_LORA ** -0.5)
    mla_q_head_g = gain((nb, MLA_QK))
    mla_k_head_g = gain((nb, MLA_QK))
    mla_w_out = normal((nb, MLA_HEADS * MLA_V, D_MODEL), (MLA_HEADS * MLA_V) ** -0.5)

    return {
        "x": x, "c": c, "positions": positions,
        "norm_g": norm_g, "w_mod": w_mod, "b_mod": b_mod,
        "ffn_w_gate": ffn_w_gate, "ffn_w_up": ffn_w_up, "ffn_w_down": ffn_w_down,
        "ssd_w_in": ssd_w_in, "ssd_conv_w": ssd_conv_w, "ssd_conv_b": ssd_conv_b,
        "ssd_dt_bias": ssd_dt_bias, "ssd_a_log": ssd_a_log, "ssd_d": ssd_d,
        "ssd_norm_g": ssd_norm_g, "ssd_w_out": ssd_w_out,
        "mla_w_in": mla_w_in, "mla_q_norm_g": mla_q_norm_g, "mla_kv_norm_g": mla_kv_norm_g,
        "mla_w_uq": mla_w_uq, "mla_w_ukv": mla_w_ukv, "mla_q_head_g": mla_q_head_g,
        "mla_k_head_g": mla_k_head_g, "mla_w_out": mla_w_out,
    }


def reference(x, c, positions, norm_g, w_mod, b_mod, ffn_w_gate, ffn_w_up, ffn_w_down,
              ssd_w_in, ssd_conv_w, ssd_conv_b, ssd_dt_bias, ssd_a_log, ssd_d,
              ssd_norm_g, ssd_w_out,
              mla_w_in, mla_q_norm_g, mla_kv_norm_g, mla_w_uq, mla_w_ukv,
              mla_q_head_g, mla_k_head_g, mla_w_out):
    cond = jax.nn.silu(c)
    for i in range(DEPTH):
        mod = (cond @ w_mod[i] + b_mod[i]).reshape(-1, N_SUBLAYERS, 3, D_MODEL)

        h = modulate(x, norm_g[i, 0], mod[:, 0, 0], mod[:, 0, 1])
        x = x + FFN_RES_WEIGHT * mod[:, 0, 2][:, None, :] * swiglu(
            h, ffn_w_gate[i, 0], ffn_w_up[i, 0], ffn_w_down[i, 0])

        h = modulate(x, norm_g[i, 1], mod[:, 1, 0], mod[:, 1, 1])
        j = i // N_MIXERS
        if i % N_MIXERS == 0:
            y = ssd_mixer(h, ssd_w_in[j], ssd_conv_w[j], ssd_conv_b[j], ssd_dt_bias[j],
                          ssd_a_log[j], ssd_d[j], ssd_norm_g[j], ssd_w_out[j])
        else:
            y = mla_mixer(h, positions, mla_w_in[j], mla_q_norm_g[j], mla_kv_norm_g[j],
                          mla_w_uq[j], mla_w_ukv[j], mla_q_head_g[j], mla_k_head_g[j],
                          mla_w_out[j])
        x = x + mod[:, 1, 2][:, None, :] * y

        h = modulate(x, norm_g[i, 2], mod[:, 2, 0], mod[:, 2, 1])
        x = x + FFN_RES_WEIGHT * mod[:, 2, 2][:, None, :] * swiglu(
            h, ffn_w_gate[i, 1], ffn_w_up[i, 1], ffn_w_down[i, 1])
    return x
```

```python
import numpy as np
from contextlib import ExitStack
import concourse.bass as bass
import concourse.mybir as mybir
from concourse.bass_utils import run_bass_kernel_spmd

F32 = mybir.dt.float32
BF16 = mybir.dt.bfloat16
I32 = mybir.dt.int32
AF = mybir.ActivationFunctionType
ALU = mybir.AluOpType
AX = mybir.AxisListType

D = 1024
S = 4096
DFF = 2816
NF = DFF // 128
NDC = D // 128
EPS = 1e-6
N_DMA_SEMS = 40
DBG_M = 0
BARRIERS = False
ENGS = ("pe", "act", "dve", "pool", "sp")


class Op:
    __slots__ = ("eng", "fn", "deps", "sig", "seq", "dma", "semid", "semval", "prev", "pos", "gidx")


class Prog:
    def __init__(self, nc):
        self.nc = nc
        self.ops = {e: [] for e in ENGS}
        self.lastw = {}
        self.readers = {}
        self.ndma = 0
        self.dma_last = [None] * N_DMA_SEMS
        self.dma_cnt = [0] * N_DMA_SEMS
        self.nops = 0
        self.bases = set()
        self.touched = {}
        self.inherit = {}
        self.seen = set()

    def base_of(self, k):
        for _ in range(4):
            if k in self.bases:
                return k
            if isinstance(k, tuple) and len(k):
                k = k[0]
            else:
                return None
        return None

    def add(self, eng, fn, reads=(), writes=(), dma=False):
        op = Op()
        op.eng = eng
        op.fn = fn
        op.dma = dma
        op.sig = False
        op.seq = 0
        op.gidx = self.nops
        self.nops += 1
        deps = set()
        for k in list(reads) + list(writes):
            b = self.base_of(k)
            if b is None:
                continue
            if k not in self.seen:
                self.seen.add(k)
                deps.update(self.inherit.get(b, ()))
            t = self.touched.setdefault(b, {})
            if dma:
                t[("dma", op.gidx)] = op
            else:
                t[eng] = op
        for k in reads:
            w = self.lastw.get(k)
            if w is not None:
                deps.add(w)
        for k in writes:
            w = self.lastw.get(k)
            if w is not None:
                deps.add(w)
            for r in self.readers.get(k, ()):
                deps.add(r)
        for k in reads:
            self.readers.setdefault(k, []).append(op)
        for k in writes:
            self.lastw[k] = op
            self.readers[k] = []
        deps.discard(op)
        op.prev = None
        if dma:
            s = self.ndma % N_DMA_SEMS
            self.ndma += 1
            op.semid = s
            self.dma_cnt[s] += 16
            op.semval = self.dma_cnt[s]
            op.prev = self.dma_last[s]
            self.dma_last[s] = op
        op.deps = deps
        op.pos = len(self.ops[eng])
        self.ops[eng].append(op)
        return op

    def dma(self, eng, out, in_, reads, writes, **kw):
        return self.add(eng, lambda e: e.dma_start(out=out, in_=in_, **kw), reads, writes, dma=True)

    def mm(self, out, lhsT, rhs, start, stop, reads, writes):
        return self.add("pe", lambda e: e.matmul(out, lhsT, rhs, start=start, stop=stop), reads, writes)

    def emit(self, eng_sems, dma_sems):
        nc = self.nc

        def needs_sync(op, d):
            if d.dma:
                return True
            if d.eng == op.eng and not op.dma:
                if op.eng == "pe":
                    return False
                return True
            if d.eng == op.eng and op.dma:
                return True
            return True

        for e in ENGS:
            for op in self.ops[e]:
                for d in op.deps:
                    if needs_sync(op, d) and not d.dma:
                        d.sig = True
        for e in ENGS:
            n = 0
            for op in self.ops[e]:
                if op.sig and not op.dma:
                    n += 1
                    op.seq = n

        def emit_engine(ename, eobj):
            waited = {}
            for op in self.ops[ename]:
                need = {}
                for d in op.deps:
                    if not needs_sync(op, d):
                        continue
                    if d.dma:
                        key = ("d", d.semid)
                        val = d.semval
                    else:
                        key = ("e", d.eng)
                        val = d.seq
                    if need.get(key, 0) < val:
                        need[key] = val
                if op.dma and op.prev is not None:
                    key = ("d", op.prev.semid)
                    if need.get(key, 0) < op.prev.semval:
                        need[key] = op.prev.semval
                pend = []
                for key, val in need.items():
                    if waited.get(key, 0) >= val:
                        continue
                    waited[key] = val
                    sem = dma_sems[key[1]] if key[0] == "d" else eng_sems[key[1]]
                    pend.append((key[0] == "d", sem, val))
                pend.sort(key=lambda t: t[0])
                if op.fn is None:
                    for _, sem, val in pend:
                        eobj.wait_ge(sem, val)
                    continue
                for _, sem, val in pend[:-1]:
                    eobj.wait_ge(sem, val)
                ins = op.fn(eobj)
                if pend:
                    ins._wait_ge(pend[-1][1], pend[-1][2])
                if op.dma:
                    ins.then_inc(dma_sems[op.semid], 16)
                elif op.sig:
                    ins.then_inc(eng_sems[ename], 1)

        with nc.Block() as block:
            @block.tensor
            def _(e):
                emit_engine("pe", e)

            @block.scalar
            def _(e):
                emit_engine("act", e)

            @block.vector
            def _(e):
                emit_engine("dve", e)

            @block.gpsimd
            def _(e):
                emit_engine("pool", e)

            @block.sync
            def _(e):
                emit_engine("sp", e)


class Arena:
    def __init__(self, handle, nbytes, prog):
        self.h = handle
        self.cap = nbytes
        self.top = 0
        self.gen = 0
        self.P = prog
        self.allocs = []

    def mark(self):
        return self.top

    def release(self, m):
        self.top = m

    def alloc(self, name, shape, dtype):
        esz = 2 if dtype == BF16 else 4
        n = 1
        for s_ in shape:
            n *= s_
        nbytes = (n * esz + 63) // 64 * 64
        off = self.top
        assert off + nbytes <= self.cap, f"SBUF arena overflow allocating {name}: {off}+{nbytes}>{self.cap}"
        self.top += nbytes
        ap = self.h[:, off // 4:(off + nbytes) // 4]
        if dtype != F32:
            ap = ap.bitcast(dtype)
        ap = ap[:, 0:n]
        if len(shape) == 2:
            ap = ap.rearrange("p (a b) -> p a b", a=shape[0])
        elif len(shape) == 3:
            ap = ap.rearrange("p (a b c) -> p a b c", a=shape[0], b=shape[1])
        elif len(shape) == 4:
            ap = ap.rearrange("p (a b c d) -> p a b c d", a=shape[0], b=shape[1], c=shape[2])
        self.gen += 1
        key = (name, self.gen)
        P = self.P
        P.bases.add(key)
        inh = set()
        for (a0, a1, ok) in self.allocs:
            if a0 < off + nbytes and off < a1:
                inh.update(P.touched.get(ok, {}).values())
                inh.update(P.inherit.get(ok, ()))
        P.inherit[key] = inh
        self.allocs = [(a0, a1, ok) for (a0, a1, ok) in self.allocs if not (a0 >= off and a1 <= off + nbytes)]
        self.allocs.append((off, off + nbytes, key))
        return ap, key


class Ctx:
    pass


def dbg(C, name, ap, key, n):
    if not C.debug:
        return
    P, A = C.P, C.arena
    st = C.dbg_stage[:, C.dbg_off:C.dbg_off + n]
    kst = ("dbgst", name)
    P.add("pool", lambda e: e.tensor_copy(st, ap), [key], [kst])
    off = C.dbg_off
    C.dbg_off += n
    C.dbg_map[name] = (off, n)
    P.dma("sp", C.d_dbg.ap()[:, off:off + n], st, [kst], [("DBG", name)])
    C.dbg_keys.append(("DBG", name))


def rr(lst, i):
    return lst[i % len(lst)]


def setup_consts(C):
    P, A = C.P, C.arena
    C.ident_f, C.k_ident_f = A.alloc("ident_f", [128], F32)
    C.ident_b, C.k_ident_b = A.alloc("ident_b", [128], BF16)
    C.onesD_b, C.k_onesD = A.alloc("onesD", [128], BF16)
    P.dma("sp", C.ident_f, C.d_ident.ap(), [], [C.k_ident_f])
    P.add("dve", lambda e: e.tensor_copy(C.ident_b, C.ident_f), [C.k_ident_f], [C.k_ident_b])
    P.add("pool", lambda e: e.memset(C.onesD_b, 1.0 / D), [], [C.k_onesD])
    C.eps_t, C.k_eps = A.alloc("eps_t", [1], F32)
    P.add("pool", lambda e: e.memset(C.eps_t, EPS), [], [C.k_eps])
    C.vec = {}
    for name, t in C.d_vecs.items():
        n = t.ap().shape[1]
        ap, k = A.alloc("v_" + name, [n], F32)
        P.dma("sp", ap, t.ap(), [], [k])
        C.vec[name] = (ap, k)


def phase_barrier(C):
    P = C.P
    last = [P.ops[e][-1] for e in ENGS if P.ops[e]]
    last = [o for o in last if o.fn is not None]
    dmas = [o for e in ENGS for o in P.ops[e] if o.dma and o.gidx >= C.bar_gidx]
    C.bar_gidx = P.nops
    for e in ENGS:
        op = P.add(e, None)
        op.deps.update(last)
        op.deps.update(dmas)
        op.deps.discard(op)


def psum_bank(C):
    i = C.ps_i % 8
    C.ps_i += 1
    return C.ps[i], ("ps", i)


def load_transpose_x(C):
    P, A = C.P, C.arena
    m0 = A.mark()
    xin = [A.alloc(f"xin{i}", [4, D], F32) for i in range(2)]
    xtt = [A.alloc(f"xtt{i}", [NDC, 512], F32) for i in range(2)]
    xd = C.d_x.ap().rearrange("(g j p) d -> g p j d", j=4, p=128)
    for g in range(S // 512):
        xi, kxi = xin[g % 2]
        xt, kxt = xtt[g % 2]
        P.dma("sp", xi, xd[g], [], [kxi])
        for dc in range(NDC):
            ps, kps = psum_bank(C)
            for j in range(4):
                P.add("pe", lambda e, ps=ps, xi=xi, j=j, dc=dc: e.transpose(
                    ps[:, j * 128:(j + 1) * 128], xi[:, j, dc * 128:(dc + 1) * 128], C.ident_f),
                    [kxi, C.k_ident_f], [kps])
            eng = "act" if dc % 2 == 0 else "dve"
            if eng == "act":
                P.add("act", lambda e, ps=ps, xt=xt, dc=dc: e.copy(xt[:, dc, :], ps), [kps], [kxt])
            else:
                P.add("dve", lambda e, ps=ps, xt=xt, dc=dc: e.tensor_copy(xt[:, dc, :], ps), [kps], [kxt])
        P.dma("sp", C.xT[:, :, g * 512:(g + 1) * 512], xt, [kxt], [("xT", g)])
    A.release(m0)


def store_transpose_out(C):
    P, A = C.P, C.arena
    m0 = A.mark()
    xtt = [A.alloc(f"oxt{i}", [NDC, 512], F32) for i in range(2)]
    xo = [A.alloc(f"oxo{i}", [4, D], F32) for i in range(2)]
    od = C.d_out.ap().rearrange("(g j p) d -> g p j d", j=4, p=128)
    for g in range(S // 512):
        xt, kxt = xtt[g % 2]
        xo_, kxo = xo[g % 2]
        P.dma("sp", xt, C.xT[:, :, g * 512:(g + 1) * 512], [("xT", g)], [kxt])
        for j in range(4):
            for half in range(2):
                ps, kps = psum_bank(C)
                for q in range(4):
                    dc = half * 4 + q
                    P.add("pe", lambda e, ps=ps, xt=xt, j=j, dc=dc, q=q: e.transpose(
                        ps[:, q * 128:(q + 1) * 128], xt[:, dc, j * 128:(j + 1) * 128], C.ident_f),
                        [kxt, C.k_ident_f], [kps])
                if half == 0:
                    P.add("act", lambda e, ps=ps, xo_=xo_, j=j: e.copy(xo_[:, j, 0:512], ps), [kps], [kxo])
                else:
                    P.add("dve", lambda e, ps=ps, xo_=xo_, j=j: e.tensor_copy(xo_[:, j, 512:1024], ps), [kps], [kxo])
        P.dma("sp", od[g], xo_, [kxo], [("OUT", g)])
    A.release(m0)
    P.add("sp", None, [("OUT", g) for g in range(S // 512)] + C.dbg_keys, [])


def compute_mod(C):
    P, A = C.P, C.arena
    cvec, kc_ = C.vec["c"]
    C.cond, C.k_cond = A.alloc("cond", [8], F32)
    P.add("act", lambda e: e.activation(C.cond, cvec, AF.Silu), [kc_], [C.k_cond])
    C.mod = []
    m0 = None
    for l in range(2):
        mod, kmod = A.alloc(f"mod{l}", [72], F32)
        C.mod.append((mod, kmod))
    m0 = A.mark()
    wb = [A.alloc(f"wmod{i}", [8, 1024], F32) for i in range(2)]
    it = 0
    for l in range(2):
        mod, kmod = C.mod[l]
        bm, kbm = C.vec[f"b_mod{l}"]
        wd = C.d_w_mod.ap()[l].rearrange("(kc p) n -> p kc n", p=128)
        ps, kps = psum_bank(C)
        for cb in range(9):
            w, kw = wb[it % 2]
            it += 1
            P.dma("sp", w, wd[:, :, cb * 1024:(cb + 1) * 1024], [], [kw])
            for j in range(8):
                col = cb * 8 + j
                for kc in range(8):
                    P.mm(ps[:, col:col + 1], w[:, kc, j * 128:(j + 1) * 128], C.cond[:, kc:kc + 1],
                         kc == 0, kc == 7, [kw, C.k_cond], [kps])
        P.add("dve", lambda e, mod=mod, ps=ps, bm=bm: e.tensor_tensor(mod, ps[:, 0:72], bm, ALU.add),
              [kps, kbm], [kmod])
    A.release(m0)
    C.modv = []
    for l in range(2):
        mod, kmod = C.mod[l]
        g, kg = C.vec[f"norm_g{l}"]
        a, ka = A.alloc(f"moda{l}", [24], F32)
        gt, kgt = A.alloc(f"modg{l}", [24], F32)
        for sub in range(3):
            sc = mod[:, (sub * 3 + 1) * 8:(sub * 3 + 2) * 8]
            P.add("dve", lambda e, a=a, sub=sub, sc=sc, g=g: e.scalar_tensor_tensor(
                a[:, sub * 8:(sub + 1) * 8], sc, 1.0, g[:, sub * 8:(sub + 1) * 8], ALU.add, ALU.mult),
                [kmod, kg], [ka])
            gsrc = mod[:, (sub * 3 + 2) * 8:(sub * 3 + 3) * 8]
            fac = 1.0 if sub == 1 else 0.5
            P.add("dve", lambda e, gt=gt, sub=sub, gsrc=gsrc, fac=fac: e.tensor_scalar(
                gt[:, sub * 8:(sub + 1) * 8], gsrc, fac, None, ALU.mult), [kmod], [kgt])
        C.modv.append(dict(a=a, ka=ka, gate=gt, kgate=kgt, mod=mod, kmod=kmod))
        if l == 0:
            dbg(C, 'mod0', mod, kmod, 72)
            dbg(C, 'a0', a, ka, 24)
            dbg(C, 'gate0', gt, kgt, 24)


def cast_engine(C):
    e = ("pool", "dve", "act")[C.cast_i % 3]
    C.cast_i += 1
    return e


def emit_cast(P, eng, out, in_, reads, writes):
    if eng == "act":
        P.add("act", lambda e: e.copy(out, in_), reads, writes)
    else:
        P.add(eng, lambda e: e.tensor_copy(out, in_), reads, writes)


def convert_ffn_weights(C, l, w):
    P, A = C.P, C.arena
    idx = l * 2 + w
    m0 = A.mark()
    stg = [A.alloc(f"cst{i}", [4096], F32) for i in range(3)]
    stb = [A.alloc(f"csb{i}", [4096], BF16) for i in range(3)]
    it = 0
    for gi, src in enumerate((C.d_wg, C.d_wu)):
        sd = src.ap()[l, w].rearrange("(kc p) n -> p kc n", p=128)
        for fb in range(0, NF, 4):
            nf = min(4, NF - fb)
            sf, ksf = stg[it % 3]
            sb, ksb = stb[it % 3]
            it += 1
            sfv = sf[:, 0:8 * nf * 128].rearrange("p (kc n) -> p kc n", kc=8)
            P.dma("sp", sfv, sd[:, :, fb * 128:(fb + nf) * 128], [], [ksf])
            sbv = sb[:, 0:nf * 8 * 128].rearrange("p (f kc m) -> p f kc m", f=nf, kc=8)
            emit_cast(P, cast_engine(C), sbv, sfv.rearrange("p kc (f m) -> p f kc m", f=nf), [ksf], [ksb])
            dst = C.WGU[idx][fb:fb + nf, :, gi, :, :].rearrange("f p kc m -> p f (kc m)")
            P.dma("sp", dst, sbv.rearrange("p f kc m -> p f (kc m)"), [ksb], [("WGU", idx, f_) for f_ in range(fb, fb + nf)])
    sd = C.d_wd.ap()[l, w].rearrange("(fc p) n -> p fc n", p=128)
    for fb in range(0, NF, 4):
        nf = min(4, NF - fb)
        sf, ksf = stg[it % 3]
        sb, ksb = stb[it % 3]
        it += 1
        sfv = sf[:, 0:nf * 1024].rearrange("p (fc n) -> p fc n", fc=nf)
        P.dma("sp", sfv, sd[:, fb:fb + nf, :], [], [ksf])
        sbv = sb[:, 0:8 * nf * 128].rearrange("p (dc fc m) -> p dc fc m", dc=8, fc=nf)
        emit_cast(P, cast_engine(C), sbv, sfv.rearrange("p fc (dc m) -> p dc fc m", dc=8), [ksf], [ksb])
        dst = C.WD[idx][:, :, fb:fb + nf, :].rearrange("dc p fc m -> p dc (fc m)")
        P.dma("sp", dst, sbv.rearrange("p dc fc m -> p dc (fc m)"), [ksb], [("WD", idx, dc) for dc in range(8)])
    A.release(m0)


def norm_modulate(C, xt, kxt, h, kh, T, l, sub, scratch):
    P = C.P
    mv = C.modv[l]
    sq, ksq, rstd, krstd, tmp, ktmp = scratch
    shift = mv["mod"][:, (sub * 3) * 8:(sub * 3 + 1) * 8]
    for st in range(T // 512):
        sl = slice(st * 512, (st + 1) * 512)
        for dc in range(NDC):
            P.add("act", lambda e, dc=dc, sl=sl: e.activation(sq[:, dc, sl], xt[:, dc, sl], AF.Square),
                  [kxt], [(ksq, st)])
        ps, kps = psum_bank(C)
        for dc in range(NDC):
            P.mm(ps, C.onesD_b, sq[:, dc, sl], dc == 0, dc == NDC - 1, [(ksq, st), C.k_onesD], [kps])
        P.add("act", lambda e, ps=ps, sl=sl: e.activation(rstd[:, sl], ps, AF.Ln, bias=C.eps_t[:, 0:1]),
              [kps, C.k_eps], [(krstd, st)])
        P.add("act", lambda e, sl=sl: e.activation(rstd[:, sl], rstd[:, sl], AF.Exp, scale=-0.5),
              [(krstd, st)], [(krstd, st)])
        for dc in range(NDC):
            tm, ktm = tmp[dc % len(tmp)]
            eng = "dve" if dc % 2 == 0 else "pool"
            P.add(eng, lambda e, tm=tm, dc=dc, sl=sl: e.tensor_tensor(tm, xt[:, dc, sl], rstd[:, sl], ALU.mult),
                  [kxt, (krstd, st)], [ktm])
            P.add("act", lambda e, tm=tm, dc=dc, sl=sl: e.activation(
                h[:, dc, sl], tm, AF.Identity, bias=shift[:, dc:dc + 1],
                scale=mv["a"][:, sub * 8 + dc:sub * 8 + dc + 1]),
                [ktm, mv["ka"], mv["kmod"]], [(kh, st)])


def ffn_phase(C, l, w):
    P, A = C.P, C.arena
    idx = l * 2 + w
    sub = 0 if w == 0 else 2
    mv = C.modv[l]
    T = 1024
    m0 = A.mark()
    xb = [A.alloc(f"fx{i}", [NDC, T], F32) for i in range(2)]
    h, kh = A.alloc("fh", [NDC, T], BF16)
    act, kact = A.alloc("fact", [NF, T], BF16)
    sq, ksq = A.alloc("fsq", [NDC, T], BF16)
    rstd, krstd = A.alloc("frstd", [T], F32)
    tmp = [A.alloc(f"ftmp{i}", [512], F32) for i in range(3)]
    sg = [A.alloc(f"fsg{i}", [512], F32) for i in range(3)]
    wgu = [A.alloc(f"fwgu{i}", [2, 8, 128], BF16) for i in range(4)]
    wdb = [A.alloc(f"fwd{i}", [NF, 128], BF16) for i in range(3)]
    NM = S // T
    items = []
    for m in range(NM):
        for f in range(NF):
            items.append(("g", f))
        for dc in range(NDC):
            items.append(("d", dc))
    issued = [0]
    cnt = {"g": 0, "d": 0}
    slot_of = {}

    def prefetch(upto):
        while issued[0] < min(upto, len(items)):
            kind, j = items[issued[0]]
            if kind == "g":
                buf, kb = wgu[cnt["g"] % 4]
                cnt["g"] += 1
                P.dma("sp", buf, C.WGU[idx][j].rearrange("p g kc m -> p g kc m"), [("WGU", idx, j)], [kb])
            else:
                buf, kb = wdb[cnt["d"] % 3]
                cnt["d"] += 1
                P.dma("sp", buf, C.WD[idx][j], [("WD", idx, j)], [kb])
            slot_of[issued[0]] = (buf, kb)
            issued[0] += 1

    def load_x(m):
        xt, kxt = xb[m % 2]
        P.dma("act", xt, C.xT[:, :, m * T:(m + 1) * T], [("xT", 2 * m), ("xT", 2 * m + 1)], [kxt])

    load_x(0)
    pos = 0
    for m in range(NM):
        xt, kxt = xb[m % 2]
        prefetch(pos + 3)
        norm_modulate(C, xt, kxt, h, kh, T, l, sub, (sq, ksq, rstd, krstd, tmp, None))
        if m + 1 < NM:
            load_x(m + 1)
        if m == DBG_M and idx == 0:
            dbg(C, 'rstd', rstd[:, 0:512], (krstd, 0), 512)
            dbg(C, 'h0', h[:, 0, 0:512], (kh, 0), 512)
            dbg(C, 'h7', h[:, 7, 0:512], (kh, 0), 512)
        for f in range(NF):
            prefetch(pos + 3)
            wbuf, kwb = slot_of.pop(pos)
            pos += 1
            for st in range(T // 512):
                sl = slice(st * 512, (st + 1) * 512)
                psg, kpsg = psum_bank(C)
                psu, kpsu = psum_bank(C)
                for kc in range(8):
                    P.mm(psg, wbuf[:, 0, kc, :], h[:, kc, sl], kc == 0, kc == 7, [kwb, (kh, st)], [kpsg])
                for kc in range(8):
                    P.mm(psu, wbuf[:, 1, kc, :], h[:, kc, sl], kc == 0, kc == 7, [kwb, (kh, st)], [kpsu])
                s_, ks_ = sg[(f * 2 + st) % 3]
                P.add("act", lambda e, s_=s_, psg=psg: e.activation(s_, psg, AF.Silu), [kpsg], [ks_])
                P.add("dve", lambda e, s_=s_, psu=psu, f=f, sl=sl: e.tensor_tensor(act[:, f, sl], s_, psu, ALU.mult),
                      [ks_, kpsu], [(kact, st)])
        for dc in range(NDC):
            prefetch(pos + 3)
            wbuf, kwb = slot_of.pop(pos)
            pos += 1
            for st in range(T // 512):
                sl = slice(st * 512, (st + 1) * 512)
                pso, kpso = psum_bank(C)
                for f in range(NF):
                    P.mm(pso, wbuf[:, f, :], act[:, f, sl], f == 0, f == NF - 1, [kwb, (kact, st)], [kpso])
                P.add("dve", lambda e, pso=pso, dc=dc, sl=sl, xt=xt: e.scalar_tensor_tensor(
                    xt[:, dc, sl], pso, mv["gate"][:, sub * 8 + dc:sub * 8 + dc + 1], xt[:, dc, sl],
                    ALU.mult, ALU.add), [kpso, mv["kgate"], kxt], [kxt])
        if m == DBG_M and idx == 0:
            dbg(C, 'act0', act[:, 0, 0:512], (kact, 0), 512)
            dbg(C, 'act21', act[:, 21, 0:512], (kact, 0), 512)
            dbg(C, 'xo0', xt[:, 0, 0:512], kxt, 512)
        P.dma("act", C.xT[:, :, m * T:(m + 1) * T], xt, [kxt], [("xT", 2 * m), ("xT", 2 * m + 1)])
    A.release(m0)


PI = float(np.pi)


def const_tile(C, name, val, dtype=F32, n=1):
    ap, k = C.arena.alloc("c_" + name, [n], dtype)
    C.P.add("pool", lambda e: e.memset(ap, val), [], [k])
    return ap, k


def load_cast_weight(C, name, src_ap, shape, eng="sp"):
    P, A = C.P, C.arena
    n = 1
    for s_ in shape:
        n *= s_
    wb, kwb = A.alloc(name, shape, BF16)
    m0 = A.mark()
    CH = 2048
    flat_b = wb
    stg = [A.alloc(f"{name}_st{i}", [CH], F32) for i in range(2)]
    A.release(m0)
    return wb, kwb, stg


def mla_phase(C):
    P, A = C.P, C.arena
    l = 1
    mv = C.modv[l]
    T = 512
    NT = S // T
    NB = S // 128
    SCALE = float(96 ** -0.5)
    m_phase = A.mark()
    ones384, k384 = const_tile(C, "o384", 1.0 / 384, BF16, 128)
    ones256, k256 = const_tile(C, "o256", 1.0 / 256, BF16, 128)
    ones96, k96 = const_tile(C, "o96", 1.0 / 96, BF16, 128)
    onesrow, krow = const_tile(C, "orow", 1.0, F32, 128)
    hpi, khpi = const_tile(C, "hpi", PI / 2)
    nhpi, knhpi = const_tile(C, "nhpi", -PI / 2)
    pmT_f, kpmf = A.alloc("pmT_f", [96], F32)
    pmT, kpm = A.alloc("pmT", [96], BF16)
    P.dma("sp", pmT_f[0:96, :], C.d_pm.ap(), [], [kpmf])
    P.add("dve", lambda e: e.tensor_copy(pmT[0:96, :], pmT_f[0:96, :]), [kpmf], [kpm])
    gq, kgq = C.vec["mla_qg"]
    gk, kgk = C.vec["mla_kg"]
    gql, kgql = C.vec["mla_qng"]
    gkvl, kgkvl = C.vec["mla_kvng"]
    invf, kinvf = C.vec["invf"]
    qn, kqn = A.alloc("qn", [3, S], BF16)
    kvn, kkvn = A.alloc("kvn", [2, S], BF16)
    kpe, kkpe = A.alloc("kpe", [S], F32)
    sqk, ksqk = A.alloc("sqk", [S], BF16)
    COS, kcos = A.alloc("COS", [S], F32)
    SIN, ksin = A.alloc("SIN", [S], F32)
    wuq, kwuq = A.alloc("wuq", [3, 1536], BF16)
    wukv, kwukv = A.alloc("wukv", [2, 2048], BF16)

    m0 = A.mark()
    posi, kposi = A.alloc("posi", [S], I32)
    ang, kang = A.alloc("ang", [S], F32)
    t1, kt1 = A.alloc("rt1", [S], F32)
    ti, kti = A.alloc("rti", [S], I32)
    P.dma("sp", posi, C.d_pos.ap(), [], [kposi])
    P.add("dve", lambda e: e.tensor_copy(ang, posi), [kposi], [kang])
    P.add("dve", lambda e: e.tensor_scalar(ang, ang, invf[:, 0:1], None, ALU.mult), [kang, kinvf], [kang])
    for (dst, kdst, shift) in ((SIN, ksin, 0.0), (COS, kcos, PI / 2)):
        P.add("dve", lambda e, shift=shift: e.tensor_scalar(t1, ang, shift, 1.0 / (2 * PI), ALU.add, ALU.mult),
              [kang], [kt1])
        P.add("dve", lambda e: e.tensor_copy(ti, t1), [kt1], [kti])
        P.add("dve", lambda e: e.tensor_copy(t1, ti), [kti], [kt1])
        P.add("dve", lambda e: e.scalar_tensor_tensor(t1, t1, -2 * PI, ang, ALU.mult, ALU.add), [kt1, kang], [kt1])
        bias_ap = nhpi if shift == 0.0 else None
        if shift == 0.0:
            P.add("act", lambda e: e.activation(t1, t1, AF.Abs, bias=nhpi[:, 0:1]), [kt1, knhpi], [kt1])
        else:
            P.add("act", lambda e: e.activation(t1, t1, AF.Abs), [kt1], [kt1])
        P.add("act", lambda e, dst=dst: e.activation(dst, t1, AF.Sin, bias=hpi[:, 0:1], scale=-1.0),
              [kt1, khpi], [kdst])
    A.release(m0)

    def load_cast(dst, kdst, src, nk, ncol, colchunk):
        stg = [A.alloc(f"wst{i}", [nk, colchunk], F32) for i in range(2)]
        it = 0
        for c0 in range(0, ncol, colchunk):
            cw = min(colchunk, ncol - c0)
            st, kst = stg[it % 2]
            it += 1
            P.dma("sp", st[:, :, 0:cw], src[:, :, c0:c0 + cw], [], [kst])
            emit_cast(P, cast_engine(C), dst[:, :, c0:c0 + cw], st[:, :, 0:cw], [kst], [kdst])

    m1 = A.mark()
    mm_ = A.mark()
    load_cast(wuq, kwuq, C.d_mla_wuq.ap().rearrange("(kc p) n -> p kc n", p=128), 3, 1536, 512)
    load_cast(wukv, kwukv, C.d_mla_wukv.ap().rearrange("(kc p) n -> p kc n", p=128), 2, 2048, 512)
    A.release(mm_)
    win, kwin = A.alloc("win", [8, 736], BF16)
    P.add("pool", lambda e: e.memset(win, 0.0), [], [kwin])
    wsrc = C.d_mla_win.ap().rearrange("(kc p) n -> p kc n", p=128)
    mm2 = A.mark()
    stg = [A.alloc(f"wst_in{i}", [8, 224], F32) for i in range(2)]
    for i, c0 in enumerate(range(0, 672, 224)):
        st, kst = stg[i % 2]
        P.dma("sp", st, wsrc[:, :, c0:c0 + 224], [], [kst])
        if c0 + 224 <= 640:
            emit_cast(P, cast_engine(C), win[:, :, c0:c0 + 224], st, [kst], [kwin])
        else:
            nl = 640 - c0
            emit_cast(P, cast_engine(C), win[:, :, c0:640], st[:, :, 0:nl], [kst], [kwin])
            emit_cast(P, cast_engine(C), win[:, :, 704:736], st[:, :, nl:nl + 32], [kst], [kwin])

    A.release(mm2)
    xb = [A.alloc(f"mx{i}", [NDC, T], F32) for i in range(2)]
    h, kh = A.alloc("mh", [NDC, T], BF16)
    sq, ksq = A.alloc("msq", [NDC, T], BF16)
    rstd, krstd = A.alloc("mrstd", [T], F32)
    tmp = [A.alloc(f"mtmp{i}", [512], F32) for i in range(3)]
    lsq, klsq = A.alloc("mlsq", [3, T], BF16)
    lrs, klrs = A.alloc("mlrs", [T], F32)

    def load_x(m):
        xt, kxt = xb[m % 2]
        P.dma("act", xt, C.xT[:, :, m * T:(m + 1) * T], [("xT", m)], [kxt])

    load_x(0)
    for m in range(NT):
        xt, kxt = xb[m % 2]
        sl = slice(m * T, (m + 1) * T)
        norm_modulate(C, xt, kxt, h, kh, T, l, 1, (sq, ksq, rstd, krstd, tmp, None))
        if m + 1 < NT:
            load_x(m + 1)
        for (c0, nch, ones, kones, dst, kdst, g) in ((0, 3, ones384, k384, qn, kqn, gql), (3, 2, ones256, k256, kvn, kkvn, gkvl)):
            banks = []
            for c in range(nch):
                ps, kps = psum_bank(C)
                banks.append((ps, kps))
                for kc in range(8):
                    P.mm(ps, win[:, kc, (c0 + c) * 128:(c0 + c + 1) * 128], h[:, kc, :], kc == 0, kc == 7,
                         [kwin, (kh, 0)], [kps])
                P.add("act", lambda e, ps=ps, c=c: e.activation(lsq[:, c, :], ps, AF.Square), [kps], [klsq])
            pss, kpss = psum_bank(C)
            for c in range(nch):
                P.mm(pss, ones, lsq[:, c, :], c == 0, c == nch - 1, [klsq, kones], [kpss])
            P.add("act", lambda e, pss=pss: e.activation(lrs, pss, AF.Ln, bias=C.eps_t[:, 0:1]), [kpss, C.k_eps], [klrs])
            P.add("act", lambda e: e.activation(lrs, lrs, AF.Exp, scale=-0.5), [klrs], [klrs])
            for c in range(nch):
                ps, kps = banks[c]
                tm, ktm = tmp[c % 3]
                P.add("dve", lambda e, tm=tm, ps=ps: e.tensor_tensor(tm, ps, lrs, ALU.mult), [kps, klrs], [ktm])
                P.add("act", lambda e, tm=tm, c=c, dst=dst, g=g, sl=sl: e.activation(
                    dst[:, c, sl], tm, AF.Identity, scale=g[:, c:c + 1]), [ktm], [kdst])
        ps, kps = psum_bank(C)
        for kc in range(8):
            P.mm(ps[0:96, :], win[:, kc, 640:736], h[:, kc, :], kc == 0, kc == 7, [kwin, (kh, 0)], [kps])
        P.add("act", lambda e, ps=ps, sl=sl: e.copy(kpe[64:96, sl], ps[64:96, :]), [kps], [kkpe])
        P.add("act", lambda e, ps=ps, sl=sl: e.activation(sqk[64:96, sl], ps[64:96, :], AF.Square), [kps], [(ksqk, "pe")])
    A.release(m1)

    kT = [A.alloc(f"kT{i}", [S], BF16) for i in range(2)]
    Vau = [A.alloc(f"Vau{i}", [NB, 128], BF16) for i in range(2)]
    OTs = [A.alloc(f"OTs{i}", [S], BF16) for i in range(2)]
    for par in range(2):
        va, kva = Vau[par]
        P.add("pool", lambda e, va=va: e.memset(va, 1.0), [], [kva])
    rsk, krsk = A.alloc("rsk", [T], F32)
    rt1 = [A.alloc(f"rp1_{i}", [T], F32) for i in range(2)]
    rt2 = [A.alloc(f"rp2_{i}", [T], F32) for i in range(2)]
    qsq, kqsq = A.alloc("qsq", [T], BF16)
    qT = [A.alloc(f"qT{i}", [T], BF16) for i in range(2)]
    PT = [A.alloc(f"PT{i}", [T], BF16) for i in range(4)]
    osb, kosb = A.alloc("osb", [T], F32)
    rl, krl = A.alloc("rl", [T], F32)
    pti = 0

    def norm_rope(src_ps, ksrc_ps, src_sb, ksrc_sb, sqt, ksqt_keys, g, kg, dstT, kdstT, sl, tl):
        pss, kpss = psum_bank(C)
        P.mm(pss[0:96, :], ones96[0:96, 0:96], sqt, True, True, ksqt_keys + [k96], [kpss])
        P.add("act", lambda e: e.activation(rsk[0:96, :], pss[0:96, :], AF.Ln, bias=C.eps_t[0:96, 0:1]),
              [kpss, C.k_eps], [krsk])
        P.add("act", lambda e: e.activation(rsk[0:96, :], rsk[0:96, :], AF.Exp, scale=-0.5), [krsk], [krsk])
        if src_sb is None:
            P.add("dve", lambda e: e.scalar_tensor_tensor(dstT[0:96, sl], src_ps[0:96, :], g[0:96, 0:1], rsk[0:96, :],
                                                          ALU.mult, ALU.mult), [ksrc_ps, kg, krsk], [kdstT])
        else:
            P.add("dve", lambda e: e.scalar_tensor_tensor(dstT[0:64, sl], src_ps[0:64, :], g[0:64, 0:1], rsk[0:64, :],
                                                          ALU.mult, ALU.mult), [ksrc_ps, kg, krsk], [kdstT])
            P.add("dve", lambda e: e.scalar_tensor_tensor(dstT[64:96, sl], src_sb[64:96, tl], g[64:96, 0:1],
                                                          rsk[64:96, :], ALU.mult, ALU.mult), [ksrc_sb, kg, krsk], [kdstT])
        psr, kpsr = psum_bank(C)
        P.mm(psr[0:96, :], pmT[0:96, 0:96], dstT[0:96, sl], True, True, [kdstT, kpm], [kpsr])
        a1, ka1 = rt1[C.ps_i % 2]
        a2, ka2 = rt2[C.ps_i % 2]
        P.add("pool", lambda e: e.tensor_tensor(a1[64:96, :], dstT[64:96, sl], COS[64:96, tl], ALU.mult),
              [kdstT, kcos], [ka1])
        P.add("dve", lambda e: e.tensor_tensor(a2[64:96, :], psr[64:96, :], SIN[64:96, tl], ALU.mult),
              [kpsr, ksin], [ka2])
        P.add("dve", lambda e: e.tensor_tensor(dstT[64:96, sl], a1[64:96, :], a2[64:96, :], ALU.add),
              [ka1, ka2], [kdstT])

    for hd in range(16):
        par = hd % 2
        kt_, kkt = kT[par]
        va, kva = Vau[par]
        ots, kots = OTs[(hd // 2) % 2]
        voff = 0 if par == 0 else 64
        oh = 0 if par == 0 else 64
        lp = 64 if par == 0 else 0
        for m in range(NT):
            tl = slice(m * T, (m + 1) * T)
            ps, kps = psum_bank(C)
            for kc in range(2):
                P.mm(ps[0:64, :], wukv[:, kc, hd * 128:hd * 128 + 64], kvn[:, kc, tl], kc == 0, kc == 1,
                     [kwukv, kkvn], [kps])
            P.add("act", lambda e, ps=ps, tl=tl: e.activation(sqk[0:64, tl], ps[0:64, :], AF.Square),
                  [kps], [(ksqk, "n", m)])
            norm_rope(ps, kps, kpe, kkpe, sqk[0:96, tl], [(ksqk, "n", m), (ksqk, "pe")], gk, kgk, kt_, kkt, tl, tl)
        for b0 in range(0, NB, 8):
            ps, kps = psum_bank(C)
            for j in range(8):
                blk = b0 + j
                for kc in range(2):
                    P.mm(ps[:, j * 64:(j + 1) * 64], kvn[:, kc, blk * 128:(blk + 1) * 128],
                         wukv[:, kc, hd * 128 + 64:hd * 128 + 128], kc == 0, kc == 1, [kwukv, kkvn], [kps])
            P.add("dve", lambda e, ps=ps, b0=b0, va=va, voff=voff: e.tensor_copy(
                va[:, b0:b0 + 8, voff:voff + 64], ps.rearrange("p (j v) -> p j v", j=8)), [kps], [kva])
        for m in range(NT):
            tl = slice(m * T, (m + 1) * T)
            q_, kq_ = qT[m % 2]
            ps, kps = psum_bank(C)
            for kc in range(3):
                P.mm(ps[0:96, :], wuq[:, kc, hd * 96:(hd + 1) * 96], qn[:, kc, tl], kc == 0, kc == 2,
                     [kwuq, kqn], [kps])
            P.add("act", lambda e, ps=ps: e.activation(qsq[0:96, :], ps[0:96, :], AF.Square), [kps], [kqsq])
            norm_rope(ps, kps, None, None, qsq[0:96, :], [kqsq], gq, kgq, q_, kq_, slice(0, T), tl)
            pso, kpso = psum_bank(C)
            for kb in range(NB):
                pss, kpss = psum_bank(C)
                if pss is pso:
                    pss, kpss = psum_bank(C)
                P.mm(pss, kt_[0:96, kb * 128:(kb + 1) * 128], q_[0:96, :], True, True, [kkt, kq_], [kpss])
                pt, kpt = PT[pti % 4]
                pti += 1
                P.add("act", lambda e, pt=pt, pss=pss: e.activation(pt, pss, AF.Exp, scale=SCALE), [kpss], [kpt])
                P.mm(pso, va[:, kb, :], pt, kb == 0, kb == NB - 1, [kva, kpt], [kpso])
            P.add("dve", lambda e, pso=pso, lp=lp: e.reciprocal(rl[lp:lp + 1, :], pso[lp:lp + 1, :]), [kpso], [krl])
            P.add("act", lambda e, pso=pso, oh=oh: e.copy(osb[oh:oh + 64, :], pso[oh:oh + 64, :]), [kpso], [kosb])
            psb, kpsb = psum_bank(C)
            P.mm(psb, onesrow[lp:lp + 1, :], rl[lp:lp + 1, :], True, True, [krl, krow], [kpsb])
            P.add("dve", lambda e, psb=psb, oh=oh, ots=ots, tl=tl: e.tensor_tensor(
                ots[oh:oh + 64, tl], osb[oh:oh + 64, :], psb[oh:oh + 64, :], ALU.mult), [kosb, kpsb], [kots])
        if par == 1:
            c = hd // 2
            P.dma("sp", C.OT[c], ots, [kots], [("OT", c)])
    A.release(m_phase)

    m3 = A.mark()
    wout, kwout = A.alloc("wout", [8, 1024], BF16)
    stg = [A.alloc(f"wst_o{i}", [8, 256], F32) for i in range(2)]
    wsrc = C.d_mla_wout.ap().rearrange("(kc p) n -> p kc n", p=128)
    for i, c0 in enumerate(range(0, 1024, 256)):
        st, kst = stg[i % 2]
        P.dma("sp", st, wsrc[:, :, c0:c0 + 256], [], [kst])
        emit_cast(P, cast_engine(C), wout[:, :, c0:c0 + 256], st, [kst], [kwout])
    xb = [A.alloc(f"ox{i}", [NDC, T], F32) for i in range(2)]
    ob = [A.alloc(f"oo{i}", [NDC, T], BF16) for i in range(2)]
    for m in range(NT):
        xt, kxt = xb[m % 2]
        ot, kot = ob[m % 2]
        tl = slice(m * T, (m + 1) * T)
        P.dma("act", xt, C.xT[:, :, tl], [("xT", m)], [kxt])
        P.dma("sp", ot, C.OT.rearrange("c p s -> p c s")[:, :, tl], [("OT", c) for c in range(8)], [kot])
        for dc in range(NDC):
            ps, kps = psum_bank(C)
            for kc in range(8):
                P.mm(ps, wout[:, kc, dc * 128:(dc + 1) * 128], ot[:, kc, :], kc == 0, kc == 7, [kwout, kot], [kps])
            P.add("dve", lambda e, ps=ps, dc=dc, xt=xt: e.scalar_tensor_tensor(
                xt[:, dc, :], ps, mv["gate"][:, 8 + dc:8 + dc + 1], xt[:, dc, :], ALU.mult, ALU.add),
                [kps, mv["kgate"], kxt], [kxt])
        P.dma("act", C.xT[:, :, tl], xt, [kxt], [("xT", m)])
    A.release(m3)


def bc_last(ap, n):
    return ap.unsqueeze(2).to_broadcast([ap.shape[0], ap.shape[1], n])


def bc_mid(ap, n):
    return ap.unsqueeze(1).to_broadcast([ap.shape[0], n, ap.shape[1]])


def ssd_phase(C):
    P, A = C.P, C.arena
    l = 0
    mv = C.modv[l]
    NB = S // 128
    T = 512
    NT = S // T
    m_phase = A.mark()
    one_t, kone = const_tile(C, "one", 1.0)
    masks, kmask = A.alloc("masks", [6, 128], F32)
    P.dma("sp", masks, C.d_masks.ap().rearrange("p (a b) -> p a b", a=6), [], [kmask])
    ones_f, konesf = const_tile(C, "ones_f", 1.0, F32, 128)
    cw, kcw = C.vec["ssd_cw"]
    cb_, kcb = C.vec["ssd_cb"]
    dtb, kdtb = C.vec["ssd_dtb"]
    alog, kalog = C.vec["ssd_alog"]
    dsk, kdsk = C.vec["ssd_dskip"]
    Aneg, kAneg = A.alloc("Aneg", [64], F32)
    P.add("act", lambda e: e.activation(Aneg, alog, AF.Exp), [kalog], [kAneg])
    P.add("dve", lambda e: e.tensor_scalar(Aneg, Aneg, -1.0, None, ALU.mult), [kAneg], [kAneg])
    h_all, khall = A.alloc("h_all", [NDC, S], BF16)

    m0 = A.mark()
    xb = [A.alloc(f"sx{i}", [NDC, T], F32) for i in range(2)]
    sq, ksq = A.alloc("ssq", [NDC, T], BF16)
    rstd, krstd = A.alloc("srstd", [T], F32)
    tmp = [A.alloc(f"stmp{i}", [512], F32) for i in range(3)]
    for m in range(NT):
        xt, kxt = xb[m % 2]
        P.dma("act", xt, C.xT[:, :, m * T:(m + 1) * T], [("xT", m)], [kxt])
        norm_modulate(C, xt, kxt, h_all[:, :, m * T:(m + 1) * T], (khall, m), T, l, 1, (sq, ksq, rstd, krstd, tmp, None))
    A.release(m0)
    hkeys = [((khall, m), 0) for m in range(NT)]

    m0 = A.mark()
    wsrc = C.d_ssd_win.ap().rearrange("(kc p) n -> p kc n", p=128)
    wst = [A.alloc(f"swst{i}", [8, 128], F32) for i in range(2)]
    wcb = [A.alloc(f"swc{i}", [8, 128], BF16) for i in range(2)]
    pre = [A.alloc(f"spre{i}", [S + 4], F32) for i in range(2)]
    acc = [A.alloc(f"sacc{i}", [S], F32) for i in range(2)]
    xo = [A.alloc(f"sxo{i}", [S], BF16) for i in range(2)]
    for i in range(2):
        pr, kpr = pre[i]
        P.add("pool", lambda e, pr=pr: e.memset(pr, 0.0), [], [kpr])
    for c in range(32):
        ws, kws = wst[c % 2]
        wc, kwc = wcb[c % 2]
        pr, kpr = pre[c % 2]
        ac, kac = acc[c % 2]
        xo_, kxo = xo[c % 2]
        P.dma("sp", ws, wsrc[:, :, 2048 + c * 128:2048 + (c + 1) * 128], [], [kws])
        P.add("pool", lambda e, wc=wc, ws=ws: e.tensor_copy(wc, ws), [kws], [kwc])
        for m in range(NT):
            ps, kps = psum_bank(C)
            for kc in range(8):
                P.mm(ps, wc[:, kc, :], h_all[:, kc, m * T:(m + 1) * T], kc == 0, kc == 7, [kwc, hkeys[m]], [kps])
            P.add("act", lambda e, ps=ps, pr=pr, m=m: e.copy(pr[:, 2 + m * T:2 + (m + 1) * T], ps), [kps], [kpr])
        for hf in range(2):
            o0 = hf * (S // 2)
            n_ = S // 2
            P.add("dve", lambda e, ac=ac, pr=pr, c=c, o0=o0, n_=n_: e.tensor_scalar(
                ac[:, o0:o0 + n_], pr[:, o0:o0 + n_], cw[:, c * 5:c * 5 + 1], cb_[:, c:c + 1], ALU.mult, ALU.add),
                [kpr, kcw, kcb], [(kac, hf)])
            for j in range(1, 5):
                P.add("dve", lambda e, ac=ac, pr=pr, c=c, j=j, o0=o0, n_=n_: e.scalar_tensor_tensor(
                    ac[:, o0:o0 + n_], pr[:, o0 + j:o0 + j + n_], cw[:, c * 5 + j:c * 5 + j + 1], ac[:, o0:o0 + n_],
                    ALU.mult, ALU.add), [kpr, kcw, (kac, hf)], [(kac, hf)])
            P.add("act", lambda e, xo_=xo_, ac=ac, o0=o0, n_=n_: e.activation(xo_[:, o0:o0 + n_], ac[:, o0:o0 + n_], AF.Silu),
                  [(kac, hf)], [(kxo, hf)])
        P.dma("sp", C.XBC[c], xo_, [(kxo, 0), (kxo, 1)], [("XBC", c)])
    A.release(m0)

    wdt, kwdt = A.alloc("wdt", [8, 64], BF16)
    m_big = A.mark()
    wz, kwz = A.alloc("wz", [8, 2048], BF16)
    m0 = A.mark()
    stg = [A.alloc(f"szst{i}", [8, 256], F32) for i in range(2)]
    it = 0
    for c0 in range(0, 2048, 256):
        st, kst = stg[it % 2]
        it += 1
        P.dma("sp", st, wsrc[:, :, c0:c0 + 256], [], [kst])
        emit_cast(P, cast_engine(C), wz[:, :, c0:c0 + 256], st, [kst], [kwz])
    st, kst = stg[it % 2]
    it += 1
    P.dma("sp", st[:, :, 0:64], wsrc[:, :, 6144:6208], [], [kst])
    emit_cast(P, cast_engine(C), wdt, st[:, :, 0:64], [kst], [kwdt])
    A.release(m0)
    ng16, kng16 = C.vec["ssd_ng16"]

    xbcT = [A.alloc(f"xbcT{i}", [32, 128], BF16) for i in range(1)]
    xs_tm, kxs = A.alloc("xs_tm", [32, 64], BF16)
    B_tm, kbt = A.alloc("B_tm", [8, 128], BF16)
    dt_, kdt = A.alloc("dt", [64], F32)
    a_, ka = A.alloc("a", [64], F32)
    dec, kdec = A.alloc("dec", [96], F32)
    dt2, kdt2 = A.alloc("dt2", [32], F32)
    xdt, kxdt = A.alloc("xdt", [32, 64], BF16)
    xdtE, kxdtE = A.alloc("xdtE", [32, 64], BF16)
    cbm, kcbm = A.alloc("cbm", [8, 128], F32)
    Lb = [A.alloc(f"Lb{i}", [4, 128], F32) for i in range(2)]
    ex = [A.alloc(f"ex{i}", [4, 128], F32) for i in range(2)]
    MT = [A.alloc(f"MT{i}", [4, 128], BF16) for i in range(2)]
    yo = [A.alloc(f"yo{i}", [256], F32) for i in range(2)]
    ydir, kydir = A.alloc("ydir", [2048], F32)
    H, kH = A.alloc("H", [2048], F32)
    Hb, kHb = A.alloc("Hb", [2048], BF16)
    yb_in, kybin = A.alloc("yb_in", [2048], F32)
    sz, ksz = A.alloc("sz", [2048], F32)
    gss, kgss = A.alloc("gss", [8], F32)
    ynb, kynb = A.alloc("ynb", [2048], BF16)
    yT, kyT = A.alloc("yT", [16, 128], BF16)
    xck, kxck = A.alloc("xck", [NDC, 128], F32)

    def chunk_step(ck, d, it_):
        tk = slice(ck * 128, (ck + 1) * 128)
        xb_, kxb = xbcT[0]
        Lm = masks[:, 0 + 2 * d, :]
        Rm = masks[:, 1 + 2 * d, :]
        Vm = masks[:, 4 + d, :]
        dc0 = d * 32
        P.dma("sp", xb_, C.XBC.rearrange("c p s -> p c s")[:, :, tk], [("XBC", c) for c in range(32)], [kxb])
        for q in range(3):
            ps, kps = psum_bank(C)
            psb = ps.bitcast(BF16)
            for j in range(8):
                c = q * 8 + j
                P.add("pe", lambda e, psb=psb, j=j, c=c, xb_=xb_: e.transpose(
                    psb[:, j * 128:(j + 1) * 128], xb_[:, c, :], C.ident_b), [kxb, C.k_ident_b], [kps])
            if q < 2:
                P.add("act", lambda e, psb=psb, q=q: e.copy(
                    xs_tm.rearrange("p a b -> p (a b)")[:, q * 1024:(q + 1) * 1024], psb), [kps], [kxs])
            else:
                P.add("dve", lambda e, psb=psb: e.tensor_copy(B_tm.rearrange("p a b -> p (a b)"), psb), [kps], [kbt])
        ps, kps = psum_bank(C)
        for kc in range(8):
            P.mm(ps[:, 0:64], h_all[:, kc, tk], wdt[:, kc, :], kc == 0, kc == 7, [hkeys[ck // 4], kwdt], [kps])
        P.add("dve", lambda e, ps=ps: e.tensor_tensor(dt_, ps[:, 0:64], dtb, ALU.add), [kps, kdtb], [kdt])
        P.add("act", lambda e: e.activation(dt_, dt_, AF.Exp), [kdt], [kdt])
        P.add("act", lambda e: e.activation(dt_, dt_, AF.Ln, bias=one_t[:, 0:1]), [kdt, kone], [kdt])
        P.add("dve", lambda e: e.tensor_tensor(a_, dt_, Aneg, ALU.mult), [kdt, kAneg], [ka])
        ps, kps = psum_bank(C)
        P.mm(ps[:, 0:32], Rm, a_[:, dc0:dc0 + 32], True, True, [kmask, ka], [kps])
        P.mm(ps[:, 32:64], Lm, a_[:, dc0:dc0 + 32], True, True, [kmask, ka], [kps])
        P.mm(ps[:, 64:96], ones_f, a_[:, dc0:dc0 + 32], True, True, [konesf, ka], [kps])
        P.add("act", lambda e, ps=ps: e.activation(dec, ps[:, 0:96], AF.Exp), [kps], [kdec])
        P.add("dve", lambda e: e.tensor_tensor(dt2, dt_[:, dc0:dc0 + 32], dec[:, 32:64], ALU.mult), [kdt, kdec], [kdt2])
        P.add("pool", lambda e: e.tensor_tensor(xdt, xs_tm, bc_last(dt_[:, dc0:dc0 + 32], 64), ALU.mult),
              [kxs, kdt], [kxdt])
        P.add("pool", lambda e: e.tensor_tensor(xdtE, xs_tm, bc_last(dt2, 64), ALU.mult), [kxs, kdt2], [kxdtE])
        for half in range(2):
            ps, kps = psum_bank(C)
            for j in range(4):
                g = half * 4 + j
                P.mm(ps[:, j * 128:(j + 1) * 128], xb_[:, 16 + g, :], xb_[:, 24 + g, :], True, True, [kxb], [kps])
            P.add("dve", lambda e, ps=ps, half=half: e.tensor_tensor(
                cbm[:, half * 4:(half + 1) * 4, :], ps.rearrange("p (a b) -> p a b", a=4), bc_mid(Vm, 4), ALU.mult),
                [kps, kmask], [kcbm])
        for g in range(8):
            lb, klb = Lb[g % 2]
            ex_, kex = ex[g % 2]
            mt, kmt = MT[g % 2]
            yo_, kyo = yo[g % 2]
            P.add("dve", lambda e, lb=lb, g=g: e.tensor_tensor(
                lb, bc_mid(Lm, 4), bc_last(a_[:, dc0 + g * 4:dc0 + g * 4 + 4], 128), ALU.mult), [kmask, ka], [klb])
            ps, kps = psum_bank(C)
            for r in range(4):
                P.mm(ps[:, r * 128:(r + 1) * 128], lb[:, r, :], Rm, True, True, [klb, kmask], [kps])
            P.add("act", lambda e, ps=ps, ex_=ex_: e.activation(ex_.rearrange("p a b -> p (a b)"), ps, AF.Exp), [kps], [kex])
            P.add("dve", lambda e, mt=mt, ex_=ex_, g=g: e.tensor_tensor(mt, ex_, bc_mid(cbm[:, g, :], 4), ALU.mult),
                  [kex, kcbm], [kmt])
            psy, kpsy = psum_bank(C)
            for r in range(4):
                P.mm(psy[:, r * 64:(r + 1) * 64], mt[:, r, :], xdt[:, g * 4 + r, :], True, True, [kmt, kxdt], [kpsy])
            P.mm(psy[:, 256:512], xb_[:, 24 + g, :], Hb[:, g * 256:(g + 1) * 256], True, True, [kxb, kHb], [kpsy])
            P.add("dve", lambda e, psy=psy, yo_=yo_, g=g: e.tensor_tensor(
                yo_.rearrange("p (a b) -> p a b", a=4), psy[:, 256:512].rearrange("p (a b) -> p a b", a=4),
                bc_last(dec[:, g * 4:g * 4 + 4], 64), ALU.mult), [kpsy, kdec], [kyo])
            P.add("dve", lambda e, psy=psy, yo_=yo_, g=g: e.tensor_tensor(
                ydir[:, g * 256:(g + 1) * 256], psy[:, 0:256], yo_, ALU.add), [kpsy, kyo], [(kydir, g)])
            pss, kpss = psum_bank(C)
            P.mm(pss[:, 0:256], B_tm[:, g, :], xdtE[:, g * 4:g * 4 + 4, :].rearrange("p a b -> p (a b)"), True, True,
                 [kbt, kxdtE], [kpss])
            P.add("dve", lambda e, g=g: e.tensor_tensor(
                H[:, g * 256:(g + 1) * 256].rearrange("p (a b) -> p a b", a=4),
                H[:, g * 256:(g + 1) * 256].rearrange("p (a b) -> p a b", a=4),
                bc_last(dec[:, 64 + g * 4:64 + g * 4 + 4], 64), ALU.mult), [(kH, g), kdec], [(kH, g)])
            P.add("dve", lambda e, pss=pss, g=g: e.tensor_tensor(
                H[:, g * 256:(g + 1) * 256], H[:, g * 256:(g + 1) * 256], pss[:, 0:256], ALU.add),
                [(kH, g), kpss], [(kH, g)])
            P.add("act", lambda e, g=g: e.copy(Hb[:, g * 256:(g + 1) * 256], H[:, g * 256:(g + 1) * 256]),
                  [(kH, g)], [kHb])

    ykeys = [(kydir, g) for g in range(8)]
    P.add("pool", lambda e: e.memset(H, 0.0), [], [(kH, g) for g in range(8)])
    P.add("pool", lambda e: e.memset(Hb, 0.0), [], [kHb])
    it_ = 0
    for ck in range(NB - 1, -1, -1):
        chunk_step(ck, 1, it_)
        it_ += 1
        P.dma("sp", C.YB[ck], ydir, ykeys, [("YB", ck)])
        tk = slice(ck * 128, (ck + 1) * 128)
        for zc in range(4):
            ps, kps = psum_bank(C)
            for kc in range(8):
                P.mm(ps, h_all[:, kc, tk], wz[:, kc, zc * 512:(zc + 1) * 512], kc == 0, kc == 7,
                     [hkeys[ck // 4], kwz], [kps])
            P.add("act", lambda e, ps=ps, zc=zc: e.activation(sz[:, zc * 512:(zc + 1) * 512], ps, AF.Silu), [kps], [ksz])
        P.dma("sp", C.SZ[ck], sz, [ksz], [("SZ", ck)])
    wout = wz.rearrange("p a b -> p (a b)").rearrange("p (k n) -> p k n", k=16)
    kwout = kwz
    stg = [(yb_in.rearrange("p (k n) -> p k n", k=2), kybin), (sz.rearrange("p (k n) -> p k n", k=2), ksz)]
    osrc = C.d_ssd_wout.ap().rearrange("(kc p) n -> p kc n", p=128)
    for i_, c0 in enumerate(range(0, 16, 2)):
        st, kst = stg[i_ % 2]
        P.dma("sp", st, osrc[:, c0:c0 + 2, :], [], [kst])
        emit_cast(P, cast_engine(C), wout[:, c0:c0 + 2, :], st, [kst], [kwout])
    P.add("pool", lambda e: e.memset(H, 0.0), [], [(kH, g) for g in range(8)])
    P.add("pool", lambda e: e.memset(Hb, 0.0), [], [kHb])
    for ck in range(NB):
        tk = slice(ck * 128, (ck + 1) * 128)
        P.dma("act", yb_in, C.YB[ck], [("YB", ck)], [kybin])
        P.dma("act", xck, C.xT[:, :, tk], [("xT", ck // 4)], [kxck])
        chunk_step(ck, 0, it_)
        it_ += 1
        P.add("dve", lambda e: e.tensor_tensor(ydir, ydir, yb_in, ALU.add), ykeys + [kybin], ykeys)
        P.add("pool", lambda e: e.tensor_tensor(yb_in.rearrange("p (a b) -> p a b", a=32), xs_tm, bc_last(dsk, 64), ALU.mult),
              [kxs, kdsk], [kybin])
        P.add("dve", lambda e: e.tensor_tensor(ydir, ydir, yb_in, ALU.add), ykeys + [kybin], ykeys)
        P.dma("sp", sz, C.SZ[ck], [("SZ", ck)], [ksz])
        P.add("dve", lambda e: e.tensor_tensor(ydir, ydir, sz, ALU.mult), ykeys + [ksz], ykeys)
        P.add("pool", lambda e: e.tensor_tensor(sz, ydir, ydir, ALU.mult), ykeys, [ksz])
        P.add("dve", lambda e: e.reduce_sum(gss, sz.rearrange("p (a b) -> p a b", a=8), AX.X), [ksz], [kgss])
        P.add("act", lambda e: e.activation(gss, gss, AF.Ln, bias=C.eps_t[:, 0:1], scale=1.0 / 256), [kgss, C.k_eps], [kgss])
        P.add("act", lambda e: e.activation(gss, gss, AF.Exp, scale=-0.5), [kgss], [kgss])
        P.add("dve", lambda e: e.tensor_tensor(ydir.rearrange("p (a b) -> p a b", a=8), ydir.rearrange("p (a b) -> p a b", a=8),
                                               bc_last(gss, 256), ALU.mult), ykeys + [kgss], ykeys)
        P.add("act", lambda e: e.copy(ynb, ydir), ykeys, [kynb])
        for q in range(2):
            ps, kps = psum_bank(C)
            psb = ps.bitcast(BF16)
            for j in range(8):
                c = q * 8 + j
                P.add("pe", lambda e, psb=psb, j=j, c=c: e.transpose(
                    psb[:, j * 128:(j + 1) * 128], ynb[:, c * 128:(c + 1) * 128], C.ident_b), [kynb, C.k_ident_b], [kps])
            for j in range(8):
                c = q * 8 + j
                P.add("act", lambda e, psb=psb, j=j, c=c: e.activation(
                    yT[:, c, :], psb[:, j * 128:(j + 1) * 128], AF.Identity, scale=ng16[:, c:c + 1]),
                    [kps, kng16], [kyT])
        for half in range(2):
            ps, kps = psum_bank(C)
            for j in range(4):
                dc = half * 4 + j
                for kc in range(16):
                    P.mm(ps[:, j * 128:(j + 1) * 128], wout[:, kc, dc * 128:(dc + 1) * 128], yT[:, kc, :],
                         kc == 0, kc == 15, [kwout, kyT], [kps])
            for j in range(4):
                dc = half * 4 + j
                P.add("dve", lambda e, ps=ps, j=j, dc=dc: e.scalar_tensor_tensor(
                    xck[:, dc, :], ps[:, j * 128:(j + 1) * 128], mv["gate"][:, 8 + dc:8 + dc + 1], xck[:, dc, :],
                    ALU.mult, ALU.add), [kps, mv["kgate"], kxck], [kxck])
        P.dma("act", C.xT[:, :, tk], xck, [kxck], [("xT", ck // 4)])
    A.release(m_phase)

def build_program(stages, seq=4096, debug=False):
    global S
    S = seq
    nc = bass.Bass("TRN2", target_bir_lowering=False)
    C = Ctx()
    C.debug = debug
    C.dbg_off = 0
    C.dbg_map = {}
    C.dbg_keys = []
    C.nc = nc
    C.P = Prog(nc)
    C.ps_i = 0
    C.bar_gidx = 0
    C.cast_i = 0
    dt = nc.dram_tensor
    C.d_x = dt("x", [S, D], F32, kind="ExternalInput")
    C.d_out = dt("out", [S, D], F32, kind="ExternalOutput")
    C.d_ident = dt("ident", [128, 128], F32, kind="ExternalInput")
    if debug:
        C.d_dbg = dt("dbg", [128, 8192], F32, kind="ExternalOutput")
    C.d_w_mod = dt("w_mod", [2, D, 9 * D], F32, kind="ExternalInput")
    C.d_wg = dt("ffn_w_gate", [2, 2, D, DFF], F32, kind="ExternalInput")
    C.d_wu = dt("ffn_w_up", [2, 2, D, DFF], F32, kind="ExternalInput")
    C.d_wd = dt("ffn_w_down", [2, 2, DFF, D], F32, kind="ExternalInput")
    C.d_pm = dt("pmT", [96, 96], F32, kind="ExternalInput")
    C.d_pos = dt("pos", [128, S], I32, kind="ExternalInput")
    C.d_mla_win = dt("mla_w_in", [D, 672], F32, kind="ExternalInput")
    C.d_mla_wuq = dt("mla_w_uq", [384, 1536], F32, kind="ExternalInput")
    C.d_mla_wukv = dt("mla_w_ukv", [256, 2048], F32, kind="ExternalInput")
    C.d_mla_wout = dt("mla_w_out", [D, D], F32, kind="ExternalInput")
    C.OT = dt("OT_scr", [8, 128, S], BF16, kind="Internal").ap()
    C.d_masks = dt("masks", [128, 768], F32, kind="ExternalInput")
    C.d_ssd_win = dt("ssd_w_in", [D, 6208], F32, kind="ExternalInput")
    C.d_ssd_wout = dt("ssd_w_out", [2048, D], F32, kind="ExternalInput")
    C.SZ = dt("SZ_scr", [S // 128, 128, 2048], F32, kind="Internal").ap()
    C.XBC = dt("XBC_scr", [32, 128, S], BF16, kind="Internal").ap()
    C.YB = dt("YB_scr", [S // 128, 128, 2048], F32, kind="Internal").ap()
    C.d_vecs = {}
    for name, n in VEC_SPECS:
        C.d_vecs[name] = dt("v_" + name, [128, n], F32, kind="ExternalInput")
    C.xT = dt("xT_scr", [128, NDC, S], F32, kind="Internal").ap()
    C.WGU = [dt(f"wgu_scr{i}", [NF, 128, 2, 8, 128], BF16, kind="Internal").ap() for i in range(4)]
    C.WD = [dt(f"wd_scr{i}", [NDC, 128, NF, 128], BF16, kind="Internal").ap() for i in range(4)]

    ARENA_BYTES = 207 * 1024
    with ExitStack() as es:
        ah = es.enter_context(nc.sbuf_tensor("arena", [128, ARENA_BYTES // 4], F32))
        C.arena = Arena(ah, ARENA_BYTES, C.P)
        C.ps = [es.enter_context(nc.psum_tensor(f"ps{i}", [128, 512], F32))[:] for i in range(8)]
        eng_sems = {e: es.enter_context(nc.semaphore(f"sem_{e}")) for e in ENGS}
        dma_sems = [es.enter_context(nc.semaphore(f"dsem{i}")) for i in range(N_DMA_SEMS)]

        setup_consts(C)
        if debug:
            C.dbg_stage, _ = C.arena.alloc('dbg_stage', [3584], F32)
        load_transpose_x(C)
        compute_mod(C)
        for (l, w) in stages.get("ffn", []):
            convert_ffn_weights(C, l, w)
        for st_ in stages.get("order", []):
            if BARRIERS:
                phase_barrier(C)
            if st_[0] == "ffn":
                ffn_phase(C, st_[1], st_[2])
            elif st_[0] == "mla":
                mla_phase(C)
            elif st_[0] == "ssd":
                ssd_phase(C)
        store_transpose_out(C)
        C.P.emit(eng_sems, dma_sems)
    C.nc = nc
    return C


VEC_SPECS = [("c", 8), ("b_mod0", 72), ("b_mod1", 72), ("norm_g0", 24), ("norm_g1", 24),
             ("mla_qg", 1), ("mla_kg", 1), ("mla_qng", 3), ("mla_kvng", 2), ("invf", 1),
             ("ssd_cw", 160), ("ssd_cb", 32), ("ssd_dtb", 64), ("ssd_alog", 64), ("ssd_dskip", 32), ("ssd_ng16", 16)]


def _consts():
    inv = (10000.0 ** (-np.arange(0, 32, 2, dtype=np.float32) / 32)).astype(np.float32)
    invf = np.zeros((128, 1), np.float32)
    invf[64:80, 0] = inv
    invf[80:96, 0] = inv
    pm = np.zeros((96, 96), np.float32)
    for i in range(16):
        pm[80 + i, 64 + i] = -1.0
        pm[64 + i, 80 + i] = 1.0
    return invf, pm


INVF, PMT = _consts()


def _masks():
    k = np.arange(128)[:, None]
    j = np.arange(128)[None, :]
    Lf = (k > j); Rf = (k <= j); Lb = (k < j); Rb = (k >= j)
    Vf = (j >= k)
    Vb = (j <= k)
    return np.concatenate([m.astype(np.float32) for m in (Lf, Rf, Lb, Rb, Vf, Vb)], axis=1)


MASKS = _masks()


def host_vecs(inputs, b):
    f = np.float32
    v = {}
    v["c"] = np.ascontiguousarray(inputs["c"][b].reshape(8, 128).T.astype(f))
    for l in range(2):
        v[f"b_mod{l}"] = np.ascontiguousarray(inputs["b_mod"][l].reshape(72, 128).T.astype(f))
        v[f"norm_g{l}"] = np.ascontiguousarray(inputs["norm_g"][l].reshape(24, 128).T.astype(f))
    def col(a, n=128):
        o = np.zeros((128, 1), f)
        o[:len(a), 0] = a
        return o
    v["mla_qg"] = col(inputs["mla_q_head_g"][0])
    v["mla_kg"] = col(inputs["mla_k_head_g"][0])
    v["mla_qng"] = np.ascontiguousarray(inputs["mla_q_norm_g"][0].reshape(3, 128).T.astype(f))
    v["mla_kvng"] = np.ascontiguousarray(inputs["mla_kv_norm_g"][0].reshape(2, 128).T.astype(f))
    v["invf"] = INVF
    cwt = inputs["ssd_conv_w"][0]
    v["ssd_cw"] = np.ascontiguousarray(cwt.reshape(5, 32, 128).transpose(2, 1, 0).reshape(128, 160).astype(f))
    v["ssd_cb"] = np.ascontiguousarray(inputs["ssd_conv_b"][0].reshape(32, 128).T.astype(f))
    v["ssd_dtb"] = np.ascontiguousarray(np.broadcast_to(inputs["ssd_dt_bias"][0].reshape(1, 64), (128, 64)).astype(f))
    v["ssd_alog"] = np.ascontiguousarray(np.broadcast_to(inputs["ssd_a_log"][0].reshape(1, 64), (128, 64)).astype(f))
    v["ssd_ng16"] = np.ascontiguousarray(inputs["ssd_norm_g"][0].reshape(16, 128).T.astype(f))
    v["ssd_dskip"] = np.ascontiguousarray(np.broadcast_to(inputs["ssd_d"][0].reshape(1, 32), (128, 32)).astype(f))
    return v


def run(inputs, stages, seq=4096, cores=8, debug=False):
    C = build_program(stages, seq, debug)
    nc = C.nc
    ident = np.eye(128, dtype=np.float32)
    in_maps = []
    for b in range(cores):
        m = {
            "x": np.ascontiguousarray(inputs["x"][b][:seq]),
            "ident": ident,
            "w_mod": inputs["w_mod"],
            "ffn_w_gate": inputs["ffn_w_gate"],
            "ffn_w_up": inputs["ffn_w_up"],
            "ffn_w_down": inputs["ffn_w_down"],
            "pmT": PMT,
            "masks": MASKS,
            "ssd_w_in": inputs["ssd_w_in"][0], "ssd_w_out": inputs["ssd_w_out"][0],

            "pos": np.ascontiguousarray(np.broadcast_to(inputs["positions"][b][None, :seq], (128, seq)).astype(np.int32)),
            "mla_w_in": inputs["mla_w_in"][0], "mla_w_uq": inputs["mla_w_uq"][0],
            "mla_w_ukv": inputs["mla_w_ukv"][0], "mla_w_out": inputs["mla_w_out"][0],
        }
        for k, a in host_vecs(inputs, b).items():
            m["v_" + k] = a
        in_maps.append(m)
    res = run_bass_kernel_spmd(nc, in_maps, core_ids=list(range(cores)))
    out = np.stack([r["out"] for r in res.results], axis=0)
    if debug:
        return out, {k: res.results[0]["dbg"][:, o:o + n] for k, (o, n) in C.dbg_map.items()}
    return out


def kernel(**inputs):
    inputs = {k: np.asarray(v) for k, v in inputs.items()}
    stages = {
        "ffn": [(0, 0), (0, 1), (1, 0), (1, 1)],
        "order": [("ffn", 0, 0), ("ssd",), ("ffn", 0, 1), ("ffn", 1, 0), ("mla",), ("ffn", 1, 1)],
    }
    return run(inputs, stages).astype(np.float32)
```

```python
import numpy as np
from contextlib import ExitStack
import concourse.bass as bass
import concourse.mybir as mybir
from concourse.bass_utils import run_bass_kernel_spmd

F32 = mybir.dt.float32
BF16 = mybir.dt.bfloat16
I32 = mybir.dt.int32
AF = mybir.ActivationFunctionType
ALU = mybir.AluOpType
AX = mybir.AxisListType

D = 1024
S = 4096
DFF = 2816
NF = DFF // 128
NDC = D // 128
EPS = 1e-6
N_DMA_SEMS = 40
DBG_M = 0
BARRIERS = False
ENGS = ("pe", "act", "dve", "pool", "sp")


class Op:
    __slots__ = ("eng", "fn", "deps", "sig", "seq", "dma", "semid", "semval", "prev", "pos", "gidx")


class Prog:
    def __init__(self, nc):
        self.nc = nc
        self.ops = {e: [] for e in ENGS}
        self.lastw = {}
        self.readers = {}
        self.ndma = 0
        self.dma_last = [None] * N_DMA_SEMS
        self.dma_cnt = [0] * N_DMA_SEMS
        self.nops = 0
        self.bases = set()
        self.touched = {}
        self.inherit = {}
        self.seen = set()

    def base_of(self, k):
        for _ in range(4):
            if k in self.bases:
                return k
            if isinstance(k, tuple) and len(k):
                k = k[0]
            else:
                return None
        return None

    def add(self, eng, fn, reads=(), writes=(), dma=False):
        op = Op()
        op.eng = eng
        op.fn = fn
        op.dma = dma
        op.sig = False
        op.seq = 0
        op.gidx = self.nops
        self.nops += 1
        deps = set()
        for k in list(reads) + list(writes):
            b = self.base_of(k)
            if b is None:
                continue
            if k not in self.seen:
                self.seen.add(k)
                deps.update(self.inherit.get(b, ()))
            t = self.touched.setdefault(b, {})
            if dma:
                t[("dma", op.gidx)] = op
            else:
                t[eng] = op
        for k in reads:
            w = self.lastw.get(k)
            if w is not None:
                deps.add(w)
        for k in writes:
            w = self.lastw.get(k)
            if w is not None:
                deps.add(w)
            for r in self.readers.get(k, ()):
                deps.add(r)
        for k in reads:
            self.readers.setdefault(k, []).append(op)
        for k in writes:
            self.lastw[k] = op
            self.readers[k] = []
        deps.discard(op)
        op.prev = None
        if dma:
            s = self.ndma % N_DMA_SEMS
            self.ndma += 1
            op.semid = s
            self.dma_cnt[s] += 16
            op.semval = self.dma_cnt[s]
            op.prev = self.dma_last[s]
            self.dma_last[s] = op
        op.deps = deps
        op.pos = len(self.ops[eng])
        self.ops[eng].append(op)
        return op

    def dma(self, eng, out, in_, reads, writes, **kw):
        return self.add(eng, lambda e: e.dma_start(out=out, in_=in_, **kw), reads, writes, dma=True)

    def mm(self, out, lhsT, rhs, start, stop, reads, writes):
        return self.add("pe", lambda e: e.matmul(out, lhsT, rhs, start=start, stop=stop), reads, writes)

    def emit(self, eng_sems, dma_sems):
        nc = self.nc

        def needs_sync(op, d):
            if d.dma:
                return True
            if d.eng == op.eng and not op.dma:
                if op.eng == "pe":
                    return False
                return True
            if d.eng == op.eng and op.dma:
                return True
            return True

        for e in ENGS:
            for op in self.ops[e]:
                for d in op.deps:
                    if needs_sync(op, d) and not d.dma:
                        d.sig = True
        for e in ENGS:
            n = 0
            for op in self.ops[e]:
                if op.sig and not op.dma:
                    n += 1
                    op.seq = n

        def emit_engine(ename, eobj):
            waited = {}
            for op in self.ops[ename]:
                need = {}
                for d in op.deps:
                    if not needs_sync(op, d):
                        continue
                    if d.dma:
                        key = ("d", d.semid)
                        val = d.semval
                    else:
                        key = ("e", d.eng)
                        val = d.seq
                    if need.get(key, 0) < val:
                        need[key] = val
                if op.dma and op.prev is not None:
                    key = ("d", op.prev.semid)
                    if need.get(key, 0) < op.prev.semval:
                        need[key] = op.prev.semval
                pend = []
                for key, val in need.items():
                    if waited.get(key, 0) >= val:
                        continue
                    waited[key] = val
                    sem = dma_sems[key[1]] if key[0] == "d" else eng_sems[key[1]]
                    pend.append((key[0] == "d", sem, val))
                pend.sort(key=lambda t: t[0])
                if op.fn is None:
                    for _, sem, val in pend:
                        eobj.wait_ge(sem, val)
                    continue
                for _, sem, val in pend[:-1]:
                    eobj.wait_ge(sem, val)
                ins = op.fn(eobj)
                if pend:
                    ins._wait_ge(pend[-1][1], pend[-1][2])
                if op.dma:
                    ins.then_inc(dma_sems[op.semid], 16)
                elif op.sig:
                    ins.then_inc(eng_sems[ename], 1)

        with nc.Block() as block:
            @block.tensor
            def _(e):
                emit_engine("pe", e)

            @block.scalar
            def _(e):
                emit_engine("act", e)

            @block.vector
            def _(e):
                emit_engine("dve", e)

            @block.gpsimd
            def _(e):
                emit_engine("pool", e)

            @block.sync
            def _(e):
                emit_engine("sp", e)


class Arena:
    def __init__(self, handle, nbytes, prog):
        self.h = handle
        self.cap = nbytes
        self.top = 0
        self.gen = 0
        self.P = prog
        self.allocs = []

    def mark(self):
        return self.top

    def release(self, m):
        self.top = m

    def alloc(self, name, shape, dtype):
        esz = 2 if dtype == BF16 else 4
        n = 1
        for s_ in shape:
            n *= s_
        nbytes = (n * esz + 63) // 64 * 64
        off = self.top
        assert off + nbytes <= self.cap, f"SBUF arena overflow allocating {name}: {off}+{nbytes}>{self.cap}"
        self.top += nbytes
        ap = self.h[:, off // 4:(off + nbytes) // 4]
        if dtype != F32:
            ap = ap.bitcast(dtype)
        ap = ap[:, 0:n]
        if len(shape) == 2:
            ap = ap.rearrange("p (a b) -> p a b", a=shape[0])
        elif len(shape) == 3:
            ap = ap.rearrange("p (a b c) -> p a b c", a=shape[0], b=shape[1])
        elif len(shape) == 4:
            ap = ap.rearrange("p (a b c d) -> p a b c d", a=shape[0], b=shape[1], c=shape[2])
        self.gen += 1
        key = (name, self.gen)
        P = self.P
        P.bases.add(key)
        inh = set()
        for (a0, a1, ok) in self.allocs:
            if a0 < off + nbytes and off < a1:
                inh.update(P.touched.get(ok, {}).values())
                inh.update(P.inherit.get(ok, ()))
        P.inherit[key] = inh
        self.allocs = [(a0, a1, ok) for (a0, a1, ok) in self.allocs if not (a0 >= off and a1 <= off + nbytes)]
        self.allocs.append((off, off + nbytes, key))
        return ap, key


class Ctx:
    pass


def dbg(C, name, ap, key, n):
    if not C.debug:
        return
    P, A = C.P, C.arena
    st = C.dbg_stage[:, C.dbg_off:C.dbg_off + n]
    kst = ("dbgst", name)
    P.add("pool", lambda e: e.tensor_copy(st, ap), [key], [kst])
    off = C.dbg_off
    C.dbg_off += n
    C.dbg_map[name] = (off, n)
    P.dma("sp", C.d_dbg.ap()[:, off:off + n], st, [kst], [("DBG", name)])
    C.dbg_keys.append(("DBG", name))


def rr(lst, i):
    return lst[i % len(lst)]


def setup_consts(C):
    P, A = C.P, C.arena
    C.ident_f, C.k_ident_f = A.alloc("ident_f", [128], F32)
    C.ident_b, C.k_ident_b = A.alloc("ident_b", [128], BF16)
    C.onesD_b, C.k_onesD = A.alloc("onesD", [128], BF16)
    P.dma("sp", C.ident_f, C.d_ident.ap(), [], [C.k_ident_f])
    P.add("dve", lambda e: e.tensor_copy(C.ident_b, C.ident_f), [C.k_ident_f], [C.k_ident_b])
    P.add("pool", lambda e: e.memset(C.onesD_b, 1.0 / D), [], [C.k_onesD])
    C.eps_t, C.k_eps = A.alloc("eps_t", [1], F32)
    P.add("pool", lambda e: e.memset(C.eps_t, EPS), [], [C.k_eps])
    C.vec = {}
    for name, t in C.d_vecs.items():
        n = t.ap().shape[1]
        ap, k = A.alloc("v_" + name, [n], F32)
        P.dma("sp", ap, t.ap(), [], [k])
        C.vec[name] = (ap, k)


def phase_barrier(C):
    P = C.P
    last = [P.ops[e][-1] for e in ENGS if P.ops[e]]
    last = [o for o in last if o.fn is not None]
    dmas = [o for e in ENGS for o in P.ops[e] if o.dma and o.gidx >= C.bar_gidx]
    C.bar_gidx = P.nops
    for e in ENGS:
        op = P.add(e, None)
        op.deps.update(last)
        op.deps.update(dmas)
        op.deps.discard(op)


def psum_bank(C):
    i = C.ps_i % 8
    C.ps_i += 1
    return C.ps[i], ("ps", i)


def load_transpose_x(C):
    P, A = C.P, C.arena
    m0 = A.mark()
    xin = [A.alloc(f"xin{i}", [4, D], F32) for i in range(2)]
    xtt = [A.alloc(f"xtt{i}", [NDC, 512], F32) for i in range(2)]
    xd = C.d_x.ap().rearrange("(g j p) d -> g p j d", j=4, p=128)
    for g in range(S // 512):
        xi, kxi = xin[g % 2]
        xt, kxt = xtt[g % 2]
        P.dma("sp", xi, xd[g], [], [kxi])
        for dc in range(NDC):
            ps, kps = psum_bank(C)
            for j in range(4):
                P.add("pe", lambda e, ps=ps, xi=xi, j=j, dc=dc: e.transpose(
                    ps[:, j * 128:(j + 1) * 128], xi[:, j, dc * 128:(dc + 1) * 128], C.ident_f),
                    [kxi, C.k_ident_f], [kps])
            eng = "act" if dc % 2 == 0 else "dve"
            if eng == "act":
                P.add("act", lambda e, ps=ps, xt=xt, dc=dc: e.copy(xt[:, dc, :], ps), [kps], [kxt])
            else:
                P.add("dve", lambda e, ps=ps, xt=xt, dc=dc: e.tensor_copy(xt[:, dc, :], ps), [kps], [kxt])
        P.dma("sp", C.xT[:, :, g * 512:(g + 1) * 512], xt, [kxt], [("xT", g)])
    A.release(m0)


def store_transpose_out(C):
    P, A = C.P, C.arena
    m0 = A.mark()
    xtt = [A.alloc(f"oxt{i}", [NDC, 512], F32) for i in range(2)]
    xo = [A.alloc(f"oxo{i}", [4, D], F32) for i in range(2)]
    od = C.d_out.ap().rearrange("(g j p) d -> g p j d", j=4, p=128)
    for g in range(S // 512):
        xt, kxt = xtt[g % 2]
        xo_, kxo = xo[g % 2]
        P.dma("sp", xt, C.xT[:, :, g * 512:(g + 1) * 512], [("xT", g)], [kxt])
        for j in range(4):
            for half in range(2):
                ps, kps = psum_bank(C)
                for q in range(4):
                    dc = half * 4 + q
                    P.add("pe", lambda e, ps=ps, xt=xt, j=j, dc=dc, q=q: e.transpose(
                        ps[:, q * 128:(q + 1) * 128], xt[:, dc, j * 128:(j + 1) * 128], C.ident_f),
                        [kxt, C.k_ident_f], [kps])
                if half == 0:
                    P.add("act", lambda e, ps=ps, xo_=xo_, j=j: e.copy(xo_[:, j, 0:512], ps), [kps], [kxo])
                else:
                    P.add("dve", lambda e, ps=ps, xo_=xo_, j=j: e.tensor_copy(xo_[:, j, 512:1024], ps), [kps], [kxo])
        P.dma("sp", od[g], xo_, [kxo], [("OUT", g)])
    A.release(m0)
    P.add("sp", None, [("OUT", g) for g in range(S // 512)] + C.dbg_keys, [])


def compute_mod(C):
    P, A = C.P, C.arena
    cvec, kc_ = C.vec["c"]
    C.cond, C.k_cond = A.alloc("cond", [8], F32)
    P.add("act", lambda e: e.activation(C.cond, cvec, AF.Silu), [kc_], [C.k_cond])
    C.mod = []
    m0 = None
    for l in range(2):
        mod, kmod = A.alloc(f"mod{l}", [72], F32)
        C.mod.append((mod, kmod))
    m0 = A.mark()
    wb = [A.alloc(f"wmod{i}", [8, 1024], F32) for i in range(2)]
    it = 0
    for l in range(2):
        mod, kmod = C.mod[l]
        bm, kbm = C.vec[f"b_mod{l}"]
        wd = C.d_w_mod.ap()[l].rearrange("(kc p) n -> p kc n", p=128)
        ps, kps = psum_bank(C)
        for cb in range(9):
            w, kw = wb[it % 2]
            it += 1
            P.dma("sp", w, wd[:, :, cb * 1024:(cb + 1) * 1024], [], [kw])
            for j in range(8):
                col = cb * 8 + j
                for kc in range(8):
                    P.mm(ps[:, col:col + 1], w[:, kc, j * 128:(j + 1) * 128], C.cond[:, kc:kc + 1],
                         kc == 0, kc == 7, [kw, C.k_cond], [kps])
        P.add("dve", lambda e, mod=mod, ps=ps, bm=bm: e.tensor_tensor(mod, ps[:, 0:72], bm, ALU.add),
              [kps, kbm], [kmod])
    A.release(m0)
    C.modv = []
    for l in range(2):
        mod, kmod = C.mod[l]
        g, kg = C.vec[f"norm_g{l}"]
        a, ka = A.alloc(f"moda{l}", [24], F32)
        gt, kgt = A.alloc(f"modg{l}", [24], F32)
        for sub in range(3):
            sc = mod[:, (sub * 3 + 1) * 8:(sub * 3 + 2) * 8]
            P.add("dve", lambda e, a=a, sub=sub, sc=sc, g=g: e.scalar_tensor_tensor(
                a[:, sub * 8:(sub + 1) * 8], sc, 1.0, g[:, sub * 8:(sub + 1) * 8], ALU.add, ALU.mult),
                [kmod, kg], [ka])
            gsrc = mod[:, (sub * 3 + 2) * 8:(sub * 3 + 3) * 8]
            fac = 1.0 if sub == 1 else 0.5
            P.add("dve", lambda e, gt=gt, sub=sub, gsrc=gsrc, fac=fac: e.tensor_scalar(
                gt[:, sub * 8:(sub + 1) * 8], gsrc, fac, None, ALU.mult), [kmod], [kgt])
        C.modv.append(dict(a=a, ka=ka, gate=gt, kgate=kgt, mod=mod, kmod=kmod))
        if l == 0:
            dbg(C, 'mod0', mod, kmod, 72)
            dbg(C, 'a0', a, ka, 24)
            dbg(C, 'gate0', gt, kgt, 24)


def cast_engine(C):
    e = ("pool", "dve", "act")[C.cast_i % 3]
    C.cast_i += 1
    return e


def emit_cast(P, eng, out, in_, reads, writes):
    if eng == "act":
        P.add("act", lambda e: e.copy(out, in_), reads, writes)
    else:
        P.add(eng, lambda e: e.tensor_copy(out, in_), reads, writes)


def convert_ffn_weights(C, l, w):
    P, A = C.P, C.arena
    idx = l * 2 + w
    m0 = A.mark()
    stg = [A.alloc(f"cst{i}", [4096], F32) for i in range(3)]
    stb = [A.alloc(f"csb{i}", [4096], BF16) for i in range(3)]
    it = 0
    for gi, src in enumerate((C.d_wg, C.d_wu)):
        sd = src.ap()[l, w].rearrange("(kc p) n -> p kc n", p=128)
        for fb in range(0, NF, 4):
            nf = min(4, NF - fb)
            sf, ksf = stg[it % 3]
            sb, ksb = stb[it % 3]
            it += 1
            sfv = sf[:, 0:8 * nf * 128].rearrange("p (kc n) -> p kc n", kc=8)
            P.dma("sp", sfv, sd[:, :, fb * 128:(fb + nf) * 128], [], [ksf])
            sbv = sb[:, 0:nf * 8 * 128].rearrange("p (f kc m) -> p f kc m", f=nf, kc=8)
            emit_cast(P, cast_engine(C), sbv, sfv.rearrange("p kc (f m) -> p f kc m", f=nf), [ksf], [ksb])
            dst = C.WGU[idx][fb:fb + nf, :, gi, :, :].rearrange("f p kc m -> p f (kc m)")
            P.dma("sp", dst, sbv.rearrange("p f kc m -> p f (kc m)"), [ksb], [("WGU", idx, f_) for f_ in range(fb, fb + nf)])
    sd = C.d_wd.ap()[l, w].rearrange("(fc p) n -> p fc n", p=128)
    for fb in range(0, NF, 4):
        nf = min(4, NF - fb)
        sf, ksf = stg[it % 3]
        sb, ksb = stb[it % 3]
        it += 1
        sfv = sf[:, 0:nf * 1024].rearrange("p (fc n) -> p fc n", fc=nf)
        P.dma("sp", sfv, sd[:, fb:fb + nf, :], [], [ksf])
        sbv = sb[:, 0:8 * nf * 128].rearrange("p (dc fc m) -> p dc fc m", dc=8, fc=nf)
        emit_cast(P, cast_engine(C), sbv, sfv.rearrange("p fc (dc m) -> p dc fc m", dc=8), [ksf], [ksb])
        dst = C.WD[idx][:, :, fb:fb + nf, :].rearrange("dc p fc m -> p dc (fc m)")
        P.dma("sp", dst, sbv.rearrange("p dc fc m -> p dc (fc m)"), [ksb], [("WD", idx, dc) for dc in range(8)])
    A.release(m0)


def norm_modulate(C, xt, kxt, h, kh, T, l, sub, scratch):
    P = C.P
    mv = C.modv[l]
    sq, ksq, rstd, krstd, tmp, ktmp = scratch
    shift = mv["mod"][:, (sub * 3) * 8:(sub * 3 + 1) * 8]
    for st in range(T // 512):
        sl = slice(st * 512, (st + 1) * 512)
        for dc in range(NDC):
            P.add("act", lambda e, dc=dc, sl=sl: e.activation(sq[:, dc, sl], xt[:, dc, sl], AF.Square),
                  [kxt], [(ksq, st)])
        ps, kps = psum_bank(C)
        for dc in range(NDC):
            P.mm(ps, C.onesD_b, sq[:, dc, sl], dc == 0, dc == NDC - 1, [(ksq, st), C.k_onesD], [kps])
        P.add("act", lambda e, ps=ps, sl=sl: e.activation(rstd[:, sl], ps, AF.Ln, bias=C.eps_t[:, 0:1]),
              [kps, C.k_eps], [(krstd, st)])
        P.add("act", lambda e, sl=sl: e.activation(rstd[:, sl], rstd[:, sl], AF.Exp, scale=-0.5),
              [(krstd, st)], [(krstd, st)])
        for dc in range(NDC):
            tm, ktm = tmp[dc % len(tmp)]
            eng = "dve" if dc % 2 == 0 else "pool"
            P.add(eng, lambda e, tm=tm, dc=dc, sl=sl: e.tensor_tensor(tm, xt[:, dc, sl], rstd[:, sl], ALU.mult),
                  [kxt, (krstd, st)], [ktm])
            P.add("act", lambda e, tm=tm, dc=dc, sl=sl: e.activation(
                h[:, dc, sl], tm, AF.Identity, bias=shift[:, dc:dc + 1],
                scale=mv["a"][:, sub * 8 + dc:sub * 8 + dc + 1]),
                [ktm, mv["ka"], mv["kmod"]], [(kh, st)])


def ffn_phase(C, l, w):
    P, A = C.P, C.arena
    idx = l * 2 + w
    sub = 0 if w == 0 else 2
    mv = C.modv[l]
    T = 1024
    m0 = A.mark()
    xb = [A.alloc(f"fx{i}", [NDC, T], F32) for i in range(2)]
    h, kh = A.alloc("fh", [NDC, T], BF16)
    act, kact = A.alloc("fact", [NF, T], BF16)
    sq, ksq = A.alloc("fsq", [NDC, T], BF16)
    rstd, krstd = A.alloc("frstd", [T], F32)
    tmp = [A.alloc(f"ftmp{i}", [512], F32) for i in range(3)]
    sg = [A.alloc(f"fsg{i}", [512], F32) for i in range(3)]
    wgu = [A.alloc(f"fwgu{i}", [2, 8, 128], BF16) for i in range(4)]
    wdb = [A.alloc(f"fwd{i}", [NF, 128], BF16) for i in range(3)]
    NM = S // T
    items = []
    for m in range(NM):
        for f in range(NF):
            items.append(("g", f))
        for dc in range(NDC):
            items.append(("d", dc))
    issued = [0]
    cnt = {"g": 0, "d": 0}
    slot_of = {}

    def prefetch(upto):
        while issued[0] < min(upto, len(items)):
            kind, j = items[issued[0]]
            if kind == "g":
                buf, kb = wgu[cnt["g"] % 4]
                cnt["g"] += 1
                P.dma("sp", buf, C.WGU[idx][j].rearrange("p g kc m -> p g kc m"), [("WGU", idx, j)], [kb])
            else:
                buf, kb = wdb[cnt["d"] % 3]
                cnt["d"] += 1
                P.dma("sp", buf, C.WD[idx][j], [("WD", idx, j)], [kb])
            slot_of[issued[0]] = (buf, kb)
            issued[0] += 1

    def load_x(m):
        xt, kxt = xb[m % 2]
        P.dma("act", xt, C.xT[:, :, m * T:(m + 1) * T], [("xT", 2 * m), ("xT", 2 * m + 1)], [kxt])

    load_x(0)
    pos = 0
    for m in range(NM):
        xt, kxt = xb[m % 2]
        prefetch(pos + 3)
        norm_modulate(C, xt, kxt, h, kh, T, l, sub, (sq, ksq, rstd, krstd, tmp, None))
        if m + 1 < NM:
            load_x(m + 1)
        if m == DBG_M and idx == 0:
            dbg(C, 'rstd', rstd[:, 0:512], (krstd, 0), 512)
            dbg(C, 'h0', h[:, 0, 0:512], (kh, 0), 512)
            dbg(C, 'h7', h[:, 7, 0:512], (kh, 0), 512)
        for f in range(NF):
            prefetch(pos + 3)
            wbuf, kwb = slot_of.pop(pos)
            pos += 1
            for st in range(T // 512):
                sl = slice(st * 512, (st + 1) * 512)
                psg, kpsg = psum_bank(C)
                psu, kpsu = psum_bank(C)
                for kc in range(8):
                    P.mm(psg, wbuf[:, 0, kc, :], h[:, kc, sl], kc == 0, kc == 7, [kwb, (kh, st)], [kpsg])
                for kc in range(8):
                    P.mm(psu, wbuf[:, 1, kc, :], h[:, kc, sl], kc == 0, kc == 7, [kwb, (kh, st)], [kpsu])
                s_, ks_ = sg[(f * 2 + st) % 3]
                P.add("act", lambda e, s_=s_, psg=psg: e.activation(s_, psg, AF.Silu), [kpsg], [ks_])
                P.add("dve", lambda e, s_=s_, psu=psu, f=f, sl=sl: e.tensor_tensor(act[:, f, sl], s_, psu, ALU.mult),
                      [ks_, kpsu], [(kact, st)])
        for dc in range(NDC):
            prefetch(pos + 3)
            wbuf, kwb = slot_of.pop(pos)
            pos += 1
            for st in range(T // 512):
                sl = slice(st * 512, (st + 1) * 512)
                pso, kpso = psum_bank(C)
                for f in range(NF):
                    P.mm(pso, wbuf[:, f, :], act[:, f, sl], f == 0, f == NF - 1, [kwb, (kact, st)], [kpso])
                P.add("dve", lambda e, pso=pso, dc=dc, sl=sl, xt=xt: e.scalar_tensor_tensor(
                    xt[:, dc, sl], pso, mv["gate"][:, sub * 8 + dc:sub * 8 + dc + 1], xt[:, dc, sl],
                    ALU.mult, ALU.add), [kpso, mv["kgate"], kxt], [kxt])
        if m == DBG_M and idx == 0:
            dbg(C, 'act0', act[:, 0, 0:512], (kact, 0), 512)
            dbg(C, 'act21', act[:, 21, 0:512], (kact, 0), 512)
            dbg(C, 'xo0', xt[:, 0, 0:512], kxt, 512)
        P.dma("act", C.xT[:, :, m * T:(m + 1) * T], xt, [kxt], [("xT", 2 * m), ("xT", 2 * m + 1)])
    A.release(m0)


PI = float(np.pi)


def const_tile(C, name, val, dtype=F32, n=1):
    ap, k = C.arena.alloc("c_" + name, [n], dtype)
    C.P.add("pool", lambda e: e.memset(ap, val), [], [k])
    return ap, k


def load_cast_weight(C, name, src_ap, shape, eng="sp"):
    P, A = C.P, C.arena
    n = 1
    for s_ in shape:
        n *= s_
    wb, kwb = A.alloc(name, shape, BF16)
    m0 = A.mark()
    CH = 2048
    flat_b = wb
    stg = [A.alloc(f"{name}_st{i}", [CH], F32) for i in range(2)]
    A.release(m0)
    return wb, kwb, stg


def mla_phase(C):
    P, A = C.P, C.arena
    l = 1
    mv = C.modv[l]
    T = 512
    NT = S // T
    NB = S // 128
    SCALE = float(96 ** -0.5)
    m_phase = A.mark()
    ones384, k384 = const_tile(C, "o384", 1.0 / 384, BF16, 128)
    ones256, k256 = const_tile(C, "o256", 1.0 / 256, BF16, 128)
    ones96, k96 = const_tile(C, "o96", 1.0 / 96, BF16, 128)
    onesrow, krow = const_tile(C, "orow", 1.0, F32, 128)
    hpi, khpi = const_tile(C, "hpi", PI / 2)
    nhpi, knhpi = const_tile(C, "nhpi", -PI / 2)
    pmT_f, kpmf = A.alloc("pmT_f", [96], F32)
    pmT, kpm = A.alloc("pmT", [96], BF16)
    P.dma("sp", pmT_f[0:96, :], C.d_pm.ap(), [], [kpmf])
    P.add("dve", lambda e: e.tensor_copy(pmT[0:96, :], pmT_f[0:96, :]), [kpmf], [kpm])
    gq, kgq = C.vec["mla_qg"]
    gk, kgk = C.vec["mla_kg"]
    gql, kgql = C.vec["mla_qng"]
    gkvl, kgkvl = C.vec["mla_kvng"]
    invf, kinvf = C.vec["invf"]
    qn, kqn = A.alloc("qn", [3, S], BF16)
    kvn, kkvn = A.alloc("kvn", [2, S], BF16)
    kpe, kkpe = A.alloc("kpe", [S], F32)
    sqk, ksqk = A.alloc("sqk", [S], BF16)
    COS, kcos = A.alloc("COS", [S], F32)
    SIN, ksin = A.alloc("SIN", [S], F32)
    wuq, kwuq = A.alloc("wuq", [3, 1536], BF16)
    wukv, kwukv = A.alloc("wukv", [2, 2048], BF16)

    m0 = A.mark()
    posi, kposi = A.alloc("posi", [S], I32)
    ang, kang = A.alloc("ang", [S], F32)
    t1, kt1 = A.alloc("rt1", [S], F32)
    ti, kti = A.alloc("rti", [S], I32)
    P.dma("sp", posi, C.d_pos.ap(), [], [kposi])
    P.add("dve", lambda e: e.tensor_copy(ang, posi), [kposi], [kang])
    P.add("dve", lambda e: e.tensor_scalar(ang, ang, invf[:, 0:1], None, ALU.mult), [kang, kinvf], [kang])
    for (dst, kdst, shift) in ((SIN, ksin, 0.0), (COS, kcos, PI / 2)):
        P.add("dve", lambda e, shift=shift: e.tensor_scalar(t1, ang, shift, 1.0 / (2 * PI), ALU.add, ALU.mult),
              [kang], [kt1])
        P.add("dve", lambda e: e.tensor_copy(ti, t1), [kt1], [kti])
        P.add("dve", lambda e: e.tensor_copy(t1, ti), [kti], [kt1])
        P.add("dve", lambda e: e.scalar_tensor_tensor(t1, t1, -2 * PI, ang, ALU.mult, ALU.add), [kt1, kang], [kt1])
        bias_ap = nhpi if shift == 0.0 else None
        if shift == 0.0:
            P.add("act", lambda e: e.activation(t1, t1, AF.Abs, bias=nhpi[:, 0:1]), [kt1, knhpi], [kt1])
        else:
            P.add("act", lambda e: e.activation(t1, t1, AF.Abs), [kt1], [kt1])
        P.add("act", lambda e, dst=dst: e.activation(dst, t1, AF.Sin, bias=hpi[:, 0:1], scale=-1.0),
              [kt1, khpi], [kdst])
    A.release(m0)

    def load_cast(dst, kdst, src, nk, ncol, colchunk):
        stg = [A.alloc(f"wst{i}", [nk, colchunk], F32) for i in range(2)]
        it = 0
        for c0 in range(0, ncol, colchunk):
            cw = min(colchunk, ncol - c0)
            st, kst = stg[it % 2]
            it += 1
            P.dma("sp", st[:, :, 0:cw], src[:, :, c0:c0 + cw], [], [kst])
            emit_cast(P, cast_engine(C), dst[:, :, c0:c0 + cw], st[:, :, 0:cw], [kst], [kdst])

    m1 = A.mark()
    mm_ = A.mark()
    load_cast(wuq, kwuq, C.d_mla_wuq.ap().rearrange("(kc p) n -> p kc n", p=128), 3, 1536, 512)
    load_cast(wukv, kwukv, C.d_mla_wukv.ap().rearrange("(kc p) n -> p kc n", p=128), 2, 2048, 512)
    A.release(mm_)
    win, kwin = A.alloc("win", [8, 736], BF16)
    P.add("pool", lambda e: e.memset(win, 0.0), [], [kwin])
    wsrc = C.d_mla_win.ap().rearrange("(kc p) n -> p kc n", p=128)
    mm2 = A.mark()
    stg = [A.alloc(f"wst_in{i}", [8, 224], F32) for i in range(2)]
    for i, c0 in enumerate(range(0, 672, 224)):
        st, kst = stg[i % 2]
        P.dma("sp", st, wsrc[:, :, c0:c0 + 224], [], [kst])
        if c0 + 224 <= 640:
            emit_cast(P, cast_engine(C), win[:, :, c0:c0 + 224], st, [kst], [kwin])
        else:
            nl = 640 - c0
            emit_cast(P, cast_engine(C), win[:, :, c0:640], st[:, :, 0:nl], [kst], [kwin])
            emit_cast(P, cast_engine(C), win[:, :, 704:736], st[:, :, nl:nl + 32], [kst], [kwin])

    A.release(mm2)
    xb = [A.alloc(f"mx{i}", [NDC, T], F32) for i in range(2)]
    h, kh = A.alloc("mh", [NDC, T], BF16)
    sq, ksq = A.alloc("msq", [NDC, T], BF16)
    rstd, krstd = A.alloc("mrstd", [T], F32)
    tmp = [A.alloc(f"mtmp{i}", [512], F32) for i in range(3)]
    lsq, klsq = A.alloc("mlsq", [3, T], BF16)
    lrs, klrs = A.alloc("mlrs", [T], F32)

    def load_x(m):
        xt, kxt = xb[m % 2]
        P.dma("act", xt, C.xT[:, :, m * T:(m + 1) * T], [("xT", m)], [kxt])

    load_x(0)
    for m in range(NT):
        xt, kxt = xb[m % 2]
        sl = slice(m * T, (m + 1) * T)
        norm_modulate(C, xt, kxt, h, kh, T, l, 1, (sq, ksq, rstd, krstd, tmp, None))
        if m + 1 < NT:
            load_x(m + 1)
        for (c0, nch, ones, kones, dst, kdst, g) in ((0, 3, ones384, k384, qn, kqn, gql), (3, 2, ones256, k256, kvn, kkvn, gkvl)):
            banks = []
            for c in range(nch):
                ps, kps = psum_bank(C)
                banks.append((ps, kps))
                for kc in range(8):
                    P.mm(ps, win[:, kc, (c0 + c) * 128:(c0 + c + 1) * 128], h[:, kc, :], kc == 0, kc == 7,
                         [kwin, (kh, 0)], [kps])
                P.add("act", lambda e, ps=ps, c=c: e.activation(lsq[:, c, :], ps, AF.Square), [kps], [klsq])
            pss, kpss = psum_bank(C)
            for c in range(nch):
                P.mm(pss, ones, lsq[:, c, :], c == 0, c == nch - 1, [klsq, kones], [kpss])
            P.add("act", lambda e, pss=pss: e.activation(lrs, pss, AF.Ln, bias=C.eps_t[:, 0:1]), [kpss, C.k_eps], [klrs])
            P.add("act", lambda e: e.activation(lrs, lrs, AF.Exp, scale=-0.5), [klrs], [klrs])
            for c in range(nch):
                ps, kps = banks[c]
                tm, ktm = tmp[c % 3]
                P.add("dve", lambda e, tm=tm, ps=ps: e.tensor_tensor(tm, ps, lrs, ALU.mult), [kps, klrs], [ktm])
                P.add("act", lambda e, tm=tm, c=c, dst=dst, g=g, sl=sl: e.activation(
                    dst[:, c, sl], tm, AF.Identity, scale=g[:, c:c + 1]), [ktm], [kdst])
        ps, kps = psum_bank(C)
        for kc in range(8):
            P.mm(ps[0:96, :], win[:, kc, 640:736], h[:, kc, :], kc == 0, kc == 7, [kwin, (kh, 0)], [kps])
        P.add("act", lambda e, ps=ps, sl=sl: e.copy(kpe[64:96, sl], ps[64:96, :]), [kps], [kkpe])
        P.add("act", lambda e, ps=ps, sl=sl: e.activation(sqk[64:96, sl], ps[64:96, :], AF.Square), [kps], [(ksqk, "pe")])
    A.release(m1)

    kT = [A.alloc(f"kT{i}", [S], BF16) for i in range(2)]
    Vau = [A.alloc(f"Vau{i}", [NB, 128], BF16) for i in range(2)]
    OTs = [A.alloc(f"OTs{i}", [S], BF16) for i in range(2)]
    for par in range(2):
        va, kva = Vau[par]
        P.add("pool", lambda e, va=va: e.memset(va, 1.0), [], [kva])
    rsk, krsk = A.alloc("rsk", [T], F32)
    rt1 = [A.alloc(f"rp1_{i}", [T], F32) for i in range(2)]
    rt2 = [A.alloc(f"rp2_{i}", [T], F32) for i in range(2)]
    qsq, kqsq = A.alloc("qsq", [T], BF16)
    qT = [A.alloc(f"qT{i}", [T], BF16) for i in range(2)]
    PT = [A.alloc(f"PT{i}", [T], BF16) for i in range(4)]
    osb, kosb = A.alloc("osb", [T], F32)
    rl, krl = A.alloc("rl", [T], F32)
    pti = 0

    def norm_rope(src_ps, ksrc_ps, src_sb, ksrc_sb, sqt, ksqt_keys, g, kg, dstT, kdstT, sl, tl):
        pss, kpss = psum_bank(C)
        P.mm(pss[0:96, :], ones96[0:96, 0:96], sqt, True, True, ksqt_keys + [k96], [kpss])
        P.add("act", lambda e: e.activation(rsk[0:96, :], pss[0:96, :], AF.Ln, bias=C.eps_t[0:96, 0:1]),
              [kpss, C.k_eps], [krsk])
        P.add("act", lambda e: e.activation(rsk[0:96, :], rsk[0:96, :], AF.Exp, scale=-0.5), [krsk], [krsk])
        if src_sb is None:
            P.add("dve", lambda e: e.scalar_tensor_tensor(dstT[0:96, sl], src_ps[0:96, :], g[0:96, 0:1], rsk[0:96, :],
                                                          ALU.mult, ALU.mult), [ksrc_ps, kg, krsk], [kdstT])
        else:
            P.add("dve", lambda e: e.scalar_tensor_tensor(dstT[0:64, sl], src_ps[0:64, :], g[0:64, 0:1], rsk[0:64, :],
                                                          ALU.mult, ALU.mult), [ksrc_ps, kg, krsk], [kdstT])
            P.add("dve", lambda e: e.scalar_tensor_tensor(dstT[64:96, sl], src_sb[64:96, tl], g[64:96, 0:1],
                                                          rsk[64:96, :], ALU.mult, ALU.mult), [ksrc_sb, kg, krsk], [kdstT])
        psr, kpsr = psum_bank(C)
        P.mm(psr[0:96, :], pmT[0:96, 0:96], dstT[0:96, sl], True, True, [kdstT, kpm], [kpsr])
        a1, ka1 = rt1[C.ps_i % 2]
        a2, ka2 = rt2[C.ps_i % 2]
        P.add("pool", lambda e: e.tensor_tensor(a1[64:96, :], dstT[64:96, sl], COS[64:96, tl], ALU.mult),
              [kdstT, kcos], [ka1])
        P.add("dve", lambda e: e.tensor_tensor(a2[64:96, :], psr[64:96, :], SIN[64:96, tl], ALU.mult),
              [kpsr, ksin], [ka2])
        P.add("dve", lambda e: e.tensor_tensor(dstT[64:96, sl], a1[64:96, :], a2[64:96, :], ALU.add),
              [ka1, ka2], [kdstT])

    LA = 3
    PTn = [A.alloc(f"PTn{i}", [T], BF16) for i in range(LA + 3)]

    def bank_excl(excl):
        while True:
            ps, kps = psum_bank(C)
            if all(ps is not x for x in excl):
                return ps, kps

    def prepK(hd, m):
        kt_, kkt = kT[hd % 2]
        tl = slice(m * T, (m + 1) * T)
        ps, kps = psum_bank(C)
        for kc in range(2):
            P.mm(ps[0:64, :], wukv[:, kc, hd * 128:hd * 128 + 64], kvn[:, kc, tl], kc == 0, kc == 1,
                 [kwukv, kkvn], [kps])
        P.add("act", lambda e, ps=ps, tl=tl: e.activation(sqk[0:64, tl], ps[0:64, :], AF.Square),
              [kps], [(ksqk, "n", m)])
        norm_rope(ps, kps, kpe, kkpe, sqk[0:96, tl], [(ksqk, "n", m), (ksqk, "pe")], gk, kgk, kt_, kkt, tl, tl)

    def prepV(hd, b0):
        va, kva = Vau[hd % 2]
        voff = 0 if hd % 2 == 0 else 64
        ps, kps = psum_bank(C)
        for j in range(8):
            blk = b0 + j
            for kc in range(2):
                P.mm(ps[:, j * 64:(j + 1) * 64], kvn[:, kc, blk * 128:(blk + 1) * 128],
                     wukv[:, kc, hd * 128 + 64:hd * 128 + 128], kc == 0, kc == 1, [kwukv, kkvn], [kps])
        P.add("dve", lambda e, ps=ps, b0=b0, va=va, voff=voff: e.tensor_copy(
            va[:, b0:b0 + 8, voff:voff + 64], ps.rearrange("p (j v) -> p j v", j=8)), [kps], [kva])

    qcount = [0]

    def prepQ(hd, m):
        tl = slice(m * T, (m + 1) * T)
        q_, kq_ = qT[qcount[0] % 2]
        qcount[0] += 1
        ps, kps = psum_bank(C)
        for kc in range(3):
            P.mm(ps[0:96, :], wuq[:, kc, hd * 96:(hd + 1) * 96], qn[:, kc, tl], kc == 0, kc == 2,
                 [kwuq, kqn], [kps])
        P.add("act", lambda e, ps=ps: e.activation(qsq[0:96, :], ps[0:96, :], AF.Square), [kps], [kqsq])
        norm_rope(ps, kps, None, None, qsq[0:96, :], [kqsq], gq, kgq, q_, kq_, slice(0, T), tl)
        return q_, kq_

    for m in range(NT):
        prepK(0, m)
    for b0 in range(0, NB, 8):
        prepV(0, b0)
    nextq = prepQ(0, 0)
    vgroups = list(range(0, NB, 8))
    for hd in range(16):
        par = hd % 2
        kt_, kkt = kT[par]
        va, kva = Vau[par]
        ots, kots = OTs[(hd // 2) % 2]
        oh = 0 if par == 0 else 64
        lp = 64 if par == 0 else 0
        for m in range(NT):
            tl = slice(m * T, (m + 1) * T)
            q_, kq_ = nextq
            if m + 1 < NT:
                nextq = prepQ(hd, m + 1)
            elif hd + 1 < 16:
                nextq = prepQ(hd + 1, 0)
            if hd + 1 < 16:
                prepK(hd + 1, m)
                if m < len(vgroups):
                    prepV(hd + 1, vgroups[m])
            pso, kpso = psum_bank(C)
            pend = []
            for kb in range(NB + LA):
                if kb < NB:
                    pss, kpss = bank_excl([pso])
                    P.mm(pss, kt_[0:96, kb * 128:(kb + 1) * 128], q_[0:96, :], True, True, [kkt, kq_], [kpss])
                    pt, kpt = PTn[pti % len(PTn)]
                    pti += 1
                    P.add("act", lambda e, pt=pt, pss=pss: e.activation(pt, pss, AF.Exp, scale=SCALE), [kpss], [kpt])
                    pend.append((kb, pt, kpt))
                if kb >= LA:
                    kb2, pt, kpt = pend.pop(0)
                    P.mm(pso, va[:, kb2, :], pt, kb2 == 0, kb2 == NB - 1, [kva, kpt], [kpso])
            P.add("dve", lambda e, pso=pso, lp=lp: e.reciprocal(rl[lp:lp + 1, :], pso[lp:lp + 1, :]), [kpso], [krl])
            P.add("act", lambda e, pso=pso, oh=oh: e.copy(osb[oh:oh + 64, :], pso[oh:oh + 64, :]), [kpso], [kosb])
            psb, kpsb = bank_excl([pso])
            P.mm(psb, onesrow[lp:lp + 1, :], rl[lp:lp + 1, :], True, True, [krl, krow], [kpsb])
            P.add("dve", lambda e, psb=psb, oh=oh, ots=ots, tl=tl: e.tensor_tensor(
                ots[oh:oh + 64, tl], osb[oh:oh + 64, :], psb[oh:oh + 64, :], ALU.mult), [kosb, kpsb], [kots])
        if par == 1:
            c = hd // 2
            P.dma("sp", C.OT[c], ots, [kots], [("OT", c)])
    A.release(m_phase)

    m3 = A.mark()
    wout, kwout = A.alloc("wout", [8, 1024], BF16)
    stg = [A.alloc(f"wst_o{i}", [8, 256], F32) for i in range(2)]
    wsrc = C.d_mla_wout.ap().rearrange("(kc p) n -> p kc n", p=128)
    for i, c0 in enumerate(range(0, 1024, 256)):
        st, kst = stg[i % 2]
        P.dma("sp", st, wsrc[:, :, c0:c0 + 256], [], [kst])
        emit_cast(P, cast_engine(C), wout[:, :, c0:c0 + 256], st, [kst], [kwout])
    xb = [A.alloc(f"ox{i}", [NDC, T], F32) for i in range(2)]
    ob = [A.alloc(f"oo{i}", [NDC, T], BF16) for i in range(2)]
    for m in range(NT):
        xt, kxt = xb[m % 2]
        ot, kot = ob[m % 2]
        tl = slice(m * T, (m + 1) * T)
        P.dma("act", xt, C.xT[:, :, tl], [("xT", m)], [kxt])
        P.dma("sp", ot, C.OT.rearrange("c p s -> p c s")[:, :, tl], [("OT", c) for c in range(8)], [kot])
        for dc in range(NDC):
            ps, kps = psum_bank(C)
            for kc in range(8):
                P.mm(ps, wout[:, kc, dc * 128:(dc + 1) * 128], ot[:, kc, :], kc == 0, kc == 7, [kwout, kot], [kps])
            P.add("dve", lambda e, ps=ps, dc=dc, xt=xt: e.scalar_tensor_tensor(
                xt[:, dc, :], ps, mv["gate"][:, 8 + dc:8 + dc + 1], xt[:, dc, :], ALU.mult, ALU.add),
                [kps, mv["kgate"], kxt], [kxt])
        P.dma("act", C.xT[:, :, tl], xt, [kxt], [("xT", m)])
    A.release(m3)


def bc_last(ap, n):
    return ap.unsqueeze(2).to_broadcast([ap.shape[0], ap.shape[1], n])


def bc_mid(ap, n):
    return ap.unsqueeze(1).to_broadcast([ap.shape[0], n, ap.shape[1]])


def ssd_phase(C):
    P, A = C.P, C.arena
    l = 0
    mv = C.modv[l]
    NB = S // 128
    T = 512
    NT = S // T
    m_phase = A.mark()
    one_t, kone = const_tile(C, "one", 1.0)
    masks, kmask = A.alloc("masks", [6, 128], F32)
    P.dma("sp", masks, C.d_masks.ap().rearrange("p (a b) -> p a b", a=6), [], [kmask])
    ones_f, konesf = const_tile(C, "ones_f", 1.0, F32, 128)
    cw, kcw = C.vec["ssd_cw"]
    cb_, kcb = C.vec["ssd_cb"]
    dtb, kdtb = C.vec["ssd_dtb"]
    alog, kalog = C.vec["ssd_alog"]
    dsk, kdsk = C.vec["ssd_dskip"]
    Aneg, kAneg = A.alloc("Aneg", [64], F32)
    P.add("act", lambda e: e.activation(Aneg, alog, AF.Exp), [kalog], [kAneg])
    P.add("dve", lambda e: e.tensor_scalar(Aneg, Aneg, -1.0, None, ALU.mult), [kAneg], [kAneg])
    h_all, khall = A.alloc("h_all", [NDC, S], BF16)

    m0 = A.mark()
    xb = [A.alloc(f"sx{i}", [NDC, T], F32) for i in range(2)]
    sq, ksq = A.alloc("ssq", [NDC, T], BF16)
    rstd, krstd = A.alloc("srstd", [T], F32)
    tmp = [A.alloc(f"stmp{i}", [512], F32) for i in range(3)]
    for m in range(NT):
        xt, kxt = xb[m % 2]
        P.dma("act", xt, C.xT[:, :, m * T:(m + 1) * T], [("xT", m)], [kxt])
        norm_modulate(C, xt, kxt, h_all[:, :, m * T:(m + 1) * T], (khall, m), T, l, 1, (sq, ksq, rstd, krstd, tmp, None))
    A.release(m0)
    hkeys = [((khall, m), 0) for m in range(NT)]

    m0 = A.mark()
    wsrc = C.d_ssd_win.ap().rearrange("(kc p) n -> p kc n", p=128)
    wst = [A.alloc(f"swst{i}", [8, 128], F32) for i in range(2)]
    wcb = [A.alloc(f"swc{i}", [8, 128], BF16) for i in range(2)]
    pre = [A.alloc(f"spre{i}", [S + 4], F32) for i in range(2)]
    acc = [A.alloc(f"sacc{i}", [S], F32) for i in range(2)]
    xo = [A.alloc(f"sxo{i}", [S], BF16) for i in range(2)]
    for i in range(2):
        pr, kpr = pre[i]
        P.add("pool", lambda e, pr=pr: e.memset(pr, 0.0), [], [kpr])
    for c in range(32):
        ws, kws = wst[c % 2]
        wc, kwc = wcb[c % 2]
        pr, kpr = pre[c % 2]
        ac, kac = acc[c % 2]
        xo_, kxo = xo[c % 2]
        P.dma("sp", ws, wsrc[:, :, 2048 + c * 128:2048 + (c + 1) * 128], [], [kws])
        P.add("pool", lambda e, wc=wc, ws=ws: e.tensor_copy(wc, ws), [kws], [kwc])
        for m in range(NT):
            ps, kps = psum_bank(C)
            for kc in range(8):
                P.mm(ps, wc[:, kc, :], h_all[:, kc, m * T:(m + 1) * T], kc == 0, kc == 7, [kwc, hkeys[m]], [kps])
            P.add("act", lambda e, ps=ps, pr=pr, m=m: e.copy(pr[:, 2 + m * T:2 + (m + 1) * T], ps), [kps], [kpr])
        for hf in range(2):
            o0 = hf * (S // 2)
            n_ = S // 2
            P.add("dve", lambda e, ac=ac, pr=pr, c=c, o0=o0, n_=n_: e.tensor_scalar(
                ac[:, o0:o0 + n_], pr[:, o0:o0 + n_], cw[:, c * 5:c * 5 + 1], cb_[:, c:c + 1], ALU.mult, ALU.add),
                [kpr, kcw, kcb], [(kac, hf)])
            for j in range(1, 5):
                P.add("dve", lambda e, ac=ac, pr=pr, c=c, j=j, o0=o0, n_=n_: e.scalar_tensor_tensor(
                    ac[:, o0:o0 + n_], pr[:, o0 + j:o0 + j + n_], cw[:, c * 5 + j:c * 5 + j + 1], ac[:, o0:o0 + n_],
                    ALU.mult, ALU.add), [kpr, kcw, (kac, hf)], [(kac, hf)])
            P.add("act", lambda e, xo_=xo_, ac=ac, o0=o0, n_=n_: e.activation(xo_[:, o0:o0 + n_], ac[:, o0:o0 + n_], AF.Silu),
                  [(kac, hf)], [(kxo, hf)])
        P.dma("sp", C.XBC[c], xo_, [(kxo, 0), (kxo, 1)], [("XBC", c)])
    A.release(m0)

    wdt, kwdt = A.alloc("wdt", [8, 64], BF16)
    m_big = A.mark()
    wz, kwz = A.alloc("wz", [8, 2048], BF16)
    m0 = A.mark()
    stg = [A.alloc(f"szst{i}", [8, 256], F32) for i in range(2)]
    it = 0
    for c0 in range(0, 2048, 256):
        st, kst = stg[it % 2]
        it += 1
        P.dma("sp", st, wsrc[:, :, c0:c0 + 256], [], [kst])
        emit_cast(P, cast_engine(C), wz[:, :, c0:c0 + 256], st, [kst], [kwz])
    st, kst = stg[it % 2]
    it += 1
    P.dma("sp", st[:, :, 0:64], wsrc[:, :, 6144:6208], [], [kst])
    emit_cast(P, cast_engine(C), wdt, st[:, :, 0:64], [kst], [kwdt])
    A.release(m0)
    ng16, kng16 = C.vec["ssd_ng16"]

    xbcT = [A.alloc(f"xbcT{i}", [32, 128], BF16) for i in range(1)]
    xs_tm, kxs = A.alloc("xs_tm", [32, 64], BF16)
    B_tm, kbt = A.alloc("B_tm", [8, 128], BF16)
    dt_, kdt = A.alloc("dt", [64], F32)
    a_, ka = A.alloc("a", [64], F32)
    dec, kdec = A.alloc("dec", [96], F32)
    dt2, kdt2 = A.alloc("dt2", [32], F32)
    xdt, kxdt = A.alloc("xdt", [32, 64], BF16)
    xdtE, kxdtE = A.alloc("xdtE", [32, 64], BF16)
    cbm, kcbm = A.alloc("cbm", [8, 128], F32)
    Lb = [A.alloc(f"Lb{i}", [4, 128], F32) for i in range(2)]
    ex = [A.alloc(f"ex{i}", [4, 128], F32) for i in range(2)]
    MT = [A.alloc(f"MT{i}", [4, 128], BF16) for i in range(2)]
    yo = [A.alloc(f"yo{i}", [256], F32) for i in range(2)]
    ydir, kydir = A.alloc("ydir", [2048], F32)
    H, kH = A.alloc("H", [2048], F32)
    Hb, kHb = A.alloc("Hb", [2048], BF16)
    yb_in, kybin = A.alloc("yb_in", [2048], F32)
    sz, ksz = A.alloc("sz", [2048], F32)
    gss, kgss = A.alloc("gss", [8], F32)
    ynb, kynb = A.alloc("ynb", [2048], BF16)
    yT, kyT = A.alloc("yT", [16, 128], BF16)
    xck, kxck = A.alloc("xck", [NDC, 128], F32)

    def chunk_step(ck, d, it_):
        tk = slice(ck * 128, (ck + 1) * 128)
        xb_, kxb = xbcT[0]
        Lm = masks[:, 0 + 2 * d, :]
        Rm = masks[:, 1 + 2 * d, :]
        Vm = masks[:, 4 + d, :]
        dc0 = d * 32
        P.dma("sp", xb_, C.XBC.rearrange("c p s -> p c s")[:, :, tk], [("XBC", c) for c in range(32)], [kxb])
        for q in range(3):
            ps, kps = psum_bank(C)
            psb = ps.bitcast(BF16)
            for j in range(8):
                c = q * 8 + j
                P.add("pe", lambda e, psb=psb, j=j, c=c, xb_=xb_: e.transpose(
                    psb[:, j * 128:(j + 1) * 128], xb_[:, c, :], C.ident_b), [kxb, C.k_ident_b], [kps])
            if q < 2:
                P.add("act", lambda e, psb=psb, q=q: e.copy(
                    xs_tm.rearrange("p a b -> p (a b)")[:, q * 1024:(q + 1) * 1024], psb), [kps], [kxs])
            else:
                P.add("dve", lambda e, psb=psb: e.tensor_copy(B_tm.rearrange("p a b -> p (a b)"), psb), [kps], [kbt])
        ps, kps = psum_bank(C)
        for kc in range(8):
            P.mm(ps[:, 0:64], h_all[:, kc, tk], wdt[:, kc, :], kc == 0, kc == 7, [hkeys[ck // 4], kwdt], [kps])
        P.add("dve", lambda e, ps=ps: e.tensor_tensor(dt_, ps[:, 0:64], dtb, ALU.add), [kps, kdtb], [kdt])
        P.add("act", lambda e: e.activation(dt_, dt_, AF.Exp), [kdt], [kdt])
        P.add("act", lambda e: e.activation(dt_, dt_, AF.Ln, bias=one_t[:, 0:1]), [kdt, kone], [kdt])
        P.add("dve", lambda e: e.tensor_tensor(a_, dt_, Aneg, ALU.mult), [kdt, kAneg], [ka])
        ps, kps = psum_bank(C)
        P.mm(ps[:, 0:32], Rm, a_[:, dc0:dc0 + 32], True, True, [kmask, ka], [kps])
        P.mm(ps[:, 32:64], Lm, a_[:, dc0:dc0 + 32], True, True, [kmask, ka], [kps])
        P.mm(ps[:, 64:96], ones_f, a_[:, dc0:dc0 + 32], True, True, [konesf, ka], [kps])
        P.add("act", lambda e, ps=ps: e.activation(dec, ps[:, 0:96], AF.Exp), [kps], [kdec])
        P.add("dve", lambda e: e.tensor_tensor(dt2, dt_[:, dc0:dc0 + 32], dec[:, 32:64], ALU.mult), [kdt, kdec], [kdt2])
        P.add("pool", lambda e: e.tensor_tensor(xdt, xs_tm, bc_last(dt_[:, dc0:dc0 + 32], 64), ALU.mult),
              [kxs, kdt], [kxdt])
        P.add("pool", lambda e: e.tensor_tensor(xdtE, xs_tm, bc_last(dt2, 64), ALU.mult), [kxs, kdt2], [kxdtE])
        for half in range(2):
            ps, kps = psum_bank(C)
            for j in range(4):
                g = half * 4 + j
                P.mm(ps[:, j * 128:(j + 1) * 128], xb_[:, 16 + g, :], xb_[:, 24 + g, :], True, True, [kxb], [kps])
            P.add("dve", lambda e, ps=ps, half=half: e.tensor_tensor(
                cbm[:, half * 4:(half + 1) * 4, :], ps.rearrange("p (a b) -> p a b", a=4), bc_mid(Vm, 4), ALU.mult),
                [kps, kmask], [kcbm])
        for g in range(8):
            lb, klb = Lb[g % 2]
            ex_, kex = ex[g % 2]
            mt, kmt = MT[g % 2]
            yo_, kyo = yo[g % 2]
            P.add("dve", lambda e, lb=lb, g=g: e.tensor_tensor(
                lb, bc_mid(Lm, 4), bc_last(a_[:, dc0 + g * 4:dc0 + g * 4 + 4], 128), ALU.mult), [kmask, ka], [klb])
            ps, kps = psum_bank(C)
            for r in range(4):
                P.mm(ps[:, r * 128:(r + 1) * 128], lb[:, r, :], Rm, True, True, [klb, kmask], [kps])
            P.add("act", lambda e, ps=ps, ex_=ex_: e.activation(ex_.rearrange("p a b -> p (a b)"), ps, AF.Exp), [kps], [kex])
            P.add("dve", lambda e, mt=mt, ex_=ex_, g=g: e.tensor_tensor(mt, ex_, bc_mid(cbm[:, g, :], 4), ALU.mult),
                  [kex, kcbm], [kmt])
            psy, kpsy = psum_bank(C)
            for r in range(4):
                P.mm(psy[:, r * 64:(r + 1) * 64], mt[:, r, :], xdt[:, g * 4 + r, :], True, True, [kmt, kxdt], [kpsy])
            P.mm(psy[:, 256:512], xb_[:, 24 + g, :], Hb[:, g * 256:(g + 1) * 256], True, True, [kxb, kHb], [kpsy])
            P.add("dve", lambda e, psy=psy, yo_=yo_, g=g: e.tensor_tensor(
                yo_.rearrange("p (a b) -> p a b", a=4), psy[:, 256:512].rearrange("p (a b) -> p a b", a=4),
                bc_last(dec[:, g * 4:g * 4 + 4], 64), ALU.mult), [kpsy, kdec], [kyo])
            P.add("dve", lambda e, psy=psy, yo_=yo_, g=g: e.tensor_tensor(
                ydir[:, g * 256:(g + 1) * 256], psy[:, 0:256], yo_, ALU.add), [kpsy, kyo], [(kydir, g)])
            pss, kpss = psum_bank(C)
            P.mm(pss[:, 0:256], B_tm[:, g, :], xdtE[:, g * 4:g * 4 + 4, :].rearrange("p a b -> p (a b)"), True, True,
                 [kbt, kxdtE], [kpss])
            P.add("dve", lambda e, g=g: e.tensor_tensor(
                H[:, g * 256:(g + 1) * 256].rearrange("p (a b) -> p a b", a=4),
                H[:, g * 256:(g + 1) * 256].rearrange("p (a b) -> p a b", a=4),
                bc_last(dec[:, 64 + g * 4:64 + g * 4 + 4], 64), ALU.mult), [(kH, g), kdec], [(kH, g)])
            P.add("dve", lambda e, pss=pss, g=g: e.tensor_tensor(
                H[:, g * 256:(g + 1) * 256], H[:, g * 256:(g + 1) * 256], pss[:, 0:256], ALU.add),
                [(kH, g), kpss], [(kH, g)])
            P.add("act", lambda e, g=g: e.copy(Hb[:, g * 256:(g + 1) * 256], H[:, g * 256:(g + 1) * 256]),
                  [(kH, g)], [kHb])

    ykeys = [(kydir, g) for g in range(8)]
    P.add("pool", lambda e: e.memset(H, 0.0), [], [(kH, g) for g in range(8)])
    P.add("pool", lambda e: e.memset(Hb, 0.0), [], [kHb])
    it_ = 0
    for ck in range(NB - 1, -1, -1):
        chunk_step(ck, 1, it_)
        it_ += 1
        P.dma("sp", C.YB[ck], ydir, ykeys, [("YB", ck)])
        tk = slice(ck * 128, (ck + 1) * 128)
        for zc in range(4):
            ps, kps = psum_bank(C)
            for kc in range(8):
                P.mm(ps, h_all[:, kc, tk], wz[:, kc, zc * 512:(zc + 1) * 512], kc == 0, kc == 7,
                     [hkeys[ck // 4], kwz], [kps])
            P.add("act", lambda e, ps=ps, zc=zc: e.activation(sz[:, zc * 512:(zc + 1) * 512], ps, AF.Silu), [kps], [ksz])
        P.dma("sp", C.SZ[ck], sz, [ksz], [("SZ", ck)])
    wout = wz.rearrange("p a b -> p (a b)").rearrange("p (k n) -> p k n", k=16)
    kwout = kwz
    stg = [(yb_in.rearrange("p (k n) -> p k n", k=2), kybin), (sz.rearrange("p (k n) -> p k n", k=2), ksz)]
    osrc = C.d_ssd_wout.ap().rearrange("(kc p) n -> p kc n", p=128)
    for i_, c0 in enumerate(range(0, 16, 2)):
        st, kst = stg[i_ % 2]
        P.dma("sp", st, osrc[:, c0:c0 + 2, :], [], [kst])
        emit_cast(P, cast_engine(C), wout[:, c0:c0 + 2, :], st, [kst], [kwout])
    P.add("pool", lambda e: e.memset(H, 0.0), [], [(kH, g) for g in range(8)])
    P.add("pool", lambda e: e.memset(Hb, 0.0), [], [kHb])
    for ck in range(NB):
        tk = slice(ck * 128, (ck + 1) * 128)
        P.dma("act", yb_in, C.YB[ck], [("YB", ck)], [kybin])
        P.dma("act", xck, C.xT[:, :, tk], [("xT", ck // 4)], [kxck])
        chunk_step(ck, 0, it_)
        it_ += 1
        P.add("dve", lambda e: e.tensor_tensor(ydir, ydir, yb_in, ALU.add), ykeys + [kybin], ykeys)
        P.add("pool", lambda e: e.tensor_tensor(yb_in.rearrange("p (a b) -> p a b", a=32), xs_tm, bc_last(dsk, 64), ALU.mult),
              [kxs, kdsk], [kybin])
        P.add("dve", lambda e: e.tensor_tensor(ydir, ydir, yb_in, ALU.add), ykeys + [kybin], ykeys)
        P.dma("sp", sz, C.SZ[ck], [("SZ", ck)], [ksz])
        P.add("dve", lambda e: e.tensor_tensor(ydir, ydir, sz, ALU.mult), ykeys + [ksz], ykeys)
        P.add("pool", lambda e: e.tensor_tensor(sz, ydir, ydir, ALU.mult), ykeys, [ksz])
        P.add("dve", lambda e: e.reduce_sum(gss, sz.rearrange("p (a b) -> p a b", a=8), AX.X), [ksz], [kgss])
        P.add("act", lambda e: e.activation(gss, gss, AF.Ln, bias=C.eps_t[:, 0:1], scale=1.0 / 256), [kgss, C.k_eps], [kgss])
        P.add("act", lambda e: e.activation(gss, gss, AF.Exp, scale=-0.5), [kgss], [kgss])
        P.add("dve", lambda e: e.tensor_tensor(ydir.rearrange("p (a b) -> p a b", a=8), ydir.rearrange("p (a b) -> p a b", a=8),
                                               bc_last(gss, 256), ALU.mult), ykeys + [kgss], ykeys)
        P.add("act", lambda e: e.copy(ynb, ydir), ykeys, [kynb])
        for q in range(2):
            ps, kps = psum_bank(C)
            psb = ps.bitcast(BF16)
            for j in range(8):
                c = q * 8 + j
                P.add("pe", lambda e, psb=psb, j=j, c=c: e.transpose(
                    psb[:, j * 128:(j + 1) * 128], ynb[:, c * 128:(c + 1) * 128], C.ident_b), [kynb, C.k_ident_b], [kps])
            for j in range(8):
                c = q * 8 + j
                P.add("act", lambda e, psb=psb, j=j, c=c: e.activation(
                    yT[:, c, :], psb[:, j * 128:(j + 1) * 128], AF.Identity, scale=ng16[:, c:c + 1]),
                    [kps, kng16], [kyT])
        for half in range(2):
            ps, kps = psum_bank(C)
            for j in range(4):
                dc = half * 4 + j
                for kc in range(16):
                    P.mm(ps[:, j * 128:(j + 1) * 128], wout[:, kc, dc * 128:(dc + 1) * 128], yT[:, kc, :],
                         kc == 0, kc == 15, [kwout, kyT], [kps])
            for j in range(4):
                dc = half * 4 + j
                P.add("dve", lambda e, ps=ps, j=j, dc=dc: e.scalar_tensor_tensor(
                    xck[:, dc, :], ps[:, j * 128:(j + 1) * 128], mv["gate"][:, 8 + dc:8 + dc + 1], xck[:, dc, :],
                    ALU.mult, ALU.add), [kps, mv["kgate"], kxck], [kxck])
        P.dma("act", C.xT[:, :, tk], xck, [kxck], [("xT", ck // 4)])
    A.release(m_phase)

def build_program(stages, seq=4096, debug=False):
    global S
    S = seq
    nc = bass.Bass("TRN2", target_bir_lowering=False)
    C = Ctx()
    C.debug = debug
    C.dbg_off = 0
    C.dbg_map = {}
    C.dbg_keys = []
    C.nc = nc
    C.P = Prog(nc)
    C.ps_i = 0
    C.bar_gidx = 0
    C.cast_i = 0
    dt = nc.dram_tensor
    C.d_x = dt("x", [S, D], F32, kind="ExternalInput")
    C.d_out = dt("out", [S, D], F32, kind="ExternalOutput")
    C.d_ident = dt("ident", [128, 128], F32, kind="ExternalInput")
    if debug:
        C.d_dbg = dt("dbg", [128, 8192], F32, kind="ExternalOutput")
    C.d_w_mod = dt("w_mod", [2, D, 9 * D], F32, kind="ExternalInput")
    C.d_wg = dt("ffn_w_gate", [2, 2, D, DFF], F32, kind="ExternalInput")
    C.d_wu = dt("ffn_w_up", [2, 2, D, DFF], F32, kind="ExternalInput")
    C.d_wd = dt("ffn_w_down", [2, 2, DFF, D], F32, kind="ExternalInput")
    C.d_pm = dt("pmT", [96, 96], F32, kind="ExternalInput")
    C.d_pos = dt("pos", [128, S], I32, kind="ExternalInput")
    C.d_mla_win = dt("mla_w_in", [D, 672], F32, kind="ExternalInput")
    C.d_mla_wuq = dt("mla_w_uq", [384, 1536], F32, kind="ExternalInput")
    C.d_mla_wukv = dt("mla_w_ukv", [256, 2048], F32, kind="ExternalInput")
    C.d_mla_wout = dt("mla_w_out", [D, D], F32, kind="ExternalInput")
    C.OT = dt("OT_scr", [8, 128, S], BF16, kind="Internal").ap()
    C.d_masks = dt("masks", [128, 768], F32, kind="ExternalInput")
    C.d_ssd_win = dt("ssd_w_in", [D, 6208], F32, kind="ExternalInput")
    C.d_ssd_wout = dt("ssd_w_out", [2048, D], F32, kind="ExternalInput")
    C.SZ = dt("SZ_scr", [S // 128, 128, 2048], F32, kind="Internal").ap()
    C.XBC = dt("XBC_scr", [32, 128, S], BF16, kind="Internal").ap()
    C.YB = dt("YB_scr", [S // 128, 128, 2048], F32, kind="Internal").ap()
    C.d_vecs = {}
    for name, n in VEC_SPECS:
        C.d_vecs[name] = dt("v_" + name, [128, n], F32, kind="ExternalInput")
    C.xT = dt("xT_scr", [128, NDC, S], F32, kind="Internal").ap()
    C.WGU = [dt(f"wgu_scr{i}", [NF, 128, 2, 8, 128], BF16, kind="Internal").ap() for i in range(4)]
    C.WD = [dt(f"wd_scr{i}", [NDC, 128, NF, 128], BF16, kind="Internal").ap() for i in range(4)]

    ARENA_BYTES = 207 * 1024
    with ExitStack() as es:
        ah = es.enter_context(nc.sbuf_tensor("arena", [128, ARENA_BYTES // 4], F32))
        C.arena = Arena(ah, ARENA_BYTES, C.P)
        C.ps = [es.enter_context(nc.psum_tensor(f"ps{i}", [128, 512], F32))[:] for i in range(8)]
        eng_sems = {e: es.enter_context(nc.semaphore(f"sem_{e}")) for e in ENGS}
        dma_sems = [es.enter_context(nc.semaphore(f"dsem{i}")) for i in range(N_DMA_SEMS)]

        setup_consts(C)
        if debug:
            C.dbg_stage, _ = C.arena.alloc('dbg_stage', [3584], F32)
        load_transpose_x(C)
        compute_mod(C)
        for (l, w) in stages.get("ffn", []):
            convert_ffn_weights(C, l, w)
        for st_ in stages.get("order", []):
            if BARRIERS:
                phase_barrier(C)
            if st_[0] == "ffn":
                ffn_phase(C, st_[1], st_[2])
            elif st_[0] == "mla":
                mla_phase(C)
            elif st_[0] == "ssd":
                ssd_phase(C)
        store_transpose_out(C)
        C.P.emit(eng_sems, dma_sems)
    C.nc = nc
    return C


VEC_SPECS = [("c", 8), ("b_mod0", 72), ("b_mod1", 72), ("norm_g0", 24), ("norm_g1", 24),
             ("mla_qg", 1), ("mla_kg", 1), ("mla_qng", 3), ("mla_kvng", 2), ("invf", 1),
             ("ssd_cw", 160), ("ssd_cb", 32), ("ssd_dtb", 64), ("ssd_alog", 64), ("ssd_dskip", 32), ("ssd_ng16", 16)]


def _consts():
    inv = (10000.0 ** (-np.arange(0, 32, 2, dtype=np.float32) / 32)).astype(np.float32)
    invf = np.zeros((128, 1), np.float32)
    invf[64:80, 0] = inv
    invf[80:96, 0] = inv
    pm = np.zeros((96, 96), np.float32)
    for i in range(16):
        pm[80 + i, 64 + i] = -1.0
        pm[64 + i, 80 + i] = 1.0
    return invf, pm


INVF, PMT = _consts()


def _masks():
    k = np.arange(128)[:, None]
    j = np.arange(128)[None, :]
    Lf = (k > j); Rf = (k <= j); Lb = (k < j); Rb = (k >= j)
    Vf = (j >= k)
    Vb = (j <= k)
    return np.concatenate([m.astype(np.float32) for m in (Lf, Rf, Lb, Rb, Vf, Vb)], axis=1)


MASKS = _masks()


def host_vecs(inputs, b):
    f = np.float32
    v = {}
    v["c"] = np.ascontiguousarray(inputs["c"][b].reshape(8, 128).T.astype(f))
    for l in range(2):
        v[f"b_mod{l}"] = np.ascontiguousarray(inputs["b_mod"][l].reshape(72, 128).T.astype(f))
        v[f"norm_g{l}"] = np.ascontiguousarray(inputs["norm_g"][l].reshape(24, 128).T.astype(f))
    def col(a, n=128):
        o = np.zeros((128, 1), f)
        o[:len(a), 0] = a
        return o
    v["mla_qg"] = col(inputs["mla_q_head_g"][0])
    v["mla_kg"] = col(inputs["mla_k_head_g"][0])
    v["mla_qng"] = np.ascontiguousarray(inputs["mla_q_norm_g"][0].reshape(3, 128).T.astype(f))
    v["mla_kvng"] = np.ascontiguousarray(inputs["mla_kv_norm_g"][0].reshape(2, 128).T.astype(f))
    v["invf"] = INVF
    cwt = inputs["ssd_conv_w"][0]
    v["ssd_cw"] = np.ascontiguousarray(cwt.reshape(5, 32, 128).transpose(2, 1, 0).reshape(128, 160).astype(f))
    v["ssd_cb"] = np.ascontiguousarray(inputs["ssd_conv_b"][0].reshape(32, 128).T.astype(f))
    v["ssd_dtb"] = np.ascontiguousarray(np.broadcast_to(inputs["ssd_dt_bias"][0].reshape(1, 64), (128, 64)).astype(f))
    v["ssd_alog"] = np.ascontiguousarray(np.broadcast_to(inputs["ssd_a_log"][0].reshape(1, 64), (128, 64)).astype(f))
    v["ssd_ng16"] = np.ascontiguousarray(inputs["ssd_norm_g"][0].reshape(16, 128).T.astype(f))
    v["ssd_dskip"] = np.ascontiguousarray(np.broadcast_to(inputs["ssd_d"][0].reshape(1, 32), (128, 32)).astype(f))
    return v


def run(inputs, stages, seq=4096, cores=8, debug=False):
    C = build_program(stages, seq, debug)
    nc = C.nc
    ident = np.eye(128, dtype=np.float32)
    in_maps = []
    for b in range(cores):
        m = {
            "x": np.ascontiguousarray(inputs["x"][b][:seq]),
            "ident": ident,
            "w_mod": inputs["w_mod"],
            "ffn_w_gate": inputs["ffn_w_gate"],
            "ffn_w_up": inputs["ffn_w_up"],
            "ffn_w_down": inputs["ffn_w_down"],
            "pmT": PMT,
            "masks": MASKS,
            "ssd_w_in": inputs["ssd_w_in"][0], "ssd_w_out": inputs["ssd_w_out"][0],

            "pos": np.ascontiguousarray(np.broadcast_to(inputs["positions"][b][None, :seq], (128, seq)).astype(np.int32)),
            "mla_w_in": inputs["mla_w_in"][0], "mla_w_uq": inputs["mla_w_uq"][0],
            "mla_w_ukv": inputs["mla_w_ukv"][0], "mla_w_out": inputs["mla_w_out"][0],
        }
        for k, a in host_vecs(inputs, b).items():
            m["v_" + k] = a
        in_maps.append(m)
    res = run_bass_kernel_spmd(nc, in_maps, core_ids=list(range(cores)))
    out = np.stack([r["out"] for r in res.results], axis=0)
    if debug:
        return out, {k: res.results[0]["dbg"][:, o:o + n] for k, (o, n) in C.dbg_map.items()}
    return out


def kernel(**inputs):
    inputs = {k: np.asarray(v) for k, v in inputs.items()}
    stages = {
        "ffn": [(0, 0), (0, 1), (1, 0), (1, 1)],
        "order": [("ffn", 0, 0), ("ssd",), ("ffn", 0, 1), ("ffn", 1, 0), ("mla",), ("ffn", 1, 1)],
    }
    return run(inputs, stages).astype(np.float32)
```

```python
import numpy as np
from contextlib import ExitStack
import concourse.bass as bass
import concourse.mybir as mybir
from concourse.bass_utils import run_bass_kernel_spmd

F32 = mybir.dt.float32
BF16 = mybir.dt.bfloat16
I32 = mybir.dt.int32
AF = mybir.ActivationFunctionType
ALU = mybir.AluOpType
AX = mybir.AxisListType

D = 1024
S = 4096
DFF = 2816
NF = DFF // 128
NDC = D // 128
EPS = 1e-6
N_DMA_SEMS = 40
DBG_M = 0
BARRIERS = False
ENGS = ("pe", "act", "dve", "pool", "sp")


class Op:
    __slots__ = ("eng", "fn", "deps", "sig", "seq", "dma", "semid", "semval", "prev", "pos", "gidx")


class Prog:
    def __init__(self, nc):
        self.nc = nc
        self.ops = {e: [] for e in ENGS}
        self.lastw = {}
        self.readers = {}
        self.ndma = 0
        self.dma_last = [None] * N_DMA_SEMS
        self.dma_cnt = [0] * N_DMA_SEMS
        self.nops = 0
        self.bases = set()
        self.touched = {}
        self.inherit = {}
        self.seen = set()

    def base_of(self, k):
        for _ in range(4):
            if k in self.bases:
                return k
            if isinstance(k, tuple) and len(k):
                k = k[0]
            else:
                return None
        return None

    def add(self, eng, fn, reads=(), writes=(), dma=False):
        op = Op()
        op.eng = eng
        op.fn = fn
        op.dma = dma
        op.sig = False
        op.seq = 0
        op.gidx = self.nops
        self.nops += 1
        deps = set()
        for k in list(reads) + list(writes):
            b = self.base_of(k)
            if b is None:
                continue
            if k not in self.seen:
                self.seen.add(k)
                deps.update(self.inherit.get(b, ()))
            t = self.touched.setdefault(b, {})
            if dma:
                t[("dma", op.gidx)] = op
            else:
                t[eng] = op
        for k in reads:
            w = self.lastw.get(k)
            if w is not None:
                deps.add(w)
        for k in writes:
            w = self.lastw.get(k)
            if w is not None:
                deps.add(w)
            for r in self.readers.get(k, ()):
                deps.add(r)
        for k in reads:
            self.readers.setdefault(k, []).append(op)
        for k in writes:
            self.lastw[k] = op
            self.readers[k] = []
        deps.discard(op)
        op.prev = None
        if dma:
            s = self.ndma % N_DMA_SEMS
            self.ndma += 1
            op.semid = s
            self.dma_cnt[s] += 16
            op.semval = self.dma_cnt[s]
            op.prev = self.dma_last[s]
            self.dma_last[s] = op
        op.deps = deps
        op.pos = len(self.ops[eng])
        self.ops[eng].append(op)
        return op

    def dma(self, eng, out, in_, reads, writes, **kw):
        return self.add(eng, lambda e: e.dma_start(out=out, in_=in_, **kw), reads, writes, dma=True)

    def mm(self, out, lhsT, rhs, start, stop, reads, writes):
        return self.add("pe", lambda e: e.matmul(out, lhsT, rhs, start=start, stop=stop), reads, writes)

    def emit(self, eng_sems, dma_sems):
        nc = self.nc

        def needs_sync(op, d):
            if d.dma:
                return True
            if d.eng == op.eng and not op.dma:
                if op.eng == "pe":
                    return False
                return True
            if d.eng == op.eng and op.dma:
                return True
            return True

        for e in ENGS:
            for op in self.ops[e]:
                for d in op.deps:
                    if needs_sync(op, d) and not d.dma:
                        d.sig = True
        for e in ENGS:
            n = 0
            for op in self.ops[e]:
                if op.sig and not op.dma:
                    n += 1
                    op.seq = n

        def emit_engine(ename, eobj):
            waited = {}
            for op in self.ops[ename]:
                need = {}
                for d in op.deps:
                    if not needs_sync(op, d):
                        continue
                    if d.dma:
                        key = ("d", d.semid)
                        val = d.semval
                    else:
                        key = ("e", d.eng)
                        val = d.seq
                    if need.get(key, 0) < val:
                        need[key] = val
                if op.dma and op.prev is not None:
                    key = ("d", op.prev.semid)
                    if need.get(key, 0) < op.prev.semval:
                        need[key] = op.prev.semval
                pend = []
                for key, val in need.items():
                    if waited.get(key, 0) >= val:
                        continue
                    waited[key] = val
                    sem = dma_sems[key[1]] if key[0] == "d" else eng_sems[key[1]]
                    pend.append((key[0] == "d", sem, val))
                pend.sort(key=lambda t: t[0])
                if op.fn is None:
                    for _, sem, val in pend:
                        eobj.wait_ge(sem, val)
                    continue
                for _, sem, val in pend[:-1]:
                    eobj.wait_ge(sem, val)
                ins = op.fn(eobj)
                if pend:
                    ins._wait_ge(pend[-1][1], pend[-1][2])
                if op.dma:
                    ins.then_inc(dma_sems[op.semid], 16)
                elif op.sig:
                    ins.then_inc(eng_sems[ename], 1)

        with nc.Block() as block:
            @block.tensor
            def _(e):
                emit_engine("pe", e)

            @block.scalar
            def _(e):
                emit_engine("act", e)

            @block.vector
            def _(e):
                emit_engine("dve", e)

            @block.gpsimd
            def _(e):
                emit_engine("pool", e)

            @block.sync
            def _(e):
                emit_engine("sp", e)


class Arena:
    def __init__(self, handle, nbytes, prog):
        self.h = handle
        self.cap = nbytes
        self.top = 0
        self.gen = 0
        self.P = prog
        self.allocs = []

    def mark(self):
        return self.top

    def release(self, m):
        self.top = m

    def alloc(self, name, shape, dtype):
        esz = 2 if dtype == BF16 else 4
        n = 1
        for s_ in shape:
            n *= s_
        nbytes = (n * esz + 63) // 64 * 64
        off = self.top
        assert off + nbytes <= self.cap, f"SBUF arena overflow allocating {name}: {off}+{nbytes}>{self.cap}"
        self.top += nbytes
        ap = self.h[:, off // 4:(off + nbytes) // 4]
        if dtype != F32:
            ap = ap.bitcast(dtype)
        ap = ap[:, 0:n]
        if len(shape) == 2:
            ap = ap.rearrange("p (a b) -> p a b", a=shape[0])
        elif len(shape) == 3:
            ap = ap.rearrange("p (a b c) -> p a b c", a=shape[0], b=shape[1])
        elif len(shape) == 4:
            ap = ap.rearrange("p (a b c d) -> p a b c d", a=shape[0], b=shape[1], c=shape[2])
        self.gen += 1
        key = (name, self.gen)
        P = self.P
        P.bases.add(key)
        inh = set()
        for (a0, a1, ok) in self.allocs:
            if a0 < off + nbytes and off < a1:
                inh.update(P.touched.get(ok, {}).values())
                inh.update(P.inherit.get(ok, ()))
        P.inherit[key] = inh
        self.allocs = [(a0, a1, ok) for (a0, a1, ok) in self.allocs if not (a0 >= off and a1 <= off + nbytes)]
        self.allocs.append((off, off + nbytes, key))
        return ap, key


class Ctx:
    pass


def dbg(C, name, ap, key, n):
    if not C.debug:
        return
    P, A = C.P, C.arena
    st = C.dbg_stage[:, C.dbg_off:C.dbg_off + n]
    kst = ("dbgst", name)
    P.add("pool", lambda e: e.tensor_copy(st, ap), [key], [kst])
    off = C.dbg_off
    C.dbg_off += n
    C.dbg_map[name] = (off, n)
    P.dma("sp", C.d_dbg.ap()[:, off:off + n], st, [kst], [("DBG", name)])
    C.dbg_keys.append(("DBG", name))


def rr(lst, i):
    return lst[i % len(lst)]


def setup_consts(C):
    P, A = C.P, C.arena
    C.ident_f, C.k_ident_f = A.alloc("ident_f", [128], F32)
    C.ident_b, C.k_ident_b = A.alloc("ident_b", [128], BF16)
    C.onesD_b, C.k_onesD = A.alloc("onesD", [128], BF16)
    P.dma("sp", C.ident_f, C.d_ident.ap(), [], [C.k_ident_f])
    P.add("dve", lambda e: e.tensor_copy(C.ident_b, C.ident_f), [C.k_ident_f], [C.k_ident_b])
    P.add("pool", lambda e: e.memset(C.onesD_b, 1.0 / D), [], [C.k_onesD])
    C.eps_t, C.k_eps = A.alloc("eps_t", [1], F32)
    P.add("pool", lambda e: e.memset(C.eps_t, EPS), [], [C.k_eps])
    C.vec = {}
    for name, t in C.d_vecs.items():
        n = t.ap().shape[1]
        ap, k = A.alloc("v_" + name, [n], F32)
        P.dma("sp", ap, t.ap(), [], [k])
        C.vec[name] = (ap, k)


def phase_barrier(C):
    P = C.P
    last = [P.ops[e][-1] for e in ENGS if P.ops[e]]
    last = [o for o in last if o.fn is not None]
    dmas = [o for e in ENGS for o in P.ops[e] if o.dma and o.gidx >= C.bar_gidx]
    C.bar_gidx = P.nops
    for e in ENGS:
        op = P.add(e, None)
        op.deps.update(last)
        op.deps.update(dmas)
        op.deps.discard(op)


def psum_bank(C):
    i = C.ps_i % 8
    C.ps_i += 1
    return C.ps[i], ("ps", i)


def load_transpose_x(C):
    P, A = C.P, C.arena
    m0 = A.mark()
    xin = [A.alloc(f"xin{i}", [4, D], F32) for i in range(2)]
    xtt = [A.alloc(f"xtt{i}", [NDC, 512], F32) for i in range(2)]
    xd = C.d_x.ap().rearrange("(g j p) d -> g p j d", j=4, p=128)
    for g in range(S // 512):
        xi, kxi = xin[g % 2]
        xt, kxt = xtt[g % 2]
        P.dma("sp", xi, xd[g], [], [kxi])
        for dc in range(NDC):
            ps, kps = psum_bank(C)
            for j in range(4):
                P.add("pe", lambda e, ps=ps, xi=xi, j=j, dc=dc: e.transpose(
                    ps[:, j * 128:(j + 1) * 128], xi[:, j, dc * 128:(dc + 1) * 128], C.ident_f),
                    [kxi, C.k_ident_f], [kps])
            eng = "act" if dc % 2 == 0 else "dve"
            if eng == "act":
                P.add("act", lambda e, ps=ps, xt=xt, dc=dc: e.copy(xt[:, dc, :], ps), [kps], [kxt])
            else:
                P.add("dve", lambda e, ps=ps, xt=xt, dc=dc: e.tensor_copy(xt[:, dc, :], ps), [kps], [kxt])
        P.dma("sp", C.xT[:, :, g * 512:(g + 1) * 512], xt, [kxt], [("xT", g)])
    A.release(m0)


def store_transpose_out(C):
    P, A = C.P, C.arena
    m0 = A.mark()
    xtt = [A.alloc(f"oxt{i}", [NDC, 512], F32) for i in range(2)]
    xo = [A.alloc(f"oxo{i}", [4, D], F32) for i in range(2)]
    od = C.d_out.ap().rearrange("(g j p) d -> g p j d", j=4, p=128)
    for g in range(S // 512):
        xt, kxt = xtt[g % 2]
        xo_, kxo = xo[g % 2]
        P.dma("sp", xt, C.xT[:, :, g * 512:(g + 1) * 512], [("xT", g)], [kxt])
        for j in range(4):
            for half in range(2):
                ps, kps = psum_bank(C)
                for q in range(4):
                    dc = half * 4 + q
                    P.add("pe", lambda e, ps=ps, xt=xt, j=j, dc=dc, q=q: e.transpose(
                        ps[:, q * 128:(q + 1) * 128], xt[:, dc, j * 128:(j + 1) * 128], C.ident_f),
                        [kxt, C.k_ident_f], [kps])
                if half == 0:
                    P.add("act", lambda e, ps=ps, xo_=xo_, j=j: e.copy(xo_[:, j, 0:512], ps), [kps], [kxo])
                else:
                    P.add("dve", lambda e, ps=ps, xo_=xo_, j=j: e.tensor_copy(xo_[:, j, 512:1024], ps), [kps], [kxo])
        P.dma("sp", od[g], xo_, [kxo], [("OUT", g)])
    A.release(m0)
    P.add("sp", None, [("OUT", g) for g in range(S // 512)] + C.dbg_keys, [])


def compute_mod(C):
    P, A = C.P, C.arena
    cvec, kc_ = C.vec["c"]
    C.cond, C.k_cond = A.alloc("cond", [8], F32)
    P.add("act", lambda e: e.activation(C.cond, cvec, AF.Silu), [kc_], [C.k_cond])
    C.mod = []
    m0 = None
    for l in range(2):
        mod, kmod = A.alloc(f"mod{l}", [72], F32)
        C.mod.append((mod, kmod))
    m0 = A.mark()
    wb = [A.alloc(f"wmod{i}", [8, 1024], F32) for i in range(2)]
    it = 0
    for l in range(2):
        mod, kmod = C.mod[l]
        bm, kbm = C.vec[f"b_mod{l}"]
        wd = C.d_w_mod.ap()[l].rearrange("(kc p) n -> p kc n", p=128)
        ps, kps = psum_bank(C)
        for cb in range(9):
            w, kw = wb[it % 2]
            it += 1
            P.dma("sp", w, wd[:, :, cb * 1024:(cb + 1) * 1024], [], [kw])
            for j in range(8):
                col = cb * 8 + j
                for kc in range(8):
                    P.mm(ps[:, col:col + 1], w[:, kc, j * 128:(j + 1) * 128], C.cond[:, kc:kc + 1],
                         kc == 0, kc == 7, [kw, C.k_cond], [kps])
        P.add("dve", lambda e, mod=mod, ps=ps, bm=bm: e.tensor_tensor(mod, ps[:, 0:72], bm, ALU.add),
              [kps, kbm], [kmod])
    A.release(m0)
    C.modv = []
    for l in range(2):
        mod, kmod = C.mod[l]
        g, kg = C.vec[f"norm_g{l}"]
        a, ka = A.alloc(f"moda{l}", [24], F32)
        gt, kgt = A.alloc(f"modg{l}", [24], F32)
        for sub in range(3):
            sc = mod[:, (sub * 3 + 1) * 8:(sub * 3 + 2) * 8]
            P.add("dve", lambda e, a=a, sub=sub, sc=sc, g=g: e.scalar_tensor_tensor(
                a[:, sub * 8:(sub + 1) * 8], sc, 1.0, g[:, sub * 8:(sub + 1) * 8], ALU.add, ALU.mult),
                [kmod, kg], [ka])
            gsrc = mod[:, (sub * 3 + 2) * 8:(sub * 3 + 3) * 8]
            fac = 1.0 if sub == 1 else 0.5
            P.add("dve", lambda e, gt=gt, sub=sub, gsrc=gsrc, fac=fac: e.tensor_scalar(
                gt[:, sub * 8:(sub + 1) * 8], gsrc, fac, None, ALU.mult), [kmod], [kgt])
        C.modv.append(dict(a=a, ka=ka, gate=gt, kgate=kgt, mod=mod, kmod=kmod))
        if l == 0:
            dbg(C, 'mod0', mod, kmod, 72)
            dbg(C, 'a0', a, ka, 24)
            dbg(C, 'gate0', gt, kgt, 24)


def cast_engine(C):
    e = ("pool", "dve", "act")[C.cast_i % 3]
    C.cast_i += 1
    return e


def emit_cast(P, eng, out, in_, reads, writes):
    if eng == "act":
        P.add("act", lambda e: e.copy(out, in_), reads, writes)
    else:
        P.add(eng, lambda e: e.tensor_copy(out, in_), reads, writes)


def convert_ffn_weights(C, l, w):
    P, A = C.P, C.arena
    idx = l * 2 + w
    m0 = A.mark()
    stg = [A.alloc(f"cst{i}", [4096], F32) for i in range(3)]
    stb = [A.alloc(f"csb{i}", [4096], BF16) for i in range(3)]
    it = 0
    for gi, src in enumerate((C.d_wg, C.d_wu)):
        sd = src.ap()[l, w].rearrange("(kc p) n -> p kc n", p=128)
        for fb in range(0, NF, 4):
            nf = min(4, NF - fb)
            sf, ksf = stg[it % 3]
            sb, ksb = stb[it % 3]
            it += 1
            sfv = sf[:, 0:8 * nf * 128].rearrange("p (kc n) -> p kc n", kc=8)
            P.dma("sp", sfv, sd[:, :, fb * 128:(fb + nf) * 128], [], [ksf])
            sbv = sb[:, 0:nf * 8 * 128].rearrange("p (f kc m) -> p f kc m", f=nf, kc=8)
            emit_cast(P, cast_engine(C), sbv, sfv.rearrange("p kc (f m) -> p f kc m", f=nf), [ksf], [ksb])
            dst = C.WGU[idx][fb:fb + nf, :, gi, :, :].rearrange("f p kc m -> p f (kc m)")
            P.dma("sp", dst, sbv.rearrange("p f kc m -> p f (kc m)"), [ksb], [("WGU", idx, f_) for f_ in range(fb, fb + nf)])
    sd = C.d_wd.ap()[l, w].rearrange("(fc p) n -> p fc n", p=128)
    for fb in range(0, NF, 4):
        nf = min(4, NF - fb)
        sf, ksf = stg[it % 3]
        sb, ksb = stb[it % 3]
        it += 1
        sfv = sf[:, 0:nf * 1024].rearrange("p (fc n) -> p fc n", fc=nf)
        P.dma("sp", sfv, sd[:, fb:fb + nf, :], [], [ksf])
        sbv = sb[:, 0:8 * nf * 128].rearrange("p (dc fc m) -> p dc fc m", dc=8, fc=nf)
        emit_cast(P, cast_engine(C), sbv, sfv.rearrange("p fc (dc m) -> p dc fc m", dc=8), [ksf], [ksb])
        dst = C.WD[idx][:, :, fb:fb + nf, :].rearrange("dc p fc m -> p dc (fc m)")
        P.dma("sp", dst, sbv.rearrange("p dc fc m -> p dc (fc m)"), [ksb], [("WD", idx, dc) for dc in range(8)])
    A.release(m0)


def norm_modulate(C, xt, kxt, h, kh, T, l, sub, scratch):
    P = C.P
    mv = C.modv[l]
    sq, ksq, rstd, krstd, tmp, ktmp = scratch
    shift = mv["mod"][:, (sub * 3) * 8:(sub * 3 + 1) * 8]
    for st in range(T // 512):
        sl = slice(st * 512, (st + 1) * 512)
        for dc in range(NDC):
            P.add("act", lambda e, dc=dc, sl=sl: e.activation(sq[:, dc, sl], xt[:, dc, sl], AF.Square),
                  [kxt], [(ksq, st)])
        ps, kps = psum_bank(C)
        for dc in range(NDC):
            P.mm(ps, C.onesD_b, sq[:, dc, sl], dc == 0, dc == NDC - 1, [(ksq, st), C.k_onesD], [kps])
        P.add("act", lambda e, ps=ps, sl=sl: e.activation(rstd[:, sl], ps, AF.Ln, bias=C.eps_t[:, 0:1]),
              [kps, C.k_eps], [(krstd, st)])
        P.add("act", lambda e, sl=sl: e.activation(rstd[:, sl], rstd[:, sl], AF.Exp, scale=-0.5),
              [(krstd, st)], [(krstd, st)])
        for dc in range(NDC):
            tm, ktm = tmp[dc % len(tmp)]
            eng = "dve" if dc % 2 == 0 else "pool"
            P.add(eng, lambda e, tm=tm, dc=dc, sl=sl: e.tensor_tensor(tm, xt[:, dc, sl], rstd[:, sl], ALU.mult),
                  [kxt, (krstd, st)], [ktm])
            P.add("act", lambda e, tm=tm, dc=dc, sl=sl: e.activation(
                h[:, dc, sl], tm, AF.Identity, bias=shift[:, dc:dc + 1],
                scale=mv["a"][:, sub * 8 + dc:sub * 8 + dc + 1]),
                [ktm, mv["ka"], mv["kmod"]], [(kh, st)])


def ffn_phase(C, l, w):
    P, A = C.P, C.arena
    idx = l * 2 + w
    sub = 0 if w == 0 else 2
    mv = C.modv[l]
    T = 1024
    m0 = A.mark()
    xb = [A.alloc(f"fx{i}", [NDC, T], F32) for i in range(2)]
    h, kh = A.alloc("fh", [NDC, T], BF16)
    act, kact = A.alloc("fact", [NF, T], BF16)
    sq, ksq = A.alloc("fsq", [NDC, T], BF16)
    rstd, krstd = A.alloc("frstd", [T], F32)
    tmp = [A.alloc(f"ftmp{i}", [512], F32) for i in range(3)]
    sg = [A.alloc(f"fsg{i}", [512], F32) for i in range(3)]
    wgu = [A.alloc(f"fwgu{i}", [2, 8, 128], BF16) for i in range(4)]
    wdb = [A.alloc(f"fwd{i}", [NF, 128], BF16) for i in range(3)]
    NM = S // T
    items = []
    for m in range(NM):
        for f in range(NF):
            items.append(("g", f))
        for dc in range(NDC):
            items.append(("d", dc))
    issued = [0]
    cnt = {"g": 0, "d": 0}
    slot_of = {}

    def prefetch(upto):
        while issued[0] < min(upto, len(items)):
            kind, j = items[issued[0]]
            if kind == "g":
                buf, kb = wgu[cnt["g"] % 4]
                cnt["g"] += 1
                P.dma("sp", buf, C.WGU[idx][j].rearrange("p g kc m -> p g kc m"), [("WGU", idx, j)], [kb])
            else:
                buf, kb = wdb[cnt["d"] % 3]
                cnt["d"] += 1
                P.dma("sp", buf, C.WD[idx][j], [("WD", idx, j)], [kb])
            slot_of[issued[0]] = (buf, kb)
            issued[0] += 1

    def load_x(m):
        xt, kxt = xb[m % 2]
        P.dma("act", xt, C.xT[:, :, m * T:(m + 1) * T], [("xT", 2 * m), ("xT", 2 * m + 1)], [kxt])

    load_x(0)
    pos = 0
    for m in range(NM):
        xt, kxt = xb[m % 2]
        prefetch(pos + 3)
        norm_modulate(C, xt, kxt, h, kh, T, l, sub, (sq, ksq, rstd, krstd, tmp, None))
        if m + 1 < NM:
            load_x(m + 1)
        if m == DBG_M and idx == 0:
            dbg(C, 'rstd', rstd[:, 0:512], (krstd, 0), 512)
            dbg(C, 'h0', h[:, 0, 0:512], (kh, 0), 512)
            dbg(C, 'h7', h[:, 7, 0:512], (kh, 0), 512)
        for f in range(NF):
            prefetch(pos + 3)
            wbuf, kwb = slot_of.pop(pos)
            pos += 1
            for st in range(T // 512):
                sl = slice(st * 512, (st + 1) * 512)
                psg, kpsg = psum_bank(C)
                psu, kpsu = psum_bank(C)
                for kc in range(8):
                    P.mm(psg, wbuf[:, 0, kc, :], h[:, kc, sl], kc == 0, kc == 7, [kwb, (kh, st)], [kpsg])
                for kc in range(8):
                    P.mm(psu, wbuf[:, 1, kc, :], h[:, kc, sl], kc == 0, kc == 7, [kwb, (kh, st)], [kpsu])
                s_, ks_ = sg[(f * 2 + st) % 3]
                P.add("act", lambda e, s_=s_, psg=psg: e.activation(s_, psg, AF.Silu), [kpsg], [ks_])
                P.add("dve", lambda e, s_=s_, psu=psu, f=f, sl=sl: e.tensor_tensor(act[:, f, sl], s_, psu, ALU.mult),
                      [ks_, kpsu], [(kact, st)])
        for dc in range(NDC):
            prefetch(pos + 3)
            wbuf, kwb = slot_of.pop(pos)
            pos += 1
            for st in range(T // 512):
                sl = slice(st * 512, (st + 1) * 512)
                pso, kpso = psum_bank(C)
                for f in range(NF):
                    P.mm(pso, wbuf[:, f, :], act[:, f, sl], f == 0, f == NF - 1, [kwb, (kact, st)], [kpso])
                P.add("dve", lambda e, pso=pso, dc=dc, sl=sl, xt=xt: e.scalar_tensor_tensor(
                    xt[:, dc, sl], pso, mv["gate"][:, sub * 8 + dc:sub * 8 + dc + 1], xt[:, dc, sl],
                    ALU.mult, ALU.add), [kpso, mv["kgate"], kxt], [kxt])
        if m == DBG_M and idx == 0:
            dbg(C, 'act0', act[:, 0, 0:512], (kact, 0), 512)
            dbg(C, 'act21', act[:, 21, 0:512], (kact, 0), 512)
            dbg(C, 'xo0', xt[:, 0, 0:512], kxt, 512)
        P.dma("act", C.xT[:, :, m * T:(m + 1) * T], xt, [kxt], [("xT", 2 * m), ("xT", 2 * m + 1)])
    A.release(m0)


PI = float(np.pi)


def const_tile(C, name, val, dtype=F32, n=1):
    ap, k = C.arena.alloc("c_" + name, [n], dtype)
    C.P.add("pool", lambda e: e.memset(ap, val), [], [k])
    return ap, k


def load_cast_weight(C, name, src_ap, shape, eng="sp"):
    P, A = C.P, C.arena
    n = 1
    for s_ in shape:
        n *= s_
    wb, kwb = A.alloc(name, shape, BF16)
    m0 = A.mark()
    CH = 2048
    flat_b = wb
    stg = [A.alloc(f"{name}_st{i}", [CH], F32) for i in range(2)]
    A.release(m0)
    return wb, kwb, stg


def mla_phase(C):
    P, A = C.P, C.arena
    l = 1
    mv = C.modv[l]
    T = 512
    NT = S // T
    NB = S // 128
    SCALE = float(96 ** -0.5)
    m_phase = A.mark()
    ones384, k384 = const_tile(C, "o384", 1.0 / 384, BF16, 128)
    ones256, k256 = const_tile(C, "o256", 1.0 / 256, BF16, 128)
    ones96, k96 = const_tile(C, "o96", 1.0 / 96, BF16, 128)
    onesrow, krow = const_tile(C, "orow", 1.0, F32, 128)
    hpi, khpi = const_tile(C, "hpi", PI / 2)
    nhpi, knhpi = const_tile(C, "nhpi", -PI / 2)
    pmT_f, kpmf = A.alloc("pmT_f", [96], F32)
    pmT, kpm = A.alloc("pmT", [96], BF16)
    P.dma("sp", pmT_f[0:96, :], C.d_pm.ap(), [], [kpmf])
    P.add("dve", lambda e: e.tensor_copy(pmT[0:96, :], pmT_f[0:96, :]), [kpmf], [kpm])
    gq, kgq = C.vec["mla_qg"]
    gk, kgk = C.vec["mla_kg"]
    gql, kgql = C.vec["mla_qng"]
    gkvl, kgkvl = C.vec["mla_kvng"]
    invf, kinvf = C.vec["invf"]
    qn, kqn = A.alloc("qn", [3, S], BF16)
    kvn, kkvn = A.alloc("kvn", [2, S], BF16)
    kpe, kkpe = A.alloc("kpe", [S], F32)
    sqk, ksqk = A.alloc("sqk", [S], BF16)
    COS, kcos = A.alloc("COS", [S], F32)
    SIN, ksin = A.alloc("SIN", [S], F32)
    wuq, kwuq = A.alloc("wuq", [3, 1536], BF16)
    wukv, kwukv = A.alloc("wukv", [2, 2048], BF16)

    m0 = A.mark()
    posi, kposi = A.alloc("posi", [S], I32)
    ang, kang = A.alloc("ang", [S], F32)
    t1, kt1 = A.alloc("rt1", [S], F32)
    ti, kti = A.alloc("rti", [S], I32)
    P.dma("sp", posi, C.d_pos.ap(), [], [kposi])
    P.add("dve", lambda e: e.tensor_copy(ang, posi), [kposi], [kang])
    P.add("dve", lambda e: e.tensor_scalar(ang, ang, invf[:, 0:1], None, ALU.mult), [kang, kinvf], [kang])
    for (dst, kdst, shift) in ((SIN, ksin, 0.0), (COS, kcos, PI / 2)):
        P.add("dve", lambda e, shift=shift: e.tensor_scalar(t1, ang, shift, 1.0 / (2 * PI), ALU.add, ALU.mult),
              [kang], [kt1])
        P.add("dve", lambda e: e.tensor_copy(ti, t1), [kt1], [kti])
        P.add("dve", lambda e: e.tensor_copy(t1, ti), [kti], [kt1])
        P.add("dve", lambda e: e.scalar_tensor_tensor(t1, t1, -2 * PI, ang, ALU.mult, ALU.add), [kt1, kang], [kt1])
        bias_ap = nhpi if shift == 0.0 else None
        if shift == 0.0:
            P.add("act", lambda e: e.activation(t1, t1, AF.Abs, bias=nhpi[:, 0:1]), [kt1, knhpi], [kt1])
        else:
            P.add("act", lambda e: e.activation(t1, t1, AF.Abs), [kt1], [kt1])
        P.add("act", lambda e, dst=dst: e.activation(dst, t1, AF.Sin, bias=hpi[:, 0:1], scale=-1.0),
              [kt1, khpi], [kdst])
    A.release(m0)

    def load_cast(dst, kdst, src, nk, ncol, colchunk):
        stg = [A.alloc(f"wst{i}", [nk, colchunk], F32) for i in range(2)]
        it = 0
        for c0 in range(0, ncol, colchunk):
            cw = min(colchunk, ncol - c0)
            st, kst = stg[it % 2]
            it += 1
            P.dma("sp", st[:, :, 0:cw], src[:, :, c0:c0 + cw], [], [kst])
            emit_cast(P, cast_engine(C), dst[:, :, c0:c0 + cw], st[:, :, 0:cw], [kst], [kdst])

    m1 = A.mark()
    mm_ = A.mark()
    load_cast(wuq, kwuq, C.d_mla_wuq.ap().rearrange("(kc p) n -> p kc n", p=128), 3, 1536, 512)
    load_cast(wukv, kwukv, C.d_mla_wukv.ap().rearrange("(kc p) n -> p kc n", p=128), 2, 2048, 512)
    A.release(mm_)
    win, kwin = A.alloc("win", [8, 736], BF16)
    P.add("pool", lambda e: e.memset(win, 0.0), [], [kwin])
    wsrc = C.d_mla_win.ap().rearrange("(kc p) n -> p kc n", p=128)
    mm2 = A.mark()
    stg = [A.alloc(f"wst_in{i}", [8, 224], F32) for i in range(2)]
    for i, c0 in enumerate(range(0, 672, 224)):
        st, kst = stg[i % 2]
        P.dma("sp", st, wsrc[:, :, c0:c0 + 224], [], [kst])
        if c0 + 224 <= 640:
            emit_cast(P, cast_engine(C), win[:, :, c0:c0 + 224], st, [kst], [kwin])
        else:
            nl = 640 - c0
            emit_cast(P, cast_engine(C), win[:, :, c0:640], st[:, :, 0:nl], [kst], [kwin])
            emit_cast(P, cast_engine(C), win[:, :, 704:736], st[:, :, nl:nl + 32], [kst], [kwin])

    A.release(mm2)
    xb = [A.alloc(f"mx{i}", [NDC, T], F32) for i in range(2)]
    h, kh = A.alloc("mh", [NDC, T], BF16)
    sq, ksq = A.alloc("msq", [NDC, T], BF16)
    rstd, krstd = A.alloc("mrstd", [T], F32)
    tmp = [A.alloc(f"mtmp{i}", [512], F32) for i in range(3)]
    lsq, klsq = A.alloc("mlsq", [3, T], BF16)
    lrs, klrs = A.alloc("mlrs", [T], F32)

    def load_x(m):
        xt, kxt = xb[m % 2]
        P.dma("act", xt, C.xT[:, :, m * T:(m + 1) * T], [("xT", m)], [kxt])

    load_x(0)
    for m in range(NT):
        xt, kxt = xb[m % 2]
        sl = slice(m * T, (m + 1) * T)
        norm_modulate(C, xt, kxt, h, kh, T, l, 1, (sq, ksq, rstd, krstd, tmp, None))
        if m + 1 < NT:
            load_x(m + 1)
        for (c0, nch, ones, kones, dst, kdst, g) in ((0, 3, ones384, k384, qn, kqn, gql), (3, 2, ones256, k256, kvn, kkvn, gkvl)):
            banks = []
            for c in range(nch):
                ps, kps = psum_bank(C)
                banks.append((ps, kps))
                for kc in range(8):
                    P.mm(ps, win[:, kc, (c0 + c) * 128:(c0 + c + 1) * 128], h[:, kc, :], kc == 0, kc == 7,
                         [kwin, (kh, 0)], [kps])
                P.add("act", lambda e, ps=ps, c=c: e.activation(lsq[:, c, :], ps, AF.Square), [kps], [klsq])
            pss, kpss = psum_bank(C)
            for c in range(nch):
                P.mm(pss, ones, lsq[:, c, :], c == 0, c == nch - 1, [klsq, kones], [kpss])
            P.add("act", lambda e, pss=pss: e.activation(lrs, pss, AF.Ln, bias=C.eps_t[:, 0:1]), [kpss, C.k_eps], [klrs])
            P.add("act", lambda e: e.activation(lrs, lrs, AF.Exp, scale=-0.5), [klrs], [klrs])
            for c in range(nch):
                ps, kps = banks[c]
                tm, ktm = tmp[c % 3]
                P.add("dve", lambda e, tm=tm, ps=ps: e.tensor_tensor(tm, ps, lrs, ALU.mult), [kps, klrs], [ktm])
                P.add("act", lambda e, tm=tm, c=c, dst=dst, g=g, sl=sl: e.activation(
                    dst[:, c, sl], tm, AF.Identity, scale=g[:, c:c + 1]), [ktm], [kdst])
        ps, kps = psum_bank(C)
        for kc in range(8):
            P.mm(ps[0:96, :], win[:, kc, 640:736], h[:, kc, :], kc == 0, kc == 7, [kwin, (kh, 0)], [kps])
        P.add("act", lambda e, ps=ps, sl=sl: e.copy(kpe[64:96, sl], ps[64:96, :]), [kps], [kkpe])
        P.add("act", lambda e, ps=ps, sl=sl: e.activation(sqk[64:96, sl], ps[64:96, :], AF.Square), [kps], [(ksqk, "pe")])
    A.release(m1)

    kT = [A.alloc(f"kT{i}", [S], BF16) for i in range(2)]
    Vau = [A.alloc(f"Vau{i}", [NB, 128], BF16) for i in range(2)]
    OTs = [A.alloc(f"OTs{i}", [S], BF16) for i in range(2)]
    for par in range(2):
        va, kva = Vau[par]
        P.add("pool", lambda e, va=va: e.memset(va, 1.0), [], [kva])
    rsk, krsk = A.alloc("rsk", [T], F32)
    rt1 = [A.alloc(f"rp1_{i}", [T], F32) for i in range(2)]
    rt2 = [A.alloc(f"rp2_{i}", [T], F32) for i in range(2)]
    qsq, kqsq = A.alloc("qsq", [T], BF16)
    qT = [A.alloc(f"qT{i}", [T], BF16) for i in range(2)]
    PT = [A.alloc(f"PT{i}", [T], BF16) for i in range(4)]
    osb, kosb = A.alloc("osb", [T], F32)
    rl, krl = A.alloc("rl", [T], F32)
    pti = 0

    C.held = []

    def bank_excl(excl):
        while True:
            ps, kps = psum_bank(C)
            if all(ps is not x for x in excl) and all(ps is not x for x in C.held):
                return ps, kps

    def norm_rope(src_ps, ksrc_ps, src_sb, ksrc_sb, sqt, ksqt_keys, g, kg, dstT, kdstT, sl, tl):
        yield
        pss, kpss = bank_excl([])
        P.mm(pss[0:96, :], ones96[0:96, 0:96], sqt, True, True, ksqt_keys + [k96], [kpss])
        P.add("act", lambda e: e.activation(rsk[0:96, :], pss[0:96, :], AF.Ln, bias=C.eps_t[0:96, 0:1]),
              [kpss, C.k_eps], [krsk])
        P.add("act", lambda e: e.activation(rsk[0:96, :], rsk[0:96, :], AF.Exp, scale=-0.5), [krsk], [krsk])
        if src_sb is None:
            P.add("dve", lambda e: e.scalar_tensor_tensor(dstT[0:96, sl], src_ps[0:96, :], g[0:96, 0:1], rsk[0:96, :],
                                                          ALU.mult, ALU.mult), [ksrc_ps, kg, krsk], [kdstT])
        else:
            P.add("dve", lambda e: e.scalar_tensor_tensor(dstT[0:64, sl], src_ps[0:64, :], g[0:64, 0:1], rsk[0:64, :],
                                                          ALU.mult, ALU.mult), [ksrc_ps, kg, krsk], [kdstT])
            P.add("dve", lambda e: e.scalar_tensor_tensor(dstT[64:96, sl], src_sb[64:96, tl], g[64:96, 0:1],
                                                          rsk[64:96, :], ALU.mult, ALU.mult), [ksrc_sb, kg, krsk], [kdstT])
        C.held[:] = [x for x in C.held if x is not src_ps]
        yield
        psr, kpsr = bank_excl([])
        P.mm(psr[0:96, :], pmT[0:96, 0:96], dstT[0:96, sl], True, True, [kdstT, kpm], [kpsr])
        a1, ka1 = rt1[C.ps_i % 2]
        a2, ka2 = rt2[C.ps_i % 2]
        P.add("pool", lambda e: e.tensor_tensor(a1[64:96, :], dstT[64:96, sl], COS[64:96, tl], ALU.mult),
              [kdstT, kcos], [ka1])
        P.add("dve", lambda e: e.tensor_tensor(a2[64:96, :], psr[64:96, :], SIN[64:96, tl], ALU.mult),
              [kpsr, ksin], [ka2])
        P.add("dve", lambda e: e.tensor_tensor(dstT[64:96, sl], a1[64:96, :], a2[64:96, :], ALU.add),
              [ka1, ka2], [kdstT])

    LA = 3
    PTn = [A.alloc(f"PTn{i}", [T], BF16) for i in range(LA + 3)]

    def prepK(hd, m):
        kt_, kkt = kT[hd % 2]
        tl = slice(m * T, (m + 1) * T)
        ps, kps = bank_excl([])
        C.held.append(ps)
        for kc in range(2):
            P.mm(ps[0:64, :], wukv[:, kc, hd * 128:hd * 128 + 64], kvn[:, kc, tl], kc == 0, kc == 1,
                 [kwukv, kkvn], [kps])
        P.add("act", lambda e, ps=ps, tl=tl: e.activation(sqk[0:64, tl], ps[0:64, :], AF.Square),
              [kps], [(ksqk, "n", m)])
        yield from norm_rope(ps, kps, kpe, kkpe, sqk[0:96, tl], [(ksqk, "n", m), (ksqk, "pe")], gk, kgk, kt_, kkt, tl, tl)

    def prepV(hd, b0):
        va, kva = Vau[hd % 2]
        voff = 0 if hd % 2 == 0 else 64
        ps, kps = bank_excl([])
        for j in range(8):
            blk = b0 + j
            for kc in range(2):
                P.mm(ps[:, j * 64:(j + 1) * 64], kvn[:, kc, blk * 128:(blk + 1) * 128],
                     wukv[:, kc, hd * 128 + 64:hd * 128 + 128], kc == 0, kc == 1, [kwukv, kkvn], [kps])
        P.add("dve", lambda e, ps=ps, b0=b0, va=va, voff=voff: e.tensor_copy(
            va[:, b0:b0 + 8, voff:voff + 64], ps.rearrange("p (j v) -> p j v", j=8)), [kps], [kva])
        yield

    qcount = [0]

    def prepQ(hd, m, q_, kq_):
        tl = slice(m * T, (m + 1) * T)
        ps, kps = bank_excl([])
        C.held.append(ps)
        for kc in range(3):
            P.mm(ps[0:96, :], wuq[:, kc, hd * 96:(hd + 1) * 96], qn[:, kc, tl], kc == 0, kc == 2,
                 [kwuq, kqn], [kps])
        P.add("act", lambda e, ps=ps: e.activation(qsq[0:96, :], ps[0:96, :], AF.Square), [kps], [kqsq])
        yield from norm_rope(ps, kps, None, None, qsq[0:96, :], [kqsq], gq, kgq, q_, kq_, slice(0, T), tl)

    def run_all(gen):
        for _ in gen:
            pass

    def next_q():
        q = qT[qcount[0] % 2]
        qcount[0] += 1
        return q

    for m in range(NT):
        run_all(prepK(0, m))
    for b0 in range(0, NB, 8):
        run_all(prepV(0, b0))
    nextq = next_q()
    run_all(prepQ(0, 0, nextq[0], nextq[1]))
    vgroups = list(range(0, NB, 8))
    for hd in range(16):
        par = hd % 2
        kt_, kkt = kT[par]
        va, kva = Vau[par]
        ots, kots = OTs[(hd // 2) % 2]
        oh = 0 if par == 0 else 64
        lp = 64 if par == 0 else 0
        for m in range(NT):
            tl = slice(m * T, (m + 1) * T)
            q_, kq_ = nextq
            gens = []
            if m + 1 < NT:
                nextq = next_q()
                gens.append(prepQ(hd, m + 1, nextq[0], nextq[1]))
            elif hd + 1 < 16:
                nextq = next_q()
                gens.append(prepQ(hd + 1, 0, nextq[0], nextq[1]))
            if hd + 1 < 16:
                gens.append(prepK(hd + 1, m))
                if m < len(vgroups):
                    gens.append(prepV(hd + 1, vgroups[m]))
            pso, kpso = bank_excl([])
            C.held.append(pso)
            pend = []
            for kb in range(NB + LA):
                if kb < NB:
                    pss, kpss = bank_excl([])
                    P.mm(pss, kt_[0:96, kb * 128:(kb + 1) * 128], q_[0:96, :], True, True, [kkt, kq_], [kpss])
                    pt, kpt = PTn[pti % len(PTn)]
                    pti += 1
                    P.add("act", lambda e, pt=pt, pss=pss: e.activation(pt, pss, AF.Exp, scale=SCALE), [kpss], [kpt])
                    pend.append((kb, pt, kpt))
                if kb >= LA:
                    kb2, pt, kpt = pend.pop(0)
                    P.mm(pso, va[:, kb2, :], pt, kb2 == 0, kb2 == NB - 1, [kva, kpt], [kpso])
                if kb % 6 == 2 and gens:
                    gi = (kb // 6) % len(gens)
                    for g_ in list(gens):
                        try:
                            next(g_)
                        except StopIteration:
                            gens.remove(g_)
            for g_ in gens:
                run_all(g_)
            P.add("dve", lambda e, pso=pso, lp=lp: e.reciprocal(rl[lp:lp + 1, :], pso[lp:lp + 1, :]), [kpso], [krl])
            P.add("act", lambda e, pso=pso, oh=oh: e.copy(osb[oh:oh + 64, :], pso[oh:oh + 64, :]), [kpso], [kosb])
            psb, kpsb = bank_excl([])
            P.mm(psb, onesrow[lp:lp + 1, :], rl[lp:lp + 1, :], True, True, [krl, krow], [kpsb])
            P.add("dve", lambda e, psb=psb, oh=oh, ots=ots, tl=tl: e.tensor_tensor(
                ots[oh:oh + 64, tl], osb[oh:oh + 64, :], psb[oh:oh + 64, :], ALU.mult), [kosb, kpsb], [kots])
            C.held[:] = [x for x in C.held if x is not pso]
        if par == 1:
            c = hd // 2
            P.dma("sp", C.OT[c], ots, [kots], [("OT", c)])
    A.release(m_phase)

    m3 = A.mark()
    wout, kwout = A.alloc("wout", [8, 1024], BF16)
    stg = [A.alloc(f"wst_o{i}", [8, 256], F32) for i in range(2)]
    wsrc = C.d_mla_wout.ap().rearrange("(kc p) n -> p kc n", p=128)
    for i, c0 in enumerate(range(0, 1024, 256)):
        st, kst = stg[i % 2]
        P.dma("sp", st, wsrc[:, :, c0:c0 + 256], [], [kst])
        emit_cast(P, cast_engine(C), wout[:, :, c0:c0 + 256], st, [kst], [kwout])
    xb = [A.alloc(f"ox{i}", [NDC, T], F32) for i in range(2)]
    ob = [A.alloc(f"oo{i}", [NDC, T], BF16) for i in range(2)]
    for m in range(NT):
        xt, kxt = xb[m % 2]
        ot, kot = ob[m % 2]
        tl = slice(m * T, (m + 1) * T)
        P.dma("act", xt, C.xT[:, :, tl], [("xT", m)], [kxt])
        P.dma("sp", ot, C.OT.rearrange("c p s -> p c s")[:, :, tl], [("OT", c) for c in range(8)], [kot])
        for dc in range(NDC):
            ps, kps = psum_bank(C)
            for kc in range(8):
                P.mm(ps, wout[:, kc, dc * 128:(dc + 1) * 128], ot[:, kc, :], kc == 0, kc == 7, [kwout, kot], [kps])
            P.add("dve", lambda e, ps=ps, dc=dc, xt=xt: e.scalar_tensor_tensor(
                xt[:, dc, :], ps, mv["gate"][:, 8 + dc:8 + dc + 1], xt[:, dc, :], ALU.mult, ALU.add),
                [kps, mv["kgate"], kxt], [kxt])
        P.dma("act", C.xT[:, :, tl], xt, [kxt], [("xT", m)])
    A.release(m3)


def bc_last(ap, n):
    return ap.unsqueeze(2).to_broadcast([ap.shape[0], ap.shape[1], n])


def bc_mid(ap, n):
    return ap.unsqueeze(1).to_broadcast([ap.shape[0], n, ap.shape[1]])


def ssd_phase(C):
    P, A = C.P, C.arena
    l = 0
    mv = C.modv[l]
    NB = S // 128
    T = 512
    NT = S // T
    m_phase = A.mark()
    one_t, kone = const_tile(C, "one", 1.0)
    masks, kmask = A.alloc("masks", [6, 128], F32)
    P.dma("sp", masks, C.d_masks.ap().rearrange("p (a b) -> p a b", a=6), [], [kmask])
    ones_f, konesf = const_tile(C, "ones_f", 1.0, F32, 128)
    cw, kcw = C.vec["ssd_cw"]
    cb_, kcb = C.vec["ssd_cb"]
    dtb, kdtb = C.vec["ssd_dtb"]
    alog, kalog = C.vec["ssd_alog"]
    dsk, kdsk = C.vec["ssd_dskip"]
    Aneg, kAneg = A.alloc("Aneg", [64], F32)
    P.add("act", lambda e: e.activation(Aneg, alog, AF.Exp), [kalog], [kAneg])
    P.add("dve", lambda e: e.tensor_scalar(Aneg, Aneg, -1.0, None, ALU.mult), [kAneg], [kAneg])
    h_all, khall = A.alloc("h_all", [NDC, S], BF16)

    m0 = A.mark()
    xb = [A.alloc(f"sx{i}", [NDC, T], F32) for i in range(2)]
    sq, ksq = A.alloc("ssq", [NDC, T], BF16)
    rstd, krstd = A.alloc("srstd", [T], F32)
    tmp = [A.alloc(f"stmp{i}", [512], F32) for i in range(3)]
    for m in range(NT):
        xt, kxt = xb[m % 2]
        P.dma("act", xt, C.xT[:, :, m * T:(m + 1) * T], [("xT", m)], [kxt])
        norm_modulate(C, xt, kxt, h_all[:, :, m * T:(m + 1) * T], (khall, m), T, l, 1, (sq, ksq, rstd, krstd, tmp, None))
    A.release(m0)
    hkeys = [((khall, m), 0) for m in range(NT)]

    m0 = A.mark()
    wsrc = C.d_ssd_win.ap().rearrange("(kc p) n -> p kc n", p=128)
    wst = [A.alloc(f"swst{i}", [8, 128], F32) for i in range(2)]
    wcb = [A.alloc(f"swc{i}", [8, 128], BF16) for i in range(2)]
    pre = [A.alloc(f"spre{i}", [S + 4], F32) for i in range(2)]
    acc = [A.alloc(f"sacc{i}", [S], F32) for i in range(2)]
    xo = [A.alloc(f"sxo{i}", [S], BF16) for i in range(2)]
    for i in range(2):
        pr, kpr = pre[i]
        P.add("pool", lambda e, pr=pr: e.memset(pr, 0.0), [], [kpr])
    for c in range(32):
        ws, kws = wst[c % 2]
        wc, kwc = wcb[c % 2]
        pr, kpr = pre[c % 2]
        ac, kac = acc[c % 2]
        xo_, kxo = xo[c % 2]
        P.dma("sp", ws, wsrc[:, :, 2048 + c * 128:2048 + (c + 1) * 128], [], [kws])
        P.add("pool", lambda e, wc=wc, ws=ws: e.tensor_copy(wc, ws), [kws], [kwc])
        for m in range(NT):
            ps, kps = psum_bank(C)
            for kc in range(8):
                P.mm(ps, wc[:, kc, :], h_all[:, kc, m * T:(m + 1) * T], kc == 0, kc == 7, [kwc, hkeys[m]], [kps])
            P.add("act", lambda e, ps=ps, pr=pr, m=m: e.copy(pr[:, 2 + m * T:2 + (m + 1) * T], ps), [kps], [kpr])
        for hf in range(2):
            o0 = hf * (S // 2)
            n_ = S // 2
            P.add("dve", lambda e, ac=ac, pr=pr, c=c, o0=o0, n_=n_: e.tensor_scalar(
                ac[:, o0:o0 + n_], pr[:, o0:o0 + n_], cw[:, c * 5:c * 5 + 1], cb_[:, c:c + 1], ALU.mult, ALU.add),
                [kpr, kcw, kcb], [(kac, hf)])
            for j in range(1, 5):
                P.add("dve", lambda e, ac=ac, pr=pr, c=c, j=j, o0=o0, n_=n_: e.scalar_tensor_tensor(
                    ac[:, o0:o0 + n_], pr[:, o0 + j:o0 + j + n_], cw[:, c * 5 + j:c * 5 + j + 1], ac[:, o0:o0 + n_],
                    ALU.mult, ALU.add), [kpr, kcw, (kac, hf)], [(kac, hf)])
            P.add("act", lambda e, xo_=xo_, ac=ac, o0=o0, n_=n_: e.activation(xo_[:, o0:o0 + n_], ac[:, o0:o0 + n_], AF.Silu),
                  [(kac, hf)], [(kxo, hf)])
        P.dma("sp", C.XBC[c], xo_, [(kxo, 0), (kxo, 1)], [("XBC", c)])
    A.release(m0)

    wdt, kwdt = A.alloc("wdt", [8, 64], BF16)
    m_big = A.mark()
    wz, kwz = A.alloc("wz", [8, 2048], BF16)
    m0 = A.mark()
    stg = [A.alloc(f"szst{i}", [8, 256], F32) for i in range(2)]
    it = 0
    for c0 in range(0, 2048, 256):
        st, kst = stg[it % 2]
        it += 1
        P.dma("sp", st, wsrc[:, :, c0:c0 + 256], [], [kst])
        emit_cast(P, cast_engine(C), wz[:, :, c0:c0 + 256], st, [kst], [kwz])
    st, kst = stg[it % 2]
    it += 1
    P.dma("sp", st[:, :, 0:64], wsrc[:, :, 6144:6208], [], [kst])
    emit_cast(P, cast_engine(C), wdt, st[:, :, 0:64], [kst], [kwdt])
    A.release(m0)
    ng16, kng16 = C.vec["ssd_ng16"]

    xbcT = [A.alloc(f"xbcT{i}", [32, 128], BF16) for i in range(1)]
    xs_tm, kxs = A.alloc("xs_tm", [32, 64], BF16)
    B_tm, kbt = A.alloc("B_tm", [8, 128], BF16)
    dt_, kdt = A.alloc("dt", [64], F32)
    a_, ka = A.alloc("a", [64], F32)
    dec, kdec = A.alloc("dec", [96], F32)
    dt2, kdt2 = A.alloc("dt2", [32], F32)
    xdt, kxdt = A.alloc("xdt", [32, 64], BF16)
    xdtE, kxdtE = A.alloc("xdtE", [32, 64], BF16)
    cbm, kcbm = A.alloc("cbm", [8, 128], F32)
    Lb = [A.alloc(f"Lb{i}", [4, 128], F32) for i in range(2)]
    ex = [A.alloc(f"ex{i}", [4, 128], F32) for i in range(2)]
    MT = [A.alloc(f"MT{i}", [4, 128], BF16) for i in range(2)]
    yo = [A.alloc(f"yo{i}", [256], F32) for i in range(2)]
    ydir, kydir = A.alloc("ydir", [2048], F32)
    H, kH = A.alloc("H", [2048], F32)
    Hb, kHb = A.alloc("Hb", [2048], BF16)
    yb_in, kybin = A.alloc("yb_in", [2048], F32)
    sz, ksz = A.alloc("sz", [2048], F32)
    gss, kgss = A.alloc("gss", [8], F32)
    ynb, kynb = A.alloc("ynb", [2048], BF16)
    yT, kyT = A.alloc("yT", [16, 128], BF16)
    xck, kxck = A.alloc("xck", [NDC, 128], F32)

    def chunk_step(ck, d, it_):
        tk = slice(ck * 128, (ck + 1) * 128)
        xb_, kxb = xbcT[0]
        Lm = masks[:, 0 + 2 * d, :]
        Rm = masks[:, 1 + 2 * d, :]
        Vm = masks[:, 4 + d, :]
        dc0 = d * 32
        P.dma("sp", xb_, C.XBC.rearrange("c p s -> p c s")[:, :, tk], [("XBC", c) for c in range(32)], [kxb])
        for q in range(3):
            ps, kps = psum_bank(C)
            psb = ps.bitcast(BF16)
            for j in range(8):
                c = q * 8 + j
                P.add("pe", lambda e, psb=psb, j=j, c=c, xb_=xb_: e.transpose(
                    psb[:, j * 128:(j + 1) * 128], xb_[:, c, :], C.ident_b), [kxb, C.k_ident_b], [kps])
            if q < 2:
                P.add("act", lambda e, psb=psb, q=q: e.copy(
                    xs_tm.rearrange("p a b -> p (a b)")[:, q * 1024:(q + 1) * 1024], psb), [kps], [kxs])
            else:
                P.add("dve", lambda e, psb=psb: e.tensor_copy(B_tm.rearrange("p a b -> p (a b)"), psb), [kps], [kbt])
        ps, kps = psum_bank(C)
        for kc in range(8):
            P.mm(ps[:, 0:64], h_all[:, kc, tk], wdt[:, kc, :], kc == 0, kc == 7, [hkeys[ck // 4], kwdt], [kps])
        P.add("dve", lambda e, ps=ps: e.tensor_tensor(dt_, ps[:, 0:64], dtb, ALU.add), [kps, kdtb], [kdt])
        P.add("act", lambda e: e.activation(dt_, dt_, AF.Exp), [kdt], [kdt])
        P.add("act", lambda e: e.activation(dt_, dt_, AF.Ln, bias=one_t[:, 0:1]), [kdt, kone], [kdt])
        P.add("dve", lambda e: e.tensor_tensor(a_, dt_, Aneg, ALU.mult), [kdt, kAneg], [ka])
        ps, kps = psum_bank(C)
        P.mm(ps[:, 0:32], Rm, a_[:, dc0:dc0 + 32], True, True, [kmask, ka], [kps])
        P.mm(ps[:, 32:64], Lm, a_[:, dc0:dc0 + 32], True, True, [kmask, ka], [kps])
        P.mm(ps[:, 64:96], ones_f, a_[:, dc0:dc0 + 32], True, True, [konesf, ka], [kps])
        P.add("act", lambda e, ps=ps: e.activation(dec, ps[:, 0:96], AF.Exp), [kps], [kdec])
        P.add("dve", lambda e: e.tensor_tensor(dt2, dt_[:, dc0:dc0 + 32], dec[:, 32:64], ALU.mult), [kdt, kdec], [kdt2])
        P.add("pool", lambda e: e.tensor_tensor(xdt, xs_tm, bc_last(dt_[:, dc0:dc0 + 32], 64), ALU.mult),
              [kxs, kdt], [kxdt])
        P.add("pool", lambda e: e.tensor_tensor(xdtE, xs_tm, bc_last(dt2, 64), ALU.mult), [kxs, kdt2], [kxdtE])
        for half in range(2):
            ps, kps = psum_bank(C)
            for j in range(4):
                g = half * 4 + j
                P.mm(ps[:, j * 128:(j + 1) * 128], xb_[:, 16 + g, :], xb_[:, 24 + g, :], True, True, [kxb], [kps])
            P.add("dve", lambda e, ps=ps, half=half: e.tensor_tensor(
                cbm[:, half * 4:(half + 1) * 4, :], ps.rearrange("p (a b) -> p a b", a=4), bc_mid(Vm, 4), ALU.mult),
                [kps, kmask], [kcbm])
        for g in range(8):
            lb, klb = Lb[g % 2]
            ex_, kex = ex[g % 2]
            mt, kmt = MT[g % 2]
            yo_, kyo = yo[g % 2]
            P.add("dve", lambda e, lb=lb, g=g: e.tensor_tensor(
                lb, bc_mid(Lm, 4), bc_last(a_[:, dc0 + g * 4:dc0 + g * 4 + 4], 128), ALU.mult), [kmask, ka], [klb])
            ps, kps = psum_bank(C)
            for r in range(4):
                P.mm(ps[:, r * 128:(r + 1) * 128], lb[:, r, :], Rm, True, True, [klb, kmask], [kps])
            P.add("act", lambda e, ps=ps, ex_=ex_: e.activation(ex_.rearrange("p a b -> p (a b)"), ps, AF.Exp), [kps], [kex])
            P.add("dve", lambda e, mt=mt, ex_=ex_, g=g: e.tensor_tensor(mt, ex_, bc_mid(cbm[:, g, :], 4), ALU.mult),
                  [kex, kcbm], [kmt])
            psy, kpsy = psum_bank(C)
            for r in range(4):
                P.mm(psy[:, r * 64:(r + 1) * 64], mt[:, r, :], xdt[:, g * 4 + r, :], True, True, [kmt, kxdt], [kpsy])
            P.mm(psy[:, 256:512], xb_[:, 24 + g, :], Hb[:, g * 256:(g + 1) * 256], True, True, [kxb, kHb], [kpsy])
            P.add("dve", lambda e, psy=psy, yo_=yo_, g=g: e.tensor_tensor(
                yo_.rearrange("p (a b) -> p a b", a=4), psy[:, 256:512].rearrange("p (a b) -> p a b", a=4),
                bc_last(dec[:, g * 4:g * 4 + 4], 64), ALU.mult), [kpsy, kdec], [kyo])
            P.add("dve", lambda e, psy=psy, yo_=yo_, g=g: e.tensor_tensor(
                ydir[:, g * 256:(g + 1) * 256], psy[:, 0:256], yo_, ALU.add), [kpsy, kyo], [(kydir, g)])
            pss, kpss = psum_bank(C)
            P.mm(pss[:, 0:256], B_tm[:, g, :], xdtE[:, g * 4:g * 4 + 4, :].rearrange("p a b -> p (a b)"), True, True,
                 [kbt, kxdtE], [kpss])
            P.add("dve", lambda e, g=g: e.tensor_tensor(
                H[:, g * 256:(g + 1) * 256].rearrange("p (a b) -> p a b", a=4),
                H[:, g * 256:(g + 1) * 256].rearrange("p (a b) -> p a b", a=4),
                bc_last(dec[:, 64 + g * 4:64 + g * 4 + 4], 64), ALU.mult), [(kH, g), kdec], [(kH, g)])
            P.add("dve", lambda e, pss=pss, g=g: e.tensor_tensor(
                H[:, g * 256:(g + 1) * 256], H[:, g * 256:(g + 1) * 256], pss[:, 0:256], ALU.add),
                [(kH, g), kpss], [(kH, g)])
            P.add("act", lambda e, g=g: e.copy(Hb[:, g * 256:(g + 1) * 256], H[:, g * 256:(g + 1) * 256]),
                  [(kH, g)], [kHb])

    ykeys = [(kydir, g) for g in range(8)]
    P.add("pool", lambda e: e.memset(H, 0.0), [], [(kH, g) for g in range(8)])
    P.add("pool", lambda e: e.memset(Hb, 0.0), [], [kHb])
    it_ = 0
    for ck in range(NB - 1, -1, -1):
        chunk_step(ck, 1, it_)
        it_ += 1
        P.dma("sp", C.YB[ck], ydir, ykeys, [("YB", ck)])
        tk = slice(ck * 128, (ck + 1) * 128)
        for zc in range(4):
            ps, kps = psum_bank(C)
            for kc in range(8):
                P.mm(ps, h_all[:, kc, tk], wz[:, kc, zc * 512:(zc + 1) * 512], kc == 0, kc == 7,
                     [hkeys[ck // 4], kwz], [kps])
            P.add("act", lambda e, ps=ps, zc=zc: e.activation(sz[:, zc * 512:(zc + 1) * 512], ps, AF.Silu), [kps], [ksz])
        P.dma("sp", C.SZ[ck], sz, [ksz], [("SZ", ck)])
    wout = wz.rearrange("p a b -> p (a b)").rearrange("p (k n) -> p k n", k=16)
    kwout = kwz
    stg = [(yb_in.rearrange("p (k n) -> p k n", k=2), kybin), (sz.rearrange("p (k n) -> p k n", k=2), ksz)]
    osrc = C.d_ssd_wout.ap().rearrange("(kc p) n -> p kc n", p=128)
    for i_, c0 in enumerate(range(0, 16, 2)):
        st, kst = stg[i_ % 2]
        P.dma("sp", st, osrc[:, c0:c0 + 2, :], [], [kst])
        emit_cast(P, cast_engine(C), wout[:, c0:c0 + 2, :], st, [kst], [kwout])
    P.add("pool", lambda e: e.memset(H, 0.0), [], [(kH, g) for g in range(8)])
    P.add("pool", lambda e: e.memset(Hb, 0.0), [], [kHb])
    for ck in range(NB):
        tk = slice(ck * 128, (ck + 1) * 128)
        P.dma("act", yb_in, C.YB[ck], [("YB", ck)], [kybin])
        P.dma("act", xck, C.xT[:, :, tk], [("xT", ck // 4)], [kxck])
        chunk_step(ck, 0, it_)
        it_ += 1
        P.add("dve", lambda e: e.tensor_tensor(ydir, ydir, yb_in, ALU.add), ykeys + [kybin], ykeys)
        P.add("pool", lambda e: e.tensor_tensor(yb_in.rearrange("p (a b) -> p a b", a=32), xs_tm, bc_last(dsk, 64), ALU.mult),
              [kxs, kdsk], [kybin])
        P.add("dve", lambda e: e.tensor_tensor(ydir, ydir, yb_in, ALU.add), ykeys + [kybin], ykeys)
        P.dma("sp", sz, C.SZ[ck], [("SZ", ck)], [ksz])
        P.add("dve", lambda e: e.tensor_tensor(ydir, ydir, sz, ALU.mult), ykeys + [ksz], ykeys)
        P.add("pool", lambda e: e.tensor_tensor(sz, ydir, ydir, ALU.mult), ykeys, [ksz])
        P.add("dve", lambda e: e.reduce_sum(gss, sz.rearrange("p (a b) -> p a b", a=8), AX.X), [ksz], [kgss])
        P.add("act", lambda e: e.activation(gss, gss, AF.Ln, bias=C.eps_t[:, 0:1], scale=1.0 / 256), [kgss, C.k_eps], [kgss])
        P.add("act", lambda e: e.activation(gss, gss, AF.Exp, scale=-0.5), [kgss], [kgss])
        P.add("dve", lambda e: e.tensor_tensor(ydir.rearrange("p (a b) -> p a b", a=8), ydir.rearrange("p (a b) -> p a b", a=8),
                                               bc_last(gss, 256), ALU.mult), ykeys + [kgss], ykeys)
        P.add("act", lambda e: e.copy(ynb, ydir), ykeys, [kynb])
        for q in range(2):
            ps, kps = psum_bank(C)
            psb = ps.bitcast(BF16)
            for j in range(8):
                c = q * 8 + j
                P.add("pe", lambda e, psb=psb, j=j, c=c: e.transpose(
                    psb[:, j * 128:(j + 1) * 128], ynb[:, c * 128:(c + 1) * 128], C.ident_b), [kynb, C.k_ident_b], [kps])
            for j in range(8):
                c = q * 8 + j
                P.add("act", lambda e, psb=psb, j=j, c=c: e.activation(
                    yT[:, c, :], psb[:, j * 128:(j + 1) * 128], AF.Identity, scale=ng16[:, c:c + 1]),
                    [kps, kng16], [kyT])
        for half in range(2):
            ps, kps = psum_bank(C)
            for j in range(4):
                dc = half * 4 + j
                for kc in range(16):
                    P.mm(ps[:, j * 128:(j + 1) * 128], wout[:, kc, dc * 128:(dc + 1) * 128], yT[:, kc, :],
                         kc == 0, kc == 15, [kwout, kyT], [kps])
            for j in range(4):
                dc = half * 4 + j
                P.add("dve", lambda e, ps=ps, j=j, dc=dc: e.scalar_tensor_tensor(
                    xck[:, dc, :], ps[:, j * 128:(j + 1) * 128], mv["gate"][:, 8 + dc:8 + dc + 1], xck[:, dc, :],
                    ALU.mult, ALU.add), [kps, mv["kgate"], kxck], [kxck])
        P.dma("act", C.xT[:, :, tk], xck, [kxck], [("xT", ck // 4)])
    A.release(m_phase)

def build_program(stages, seq=4096, debug=False):
    global S
    S = seq
    nc = bass.Bass("TRN2", target_bir_lowering=False)
    C = Ctx()
    C.debug = debug
    C.dbg_off = 0
    C.dbg_map = {}
    C.dbg_keys = []
    C.nc = nc
    C.P = Prog(nc)
    C.ps_i = 0
    C.bar_gidx = 0
    C.cast_i = 0
    dt = nc.dram_tensor
    C.d_x = dt("x", [S, D], F32, kind="ExternalInput")
    C.d_out = dt("out", [S, D], F32, kind="ExternalOutput")
    C.d_ident = dt("ident", [128, 128], F32, kind="ExternalInput")
    if debug:
        C.d_dbg = dt("dbg", [128, 8192], F32, kind="ExternalOutput")
    C.d_w_mod = dt("w_mod", [2, D, 9 * D], F32, kind="ExternalInput")
    C.d_wg = dt("ffn_w_gate", [2, 2, D, DFF], F32, kind="ExternalInput")
    C.d_wu = dt("ffn_w_up", [2, 2, D, DFF], F32, kind="ExternalInput")
    C.d_wd = dt("ffn_w_down", [2, 2, DFF, D], F32, kind="ExternalInput")
    C.d_pm = dt("pmT", [96, 96], F32, kind="ExternalInput")
    C.d_pos = dt("pos", [128, S], I32, kind="ExternalInput")
    C.d_mla_win = dt("mla_w_in", [D, 672], F32, kind="ExternalInput")
    C.d_mla_wuq = dt("mla_w_uq", [384, 1536], F32, kind="ExternalInput")
    C.d_mla_wukv = dt("mla_w_ukv", [256, 2048], F32, kind="ExternalInput")
    C.d_mla_wout = dt("mla_w_out", [D, D], F32, kind="ExternalInput")
    C.OT = dt("OT_scr", [8, 128, S], BF16, kind="Internal").ap()
    C.d_masks = dt("masks", [128, 768], F32, kind="ExternalInput")
    C.d_ssd_win = dt("ssd_w_in", [D, 6208], F32, kind="ExternalInput")
    C.d_ssd_wout = dt("ssd_w_out", [2048, D], F32, kind="ExternalInput")
    C.SZ = dt("SZ_scr", [S // 128, 128, 2048], F32, kind="Internal").ap()
    C.XBC = dt("XBC_scr", [32, 128, S], BF16, kind="Internal").ap()
    C.YB = dt("YB_scr", [S // 128, 128, 2048], F32, kind="Internal").ap()
    C.d_vecs = {}
    for name, n in VEC_SPECS:
        C.d_vecs[name] = dt("v_" + name, [128, n], F32, kind="ExternalInput")
    C.xT = dt("xT_scr", [128, NDC, S], F32, kind="Internal").ap()
    C.WGU = [dt(f"wgu_scr{i}", [NF, 128, 2, 8, 128], BF16, kind="Internal").ap() for i in range(4)]
    C.WD = [dt(f"wd_scr{i}", [NDC, 128, NF, 128], BF16, kind="Internal").ap() for i in range(4)]

    ARENA_BYTES = 207 * 1024
    with ExitStack() as es:
        ah = es.enter_context(nc.sbuf_tensor("arena", [128, ARENA_BYTES // 4], F32))
        C.arena = Arena(ah, ARENA_BYTES, C.P)
        C.ps = [es.enter_context(nc.psum_tensor(f"ps{i}", [128, 512], F32))[:] for i in range(8)]
        eng_sems = {e: es.enter_context(nc.semaphore(f"sem_{e}")) for e in ENGS}
        dma_sems = [es.enter_context(nc.semaphore(f"dsem{i}")) for i in range(N_DMA_SEMS)]

        setup_consts(C)
        if debug:
            C.dbg_stage, _ = C.arena.alloc('dbg_stage', [3584], F32)
        load_transpose_x(C)
        compute_mod(C)
        for (l, w) in stages.get("ffn", []):
            convert_ffn_weights(C, l, w)
        for st_ in stages.get("order", []):
            if BARRIERS:
                phase_barrier(C)
            if st_[0] == "ffn":
                ffn_phase(C, st_[1], st_[2])
            elif st_[0] == "mla":
                mla_phase(C)
            elif st_[0] == "ssd":
                ssd_phase(C)
        store_transpose_out(C)
        C.P.emit(eng_sems, dma_sems)
    C.nc = nc
    return C


VEC_SPECS = [("c", 8), ("b_mod0", 72), ("b_mod1", 72), ("norm_g0", 24), ("norm_g1", 24),
             ("mla_qg", 1), ("mla_kg", 1), ("mla_qng", 3), ("mla_kvng", 2), ("invf", 1),
             ("ssd_cw", 160), ("ssd_cb", 32), ("ssd_dtb", 64), ("ssd_alog", 64), ("ssd_dskip", 32), ("ssd_ng16", 16)]


def _consts():
    inv = (10000.0 ** (-np.arange(0, 32, 2, dtype=np.float32) / 32)).astype(np.float32)
    invf = np.zeros((128, 1), np.float32)
    invf[64:80, 0] = inv
    invf[80:96, 0] = inv
    pm = np.zeros((96, 96), np.float32)
    for i in range(16):
        pm[80 + i, 64 + i] = -1.0
        pm[64 + i, 80 + i] = 1.0
    return invf, pm


INVF, PMT = _consts()


def _masks():
    k = np.arange(128)[:, None]
    j = np.arange(128)[None, :]
    Lf = (k > j); Rf = (k <= j); Lb = (k < j); Rb = (k >= j)
    Vf = (j >= k)
    Vb = (j <= k)
    return np.concatenate([m.astype(np.float32) for m in (Lf, Rf, Lb, Rb, Vf, Vb)], axis=1)


MASKS = _masks()


def host_vecs(inputs, b):
    f = np.float32
    v = {}
    v["c"] = np.ascontiguousarray(inputs["c"][b].reshape(8, 128).T.astype(f))
    for l in range(2):
        v[f"b_mod{l}"] = np.ascontiguousarray(inputs["b_mod"][l].reshape(72, 128).T.astype(f))
        v[f"norm_g{l}"] = np.ascontiguousarray(inputs["norm_g"][l].reshape(24, 128).T.astype(f))
    def col(a, n=128):
        o = np.zeros((128, 1), f)
        o[:len(a), 0] = a
        return o
    v["mla_qg"] = col(inputs["mla_q_head_g"][0])
    v["mla_kg"] = col(inputs["mla_k_head_g"][0])
    v["mla_qng"] = np.ascontiguousarray(inputs["mla_q_norm_g"][0].reshape(3, 128).T.astype(f))
    v["mla_kvng"] = np.ascontiguousarray(inputs["mla_kv_norm_g"][0].reshape(2, 128).T.astype(f))
    v["invf"] = INVF
    cwt = inputs["ssd_conv_w"][0]
    v["ssd_cw"] = np.ascontiguousarray(cwt.reshape(5, 32, 128).transpose(2, 1, 0).reshape(128, 160).astype(f))
    v["ssd_cb"] = np.ascontiguousarray(inputs["ssd_conv_b"][0].reshape(32, 128).T.astype(f))
    v["ssd_dtb"] = np.ascontiguousarray(np.broadcast_to(inputs["ssd_dt_bias"][0].reshape(1, 64), (128, 64)).astype(f))
    v["ssd_alog"] = np.ascontiguousarray(np.broadcast_to(inputs["ssd_a_log"][0].reshape(1, 64), (128, 64)).astype(f))
    v["ssd_ng16"] = np.ascontiguousarray(inputs["ssd_norm_g"][0].reshape(16, 128).T.astype(f))
    v["ssd_dskip"] = np.ascontiguousarray(np.broadcast_to(inputs["ssd_d"][0].reshape(1, 32), (128, 32)).astype(f))
    return v


def run(inputs, stages, seq=4096, cores=8, debug=False):
    C = build_program(stages, seq, debug)
    nc = C.nc
    ident = np.eye(128, dtype=np.float32)
    in_maps = []
    for b in range(cores):
        m = {
            "x": np.ascontiguousarray(inputs["x"][b][:seq]),
            "ident": ident,
            "w_mod": inputs["w_mod"],
            "ffn_w_gate": inputs["ffn_w_gate"],
            "ffn_w_up": inputs["ffn_w_up"],
            "ffn_w_down": inputs["ffn_w_down"],
            "pmT": PMT,
            "masks": MASKS,
            "ssd_w_in": inputs["ssd_w_in"][0], "ssd_w_out": inputs["ssd_w_out"][0],

            "pos": np.ascontiguousarray(np.broadcast_to(inputs["positions"][b][None, :seq], (128, seq)).astype(np.int32)),
            "mla_w_in": inputs["mla_w_in"][0], "mla_w_uq": inputs["mla_w_uq"][0],
            "mla_w_ukv": inputs["mla_w_ukv"][0], "mla_w_out": inputs["mla_w_out"][0],
        }
        for k, a in host_vecs(inputs, b).items():
            m["v_" + k] = a
        in_maps.append(m)
    res = run_bass_kernel_spmd(nc, in_maps, core_ids=list(range(cores)))
    out = np.stack([r["out"] for r in res.results], axis=0)
    if debug:
        return out, {k: res.results[0]["dbg"][:, o:o + n] for k, (o, n) in C.dbg_map.items()}
    return out


def kernel(**inputs):
    inputs = {k: np.asarray(v) for k, v in inputs.items()}
    stages = {
        "ffn": [(0, 0), (0, 1), (1, 0), (1, 1)],
        "order": [("ffn", 0, 0), ("ssd",), ("ffn", 0, 1), ("ffn", 1, 0), ("mla",), ("ffn", 1, 1)],
    }
    return run(inputs, stages).astype(np.float32)
```

```python
import numpy as np
from contextlib import ExitStack
import concourse.bass as bass
import concourse.mybir as mybir
from concourse.bass_utils import run_bass_kernel_spmd

F32 = mybir.dt.float32
BF16 = mybir.dt.bfloat16
I32 = mybir.dt.int32
AF = mybir.ActivationFunctionType
ALU = mybir.AluOpType
AX = mybir.AxisListType

D = 1024
S = 4096
DFF = 2816
NF = DFF // 128
NDC = D // 128
EPS = 1e-6
N_DMA_SEMS = 40
DBG_M = 0
BARRIERS = False
SKIP_SAME_ENG_NONRAW = False
ENGS = ("pe", "act", "dve", "pool", "sp")


class Op:
    __slots__ = ("eng", "fn", "deps", "sig", "seq", "dma", "semid", "semval", "prev", "pos", "gidx", "raw")


class Prog:
    def __init__(self, nc):
        self.nc = nc
        self.ops = {e: [] for e in ENGS}
        self.lastw = {}
        self.readers = {}
        self.ndma = 0
        self.dma_last = [None] * N_DMA_SEMS
        self.dma_cnt = [0] * N_DMA_SEMS
        self.nops = 0
        self.bases = set()
        self.touched = {}
        self.inherit = {}
        self.seen = set()

    def base_of(self, k):
        for _ in range(4):
            if k in self.bases:
                return k
            if isinstance(k, tuple) and len(k):
                k = k[0]
            else:
                return None
        return None

    def add(self, eng, fn, reads=(), writes=(), dma=False):
        op = Op()
        op.eng = eng
        op.fn = fn
        op.dma = dma
        op.sig = False
        op.seq = 0
        op.gidx = self.nops
        self.nops += 1
        deps = set()
        raw = set()
        for k in reads:
            w = self.lastw.get(k)
            if w is not None:
                raw.add(w)
        op.raw = raw
        for k in list(reads) + list(writes):
            b = self.base_of(k)
            if b is None:
                continue
            if k not in self.seen:
                self.seen.add(k)
                deps.update(self.inherit.get(b, ()))
            t = self.touched.setdefault(b, {})
            if dma:
                t[("dma", op.gidx)] = op
            else:
                t[eng] = op
        for k in reads:
            w = self.lastw.get(k)
            if w is not None:
                deps.add(w)
        for k in writes:
            w = self.lastw.get(k)
            if w is not None:
                deps.add(w)
            for r in self.readers.get(k, ()):
                deps.add(r)
        for k in reads:
            self.readers.setdefault(k, []).append(op)
        for k in writes:
            self.lastw[k] = op
            self.readers[k] = []
        deps.discard(op)
        op.prev = None
        if dma:
            s = self.ndma % N_DMA_SEMS
            self.ndma += 1
            op.semid = s
            self.dma_cnt[s] += 16
            op.semval = self.dma_cnt[s]
            op.prev = self.dma_last[s]
            self.dma_last[s] = op
        op.deps = deps
        op.pos = len(self.ops[eng])
        self.ops[eng].append(op)
        return op

    def dma(self, eng, out, in_, reads, writes, **kw):
        return self.add(eng, lambda e: e.dma_start(out=out, in_=in_, **kw), reads, writes, dma=True)

    def mm(self, out, lhsT, rhs, start, stop, reads, writes):
        return self.add("pe", lambda e: e.matmul(out, lhsT, rhs, start=start, stop=stop), reads, writes)

    def emit(self, eng_sems, dma_sems):
        nc = self.nc

        def needs_sync(op, d):
            if d.dma:
                return True
            if d.eng == op.eng and not op.dma:
                if op.eng == "pe":
                    return False
                return (d in op.raw) or not SKIP_SAME_ENG_NONRAW
            if d.eng == op.eng and op.dma:
                return True
            return True

        for e in ENGS:
            for op in self.ops[e]:
                for d in op.deps:
                    if needs_sync(op, d) and not d.dma:
                        d.sig = True
        for e in ENGS:
            n = 0
            for op in self.ops[e]:
                if op.sig and not op.dma:
                    n += 1
                    op.seq = n

        def emit_engine(ename, eobj):
            waited = {}
            for op in self.ops[ename]:
                need = {}
                for d in op.deps:
                    if not needs_sync(op, d):
                        continue
                    if d.dma:
                        key = ("d", d.semid)
                        val = d.semval
                    else:
                        key = ("e", d.eng)
                        val = d.seq
                    if need.get(key, 0) < val:
                        need[key] = val
                if op.dma and op.prev is not None:
                    key = ("d", op.prev.semid)
                    if need.get(key, 0) < op.prev.semval:
                        need[key] = op.prev.semval
                pend = []
                for key, val in need.items():
                    if waited.get(key, 0) >= val:
                        continue
                    waited[key] = val
                    sem = dma_sems[key[1]] if key[0] == "d" else eng_sems[key[1]]
                    pend.append((key[0] == "d", sem, val))
                pend.sort(key=lambda t: t[0])
                if op.fn is None:
                    for _, sem, val in pend:
                        eobj.wait_ge(sem, val)
                    continue
                for _, sem, val in pend[:-1]:
                    eobj.wait_ge(sem, val)
                ins = op.fn(eobj)
                if pend:
                    ins._wait_ge(pend[-1][1], pend[-1][2])
                if op.dma:
                    ins.then_inc(dma_sems[op.semid], 16)
                elif op.sig:
                    ins.then_inc(eng_sems[ename], 1)

        with nc.Block() as block:
            @block.tensor
            def _(e):
                emit_engine("pe", e)

            @block.scalar
            def _(e):
                emit_engine("act", e)

            @block.vector
            def _(e):
                emit_engine("dve", e)

            @block.gpsimd
            def _(e):
                emit_engine("pool", e)

            @block.sync
            def _(e):
                emit_engine("sp", e)


class Arena:
    def __init__(self, handle, nbytes, prog):
        self.h = handle
        self.cap = nbytes
        self.top = 0
        self.gen = 0
        self.P = prog
        self.allocs = []

    def mark(self):
        return self.top

    def release(self, m):
        self.top = m

    def alloc(self, name, shape, dtype):
        esz = 2 if dtype == BF16 else 4
        n = 1
        for s_ in shape:
            n *= s_
        nbytes = (n * esz + 63) // 64 * 64
        off = self.top
        assert off + nbytes <= self.cap, f"SBUF arena overflow allocating {name}: {off}+{nbytes}>{self.cap}"
        self.top += nbytes
        ap = self.h[:, off // 4:(off + nbytes) // 4]
        if dtype != F32:
            ap = ap.bitcast(dtype)
        ap = ap[:, 0:n]
        if len(shape) == 2:
            ap = ap.rearrange("p (a b) -> p a b", a=shape[0])
        elif len(shape) == 3:
            ap = ap.rearrange("p (a b c) -> p a b c", a=shape[0], b=shape[1])
        elif len(shape) == 4:
            ap = ap.rearrange("p (a b c d) -> p a b c d", a=shape[0], b=shape[1], c=shape[2])
        self.gen += 1
        key = (name, self.gen)
        P = self.P
        P.bases.add(key)
        inh = set()
        for (a0, a1, ok) in self.allocs:
            if a0 < off + nbytes and off < a1:
                inh.update(P.touched.get(ok, {}).values())
                inh.update(P.inherit.get(ok, ()))
        P.inherit[key] = inh
        self.allocs = [(a0, a1, ok) for (a0, a1, ok) in self.allocs if not (a0 >= off and a1 <= off + nbytes)]
        self.allocs.append((off, off + nbytes, key))
        return ap, key


class Ctx:
    pass


def dbg(C, name, ap, key, n):
    if not C.debug:
        return
    P, A = C.P, C.arena
    st = C.dbg_stage[:, C.dbg_off:C.dbg_off + n]
    kst = ("dbgst", name)
    P.add("pool", lambda e: e.tensor_copy(st, ap), [key], [kst])
    off = C.dbg_off
    C.dbg_off += n
    C.dbg_map[name] = (off, n)
    P.dma("sp", C.d_dbg.ap()[:, off:off + n], st, [kst], [("DBG", name)])
    C.dbg_keys.append(("DBG", name))


def rr(lst, i):
    return lst[i % len(lst)]


def setup_consts(C):
    P, A = C.P, C.arena
    C.ident_f, C.k_ident_f = A.alloc("ident_f", [128], F32)
    C.ident_b, C.k_ident_b = A.alloc("ident_b", [128], BF16)
    C.onesD_b, C.k_onesD = A.alloc("onesD", [128], BF16)
    P.dma("sp", C.ident_f, C.d_ident.ap(), [], [C.k_ident_f])
    P.add("dve", lambda e: e.tensor_copy(C.ident_b, C.ident_f), [C.k_ident_f], [C.k_ident_b])
    P.add("pool", lambda e: e.memset(C.onesD_b, 1.0 / D), [], [C.k_onesD])
    C.eps_t, C.k_eps = A.alloc("eps_t", [1], F32)
    P.add("pool", lambda e: e.memset(C.eps_t, EPS), [], [C.k_eps])
    C.vec = {}
    for name, t in C.d_vecs.items():
        n = t.ap().shape[1]
        ap, k = A.alloc("v_" + name, [n], F32)
        P.dma("sp", ap, t.ap(), [], [k])
        C.vec[name] = (ap, k)


def phase_barrier(C):
    P = C.P
    last = [P.ops[e][-1] for e in ENGS if P.ops[e]]
    last = [o for o in last if o.fn is not None]
    dmas = [o for e in ENGS for o in P.ops[e] if o.dma and o.gidx >= C.bar_gidx]
    C.bar_gidx = P.nops
    for e in ENGS:
        op = P.add(e, None)
        op.deps.update(last)
        op.deps.update(dmas)
        op.deps.discard(op)


def psum_bank(C):
    i = C.ps_i % 8
    C.ps_i += 1
    return C.ps[i], ("ps", i)


def load_transpose_x(C):
    P, A = C.P, C.arena
    m0 = A.mark()
    xin = [A.alloc(f"xin{i}", [4, D], F32) for i in range(2)]
    xtt = [A.alloc(f"xtt{i}", [NDC, 512], F32) for i in range(2)]
    xd = C.d_x.ap().rearrange("(g j p) d -> g p j d", j=4, p=128)
    for g in range(S // 512):
        xi, kxi = xin[g % 2]
        xt, kxt = xtt[g % 2]
        P.dma("sp", xi, xd[g], [], [kxi])
        for dc in range(NDC):
            ps, kps = psum_bank(C)
            for j in range(4):
                P.add("pe", lambda e, ps=ps, xi=xi, j=j, dc=dc: e.transpose(
                    ps[:, j * 128:(j + 1) * 128], xi[:, j, dc * 128:(dc + 1) * 128], C.ident_f),
                    [kxi, C.k_ident_f], [kps])
            eng = "act" if dc % 2 == 0 else "dve"
            if eng == "act":
                P.add("act", lambda e, ps=ps, xt=xt, dc=dc: e.copy(xt[:, dc, :], ps), [kps], [kxt])
            else:
                P.add("dve", lambda e, ps=ps, xt=xt, dc=dc: e.tensor_copy(xt[:, dc, :], ps), [kps], [kxt])
        P.dma("sp", C.xT[:, :, g * 512:(g + 1) * 512], xt, [kxt], [("xT", g)])
    A.release(m0)


def store_transpose_out(C):
    P, A = C.P, C.arena
    m0 = A.mark()
    xtt = [A.alloc(f"oxt{i}", [NDC, 512], F32) for i in range(2)]
    xo = [A.alloc(f"oxo{i}", [4, D], F32) for i in range(2)]
    od = C.d_out.ap().rearrange("(g j p) d -> g p j d", j=4, p=128)
    for g in range(S // 512):
        xt, kxt = xtt[g % 2]
        xo_, kxo = xo[g % 2]
        P.dma("sp", xt, C.xT[:, :, g * 512:(g + 1) * 512], [("xT", g)], [kxt])
        for j in range(4):
            for half in range(2):
                ps, kps = psum_bank(C)
                for q in range(4):
                    dc = half * 4 + q
                    P.add("pe", lambda e, ps=ps, xt=xt, j=j, dc=dc, q=q: e.transpose(
                        ps[:, q * 128:(q + 1) * 128], xt[:, dc, j * 128:(j + 1) * 128], C.ident_f),
                        [kxt, C.k_ident_f], [kps])
                if half == 0:
                    P.add("act", lambda e, ps=ps, xo_=xo_, j=j: e.copy(xo_[:, j, 0:512], ps), [kps], [kxo])
                else:
                    P.add("dve", lambda e, ps=ps, xo_=xo_, j=j: e.tensor_copy(xo_[:, j, 512:1024], ps), [kps], [kxo])
        P.dma("sp", od[g], xo_, [kxo], [("OUT", g)])
    A.release(m0)
    P.add("sp", None, [("OUT", g) for g in range(S // 512)] + C.dbg_keys, [])


def compute_mod(C):
    P, A = C.P, C.arena
    cvec, kc_ = C.vec["c"]
    C.cond, C.k_cond = A.alloc("cond", [8], F32)
    P.add("act", lambda e: e.activation(C.cond, cvec, AF.Silu), [kc_], [C.k_cond])
    C.mod = []
    m0 = None
    for l in range(2):
        mod, kmod = A.alloc(f"mod{l}", [72], F32)
        C.mod.append((mod, kmod))
    m0 = A.mark()
    wb = [A.alloc(f"wmod{i}", [8, 1024], F32) for i in range(2)]
    it = 0
    for l in range(2):
        mod, kmod = C.mod[l]
        bm, kbm = C.vec[f"b_mod{l}"]
        wd = C.d_w_mod.ap()[l].rearrange("(kc p) n -> p kc n", p=128)
        ps, kps = psum_bank(C)
        for cb in range(9):
            w, kw = wb[it % 2]
            it += 1
            P.dma("sp", w, wd[:, :, cb * 1024:(cb + 1) * 1024], [], [kw])
            for j in range(8):
                col = cb * 8 + j
                for kc in range(8):
                    P.mm(ps[:, col:col + 1], w[:, kc, j * 128:(j + 1) * 128], C.cond[:, kc:kc + 1],
                         kc == 0, kc == 7, [kw, C.k_cond], [kps])
        P.add("dve", lambda e, mod=mod, ps=ps, bm=bm: e.tensor_tensor(mod, ps[:, 0:72], bm, ALU.add),
              [kps, kbm], [kmod])
    A.release(m0)
    C.modv = []
    for l in range(2):
        mod, kmod = C.mod[l]
        g, kg = C.vec[f"norm_g{l}"]
        a, ka = A.alloc(f"moda{l}", [24], F32)
        gt, kgt = A.alloc(f"modg{l}", [24], F32)
        for sub in range(3):
            sc = mod[:, (sub * 3 + 1) * 8:(sub * 3 + 2) * 8]
            P.add("dve", lambda e, a=a, sub=sub, sc=sc, g=g: e.scalar_tensor_tensor(
                a[:, sub * 8:(sub + 1) * 8], sc, 1.0, g[:, sub * 8:(sub + 1) * 8], ALU.add, ALU.mult),
                [kmod, kg], [ka])
            gsrc = mod[:, (sub * 3 + 2) * 8:(sub * 3 + 3) * 8]
            fac = 1.0 if sub == 1 else 0.5
            P.add("dve", lambda e, gt=gt, sub=sub, gsrc=gsrc, fac=fac: e.tensor_scalar(
                gt[:, sub * 8:(sub + 1) * 8], gsrc, fac, None, ALU.mult), [kmod], [kgt])
        C.modv.append(dict(a=a, ka=ka, gate=gt, kgate=kgt, mod=mod, kmod=kmod))
        if l == 0:
            dbg(C, 'mod0', mod, kmod, 72)
            dbg(C, 'a0', a, ka, 24)
            dbg(C, 'gate0', gt, kgt, 24)


def cast_engine(C):
    e = ("pool", "dve", "act")[C.cast_i % 3]
    C.cast_i += 1
    return e


def emit_cast(P, eng, out, in_, reads, writes):
    if eng == "act":
        P.add("act", lambda e: e.copy(out, in_), reads, writes)
    else:
        P.add(eng, lambda e: e.tensor_copy(out, in_), reads, writes)


def convert_ffn_weights(C, l, w):
    P, A = C.P, C.arena
    idx = l * 2 + w
    m0 = A.mark()
    stg = [A.alloc(f"cst{i}", [4096], F32) for i in range(3)]
    stb = [A.alloc(f"csb{i}", [4096], BF16) for i in range(3)]
    it = 0
    for gi, src in enumerate((C.d_wg, C.d_wu)):
        sd = src.ap()[l, w].rearrange("(kc p) n -> p kc n", p=128)
        for fb in range(0, NF, 4):
            nf = min(4, NF - fb)
            sf, ksf = stg[it % 3]
            sb, ksb = stb[it % 3]
            it += 1
            sfv = sf[:, 0:8 * nf * 128].rearrange("p (kc n) -> p kc n", kc=8)
            P.dma("sp", sfv, sd[:, :, fb * 128:(fb + nf) * 128], [], [ksf])
            sbv = sb[:, 0:nf * 8 * 128].rearrange("p (f kc m) -> p f kc m", f=nf, kc=8)
            emit_cast(P, cast_engine(C), sbv, sfv.rearrange("p kc (f m) -> p f kc m", f=nf), [ksf], [ksb])
            dst = C.WGU[idx][fb:fb + nf, :, gi, :, :].rearrange("f p kc m -> p f (kc m)")
            P.dma("sp", dst, sbv.rearrange("p f kc m -> p f (kc m)"), [ksb], [("WGU", idx, f_) for f_ in range(fb, fb + nf)])
    sd = C.d_wd.ap()[l, w].rearrange("(fc p) n -> p fc n", p=128)
    for fb in range(0, NF, 4):
        nf = min(4, NF - fb)
        sf, ksf = stg[it % 3]
        sb, ksb = stb[it % 3]
        it += 1
        sfv = sf[:, 0:nf * 1024].rearrange("p (fc n) -> p fc n", fc=nf)
        P.dma("sp", sfv, sd[:, fb:fb + nf, :], [], [ksf])
        sbv = sb[:, 0:8 * nf * 128].rearrange("p (dc fc m) -> p dc fc m", dc=8, fc=nf)
        emit_cast(P, cast_engine(C), sbv, sfv.rearrange("p fc (dc m) -> p dc fc m", dc=8), [ksf], [ksb])
        dst = C.WD[idx][:, :, fb:fb + nf, :].rearrange("dc p fc m -> p dc (fc m)")
        P.dma("sp", dst, sbv.rearrange("p dc fc m -> p dc (fc m)"), [ksb], [("WD", idx, dc) for dc in range(8)])
    A.release(m0)


def norm_modulate(C, xt, kxt, h, kh, T, l, sub, scratch):
    P = C.P
    mv = C.modv[l]
    sq, ksq, rstd, krstd, tmp, ktmp = scratch
    shift = mv["mod"][:, (sub * 3) * 8:(sub * 3 + 1) * 8]
    for st in range(T // 512):
        sl = slice(st * 512, (st + 1) * 512)
        for dc in range(NDC):
            P.add("act", lambda e, dc=dc, sl=sl: e.activation(sq[:, dc, sl], xt[:, dc, sl], AF.Square),
                  [kxt], [(ksq, st)])
        ps, kps = psum_bank(C)
        for dc in range(NDC):
            P.mm(ps, C.onesD_b, sq[:, dc, sl], dc == 0, dc == NDC - 1, [(ksq, st), C.k_onesD], [kps])
        P.add("act", lambda e, ps=ps, sl=sl: e.activation(rstd[:, sl], ps, AF.Ln, bias=C.eps_t[:, 0:1]),
              [kps, C.k_eps], [(krstd, st)])
        P.add("act", lambda e, sl=sl: e.activation(rstd[:, sl], rstd[:, sl], AF.Exp, scale=-0.5),
              [(krstd, st)], [(krstd, st)])
        for dc in range(NDC):
            tm, ktm = tmp[dc % len(tmp)]
            eng = "dve" if dc % 2 == 0 else "pool"
            P.add(eng, lambda e, tm=tm, dc=dc, sl=sl: e.tensor_tensor(tm, xt[:, dc, sl], rstd[:, sl], ALU.mult),
                  [kxt, (krstd, st)], [ktm])
            P.add("act", lambda e, tm=tm, dc=dc, sl=sl: e.activation(
                h[:, dc, sl], tm, AF.Identity, bias=shift[:, dc:dc + 1],
                scale=mv["a"][:, sub * 8 + dc:sub * 8 + dc + 1]),
                [ktm, mv["ka"], mv["kmod"]], [(kh, st)])


def ffn_phase(C, l, w):
    P, A = C.P, C.arena
    idx = l * 2 + w
    sub = 0 if w == 0 else 2
    mv = C.modv[l]
    T = 1024
    m0 = A.mark()
    xb = [A.alloc(f"fx{i}", [NDC, T], F32) for i in range(2)]
    h, kh = A.alloc("fh", [NDC, T], BF16)
    act, kact = A.alloc("fact", [NF, T], BF16)
    sq, ksq = A.alloc("fsq", [NDC, T], BF16)
    rstd, krstd = A.alloc("frstd", [T], F32)
    tmp = [A.alloc(f"ftmp{i}", [512], F32) for i in range(3)]
    sg = [A.alloc(f"fsg{i}", [512], F32) for i in range(3)]
    wgu = [A.alloc(f"fwgu{i}", [2, 8, 128], BF16) for i in range(4)]
    wdb = [A.alloc(f"fwd{i}", [NF, 128], BF16) for i in range(3)]
    NM = S // T
    items = []
    for m in range(NM):
        for f in range(NF):
            items.append(("g", f))
        for dc in range(NDC):
            items.append(("d", dc))
    issued = [0]
    cnt = {"g": 0, "d": 0}
    slot_of = {}

    def prefetch(upto):
        while issued[0] < min(upto, len(items)):
            kind, j = items[issued[0]]
            if kind == "g":
                buf, kb = wgu[cnt["g"] % 4]
                cnt["g"] += 1
                P.dma("sp", buf, C.WGU[idx][j].rearrange("p g kc m -> p g kc m"), [("WGU", idx, j)], [kb])
            else:
                buf, kb = wdb[cnt["d"] % 3]
                cnt["d"] += 1
                P.dma("sp", buf, C.WD[idx][j], [("WD", idx, j)], [kb])
            slot_of[issued[0]] = (buf, kb)
            issued[0] += 1

    def load_x(m):
        xt, kxt = xb[m % 2]
        P.dma("act", xt, C.xT[:, :, m * T:(m + 1) * T], [("xT", 2 * m), ("xT", 2 * m + 1)], [kxt])

    load_x(0)
    pos = 0
    for m in range(NM):
        xt, kxt = xb[m % 2]
        prefetch(pos + 3)
        norm_modulate(C, xt, kxt, h, kh, T, l, sub, (sq, ksq, rstd, krstd, tmp, None))
        if m + 1 < NM:
            load_x(m + 1)
        if m == DBG_M and idx == 0:
            dbg(C, 'rstd', rstd[:, 0:512], (krstd, 0), 512)
            dbg(C, 'h0', h[:, 0, 0:512], (kh, 0), 512)
            dbg(C, 'h7', h[:, 7, 0:512], (kh, 0), 512)
        for f in range(NF):
            prefetch(pos + 3)
            wbuf, kwb = slot_of.pop(pos)
            pos += 1
            for st in range(T // 512):
                sl = slice(st * 512, (st + 1) * 512)
                psg, kpsg = psum_bank(C)
                psu, kpsu = psum_bank(C)
                for kc in range(8):
                    P.mm(psg, wbuf[:, 0, kc, :], h[:, kc, sl], kc == 0, kc == 7, [kwb, (kh, st)], [kpsg])
                for kc in range(8):
                    P.mm(psu, wbuf[:, 1, kc, :], h[:, kc, sl], kc == 0, kc == 7, [kwb, (kh, st)], [kpsu])
                s_, ks_ = sg[(f * 2 + st) % 3]
                P.add("act", lambda e, s_=s_, psg=psg: e.activation(s_, psg, AF.Silu), [kpsg], [ks_])
                P.add("dve", lambda e, s_=s_, psu=psu, f=f, sl=sl: e.tensor_tensor(act[:, f, sl], s_, psu, ALU.mult),
                      [ks_, kpsu], [(kact, st)])
        for dc in range(NDC):
            prefetch(pos + 3)
            wbuf, kwb = slot_of.pop(pos)
            pos += 1
            for st in range(T // 512):
                sl = slice(st * 512, (st + 1) * 512)
                pso, kpso = psum_bank(C)
                for f in range(NF):
                    P.mm(pso, wbuf[:, f, :], act[:, f, sl], f == 0, f == NF - 1, [kwb, (kact, st)], [kpso])
                P.add("dve", lambda e, pso=pso, dc=dc, sl=sl, xt=xt: e.scalar_tensor_tensor(
                    xt[:, dc, sl], pso, mv["gate"][:, sub * 8 + dc:sub * 8 + dc + 1], xt[:, dc, sl],
                    ALU.mult, ALU.add), [kpso, mv["kgate"], kxt], [kxt])
        if m == DBG_M and idx == 0:
            dbg(C, 'act0', act[:, 0, 0:512], (kact, 0), 512)
            dbg(C, 'act21', act[:, 21, 0:512], (kact, 0), 512)
            dbg(C, 'xo0', xt[:, 0, 0:512], kxt, 512)
        P.dma("act", C.xT[:, :, m * T:(m + 1) * T], xt, [kxt], [("xT", 2 * m), ("xT", 2 * m + 1)])
    A.release(m0)


PI = float(np.pi)


def const_tile(C, name, val, dtype=F32, n=1):
    ap, k = C.arena.alloc("c_" + name, [n], dtype)
    C.P.add("pool", lambda e: e.memset(ap, val), [], [k])
    return ap, k


def load_cast_weight(C, name, src_ap, shape, eng="sp"):
    P, A = C.P, C.arena
    n = 1
    for s_ in shape:
        n *= s_
    wb, kwb = A.alloc(name, shape, BF16)
    m0 = A.mark()
    CH = 2048
    flat_b = wb
    stg = [A.alloc(f"{name}_st{i}", [CH], F32) for i in range(2)]
    A.release(m0)
    return wb, kwb, stg


def mla_phase(C):
    P, A = C.P, C.arena
    l = 1
    mv = C.modv[l]
    T = 512
    NT = S // T
    NB = S // 128
    SCALE = float(96 ** -0.5)
    m_phase = A.mark()
    ones384, k384 = const_tile(C, "o384", 1.0 / 384, BF16, 128)
    ones256, k256 = const_tile(C, "o256", 1.0 / 256, BF16, 128)
    ones96, k96 = const_tile(C, "o96", 1.0 / 96, BF16, 128)
    onesrow, krow = const_tile(C, "orow", 1.0, F32, 128)
    hpi, khpi = const_tile(C, "hpi", PI / 2)
    nhpi, knhpi = const_tile(C, "nhpi", -PI / 2)
    pmT_f, kpmf = A.alloc("pmT_f", [96], F32)
    pmT, kpm = A.alloc("pmT", [96], BF16)
    P.dma("sp", pmT_f[0:96, :], C.d_pm.ap(), [], [kpmf])
    P.add("dve", lambda e: e.tensor_copy(pmT[0:96, :], pmT_f[0:96, :]), [kpmf], [kpm])
    gq, kgq = C.vec["mla_qg"]
    gk, kgk = C.vec["mla_kg"]
    gql, kgql = C.vec["mla_qng"]
    gkvl, kgkvl = C.vec["mla_kvng"]
    invf, kinvf = C.vec["invf"]
    qn, kqn = A.alloc("qn", [3, S], BF16)
    kvn, kkvn = A.alloc("kvn", [2, S], BF16)
    kpe, kkpe = A.alloc("kpe", [S], F32)
    sqk, ksqk = A.alloc("sqk", [S], BF16)
    COS, kcos = A.alloc("COS", [S], F32)
    SIN, ksin = A.alloc("SIN", [S], F32)
    wuq, kwuq = A.alloc("wuq", [3, 1536], BF16)
    wukv, kwukv = A.alloc("wukv", [2, 2048], BF16)

    m0 = A.mark()
    posi, kposi = A.alloc("posi", [S], I32)
    ang, kang = A.alloc("ang", [S], F32)
    t1, kt1 = A.alloc("rt1", [S], F32)
    ti, kti = A.alloc("rti", [S], I32)
    P.dma("sp", posi, C.d_pos.ap(), [], [kposi])
    P.add("dve", lambda e: e.tensor_copy(ang, posi), [kposi], [kang])
    P.add("dve", lambda e: e.tensor_scalar(ang, ang, invf[:, 0:1], None, ALU.mult), [kang, kinvf], [kang])
    for (dst, kdst, shift) in ((SIN, ksin, 0.0), (COS, kcos, PI / 2)):
        P.add("dve", lambda e, shift=shift: e.tensor_scalar(t1, ang, shift, 1.0 / (2 * PI), ALU.add, ALU.mult),
              [kang], [kt1])
        P.add("dve", lambda e: e.tensor_copy(ti, t1), [kt1], [kti])
        P.add("dve", lambda e: e.tensor_copy(t1, ti), [kti], [kt1])
        P.add("dve", lambda e: e.scalar_tensor_tensor(t1, t1, -2 * PI, ang, ALU.mult, ALU.add), [kt1, kang], [kt1])
        bias_ap = nhpi if shift == 0.0 else None
        if shift == 0.0:
            P.add("act", lambda e: e.activation(t1, t1, AF.Abs, bias=nhpi[:, 0:1]), [kt1, knhpi], [kt1])
        else:
            P.add("act", lambda e: e.activation(t1, t1, AF.Abs), [kt1], [kt1])
        P.add("act", lambda e, dst=dst: e.activation(dst, t1, AF.Sin, bias=hpi[:, 0:1], scale=-1.0),
              [kt1, khpi], [kdst])
    A.release(m0)

    def load_cast(dst, kdst, src, nk, ncol, colchunk):
        stg = [A.alloc(f"wst{i}", [nk, colchunk], F32) for i in range(2)]
        it = 0
        for c0 in range(0, ncol, colchunk):
            cw = min(colchunk, ncol - c0)
            st, kst = stg[it % 2]
            it += 1
            P.dma("sp", st[:, :, 0:cw], src[:, :, c0:c0 + cw], [], [kst])
            emit_cast(P, cast_engine(C), dst[:, :, c0:c0 + cw], st[:, :, 0:cw], [kst], [kdst])

    m1 = A.mark()
    mm_ = A.mark()
    load_cast(wuq, kwuq, C.d_mla_wuq.ap().rearrange("(kc p) n -> p kc n", p=128), 3, 1536, 512)
    load_cast(wukv, kwukv, C.d_mla_wukv.ap().rearrange("(kc p) n -> p kc n", p=128), 2, 2048, 512)
    A.release(mm_)
    win, kwin = A.alloc("win", [8, 736], BF16)
    P.add("pool", lambda e: e.memset(win, 0.0), [], [kwin])
    wsrc = C.d_mla_win.ap().rearrange("(kc p) n -> p kc n", p=128)
    mm2 = A.mark()
    stg = [A.alloc(f"wst_in{i}", [8, 224], F32) for i in range(2)]
    for i, c0 in enumerate(range(0, 672, 224)):
        st, kst = stg[i % 2]
        P.dma("sp", st, wsrc[:, :, c0:c0 + 224], [], [kst])
        if c0 + 224 <= 640:
            emit_cast(P, cast_engine(C), win[:, :, c0:c0 + 224], st, [kst], [kwin])
        else:
            nl = 640 - c0
            emit_cast(P, cast_engine(C), win[:, :, c0:640], st[:, :, 0:nl], [kst], [kwin])
            emit_cast(P, cast_engine(C), win[:, :, 704:736], st[:, :, nl:nl + 32], [kst], [kwin])

    A.release(mm2)
    xb = [A.alloc(f"mx{i}", [NDC, T], F32) for i in range(2)]
    h, kh = A.alloc("mh", [NDC, T], BF16)
    sq, ksq = A.alloc("msq", [NDC, T], BF16)
    rstd, krstd = A.alloc("mrstd", [T], F32)
    tmp = [A.alloc(f"mtmp{i}", [512], F32) for i in range(3)]
    lsq, klsq = A.alloc("mlsq", [3, T], BF16)
    lrs, klrs = A.alloc("mlrs", [T], F32)

    def load_x(m):
        xt, kxt = xb[m % 2]
        P.dma("act", xt, C.xT[:, :, m * T:(m + 1) * T], [("xT", m)], [kxt])

    load_x(0)
    for m in range(NT):
        xt, kxt = xb[m % 2]
        sl = slice(m * T, (m + 1) * T)
        norm_modulate(C, xt, kxt, h, kh, T, l, 1, (sq, ksq, rstd, krstd, tmp, None))
        if m + 1 < NT:
            load_x(m + 1)
        for (c0, nch, ones, kones, dst, kdst, g) in ((0, 3, ones384, k384, qn, kqn, gql), (3, 2, ones256, k256, kvn, kkvn, gkvl)):
            banks = []
            for c in range(nch):
                ps, kps = psum_bank(C)
                banks.append((ps, kps))
                for kc in range(8):
                    P.mm(ps, win[:, kc, (c0 + c) * 128:(c0 + c + 1) * 128], h[:, kc, :], kc == 0, kc == 7,
                         [kwin, (kh, 0)], [kps])
                P.add("act", lambda e, ps=ps, c=c: e.activation(lsq[:, c, :], ps, AF.Square), [kps], [klsq])
            pss, kpss = psum_bank(C)
            for c in range(nch):
                P.mm(pss, ones, lsq[:, c, :], c == 0, c == nch - 1, [klsq, kones], [kpss])
            P.add("act", lambda e, pss=pss: e.activation(lrs, pss, AF.Ln, bias=C.eps_t[:, 0:1]), [kpss, C.k_eps], [klrs])
            P.add("act", lambda e: e.activation(lrs, lrs, AF.Exp, scale=-0.5), [klrs], [klrs])
            for c in range(nch):
                ps, kps = banks[c]
                tm, ktm = tmp[c % 3]
                P.add("dve", lambda e, tm=tm, ps=ps: e.tensor_tensor(tm, ps, lrs, ALU.mult), [kps, klrs], [ktm])
                P.add("act", lambda e, tm=tm, c=c, dst=dst, g=g, sl=sl: e.activation(
                    dst[:, c, sl], tm, AF.Identity, scale=g[:, c:c + 1]), [ktm], [kdst])
        ps, kps = psum_bank(C)
        for kc in range(8):
            P.mm(ps[0:96, :], win[:, kc, 640:736], h[:, kc, :], kc == 0, kc == 7, [kwin, (kh, 0)], [kps])
        P.add("act", lambda e, ps=ps, sl=sl: e.copy(kpe[64:96, sl], ps[64:96, :]), [kps], [kkpe])
        P.add("act", lambda e, ps=ps, sl=sl: e.activation(sqk[64:96, sl], ps[64:96, :], AF.Square), [kps], [(ksqk, "pe")])
    A.release(m1)

    kT = [A.alloc(f"kT{i}", [S], BF16) for i in range(2)]
    Vau = [A.alloc(f"Vau{i}", [NB, 128], BF16) for i in range(2)]
    OTs = [A.alloc(f"OTs{i}", [S], BF16) for i in range(2)]
    for par in range(2):
        va, kva = Vau[par]
        P.add("pool", lambda e, va=va: e.memset(va, 1.0), [], [kva])
    rsk, krsk = A.alloc("rsk", [T], F32)
    rt1 = [A.alloc(f"rp1_{i}", [T], F32) for i in range(2)]
    rt2 = [A.alloc(f"rp2_{i}", [T], F32) for i in range(2)]
    qsq, kqsq = A.alloc("qsq", [T], BF16)
    qT = [A.alloc(f"qT{i}", [T], BF16) for i in range(2)]
    PT = [A.alloc(f"PT{i}", [T], BF16) for i in range(4)]
    osb, kosb = A.alloc("osb", [T], F32)
    rl, krl = A.alloc("rl", [T], F32)
    pti = 0

    C.held = []

    def bank_excl(excl):
        while True:
            ps, kps = psum_bank(C)
            if all(ps is not x for x in excl) and all(ps is not x for x in C.held):
                return ps, kps

    def norm_rope(src_ps, ksrc_ps, src_sb, ksrc_sb, sqt, ksqt_keys, g, kg, dstT, kdstT, sl, tl):
        yield
        pss, kpss = bank_excl([])
        P.mm(pss[0:96, :], ones96[0:96, 0:96], sqt, True, True, ksqt_keys + [k96], [kpss])
        P.add("act", lambda e: e.activation(rsk[0:96, :], pss[0:96, :], AF.Ln, bias=C.eps_t[0:96, 0:1]),
              [kpss, C.k_eps], [krsk])
        P.add("act", lambda e: e.activation(rsk[0:96, :], rsk[0:96, :], AF.Exp, scale=-0.5), [krsk], [krsk])
        if src_sb is None:
            P.add("dve", lambda e: e.scalar_tensor_tensor(dstT[0:96, sl], src_ps[0:96, :], g[0:96, 0:1], rsk[0:96, :],
                                                          ALU.mult, ALU.mult), [ksrc_ps, kg, krsk], [kdstT])
        else:
            P.add("dve", lambda e: e.scalar_tensor_tensor(dstT[0:64, sl], src_ps[0:64, :], g[0:64, 0:1], rsk[0:64, :],
                                                          ALU.mult, ALU.mult), [ksrc_ps, kg, krsk], [kdstT])
            P.add("dve", lambda e: e.scalar_tensor_tensor(dstT[64:96, sl], src_sb[64:96, tl], g[64:96, 0:1],
                                                          rsk[64:96, :], ALU.mult, ALU.mult), [ksrc_sb, kg, krsk], [kdstT])
        C.held[:] = [x for x in C.held if x is not src_ps]
        yield
        psr, kpsr = bank_excl([])
        P.mm(psr[0:96, :], pmT[0:96, 0:96], dstT[0:96, sl], True, True, [kdstT, kpm], [kpsr])
        a1, ka1 = rt1[C.ps_i % 2]
        a2, ka2 = rt2[C.ps_i % 2]
        P.add("pool", lambda e: e.tensor_tensor(a1[64:96, :], dstT[64:96, sl], COS[64:96, tl], ALU.mult),
              [kdstT, kcos], [ka1])
        P.add("dve", lambda e: e.tensor_tensor(a2[64:96, :], psr[64:96, :], SIN[64:96, tl], ALU.mult),
              [kpsr, ksin], [ka2])
        P.add("dve", lambda e: e.tensor_tensor(dstT[64:96, sl], a1[64:96, :], a2[64:96, :], ALU.add),
              [ka1, ka2], [kdstT])

    LA = 3
    PTn = [A.alloc(f"PTn{i}", [T], BF16) for i in range(LA + 3)]

    def prepK(hd, m):
        kt_, kkt = kT[hd % 2]
        tl = slice(m * T, (m + 1) * T)
        ps, kps = bank_excl([])
        C.held.append(ps)
        for kc in range(2):
            P.mm(ps[0:64, :], wukv[:, kc, hd * 128:hd * 128 + 64], kvn[:, kc, tl], kc == 0, kc == 1,
                 [kwukv, kkvn], [kps])
        P.add("act", lambda e, ps=ps, tl=tl: e.activation(sqk[0:64, tl], ps[0:64, :], AF.Square),
              [kps], [(ksqk, "n", m)])
        yield from norm_rope(ps, kps, kpe, kkpe, sqk[0:96, tl], [(ksqk, "n", m), (ksqk, "pe")], gk, kgk, kt_, kkt, tl, tl)

    def prepV(hd, b0):
        va, kva = Vau[hd % 2]
        voff = 0 if hd % 2 == 0 else 64
        ps, kps = bank_excl([])
        for j in range(8):
            blk = b0 + j
            for kc in range(2):
                P.mm(ps[:, j * 64:(j + 1) * 64], kvn[:, kc, blk * 128:(blk + 1) * 128],
                     wukv[:, kc, hd * 128 + 64:hd * 128 + 128], kc == 0, kc == 1, [kwukv, kkvn], [kps])
        P.add("dve", lambda e, ps=ps, b0=b0, va=va, voff=voff: e.tensor_copy(
            va[:, b0:b0 + 8, voff:voff + 64], ps.rearrange("p (j v) -> p j v", j=8)), [kps], [kva])
        yield

    qcount = [0]

    def prepQ(hd, m, q_, kq_):
        tl = slice(m * T, (m + 1) * T)
        ps, kps = bank_excl([])
        C.held.append(ps)
        for kc in range(3):
            P.mm(ps[0:96, :], wuq[:, kc, hd * 96:(hd + 1) * 96], qn[:, kc, tl], kc == 0, kc == 2,
                 [kwuq, kqn], [kps])
        P.add("act", lambda e, ps=ps: e.activation(qsq[0:96, :], ps[0:96, :], AF.Square), [kps], [kqsq])
        yield from norm_rope(ps, kps, None, None, qsq[0:96, :], [kqsq], gq, kgq, q_, kq_, slice(0, T), tl)

    def run_all(gen):
        for _ in gen:
            pass

    def next_q():
        q = qT[qcount[0] % 2]
        qcount[0] += 1
        return q

    for m in range(NT):
        run_all(prepK(0, m))
    for b0 in range(0, NB, 8):
        run_all(prepV(0, b0))
    nextq = next_q()
    run_all(prepQ(0, 0, nextq[0], nextq[1]))
    vgroups = list(range(0, NB, 8))
    for hd in range(16):
        par = hd % 2
        kt_, kkt = kT[par]
        va, kva = Vau[par]
        ots, kots = OTs[(hd // 2) % 2]
        oh = 0 if par == 0 else 64
        lp = 64 if par == 0 else 0
        for m in range(NT):
            tl = slice(m * T, (m + 1) * T)
            q_, kq_ = nextq
            gens = []
            if m + 1 < NT:
                nextq = next_q()
                gens.append(prepQ(hd, m + 1, nextq[0], nextq[1]))
            elif hd + 1 < 16:
                nextq = next_q()
                gens.append(prepQ(hd + 1, 0, nextq[0], nextq[1]))
            if hd + 1 < 16:
                gens.append(prepK(hd + 1, m))
                if m < len(vgroups):
                    gens.append(prepV(hd + 1, vgroups[m]))
            pso, kpso = bank_excl([])
            C.held.append(pso)
            pend = []
            for kb in range(NB + LA):
                if kb < NB:
                    pss, kpss = bank_excl([])
                    P.mm(pss, kt_[0:96, kb * 128:(kb + 1) * 128], q_[0:96, :], True, True, [kkt, kq_], [kpss])
                    pt, kpt = PTn[pti % len(PTn)]
                    pti += 1
                    P.add("act", lambda e, pt=pt, pss=pss: e.activation(pt, pss, AF.Exp, scale=SCALE), [kpss], [kpt])
                    pend.append((kb, pt, kpt))
                if kb >= LA:
                    kb2, pt, kpt = pend.pop(0)
                    P.mm(pso, va[:, kb2, :], pt, kb2 == 0, kb2 == NB - 1, [kva, kpt], [kpso])
                if kb % 6 == 2 and gens:
                    gi = (kb // 6) % len(gens)
                    for g_ in list(gens):
                        try:
                            next(g_)
                        except StopIteration:
                            gens.remove(g_)
            for g_ in gens:
                run_all(g_)
            P.add("dve", lambda e, pso=pso, lp=lp: e.reciprocal(rl[lp:lp + 1, :], pso[lp:lp + 1, :]), [kpso], [krl])
            P.add("act", lambda e, pso=pso, oh=oh: e.copy(osb[oh:oh + 64, :], pso[oh:oh + 64, :]), [kpso], [kosb])
            psb, kpsb = bank_excl([])
            P.mm(psb, onesrow[lp:lp + 1, :], rl[lp:lp + 1, :], True, True, [krl, krow], [kpsb])
            P.add("dve", lambda e, psb=psb, oh=oh, ots=ots, tl=tl: e.tensor_tensor(
                ots[oh:oh + 64, tl], osb[oh:oh + 64, :], psb[oh:oh + 64, :], ALU.mult), [kosb, kpsb], [kots])
            C.held[:] = [x for x in C.held if x is not pso]
        if par == 1:
            c = hd // 2
            P.dma("sp", C.OT[c], ots, [kots], [("OT", c)])
    A.release(m_phase)

    m3 = A.mark()
    wout, kwout = A.alloc("wout", [8, 1024], BF16)
    stg = [A.alloc(f"wst_o{i}", [8, 256], F32) for i in range(2)]
    wsrc = C.d_mla_wout.ap().rearrange("(kc p) n -> p kc n", p=128)
    for i, c0 in enumerate(range(0, 1024, 256)):
        st, kst = stg[i % 2]
        P.dma("sp", st, wsrc[:, :, c0:c0 + 256], [], [kst])
        emit_cast(P, cast_engine(C), wout[:, :, c0:c0 + 256], st, [kst], [kwout])
    xb = [A.alloc(f"ox{i}", [NDC, T], F32) for i in range(2)]
    ob = [A.alloc(f"oo{i}", [NDC, T], BF16) for i in range(2)]
    for m in range(NT):
        xt, kxt = xb[m % 2]
        ot, kot = ob[m % 2]
        tl = slice(m * T, (m + 1) * T)
        P.dma("act", xt, C.xT[:, :, tl], [("xT", m)], [kxt])
        P.dma("sp", ot, C.OT.rearrange("c p s -> p c s")[:, :, tl], [("OT", c) for c in range(8)], [kot])
        for dc in range(NDC):
            ps, kps = psum_bank(C)
            for kc in range(8):
                P.mm(ps, wout[:, kc, dc * 128:(dc + 1) * 128], ot[:, kc, :], kc == 0, kc == 7, [kwout, kot], [kps])
            P.add("dve", lambda e, ps=ps, dc=dc, xt=xt: e.scalar_tensor_tensor(
                xt[:, dc, :], ps, mv["gate"][:, 8 + dc:8 + dc + 1], xt[:, dc, :], ALU.mult, ALU.add),
                [kps, mv["kgate"], kxt], [kxt])
        P.dma("act", C.xT[:, :, tl], xt, [kxt], [("xT", m)])
    A.release(m3)


def bc_last(ap, n):
    return ap.unsqueeze(2).to_broadcast([ap.shape[0], ap.shape[1], n])


def bc_mid(ap, n):
    return ap.unsqueeze(1).to_broadcast([ap.shape[0], n, ap.shape[1]])


def ssd_phase(C):
    P, A = C.P, C.arena
    l = 0
    mv = C.modv[l]
    NB = S // 128
    T = 512
    NT = S // T
    m_phase = A.mark()
    one_t, kone = const_tile(C, "one", 1.0)
    masks, kmask = A.alloc("masks", [6, 128], F32)
    P.dma("sp", masks, C.d_masks.ap().rearrange("p (a b) -> p a b", a=6), [], [kmask])
    ones_f, konesf = const_tile(C, "ones_f", 1.0, F32, 128)
    cw, kcw = C.vec["ssd_cw"]
    cb_, kcb = C.vec["ssd_cb"]
    dtb, kdtb = C.vec["ssd_dtb"]
    alog, kalog = C.vec["ssd_alog"]
    dsk, kdsk = C.vec["ssd_dskip"]
    Aneg, kAneg = A.alloc("Aneg", [64], F32)
    P.add("act", lambda e: e.activation(Aneg, alog, AF.Exp), [kalog], [kAneg])
    P.add("dve", lambda e: e.tensor_scalar(Aneg, Aneg, -1.0, None, ALU.mult), [kAneg], [kAneg])
    h_all, khall = A.alloc("h_all", [NDC, S], BF16)

    m0 = A.mark()
    xb = [A.alloc(f"sx{i}", [NDC, T], F32) for i in range(2)]
    sq, ksq = A.alloc("ssq", [NDC, T], BF16)
    rstd, krstd = A.alloc("srstd", [T], F32)
    tmp = [A.alloc(f"stmp{i}", [512], F32) for i in range(3)]
    for m in range(NT):
        xt, kxt = xb[m % 2]
        P.dma("act", xt, C.xT[:, :, m * T:(m + 1) * T], [("xT", m)], [kxt])
        norm_modulate(C, xt, kxt, h_all[:, :, m * T:(m + 1) * T], (khall, m), T, l, 1, (sq, ksq, rstd, krstd, tmp, None))
    A.release(m0)
    hkeys = [((khall, m), 0) for m in range(NT)]

    m0 = A.mark()
    wsrc = C.d_ssd_win.ap().rearrange("(kc p) n -> p kc n", p=128)
    wst = [A.alloc(f"swst{i}", [8, 128], F32) for i in range(2)]
    wcb = [A.alloc(f"swc{i}", [8, 128], BF16) for i in range(2)]
    pre = [A.alloc(f"spre{i}", [S + 4], F32) for i in range(2)]
    acc = [A.alloc(f"sacc{i}", [S], F32) for i in range(2)]
    xo = [A.alloc(f"sxo{i}", [S], BF16) for i in range(2)]
    for i in range(2):
        pr, kpr = pre[i]
        P.add("pool", lambda e, pr=pr: e.memset(pr, 0.0), [], [kpr])
    for c in range(32):
        ws, kws = wst[c % 2]
        wc, kwc = wcb[c % 2]
        pr, kpr = pre[c % 2]
        ac, kac = acc[c % 2]
        xo_, kxo = xo[c % 2]
        P.dma("sp", ws, wsrc[:, :, 2048 + c * 128:2048 + (c + 1) * 128], [], [kws])
        P.add("pool", lambda e, wc=wc, ws=ws: e.tensor_copy(wc, ws), [kws], [kwc])
        for m in range(NT):
            ps, kps = psum_bank(C)
            for kc in range(8):
                P.mm(ps, wc[:, kc, :], h_all[:, kc, m * T:(m + 1) * T], kc == 0, kc == 7, [kwc, hkeys[m]], [kps])
            P.add("act", lambda e, ps=ps, pr=pr, m=m: e.copy(pr[:, 2 + m * T:2 + (m + 1) * T], ps), [kps], [kpr])
        for hf in range(2):
            o0 = hf * (S // 2)
            n_ = S // 2
            P.add("dve", lambda e, ac=ac, pr=pr, c=c, o0=o0, n_=n_: e.tensor_scalar(
                ac[:, o0:o0 + n_], pr[:, o0:o0 + n_], cw[:, c * 5:c * 5 + 1], cb_[:, c:c + 1], ALU.mult, ALU.add),
                [kpr, kcw, kcb], [(kac, hf)])
            for j in range(1, 5):
                P.add("dve", lambda e, ac=ac, pr=pr, c=c, j=j, o0=o0, n_=n_: e.scalar_tensor_tensor(
                    ac[:, o0:o0 + n_], pr[:, o0 + j:o0 + j + n_], cw[:, c * 5 + j:c * 5 + j + 1], ac[:, o0:o0 + n_],
                    ALU.mult, ALU.add), [kpr, kcw, (kac, hf)], [(kac, hf)])
            P.add("act", lambda e, xo_=xo_, ac=ac, o0=o0, n_=n_: e.activation(xo_[:, o0:o0 + n_], ac[:, o0:o0 + n_], AF.Silu),
                  [(kac, hf)], [(kxo, hf)])
        P.dma("sp", C.XBC[c], xo_, [(kxo, 0), (kxo, 1)], [("XBC", c)])
    A.release(m0)

    wdt, kwdt = A.alloc("wdt", [8, 64], BF16)
    m_big = A.mark()
    wz, kwz = A.alloc("wz", [8, 2048], BF16)
    m0 = A.mark()
    stg = [A.alloc(f"szst{i}", [8, 256], F32) for i in range(2)]
    it = 0
    for c0 in range(0, 2048, 256):
        st, kst = stg[it % 2]
        it += 1
        P.dma("sp", st, wsrc[:, :, c0:c0 + 256], [], [kst])
        emit_cast(P, cast_engine(C), wz[:, :, c0:c0 + 256], st, [kst], [kwz])
    st, kst = stg[it % 2]
    it += 1
    P.dma("sp", st[:, :, 0:64], wsrc[:, :, 6144:6208], [], [kst])
    emit_cast(P, cast_engine(C), wdt, st[:, :, 0:64], [kst], [kwdt])
    A.release(m0)
    ng16, kng16 = C.vec["ssd_ng16"]

    xbcT = [A.alloc(f"xbcT{i}", [32, 128], BF16) for i in range(2)]
    steps = [(ck, 1) for ck in range(NB - 1, -1, -1)] + [(ck, 0) for ck in range(NB)]

    def load_xbc(i):
        ck_ = steps[i][0]
        xb_, kxb = xbcT[i % 2]
        P.dma("sp", xb_, C.XBC.rearrange("c p s -> p c s")[:, :, ck_ * 128:(ck_ + 1) * 128],
              [("XBC", c) for c in range(32)], [kxb])
    xs_tm, kxs = A.alloc("xs_tm", [32, 64], BF16)
    B_tm, kbt = A.alloc("B_tm", [8, 128], BF16)
    dt_, kdt = A.alloc("dt", [64], F32)
    a_, ka = A.alloc("a", [64], F32)
    dec, kdec = A.alloc("dec", [96], F32)
    dt2, kdt2 = A.alloc("dt2", [32], F32)
    xdt, kxdt = A.alloc("xdt", [32, 64], BF16)
    xdtE, kxdtE = A.alloc("xdtE", [32, 64], BF16)
    cbm, kcbm = A.alloc("cbm", [8, 128], F32)
    Lb = [A.alloc(f"Lb{i}", [4, 128], F32) for i in range(2)]
    ex = [A.alloc(f"ex{i}", [4, 128], F32) for i in range(2)]
    MT = [A.alloc(f"MT{i}", [4, 128], BF16) for i in range(2)]
    yo = [A.alloc(f"yo{i}", [256], F32) for i in range(2)]
    ydir, kydir = A.alloc("ydir", [2048], F32)
    H, kH = A.alloc("H", [2048], F32)
    Hb, kHb = A.alloc("Hb", [2048], BF16)
    yb_in, kybin = A.alloc("yb_in", [2048], F32)
    sz, ksz = A.alloc("sz", [2048], F32)
    gss, kgss = A.alloc("gss", [8], F32)
    ynb, kynb = A.alloc("ynb", [2048], BF16)
    yT, kyT = A.alloc("yT", [16, 128], BF16)
    xck, kxck = A.alloc("xck", [NDC, 128], F32)

    def chunk_step(ck, d, it_):
        tk = slice(ck * 128, (ck + 1) * 128)
        xb_, kxb = xbcT[it_ % 2]
        if it_ + 1 < len(steps):
            load_xbc(it_ + 1)
        Lm = masks[:, 0 + 2 * d, :]
        Rm = masks[:, 1 + 2 * d, :]
        Vm = masks[:, 4 + d, :]
        dc0 = d * 32
        for q in range(3):
            ps, kps = psum_bank(C)
            psb = ps.bitcast(BF16)
            for j in range(8):
                c = q * 8 + j
                P.add("pe", lambda e, psb=psb, j=j, c=c, xb_=xb_: e.transpose(
                    psb[:, j * 128:(j + 1) * 128], xb_[:, c, :], C.ident_b), [kxb, C.k_ident_b], [kps])
            if q < 2:
                P.add("act", lambda e, psb=psb, q=q: e.copy(
                    xs_tm.rearrange("p a b -> p (a b)")[:, q * 1024:(q + 1) * 1024], psb), [kps], [kxs])
            else:
                P.add("dve", lambda e, psb=psb: e.tensor_copy(B_tm.rearrange("p a b -> p (a b)"), psb), [kps], [kbt])
        ps, kps = psum_bank(C)
        for kc in range(8):
            P.mm(ps[:, 0:64], h_all[:, kc, tk], wdt[:, kc, :], kc == 0, kc == 7, [hkeys[ck // 4], kwdt], [kps])
        P.add("dve", lambda e, ps=ps: e.tensor_tensor(dt_, ps[:, 0:64], dtb, ALU.add), [kps, kdtb], [kdt])
        P.add("act", lambda e: e.activation(dt_, dt_, AF.Exp), [kdt], [kdt])
        P.add("act", lambda e: e.activation(dt_, dt_, AF.Ln, bias=one_t[:, 0:1]), [kdt, kone], [kdt])
        P.add("dve", lambda e: e.tensor_tensor(a_, dt_, Aneg, ALU.mult), [kdt, kAneg], [ka])
        ps, kps = psum_bank(C)
        P.mm(ps[:, 0:32], Rm, a_[:, dc0:dc0 + 32], True, True, [kmask, ka], [kps])
        P.mm(ps[:, 32:64], Lm, a_[:, dc0:dc0 + 32], True, True, [kmask, ka], [kps])
        P.mm(ps[:, 64:96], ones_f, a_[:, dc0:dc0 + 32], True, True, [konesf, ka], [kps])
        P.add("act", lambda e, ps=ps: e.activation(dec, ps[:, 0:96], AF.Exp), [kps], [kdec])
        P.add("dve", lambda e: e.tensor_tensor(dt2, dt_[:, dc0:dc0 + 32], dec[:, 32:64], ALU.mult), [kdt, kdec], [kdt2])
        P.add("dve", lambda e: e.tensor_tensor(xdt, xs_tm, bc_last(dt_[:, dc0:dc0 + 32], 64), ALU.mult),
              [kxs, kdt], [kxdt])
        P.add("dve", lambda e: e.tensor_tensor(xdtE, xs_tm, bc_last(dt2, 64), ALU.mult), [kxs, kdt2], [kxdtE])
        for half in range(2):
            ps, kps = psum_bank(C)
            for j in range(4):
                g = half * 4 + j
                P.mm(ps[:, j * 128:(j + 1) * 128], xb_[:, 16 + g, :], xb_[:, 24 + g, :], True, True, [kxb], [kps])
            P.add("dve", lambda e, ps=ps, half=half: e.tensor_tensor(
                cbm[:, half * 4:(half + 1) * 4, :], ps.rearrange("p (a b) -> p a b", a=4), bc_mid(Vm, 4), ALU.mult),
                [kps, kmask], [kcbm])
        for g in range(8):
            lb, klb = Lb[g % 2]
            ex_, kex = ex[g % 2]
            mt, kmt = MT[g % 2]
            yo_, kyo = yo[g % 2]
            P.add("dve", lambda e, lb=lb, g=g: e.tensor_tensor(
                lb, bc_mid(Lm, 4), bc_last(a_[:, dc0 + g * 4:dc0 + g * 4 + 4], 128), ALU.mult), [kmask, ka], [klb])
            ps, kps = psum_bank(C)
            for r in range(4):
                P.mm(ps[:, r * 128:(r + 1) * 128], lb[:, r, :], Rm, True, True, [klb, kmask], [kps])
            P.add("act", lambda e, ps=ps, ex_=ex_: e.activation(ex_.rearrange("p a b -> p (a b)"), ps, AF.Exp), [kps], [kex])
            P.add("dve", lambda e, mt=mt, ex_=ex_, g=g: e.tensor_tensor(mt, ex_, bc_mid(cbm[:, g, :], 4), ALU.mult),
                  [kex, kcbm], [kmt])
            psy, kpsy = psum_bank(C)
            for r in range(4):
                P.mm(psy[:, r * 64:(r + 1) * 64], mt[:, r, :], xdt[:, g * 4 + r, :], True, True, [kmt, kxdt], [kpsy])
            P.mm(psy[:, 256:512], xb_[:, 24 + g, :], Hb[:, g * 256:(g + 1) * 256], True, True, [kxb, kHb], [kpsy])
            P.add("dve", lambda e, psy=psy, yo_=yo_, g=g: e.tensor_tensor(
                yo_.rearrange("p (a b) -> p a b", a=4), psy[:, 256:512].rearrange("p (a b) -> p a b", a=4),
                bc_last(dec[:, g * 4:g * 4 + 4], 64), ALU.mult), [kpsy, kdec], [kyo])
            P.add("dve", lambda e, psy=psy, yo_=yo_, g=g: e.tensor_tensor(
                ydir[:, g * 256:(g + 1) * 256], psy[:, 0:256], yo_, ALU.add), [kpsy, kyo], [(kydir, g)])
            pss, kpss = psum_bank(C)
            P.mm(pss[:, 0:256], B_tm[:, g, :], xdtE[:, g * 4:g * 4 + 4, :].rearrange("p a b -> p (a b)"), True, True,
                 [kbt, kxdtE], [kpss])
            P.add("dve", lambda e, g=g: e.tensor_tensor(
                H[:, g * 256:(g + 1) * 256].rearrange("p (a b) -> p a b", a=4),
                H[:, g * 256:(g + 1) * 256].rearrange("p (a b) -> p a b", a=4),
                bc_last(dec[:, 64 + g * 4:64 + g * 4 + 4], 64), ALU.mult), [(kH, g), kdec], [(kH, g)])
            P.add("dve", lambda e, pss=pss, g=g: e.tensor_tensor(
                H[:, g * 256:(g + 1) * 256], H[:, g * 256:(g + 1) * 256], pss[:, 0:256], ALU.add),
                [(kH, g), kpss], [(kH, g)])
            P.add("act", lambda e, g=g: e.copy(Hb[:, g * 256:(g + 1) * 256], H[:, g * 256:(g + 1) * 256]),
                  [(kH, g)], [kHb])

    ykeys = [(kydir, g) for g in range(8)]
    P.add("pool", lambda e: e.memset(H, 0.0), [], [(kH, g) for g in range(8)])
    P.add("pool", lambda e: e.memset(Hb, 0.0), [], [kHb])
    it_ = 0
    load_xbc(0)
    for ck in range(NB - 1, -1, -1):
        chunk_step(ck, 1, it_)
        it_ += 1
        P.dma("sp", C.YB[ck], ydir, ykeys, [("YB", ck)])
        tk = slice(ck * 128, (ck + 1) * 128)
        for zc in range(4):
            ps, kps = psum_bank(C)
            for kc in range(8):
                P.mm(ps, h_all[:, kc, tk], wz[:, kc, zc * 512:(zc + 1) * 512], kc == 0, kc == 7,
                     [hkeys[ck // 4], kwz], [kps])
            P.add("act", lambda e, ps=ps, zc=zc: e.activation(sz[:, zc * 512:(zc + 1) * 512], ps, AF.Silu), [kps], [ksz])
        P.dma("sp", C.SZ[ck], sz, [ksz], [("SZ", ck)])
    wout = wz.rearrange("p a b -> p (a b)").rearrange("p (k n) -> p k n", k=16)
    kwout = kwz
    stg = [(yb_in.rearrange("p (k n) -> p k n", k=2), kybin), (sz.rearrange("p (k n) -> p k n", k=2), ksz)]
    osrc = C.d_ssd_wout.ap().rearrange("(kc p) n -> p kc n", p=128)
    for i_, c0 in enumerate(range(0, 16, 2)):
        st, kst = stg[i_ % 2]
        P.dma("sp", st, osrc[:, c0:c0 + 2, :], [], [kst])
        emit_cast(P, cast_engine(C), wout[:, c0:c0 + 2, :], st, [kst], [kwout])
    P.add("pool", lambda e: e.memset(H, 0.0), [], [(kH, g) for g in range(8)])
    P.add("pool", lambda e: e.memset(Hb, 0.0), [], [kHb])
    for ck in range(NB):
        tk = slice(ck * 128, (ck + 1) * 128)
        P.dma("act", yb_in, C.YB[ck], [("YB", ck)], [kybin])
        P.dma("act", xck, C.xT[:, :, tk], [("xT", ck // 4)], [kxck])
        chunk_step(ck, 0, it_)
        it_ += 1
        P.add("dve", lambda e: e.tensor_tensor(ydir, ydir, yb_in, ALU.add), ykeys + [kybin], ykeys)
        P.add("dve", lambda e: e.tensor_tensor(yb_in.rearrange("p (a b) -> p a b", a=32), xs_tm, bc_last(dsk, 64), ALU.mult),
              [kxs, kdsk], [kybin])
        P.add("dve", lambda e: e.tensor_tensor(ydir, ydir, yb_in, ALU.add), ykeys + [kybin], ykeys)
        P.dma("sp", sz, C.SZ[ck], [("SZ", ck)], [ksz])
        P.add("dve", lambda e: e.tensor_tensor(ydir, ydir, sz, ALU.mult), ykeys + [ksz], ykeys)
        P.add("act", lambda e: e.activation(sz, ydir, AF.Square), ykeys, [ksz])
        P.add("dve", lambda e: e.reduce_sum(gss, sz.rearrange("p (a b) -> p a b", a=8), AX.X), [ksz], [kgss])
        P.add("act", lambda e: e.activation(gss, gss, AF.Ln, bias=C.eps_t[:, 0:1], scale=1.0 / 256), [kgss, C.k_eps], [kgss])
        P.add("act", lambda e: e.activation(gss, gss, AF.Exp, scale=-0.5), [kgss], [kgss])
        P.add("dve", lambda e: e.tensor_tensor(ydir.rearrange("p (a b) -> p a b", a=8), ydir.rearrange("p (a b) -> p a b", a=8),
                                               bc_last(gss, 256), ALU.mult), ykeys + [kgss], ykeys)
        P.add("act", lambda e: e.copy(ynb, ydir), ykeys, [kynb])
        for q in range(2):
            ps, kps = psum_bank(C)
            psb = ps.bitcast(BF16)
            for j in range(8):
                c = q * 8 + j
                P.add("pe", lambda e, psb=psb, j=j, c=c: e.transpose(
                    psb[:, j * 128:(j + 1) * 128], ynb[:, c * 128:(c + 1) * 128], C.ident_b), [kynb, C.k_ident_b], [kps])
            for j in range(8):
                c = q * 8 + j
                P.add("act", lambda e, psb=psb, j=j, c=c: e.activation(
                    yT[:, c, :], psb[:, j * 128:(j + 1) * 128], AF.Identity, scale=ng16[:, c:c + 1]),
                    [kps, kng16], [kyT])
        for half in range(2):
            ps, kps = psum_bank(C)
            for j in range(4):
                dc = half * 4 + j
                for kc in range(16):
                    P.mm(ps[:, j * 128:(j + 1) * 128], wout[:, kc, dc * 128:(dc + 1) * 128], yT[:, kc, :],
                         kc == 0, kc == 15, [kwout, kyT], [kps])
            for j in range(4):
                dc = half * 4 + j
                P.add("dve", lambda e, ps=ps, j=j, dc=dc: e.scalar_tensor_tensor(
                    xck[:, dc, :], ps[:, j * 128:(j + 1) * 128], mv["gate"][:, 8 + dc:8 + dc + 1], xck[:, dc, :],
                    ALU.mult, ALU.add), [kps, mv["kgate"], kxck], [kxck])
        P.dma("act", C.xT[:, :, tk], xck, [kxck], [("xT", ck // 4)])
    A.release(m_phase)

def build_program(stages, seq=4096, debug=False):
    global S
    S = seq
    nc = bass.Bass("TRN2", target_bir_lowering=False)
    C = Ctx()
    C.debug = debug
    C.dbg_off = 0
    C.dbg_map = {}
    C.dbg_keys = []
    C.nc = nc
    C.P = Prog(nc)
    C.ps_i = 0
    C.bar_gidx = 0
    C.cast_i = 0
    dt = nc.dram_tensor
    C.d_x = dt("x", [S, D], F32, kind="ExternalInput")
    C.d_out = dt("out", [S, D], F32, kind="ExternalOutput")
    C.d_ident = dt("ident", [128, 128], F32, kind="ExternalInput")
    if debug:
        C.d_dbg = dt("dbg", [128, 8192], F32, kind="ExternalOutput")
    C.d_w_mod = dt("w_mod", [2, D, 9 * D], F32, kind="ExternalInput")
    C.d_wg = dt("ffn_w_gate", [2, 2, D, DFF], F32, kind="ExternalInput")
    C.d_wu = dt("ffn_w_up", [2, 2, D, DFF], F32, kind="ExternalInput")
    C.d_wd = dt("ffn_w_down", [2, 2, DFF, D], F32, kind="ExternalInput")
    C.d_pm = dt("pmT", [96, 96], F32, kind="ExternalInput")
    C.d_pos = dt("pos", [128, S], I32, kind="ExternalInput")
    C.d_mla_win = dt("mla_w_in", [D, 672], F32, kind="ExternalInput")
    C.d_mla_wuq = dt("mla_w_uq", [384, 1536], F32, kind="ExternalInput")
    C.d_mla_wukv = dt("mla_w_ukv", [256, 2048], F32, kind="ExternalInput")
    C.d_mla_wout = dt("mla_w_out", [D, D], F32, kind="ExternalInput")
    C.OT = dt("OT_scr", [8, 128, S], BF16, kind="Internal").ap()
    C.d_masks = dt("masks", [128, 768], F32, kind="ExternalInput")
    C.d_ssd_win = dt("ssd_w_in", [D, 6208], F32, kind="ExternalInput")
    C.d_ssd_wout = dt("ssd_w_out", [2048, D], F32, kind="ExternalInput")
    C.SZ = dt("SZ_scr", [S // 128, 128, 2048], F32, kind="Internal").ap()
    C.XBC = dt("XBC_scr", [32, 128, S], BF16, kind="Internal").ap()
    C.YB = dt("YB_scr", [S // 128, 128, 2048], F32, kind="Internal").ap()
    C.d_vecs = {}
    for name, n in VEC_SPECS:
        C.d_vecs[name] = dt("v_" + name, [128, n], F32, kind="ExternalInput")
    C.xT = dt("xT_scr", [128, NDC, S], F32, kind="Internal").ap()
    C.WGU = [dt(f"wgu_scr{i}", [NF, 128, 2, 8, 128], BF16, kind="Internal").ap() for i in range(4)]
    C.WD = [dt(f"wd_scr{i}", [NDC, 128, NF, 128], BF16, kind="Internal").ap() for i in range(4)]

    ARENA_BYTES = 207 * 1024
    with ExitStack() as es:
        ah = es.enter_context(nc.sbuf_tensor("arena", [128, ARENA_BYTES // 4], F32))
        C.arena = Arena(ah, ARENA_BYTES, C.P)
        C.ps = [es.enter_context(nc.psum_tensor(f"ps{i}", [128, 512], F32))[:] for i in range(8)]
        eng_sems = {e: es.enter_context(nc.semaphore(f"sem_{e}")) for e in ENGS}
        dma_sems = [es.enter_context(nc.semaphore(f"dsem{i}")) for i in range(N_DMA_SEMS)]

        setup_consts(C)
        if debug:
            C.dbg_stage, _ = C.arena.alloc('dbg_stage', [3584], F32)
        load_transpose_x(C)
        compute_mod(C)
        for (l, w) in stages.get("ffn", []):
            convert_ffn_weights(C, l, w)
        for st_ in stages.get("order", []):
            if BARRIERS:
                phase_barrier(C)
            if st_[0] == "ffn":
                ffn_phase(C, st_[1], st_[2])
            elif st_[0] == "mla":
                mla_phase(C)
            elif st_[0] == "ssd":
                ssd_phase(C)
        store_transpose_out(C)
        C.P.emit(eng_sems, dma_sems)
    C.nc = nc
    return C


VEC_SPECS = [("c", 8), ("b_mod0", 72), ("b_mod1", 72), ("norm_g0", 24), ("norm_g1", 24),
             ("mla_qg", 1), ("mla_kg", 1), ("mla_qng", 3), ("mla_kvng", 2), ("invf", 1),
             ("ssd_cw", 160), ("ssd_cb", 32), ("ssd_dtb", 64), ("ssd_alog", 64), ("ssd_dskip", 32), ("ssd_ng16", 16)]


def _consts():
    inv = (10000.0 ** (-np.arange(0, 32, 2, dtype=np.float32) / 32)).astype(np.float32)
    invf = np.zeros((128, 1), np.float32)
    invf[64:80, 0] = inv
    invf[80:96, 0] = inv
    pm = np.zeros((96, 96), np.float32)
    for i in range(16):
        pm[80 + i, 64 + i] = -1.0
        pm[64 + i, 80 + i] = 1.0
    return invf, pm


INVF, PMT = _consts()


def _masks():
    k = np.arange(128)[:, None]
    j = np.arange(128)[None, :]
    Lf = (k > j); Rf = (k <= j); Lb = (k < j); Rb = (k >= j)
    Vf = (j >= k)
    Vb = (j <= k)
    return np.concatenate([m.astype(np.float32) for m in (Lf, Rf, Lb, Rb, Vf, Vb)], axis=1)


MASKS = _masks()


def host_vecs(inputs, b):
    f = np.float32
    v = {}
    v["c"] = np.ascontiguousarray(inputs["c"][b].reshape(8, 128).T.astype(f))
    for l in range(2):
        v[f"b_mod{l}"] = np.ascontiguousarray(inputs["b_mod"][l].reshape(72, 128).T.astype(f))
        v[f"norm_g{l}"] = np.ascontiguousarray(inputs["norm_g"][l].reshape(24, 128).T.astype(f))
    def col(a, n=128):
        o = np.zeros((128, 1), f)
        o[:len(a), 0] = a
        return o
    v["mla_qg"] = col(inputs["mla_q_head_g"][0])
    v["mla_kg"] = col(inputs["mla_k_head_g"][0])
    v["mla_qng"] = np.ascontiguousarray(inputs["mla_q_norm_g"][0].reshape(3, 128).T.astype(f))
    v["mla_kvng"] = np.ascontiguousarray(inputs["mla_kv_norm_g"][0].reshape(2, 128).T.astype(f))
    v["invf"] = INVF
    cwt = inputs["ssd_conv_w"][0]
    v["ssd_cw"] = np.ascontiguousarray(cwt.reshape(5, 32, 128).transpose(2, 1, 0).reshape(128, 160).astype(f))
    v["ssd_cb"] = np.ascontiguousarray(inputs["ssd_conv_b"][0].reshape(32, 128).T.astype(f))
    v["ssd_dtb"] = np.ascontiguousarray(np.broadcast_to(inputs["ssd_dt_bias"][0].reshape(1, 64), (128, 64)).astype(f))
    v["ssd_alog"] = np.ascontiguousarray(np.broadcast_to(inputs["ssd_a_log"][0].reshape(1, 64), (128, 64)).astype(f))
    v["ssd_ng16"] = np.ascontiguousarray(inputs["ssd_norm_g"][0].reshape(16, 128).T.astype(f))
    v["ssd_dskip"] = np.ascontiguousarray(np.broadcast_to(inputs["ssd_d"][0].reshape(1, 32), (128, 32)).astype(f))
    return v


def run(inputs, stages, seq=4096, cores=8, debug=False):
    C = build_program(stages, seq, debug)
    nc = C.nc
    ident = np.eye(128, dtype=np.float32)
    in_maps = []
    for b in range(cores):
        m = {
            "x": np.ascontiguousarray(inputs["x"][b][:seq]),
            "ident": ident,
            "w_mod": inputs["w_mod"],
            "ffn_w_gate": inputs["ffn_w_gate"],
            "ffn_w_up": inputs["ffn_w_up"],
            "ffn_w_down": inputs["ffn_w_down"],
            "pmT": PMT,
            "masks": MASKS,
            "ssd_w_in": inputs["ssd_w_in"][0], "ssd_w_out": inputs["ssd_w_out"][0],

            "pos": np.ascontiguousarray(np.broadcast_to(inputs["positions"][b][None, :seq], (128, seq)).astype(np.int32)),
            "mla_w_in": inputs["mla_w_in"][0], "mla_w_uq": inputs["mla_w_uq"][0],
            "mla_w_ukv": inputs["mla_w_ukv"][0], "mla_w_out": inputs["mla_w_out"][0],
        }
        for k, a in host_vecs(inputs, b).items():
            m["v_" + k] = a
        in_maps.append(m)
    res = run_bass_kernel_spmd(nc, in_maps, core_ids=list(range(cores)))
    out = np.stack([r["out"] for r in res.results], axis=0)
    if debug:
        return out, {k: res.results[0]["dbg"][:, o:o + n] for k, (o, n) in C.dbg_map.items()}
    return out


def kernel(**inputs):
    inputs = {k: np.asarray(v) for k, v in inputs.items()}
    stages = {
        "ffn": [(0, 0), (0, 1), (1, 0), (1, 1)],
        "order": [("ffn", 0, 0), ("ssd",), ("ffn", 0, 1), ("ffn", 1, 0), ("mla",), ("ffn", 1, 1)],
    }
    return run(inputs, stages).astype(np.float32)
```

```python
import numpy as np
from contextlib import ExitStack
import concourse.bass as bass
import concourse.mybir as mybir
from concourse.bass_utils import run_bass_kernel_spmd

F32 = mybir.dt.float32
BF16 = mybir.dt.bfloat16
I32 = mybir.dt.int32
AF = mybir.ActivationFunctionType
ALU = mybir.AluOpType
AX = mybir.AxisListType

D = 1024
S = 4096
DFF = 2816
NF = DFF // 128
NDC = D // 128
EPS = 1e-6
N_DMA_SEMS = 40
DBG_M = 0
BARRIERS = False
SKIP_SAME_ENG_NONRAW = False
ENGS = ("pe", "act", "dve", "pool", "sp")


class Op:
    __slots__ = ("eng", "fn", "deps", "sig", "seq", "dma", "semid", "semval", "prev", "pos", "gidx", "raw")


class Prog:
    def __init__(self, nc):
        self.nc = nc
        self.ops = {e: [] for e in ENGS}
        self.lastw = {}
        self.readers = {}
        self.ndma = 0
        self.dma_last = [None] * N_DMA_SEMS
        self.dma_cnt = [0] * N_DMA_SEMS
        self.nops = 0
        self.bases = set()
        self.touched = {}
        self.inherit = {}
        self.seen = set()

    def base_of(self, k):
        for _ in range(4):
            if k in self.bases:
                return k
            if isinstance(k, tuple) and len(k):
                k = k[0]
            else:
                return None
        return None

    def add(self, eng, fn, reads=(), writes=(), dma=False):
        op = Op()
        op.eng = eng
        op.fn = fn
        op.dma = dma
        op.sig = False
        op.seq = 0
        op.gidx = self.nops
        self.nops += 1
        deps = set()
        raw = set()
        for k in reads:
            w = self.lastw.get(k)
            if w is not None:
                raw.add(w)
        op.raw = raw
        for k in list(reads) + list(writes):
            b = self.base_of(k)
            if b is None:
                continue
            if k not in self.seen:
                self.seen.add(k)
                deps.update(self.inherit.get(b, ()))
            t = self.touched.setdefault(b, {})
            if dma:
                t[("dma", op.gidx)] = op
            else:
                t[eng] = op
        for k in reads:
            w = self.lastw.get(k)
            if w is not None:
                deps.add(w)
        for k in writes:
            w = self.lastw.get(k)
            if w is not None:
                deps.add(w)
            for r in self.readers.get(k, ()):
                deps.add(r)
        for k in reads:
            self.readers.setdefault(k, []).append(op)
        for k in writes:
            self.lastw[k] = op
            self.readers[k] = []
        deps.discard(op)
        op.prev = None
        if dma:
            s = self.ndma % N_DMA_SEMS
            self.ndma += 1
            op.semid = s
            self.dma_cnt[s] += 16
            op.semval = self.dma_cnt[s]
            op.prev = self.dma_last[s]
            self.dma_last[s] = op
        op.deps = deps
        op.pos = len(self.ops[eng])
        self.ops[eng].append(op)
        return op

    def dma(self, eng, out, in_, reads, writes, **kw):
        return self.add(eng, lambda e: e.dma_start(out=out, in_=in_, **kw), reads, writes, dma=True)

    def mm(self, out, lhsT, rhs, start, stop, reads, writes):
        return self.add("pe", lambda e: e.matmul(out, lhsT, rhs, start=start, stop=stop), reads, writes)

    def emit(self, eng_sems, dma_sems):
        nc = self.nc

        def needs_sync(op, d):
            if d.dma:
                return True
            if d.eng == op.eng and not op.dma:
                if op.eng == "pe":
                    return False
                return (d in op.raw) or not SKIP_SAME_ENG_NONRAW
            if d.eng == op.eng and op.dma:
                return True
            return True

        for e in ENGS:
            for op in self.ops[e]:
                for d in op.deps:
                    if needs_sync(op, d) and not d.dma:
                        d.sig = True
        for e in ENGS:
            n = 0
            for op in self.ops[e]:
                if op.sig and not op.dma:
                    n += 1
                    op.seq = n

        def emit_engine(ename, eobj):
            waited = {}
            for op in self.ops[ename]:
                need = {}
                for d in op.deps:
                    if not needs_sync(op, d):
                        continue
                    if d.dma:
                        key = ("d", d.semid)
                        val = d.semval
                    else:
                        key = ("e", d.eng)
                        val = d.seq
                    if need.get(key, 0) < val:
                        need[key] = val
                if op.dma and op.prev is not None:
                    key = ("d", op.prev.semid)
                    if need.get(key, 0) < op.prev.semval:
                        need[key] = op.prev.semval
                pend = []
                for key, val in need.items():
                    if waited.get(key, 0) >= val:
                        continue
                    waited[key] = val
                    sem = dma_sems[key[1]] if key[0] == "d" else eng_sems[key[1]]
                    pend.append((key[0] == "d", sem, val))
                pend.sort(key=lambda t: t[0])
                if op.fn is None:
                    for _, sem, val in pend:
                        eobj.wait_ge(sem, val)
                    continue
                for _, sem, val in pend[:-1]:
                    eobj.wait_ge(sem, val)
                ins = op.fn(eobj)
                if pend:
                    ins._wait_ge(pend[-1][1], pend[-1][2])
                if op.dma:
                    ins.then_inc(dma_sems[op.semid], 16)
                elif op.sig:
                    ins.then_inc(eng_sems[ename], 1)

        with nc.Block() as block:
            @block.tensor
            def _(e):
                emit_engine("pe", e)

            @block.scalar
            def _(e):
                emit_engine("act", e)

            @block.vector
            def _(e):
                emit_engine("dve", e)

            @block.gpsimd
            def _(e):
                emit_engine("pool", e)

            @block.sync
            def _(e):
                emit_engine("sp", e)


class Arena:
    def __init__(self, handle, nbytes, prog):
        self.h = handle
        self.cap = nbytes
        self.top = 0
        self.gen = 0
        self.P = prog
        self.allocs = []

    def mark(self):
        return self.top

    def release(self, m):
        self.top = m

    def alloc(self, name, shape, dtype):
        esz = 2 if dtype == BF16 else 4
        n = 1
        for s_ in shape:
            n *= s_
        nbytes = (n * esz + 63) // 64 * 64
        off = self.top
        assert off + nbytes <= self.cap, f"SBUF arena overflow allocating {name}: {off}+{nbytes}>{self.cap}"
        self.top += nbytes
        ap = self.h[:, off // 4:(off + nbytes) // 4]
        if dtype != F32:
            ap = ap.bitcast(dtype)
        ap = ap[:, 0:n]
        if len(shape) == 2:
            ap = ap.rearrange("p (a b) -> p a b", a=shape[0])
        elif len(shape) == 3:
            ap = ap.rearrange("p (a b c) -> p a b c", a=shape[0], b=shape[1])
        elif len(shape) == 4:
            ap = ap.rearrange("p (a b c d) -> p a b c d", a=shape[0], b=shape[1], c=shape[2])
        self.gen += 1
        key = (name, self.gen)
        P = self.P
        P.bases.add(key)
        inh = set()
        for (a0, a1, ok) in self.allocs:
            if a0 < off + nbytes and off < a1:
                inh.update(P.touched.get(ok, {}).values())
                inh.update(P.inherit.get(ok, ()))
        P.inherit[key] = inh
        self.allocs = [(a0, a1, ok) for (a0, a1, ok) in self.allocs if not (a0 >= off and a1 <= off + nbytes)]
        self.allocs.append((off, off + nbytes, key))
        return ap, key


class Ctx:
    pass


def dbg(C, name, ap, key, n):
    if not C.debug:
        return
    P, A = C.P, C.arena
    st = C.dbg_stage[:, C.dbg_off:C.dbg_off + n]
    kst = ("dbgst", name)
    P.add("pool", lambda e: e.tensor_copy(st, ap), [key], [kst])
    off = C.dbg_off
    C.dbg_off += n
    C.dbg_map[name] = (off, n)
    P.dma("sp", C.d_dbg.ap()[:, off:off + n], st, [kst], [("DBG", name)])
    C.dbg_keys.append(("DBG", name))


def rr(lst, i):
    return lst[i % len(lst)]


def setup_consts(C):
    P, A = C.P, C.arena
    C.ident_f, C.k_ident_f = A.alloc("ident_f", [128], F32)
    C.ident_b, C.k_ident_b = A.alloc("ident_b", [128], BF16)
    C.onesD_b, C.k_onesD = A.alloc("onesD", [128], BF16)
    P.dma("sp", C.ident_f, C.d_ident.ap(), [], [C.k_ident_f])
    P.add("dve", lambda e: e.tensor_copy(C.ident_b, C.ident_f), [C.k_ident_f], [C.k_ident_b])
    P.add("pool", lambda e: e.memset(C.onesD_b, 1.0 / D), [], [C.k_onesD])
    C.eps_t, C.k_eps = A.alloc("eps_t", [1], F32)
    P.add("pool", lambda e: e.memset(C.eps_t, EPS), [], [C.k_eps])
    C.vec = {}
    for name, t in C.d_vecs.items():
        n = t.ap().shape[1]
        ap, k = A.alloc("v_" + name, [n], F32)
        P.dma("sp", ap, t.ap(), [], [k])
        C.vec[name] = (ap, k)


def phase_barrier(C):
    P = C.P
    last = [P.ops[e][-1] for e in ENGS if P.ops[e]]
    last = [o for o in last if o.fn is not None]
    dmas = [o for e in ENGS for o in P.ops[e] if o.dma and o.gidx >= C.bar_gidx]
    C.bar_gidx = P.nops
    for e in ENGS:
        op = P.add(e, None)
        op.deps.update(last)
        op.deps.update(dmas)
        op.deps.discard(op)


def psum_bank(C):
    i = C.ps_i % 8
    C.ps_i += 1
    return C.ps[i], ("ps", i)


def load_transpose_x(C):
    P, A = C.P, C.arena
    m0 = A.mark()
    xin = [A.alloc(f"xin{i}", [4, D], F32) for i in range(2)]
    xtt = [A.alloc(f"xtt{i}", [NDC, 512], F32) for i in range(2)]
    xd = C.d_x.ap().rearrange("(g j p) d -> g p j d", j=4, p=128)
    for g in range(S // 512):
        xi, kxi = xin[g % 2]
        xt, kxt = xtt[g % 2]
        P.dma("sp", xi, xd[g], [], [kxi])
        for dc in range(NDC):
            ps, kps = psum_bank(C)
            for j in range(4):
                P.add("pe", lambda e, ps=ps, xi=xi, j=j, dc=dc: e.transpose(
                    ps[:, j * 128:(j + 1) * 128], xi[:, j, dc * 128:(dc + 1) * 128], C.ident_f),
                    [kxi, C.k_ident_f], [kps])
            eng = "act" if dc % 2 == 0 else "dve"
            if eng == "act":
                P.add("act", lambda e, ps=ps, xt=xt, dc=dc: e.copy(xt[:, dc, :], ps), [kps], [kxt])
            else:
                P.add("dve", lambda e, ps=ps, xt=xt, dc=dc: e.tensor_copy(xt[:, dc, :], ps), [kps], [kxt])
        P.dma("sp", C.xT[:, :, g * 512:(g + 1) * 512], xt, [kxt], [("xT", g)])
    A.release(m0)


def store_transpose_out(C):
    P, A = C.P, C.arena
    m0 = A.mark()
    xtt = [A.alloc(f"oxt{i}", [NDC, 512], F32) for i in range(2)]
    xo = [A.alloc(f"oxo{i}", [4, D], F32) for i in range(2)]
    od = C.d_out.ap().rearrange("(g j p) d -> g p j d", j=4, p=128)
    for g in range(S // 512):
        xt, kxt = xtt[g % 2]
        xo_, kxo = xo[g % 2]
        P.dma("sp", xt, C.xT[:, :, g * 512:(g + 1) * 512], [("xT", g)], [kxt])
        for j in range(4):
            for half in range(2):
                ps, kps = psum_bank(C)
                for q in range(4):
                    dc = half * 4 + q
                    P.add("pe", lambda e, ps=ps, xt=xt, j=j, dc=dc, q=q: e.transpose(
                        ps[:, q * 128:(q + 1) * 128], xt[:, dc, j * 128:(j + 1) * 128], C.ident_f),
                        [kxt, C.k_ident_f], [kps])
                if half == 0:
                    P.add("act", lambda e, ps=ps, xo_=xo_, j=j: e.copy(xo_[:, j, 0:512], ps), [kps], [kxo])
                else:
                    P.add("dve", lambda e, ps=ps, xo_=xo_, j=j: e.tensor_copy(xo_[:, j, 512:1024], ps), [kps], [kxo])
        P.dma("sp", od[g], xo_, [kxo], [("OUT", g)])
    A.release(m0)
    P.add("sp", None, [("OUT", g) for g in range(S // 512)] + C.dbg_keys, [])


def compute_mod(C):
    P, A = C.P, C.arena
    cvec, kc_ = C.vec["c"]
    C.cond, C.k_cond = A.alloc("cond", [8], F32)
    P.add("act", lambda e: e.activation(C.cond, cvec, AF.Silu), [kc_], [C.k_cond])
    C.mod = []
    m0 = None
    for l in range(2):
        mod, kmod = A.alloc(f"mod{l}", [72], F32)
        C.mod.append((mod, kmod))
    m0 = A.mark()
    wb = [A.alloc(f"wmod{i}", [8, 1024], F32) for i in range(2)]
    it = 0
    for l in range(2):
        mod, kmod = C.mod[l]
        bm, kbm = C.vec[f"b_mod{l}"]
        wd = C.d_w_mod.ap()[l].rearrange("(kc p) n -> p kc n", p=128)
        ps, kps = psum_bank(C)
        for cb in range(9):
            w, kw = wb[it % 2]
            it += 1
            P.dma("sp", w, wd[:, :, cb * 1024:(cb + 1) * 1024], [], [kw])
            for j in range(8):
                col = cb * 8 + j
                for kc in range(8):
                    P.mm(ps[:, col:col + 1], w[:, kc, j * 128:(j + 1) * 128], C.cond[:, kc:kc + 1],
                         kc == 0, kc == 7, [kw, C.k_cond], [kps])
        P.add("dve", lambda e, mod=mod, ps=ps, bm=bm: e.tensor_tensor(mod, ps[:, 0:72], bm, ALU.add),
              [kps, kbm], [kmod])
    A.release(m0)
    C.modv = []
    for l in range(2):
        mod, kmod = C.mod[l]
        g, kg = C.vec[f"norm_g{l}"]
        a, ka = A.alloc(f"moda{l}", [24], F32)
        gt, kgt = A.alloc(f"modg{l}", [24], F32)
        for sub in range(3):
            sc = mod[:, (sub * 3 + 1) * 8:(sub * 3 + 2) * 8]
            P.add("dve", lambda e, a=a, sub=sub, sc=sc, g=g: e.scalar_tensor_tensor(
                a[:, sub * 8:(sub + 1) * 8], sc, 1.0, g[:, sub * 8:(sub + 1) * 8], ALU.add, ALU.mult),
                [kmod, kg], [ka])
            gsrc = mod[:, (sub * 3 + 2) * 8:(sub * 3 + 3) * 8]
            fac = 1.0 if sub == 1 else 0.5
            P.add("dve", lambda e, gt=gt, sub=sub, gsrc=gsrc, fac=fac: e.tensor_scalar(
                gt[:, sub * 8:(sub + 1) * 8], gsrc, fac, None, ALU.mult), [kmod], [kgt])
        C.modv.append(dict(a=a, ka=ka, gate=gt, kgate=kgt, mod=mod, kmod=kmod))
        if l == 0:
            dbg(C, 'mod0', mod, kmod, 72)
            dbg(C, 'a0', a, ka, 24)
            dbg(C, 'gate0', gt, kgt, 24)


def cast_engine(C):
    e = ("dve", "act")[C.cast_i % 2]
    C.cast_i += 1
    return e


def emit_cast(P, eng, out, in_, reads, writes):
    if eng == "act":
        P.add("act", lambda e: e.copy(out, in_), reads, writes)
    else:
        P.add(eng, lambda e: e.tensor_copy(out, in_), reads, writes)


def convert_ffn_weights(C, l, w):
    P, A = C.P, C.arena
    idx = l * 2 + w
    m0 = A.mark()
    stg = [A.alloc(f"cst{i}", [4096], F32) for i in range(3)]
    stb = [A.alloc(f"csb{i}", [4096], BF16) for i in range(3)]
    it = 0
    for gi, src in enumerate((C.d_wg, C.d_wu)):
        sd = src.ap()[l, w].rearrange("(kc p) n -> p kc n", p=128)
        for fb in range(0, NF, 4):
            nf = min(4, NF - fb)
            sf, ksf = stg[it % 3]
            sb, ksb = stb[it % 3]
            it += 1
            sfv = sf[:, 0:8 * nf * 128].rearrange("p (kc n) -> p kc n", kc=8)
            P.dma("sp", sfv, sd[:, :, fb * 128:(fb + nf) * 128], [], [ksf])
            sbv = sb[:, 0:nf * 8 * 128].rearrange("p (f kc m) -> p f kc m", f=nf, kc=8)
            emit_cast(P, cast_engine(C), sbv, sfv.rearrange("p kc (f m) -> p f kc m", f=nf), [ksf], [ksb])
            dst = C.WGU[idx][fb:fb + nf, :, gi, :, :].rearrange("f p kc m -> p f (kc m)")
            P.dma("sp", dst, sbv.rearrange("p f kc m -> p f (kc m)"), [ksb], [("WGU", idx, f_) for f_ in range(fb, fb + nf)])
    sd = C.d_wd.ap()[l, w].rearrange("(fc p) n -> p fc n", p=128)
    for fb in range(0, NF, 4):
        nf = min(4, NF - fb)
        sf, ksf = stg[it % 3]
        sb, ksb = stb[it % 3]
        it += 1
        sfv = sf[:, 0:nf * 1024].rearrange("p (fc n) -> p fc n", fc=nf)
        P.dma("sp", sfv, sd[:, fb:fb + nf, :], [], [ksf])
        sbv = sb[:, 0:8 * nf * 128].rearrange("p (dc fc m) -> p dc fc m", dc=8, fc=nf)
        emit_cast(P, cast_engine(C), sbv, sfv.rearrange("p fc (dc m) -> p dc fc m", dc=8), [ksf], [ksb])
        dst = C.WD[idx][:, :, fb:fb + nf, :].rearrange("dc p fc m -> p dc (fc m)")
        P.dma("sp", dst, sbv.rearrange("p dc fc m -> p dc (fc m)"), [ksb], [("WD", idx, dc) for dc in range(8)])
    A.release(m0)


def norm_modulate(C, xt, kxt, h, kh, T, l, sub, scratch):
    P = C.P
    mv = C.modv[l]
    sq, ksq, rstd, krstd, tmp, ktmp = scratch
    shift = mv["mod"][:, (sub * 3) * 8:(sub * 3 + 1) * 8]
    for st in range(T // 512):
        sl = slice(st * 512, (st + 1) * 512)
        for dc in range(NDC):
            P.add("act", lambda e, dc=dc, sl=sl: e.activation(sq[:, dc, sl], xt[:, dc, sl], AF.Square),
                  [kxt], [(ksq, st)])
        ps, kps = psum_bank(C)
        for dc in range(NDC):
            P.mm(ps, C.onesD_b, sq[:, dc, sl], dc == 0, dc == NDC - 1, [(ksq, st), C.k_onesD], [kps])
        P.add("act", lambda e, ps=ps, sl=sl: e.activation(rstd[:, sl], ps, AF.Ln, bias=C.eps_t[:, 0:1]),
              [kps, C.k_eps], [(krstd, st)])
        P.add("act", lambda e, sl=sl: e.activation(rstd[:, sl], rstd[:, sl], AF.Exp, scale=-0.5),
              [(krstd, st)], [(krstd, st)])
        for dc in range(NDC):
            tm, ktm = tmp[dc % len(tmp)]
            eng = "dve"
            P.add(eng, lambda e, tm=tm, dc=dc, sl=sl: e.tensor_tensor(tm, xt[:, dc, sl], rstd[:, sl], ALU.mult),
                  [kxt, (krstd, st)], [ktm])
            P.add("act", lambda e, tm=tm, dc=dc, sl=sl: e.activation(
                h[:, dc, sl], tm, AF.Identity, bias=shift[:, dc:dc + 1],
                scale=mv["a"][:, sub * 8 + dc:sub * 8 + dc + 1]),
                [ktm, mv["ka"], mv["kmod"]], [(kh, st)])


def ffn_phase(C, l, w):
    P, A = C.P, C.arena
    idx = l * 2 + w
    sub = 0 if w == 0 else 2
    mv = C.modv[l]
    T = 1024
    m0 = A.mark()
    xb = [A.alloc(f"fx{i}", [NDC, T], F32) for i in range(2)]
    h, kh = A.alloc("fh", [NDC, T], BF16)
    act, kact = A.alloc("fact", [NF, T], BF16)
    sq, ksq = A.alloc("fsq", [NDC, T], BF16)
    rstd, krstd = A.alloc("frstd", [T], F32)
    tmp = [A.alloc(f"ftmp{i}", [512], F32) for i in range(3)]
    sg = [A.alloc(f"fsg{i}", [512], F32) for i in range(3)]
    wgu = [A.alloc(f"fwgu{i}", [2, 8, 128], BF16) for i in range(4)]
    wdb = [A.alloc(f"fwd{i}", [NF, 128], BF16) for i in range(3)]
    NM = S // T
    items = []
    for m in range(NM):
        for f in range(NF):
            items.append(("g", f))
        for dc in range(NDC):
            items.append(("d", dc))
    issued = [0]
    cnt = {"g": 0, "d": 0}
    slot_of = {}

    def prefetch(upto):
        while issued[0] < min(upto, len(items)):
            kind, j = items[issued[0]]
            if kind == "g":
                buf, kb = wgu[cnt["g"] % 4]
                cnt["g"] += 1
                P.dma("sp", buf, C.WGU[idx][j].rearrange("p g kc m -> p g kc m"), [("WGU", idx, j)], [kb])
            else:
                buf, kb = wdb[cnt["d"] % 3]
                cnt["d"] += 1
                P.dma("sp", buf, C.WD[idx][j], [("WD", idx, j)], [kb])
            slot_of[issued[0]] = (buf, kb)
            issued[0] += 1

    def load_x(m):
        xt, kxt = xb[m % 2]
        P.dma("act", xt, C.xT[:, :, m * T:(m + 1) * T], [("xT", 2 * m), ("xT", 2 * m + 1)], [kxt])

    load_x(0)
    pos = 0
    for m in range(NM):
        xt, kxt = xb[m % 2]
        prefetch(pos + 3)
        norm_modulate(C, xt, kxt, h, kh, T, l, sub, (sq, ksq, rstd, krstd, tmp, None))
        if m + 1 < NM:
            load_x(m + 1)
        if m == DBG_M and idx == 0:
            dbg(C, 'rstd', rstd[:, 0:512], (krstd, 0), 512)
            dbg(C, 'h0', h[:, 0, 0:512], (kh, 0), 512)
            dbg(C, 'h7', h[:, 7, 0:512], (kh, 0), 512)
        for f in range(NF):
            prefetch(pos + 3)
            wbuf, kwb = slot_of.pop(pos)
            pos += 1
            for st in range(T // 512):
                sl = slice(st * 512, (st + 1) * 512)
                psg, kpsg = psum_bank(C)
                psu, kpsu = psum_bank(C)
                for kc in range(8):
                    P.mm(psg, wbuf[:, 0, kc, :], h[:, kc, sl], kc == 0, kc == 7, [kwb, (kh, st)], [kpsg])
                for kc in range(8):
                    P.mm(psu, wbuf[:, 1, kc, :], h[:, kc, sl], kc == 0, kc == 7, [kwb, (kh, st)], [kpsu])
                s_, ks_ = sg[(f * 2 + st) % 3]
                P.add("act", lambda e, s_=s_, psg=psg: e.activation(s_, psg, AF.Silu), [kpsg], [ks_])
                P.add("dve", lambda e, s_=s_, psu=psu, f=f, sl=sl: e.tensor_tensor(act[:, f, sl], s_, psu, ALU.mult),
                      [ks_, kpsu], [(kact, st)])
        for dc in range(NDC):
            prefetch(pos + 3)
            wbuf, kwb = slot_of.pop(pos)
            pos += 1
            for st in range(T // 512):
                sl = slice(st * 512, (st + 1) * 512)
                pso, kpso = psum_bank(C)
                for f in range(NF):
                    P.mm(pso, wbuf[:, f, :], act[:, f, sl], f == 0, f == NF - 1, [kwb, (kact, st)], [kpso])
                P.add("dve", lambda e, pso=pso, dc=dc, sl=sl, xt=xt: e.scalar_tensor_tensor(
                    xt[:, dc, sl], pso, mv["gate"][:, sub * 8 + dc:sub * 8 + dc + 1], xt[:, dc, sl],
                    ALU.mult, ALU.add), [kpso, mv["kgate"], kxt], [kxt])
        if m == DBG_M and idx == 0:
            dbg(C, 'act0', act[:, 0, 0:512], (kact, 0), 512)
            dbg(C, 'act21', act[:, 21, 0:512], (kact, 0), 512)
            dbg(C, 'xo0', xt[:, 0, 0:512], kxt, 512)
        P.dma("act", C.xT[:, :, m * T:(m + 1) * T], xt, [kxt], [("xT", 2 * m), ("xT", 2 * m + 1)])
    A.release(m0)


PI = float(np.pi)


def const_tile(C, name, val, dtype=F32, n=1):
    ap, k = C.arena.alloc("c_" + name, [n], dtype)
    C.P.add("pool", lambda e: e.memset(ap, val), [], [k])
    return ap, k


def load_cast_weight(C, name, src_ap, shape, eng="sp"):
    P, A = C.P, C.arena
    n = 1
    for s_ in shape:
        n *= s_
    wb, kwb = A.alloc(name, shape, BF16)
    m0 = A.mark()
    CH = 2048
    flat_b = wb
    stg = [A.alloc(f"{name}_st{i}", [CH], F32) for i in range(2)]
    A.release(m0)
    return wb, kwb, stg


def mla_phase(C):
    P, A = C.P, C.arena
    l = 1
    mv = C.modv[l]
    T = 512
    NT = S // T
    NB = S // 128
    SCALE = float(96 ** -0.5)
    m_phase = A.mark()
    ones384, k384 = const_tile(C, "o384", 1.0 / 384, BF16, 128)
    ones256, k256 = const_tile(C, "o256", 1.0 / 256, BF16, 128)
    ones96, k96 = const_tile(C, "o96", 1.0 / 96, BF16, 128)
    onesrow, krow = const_tile(C, "orow", 1.0, F32, 128)
    hpi, khpi = const_tile(C, "hpi", PI / 2)
    nhpi, knhpi = const_tile(C, "nhpi", -PI / 2)
    pmT_f, kpmf = A.alloc("pmT_f", [96], F32)
    pmT, kpm = A.alloc("pmT", [96], BF16)
    P.dma("sp", pmT_f[0:96, :], C.d_pm.ap(), [], [kpmf])
    P.add("dve", lambda e: e.tensor_copy(pmT[0:96, :], pmT_f[0:96, :]), [kpmf], [kpm])
    gq, kgq = C.vec["mla_qg"]
    gk, kgk = C.vec["mla_kg"]
    gql, kgql = C.vec["mla_qng"]
    gkvl, kgkvl = C.vec["mla_kvng"]
    invf, kinvf = C.vec["invf"]
    qn, kqn = A.alloc("qn", [3, S], BF16)
    kvn, kkvn = A.alloc("kvn", [2, S], BF16)
    kpe, kkpe = A.alloc("kpe", [S], F32)
    sqk, ksqk = A.alloc("sqk", [S], BF16)
    COS, kcos = A.alloc("COS", [S], F32)
    SIN, ksin = A.alloc("SIN", [S], F32)
    wuq, kwuq = A.alloc("wuq", [3, 1536], BF16)
    wukv, kwukv = A.alloc("wukv", [2, 2048], BF16)

    m0 = A.mark()
    posi, kposi = A.alloc("posi", [S], I32)
    ang, kang = A.alloc("ang", [S], F32)
    t1, kt1 = A.alloc("rt1", [S], F32)
    ti, kti = A.alloc("rti", [S], I32)
    P.dma("sp", posi, C.d_pos.ap(), [], [kposi])
    P.add("dve", lambda e: e.tensor_copy(ang, posi), [kposi], [kang])
    P.add("dve", lambda e: e.tensor_scalar(ang, ang, invf[:, 0:1], None, ALU.mult), [kang, kinvf], [kang])
    for (dst, kdst, shift) in ((SIN, ksin, 0.0), (COS, kcos, PI / 2)):
        P.add("dve", lambda e, shift=shift: e.tensor_scalar(t1, ang, shift, 1.0 / (2 * PI), ALU.add, ALU.mult),
              [kang], [kt1])
        P.add("dve", lambda e: e.tensor_copy(ti, t1), [kt1], [kti])
        P.add("dve", lambda e: e.tensor_copy(t1, ti), [kti], [kt1])
        P.add("dve", lambda e: e.scalar_tensor_tensor(t1, t1, -2 * PI, ang, ALU.mult, ALU.add), [kt1, kang], [kt1])
        bias_ap = nhpi if shift == 0.0 else None
        if shift == 0.0:
            P.add("act", lambda e: e.activation(t1, t1, AF.Abs, bias=nhpi[:, 0:1]), [kt1, knhpi], [kt1])
        else:
            P.add("act", lambda e: e.activation(t1, t1, AF.Abs), [kt1], [kt1])
        P.add("act", lambda e, dst=dst: e.activation(dst, t1, AF.Sin, bias=hpi[:, 0:1], scale=-1.0),
              [kt1, khpi], [kdst])
    A.release(m0)

    def load_cast(dst, kdst, src, nk, ncol, colchunk):
        stg = [A.alloc(f"wst{i}", [nk, colchunk], F32) for i in range(2)]
        it = 0
        for c0 in range(0, ncol, colchunk):
            cw = min(colchunk, ncol - c0)
            st, kst = stg[it % 2]
            it += 1
            P.dma("sp", st[:, :, 0:cw], src[:, :, c0:c0 + cw], [], [kst])
            emit_cast(P, cast_engine(C), dst[:, :, c0:c0 + cw], st[:, :, 0:cw], [kst], [kdst])

    m1 = A.mark()
    mm_ = A.mark()
    load_cast(wuq, kwuq, C.d_mla_wuq.ap().rearrange("(kc p) n -> p kc n", p=128), 3, 1536, 512)
    load_cast(wukv, kwukv, C.d_mla_wukv.ap().rearrange("(kc p) n -> p kc n", p=128), 2, 2048, 512)
    A.release(mm_)
    win, kwin = A.alloc("win", [8, 736], BF16)
    P.add("pool", lambda e: e.memset(win, 0.0), [], [kwin])
    wsrc = C.d_mla_win.ap().rearrange("(kc p) n -> p kc n", p=128)
    mm2 = A.mark()
    stg = [A.alloc(f"wst_in{i}", [8, 224], F32) for i in range(2)]
    for i, c0 in enumerate(range(0, 672, 224)):
        st, kst = stg[i % 2]
        P.dma("sp", st, wsrc[:, :, c0:c0 + 224], [], [kst])
        if c0 + 224 <= 640:
            emit_cast(P, cast_engine(C), win[:, :, c0:c0 + 224], st, [kst], [kwin])
        else:
            nl = 640 - c0
            emit_cast(P, cast_engine(C), win[:, :, c0:640], st[:, :, 0:nl], [kst], [kwin])
            emit_cast(P, cast_engine(C), win[:, :, 704:736], st[:, :, nl:nl + 32], [kst], [kwin])

    A.release(mm2)
    xb = [A.alloc(f"mx{i}", [NDC, T], F32) for i in range(2)]
    h, kh = A.alloc("mh", [NDC, T], BF16)
    sq, ksq = A.alloc("msq", [NDC, T], BF16)
    rstd, krstd = A.alloc("mrstd", [T], F32)
    tmp = [A.alloc(f"mtmp{i}", [512], F32) for i in range(3)]
    lsq, klsq = A.alloc("mlsq", [3, T], BF16)
    lrs, klrs = A.alloc("mlrs", [T], F32)

    def load_x(m):
        xt, kxt = xb[m % 2]
        P.dma("act", xt, C.xT[:, :, m * T:(m + 1) * T], [("xT", m)], [kxt])

    load_x(0)
    for m in range(NT):
        xt, kxt = xb[m % 2]
        sl = slice(m * T, (m + 1) * T)
        norm_modulate(C, xt, kxt, h, kh, T, l, 1, (sq, ksq, rstd, krstd, tmp, None))
        if m + 1 < NT:
            load_x(m + 1)
        for (c0, nch, ones, kones, dst, kdst, g) in ((0, 3, ones384, k384, qn, kqn, gql), (3, 2, ones256, k256, kvn, kkvn, gkvl)):
            banks = []
            for c in range(nch):
                ps, kps = psum_bank(C)
                banks.append((ps, kps))
                for kc in range(8):
                    P.mm(ps, win[:, kc, (c0 + c) * 128:(c0 + c + 1) * 128], h[:, kc, :], kc == 0, kc == 7,
                         [kwin, (kh, 0)], [kps])
                P.add("act", lambda e, ps=ps, c=c: e.activation(lsq[:, c, :], ps, AF.Square), [kps], [klsq])
            pss, kpss = psum_bank(C)
            for c in range(nch):
                P.mm(pss, ones, lsq[:, c, :], c == 0, c == nch - 1, [klsq, kones], [kpss])
            P.add("act", lambda e, pss=pss: e.activation(lrs, pss, AF.Ln, bias=C.eps_t[:, 0:1]), [kpss, C.k_eps], [klrs])
            P.add("act", lambda e: e.activation(lrs, lrs, AF.Exp, scale=-0.5), [klrs], [klrs])
            for c in range(nch):
                ps, kps = banks[c]
                tm, ktm = tmp[c % 3]
                P.add("dve", lambda e, tm=tm, ps=ps: e.tensor_tensor(tm, ps, lrs, ALU.mult), [kps, klrs], [ktm])
                P.add("act", lambda e, tm=tm, c=c, dst=dst, g=g, sl=sl: e.activation(
                    dst[:, c, sl], tm, AF.Identity, scale=g[:, c:c + 1]), [ktm], [kdst])
        ps, kps = psum_bank(C)
        for kc in range(8):
            P.mm(ps[0:96, :], win[:, kc, 640:736], h[:, kc, :], kc == 0, kc == 7, [kwin, (kh, 0)], [kps])
        P.add("act", lambda e, ps=ps, sl=sl: e.copy(kpe[64:96, sl], ps[64:96, :]), [kps], [kkpe])
        P.add("act", lambda e, ps=ps, sl=sl: e.activation(sqk[64:96, sl], ps[64:96, :], AF.Square), [kps], [(ksqk, "pe")])
    A.release(m1)

    kT = [A.alloc(f"kT{i}", [S], BF16) for i in range(2)]
    Vau = [A.alloc(f"Vau{i}", [NB, 128], BF16) for i in range(2)]
    OTs = [A.alloc(f"OTs{i}", [S], BF16) for i in range(2)]
    for par in range(2):
        va, kva = Vau[par]
        P.add("pool", lambda e, va=va: e.memset(va, 1.0), [], [kva])
    rsk, krsk = A.alloc("rsk", [T], F32)
    rt1 = [A.alloc(f"rp1_{i}", [T], F32) for i in range(2)]
    rt2 = [A.alloc(f"rp2_{i}", [T], F32) for i in range(2)]
    qsq, kqsq = A.alloc("qsq", [T], BF16)
    qT = [A.alloc(f"qT{i}", [T], BF16) for i in range(2)]
    PT = [A.alloc(f"PT{i}", [T], BF16) for i in range(4)]
    osb, kosb = A.alloc("osb", [T], F32)
    rl, krl = A.alloc("rl", [T], F32)
    pti = 0

    C.held = []

    def bank_excl(excl):
        while True:
            ps, kps = psum_bank(C)
            if all(ps is not x for x in excl) and all(ps is not x for x in C.held):
                return ps, kps

    def norm_rope(src_ps, ksrc_ps, src_sb, ksrc_sb, sqt, ksqt_keys, g, kg, dstT, kdstT, sl, tl):
        yield
        pss, kpss = bank_excl([])
        P.mm(pss[0:96, :], ones96[0:96, 0:96], sqt, True, True, ksqt_keys + [k96], [kpss])
        P.add("act", lambda e: e.activation(rsk[0:96, :], pss[0:96, :], AF.Ln, bias=C.eps_t[0:96, 0:1]),
              [kpss, C.k_eps], [krsk])
        P.add("act", lambda e: e.activation(rsk[0:96, :], rsk[0:96, :], AF.Exp, scale=-0.5), [krsk], [krsk])
        if src_sb is None:
            P.add("dve", lambda e: e.scalar_tensor_tensor(dstT[0:96, sl], src_ps[0:96, :], g[0:96, 0:1], rsk[0:96, :],
                                                          ALU.mult, ALU.mult), [ksrc_ps, kg, krsk], [kdstT])
        else:
            P.add("dve", lambda e: e.scalar_tensor_tensor(dstT[0:64, sl], src_ps[0:64, :], g[0:64, 0:1], rsk[0:64, :],
                                                          ALU.mult, ALU.mult), [ksrc_ps, kg, krsk], [kdstT])
            P.add("dve", lambda e: e.scalar_tensor_tensor(dstT[64:96, sl], src_sb[64:96, tl], g[64:96, 0:1],
                                                          rsk[64:96, :], ALU.mult, ALU.mult), [ksrc_sb, kg, krsk], [kdstT])
        C.held[:] = [x for x in C.held if x is not src_ps]
        yield
        psr, kpsr = bank_excl([])
        P.mm(psr[0:96, :], pmT[0:96, 0:96], dstT[0:96, sl], True, True, [kdstT, kpm], [kpsr])
        a1, ka1 = rt1[C.ps_i % 2]
        a2, ka2 = rt2[C.ps_i % 2]
        P.add("dve", lambda e: e.tensor_tensor(a1[64:96, :], dstT[64:96, sl], COS[64:96, tl], ALU.mult),
              [kdstT, kcos], [ka1])
        P.add("dve", lambda e: e.tensor_tensor(a2[64:96, :], psr[64:96, :], SIN[64:96, tl], ALU.mult),
              [kpsr, ksin], [ka2])
        P.add("dve", lambda e: e.tensor_tensor(dstT[64:96, sl], a1[64:96, :], a2[64:96, :], ALU.add),
              [ka1, ka2], [kdstT])

    LA = 3
    PTn = [A.alloc(f"PTn{i}", [T], BF16) for i in range(LA + 3)]

    def prepK(hd, m):
        kt_, kkt = kT[hd % 2]
        tl = slice(m * T, (m + 1) * T)
        ps, kps = bank_excl([])
        C.held.append(ps)
        for kc in range(2):
            P.mm(ps[0:64, :], wukv[:, kc, hd * 128:hd * 128 + 64], kvn[:, kc, tl], kc == 0, kc == 1,
                 [kwukv, kkvn], [kps])
        P.add("act", lambda e, ps=ps, tl=tl: e.activation(sqk[0:64, tl], ps[0:64, :], AF.Square),
              [kps], [(ksqk, "n", m)])
        yield from norm_rope(ps, kps, kpe, kkpe, sqk[0:96, tl], [(ksqk, "n", m), (ksqk, "pe")], gk, kgk, kt_, kkt, tl, tl)

    def prepV(hd, b0):
        va, kva = Vau[hd % 2]
        voff = 0 if hd % 2 == 0 else 64
        ps, kps = bank_excl([])
        for j in range(8):
            blk = b0 + j
            for kc in range(2):
                P.mm(ps[:, j * 64:(j + 1) * 64], kvn[:, kc, blk * 128:(blk + 1) * 128],
                     wukv[:, kc, hd * 128 + 64:hd * 128 + 128], kc == 0, kc == 1, [kwukv, kkvn], [kps])
        P.add("dve", lambda e, ps=ps, b0=b0, va=va, voff=voff: e.tensor_copy(
            va[:, b0:b0 + 8, voff:voff + 64], ps.rearrange("p (j v) -> p j v", j=8)), [kps], [kva])
        yield

    qcount = [0]

    def prepQ(hd, m, q_, kq_):
        tl = slice(m * T, (m + 1) * T)
        ps, kps = bank_excl([])
        C.held.append(ps)
        for kc in range(3):
            P.mm(ps[0:96, :], wuq[:, kc, hd * 96:(hd + 1) * 96], qn[:, kc, tl], kc == 0, kc == 2,
                 [kwuq, kqn], [kps])
        P.add("act", lambda e, ps=ps: e.activation(qsq[0:96, :], ps[0:96, :], AF.Square), [kps], [kqsq])
        yield from norm_rope(ps, kps, None, None, qsq[0:96, :], [kqsq], gq, kgq, q_, kq_, slice(0, T), tl)

    def run_all(gen):
        for _ in gen:
            pass

    def next_q():
        q = qT[qcount[0] % 2]
        qcount[0] += 1
        return q

    for m in range(NT):
        run_all(prepK(0, m))
    for b0 in range(0, NB, 8):
        run_all(prepV(0, b0))
    nextq = next_q()
    run_all(prepQ(0, 0, nextq[0], nextq[1]))
    vgroups = list(range(0, NB, 8))
    for hd in range(16):
        par = hd % 2
        kt_, kkt = kT[par]
        va, kva = Vau[par]
        ots, kots = OTs[(hd // 2) % 2]
        oh = 0 if par == 0 else 64
        lp = 64 if par == 0 else 0
        for m in range(NT):
            tl = slice(m * T, (m + 1) * T)
            q_, kq_ = nextq
            gens = []
            if m + 1 < NT:
                nextq = next_q()
                gens.append(prepQ(hd, m + 1, nextq[0], nextq[1]))
            elif hd + 1 < 16:
                nextq = next_q()
                gens.append(prepQ(hd + 1, 0, nextq[0], nextq[1]))
            if hd + 1 < 16:
                gens.append(prepK(hd + 1, m))
                if m < len(vgroups):
                    gens.append(prepV(hd + 1, vgroups[m]))
            pso, kpso = bank_excl([])
            C.held.append(pso)
            pend = []
            for kb in range(NB + LA):
                if kb < NB:
                    pss, kpss = bank_excl([])
                    P.mm(pss, kt_[0:96, kb * 128:(kb + 1) * 128], q_[0:96, :], True, True, [kkt, kq_], [kpss])
                    pt, kpt = PTn[pti % len(PTn)]
                    pti += 1
                    P.add("act", lambda e, pt=pt, pss=pss: e.activation(pt, pss, AF.Exp, scale=SCALE), [kpss], [kpt])
                    pend.append((kb, pt, kpt))
                if kb >= LA:
                    kb2, pt, kpt = pend.pop(0)
                    P.mm(pso, va[:, kb2, :], pt, kb2 == 0, kb2 == NB - 1, [kva, kpt], [kpso])
                if kb % 6 == 2 and gens:
                    gi = (kb // 6) % len(gens)
                    for g_ in list(gens):
                        try:
                            next(g_)
                        except StopIteration:
                            gens.remove(g_)
            for g_ in gens:
                run_all(g_)
            P.add("dve", lambda e, pso=pso, lp=lp: e.reciprocal(rl[lp:lp + 1, :], pso[lp:lp + 1, :]), [kpso], [krl])
            P.add("act", lambda e, pso=pso, oh=oh: e.copy(osb[oh:oh + 64, :], pso[oh:oh + 64, :]), [kpso], [kosb])
            psb, kpsb = bank_excl([])
            P.mm(psb, onesrow[lp:lp + 1, :], rl[lp:lp + 1, :], True, True, [krl, krow], [kpsb])
            P.add("dve", lambda e, psb=psb, oh=oh, ots=ots, tl=tl: e.tensor_tensor(
                ots[oh:oh + 64, tl], osb[oh:oh + 64, :], psb[oh:oh + 64, :], ALU.mult), [kosb, kpsb], [kots])
            C.held[:] = [x for x in C.held if x is not pso]
        if par == 1:
            c = hd // 2
            P.dma("sp", C.OT[c], ots, [kots], [("OT", c)])
    A.release(m_phase)

    m3 = A.mark()
    wout, kwout = A.alloc("wout", [8, 1024], BF16)
    stg = [A.alloc(f"wst_o{i}", [8, 256], F32) for i in range(2)]
    wsrc = C.d_mla_wout.ap().rearrange("(kc p) n -> p kc n", p=128)
    for i, c0 in enumerate(range(0, 1024, 256)):
        st, kst = stg[i % 2]
        P.dma("sp", st, wsrc[:, :, c0:c0 + 256], [], [kst])
        emit_cast(P, cast_engine(C), wout[:, :, c0:c0 + 256], st, [kst], [kwout])
    xb = [A.alloc(f"ox{i}", [NDC, T], F32) for i in range(2)]
    ob = [A.alloc(f"oo{i}", [NDC, T], BF16) for i in range(2)]
    for m in range(NT):
        xt, kxt = xb[m % 2]
        ot, kot = ob[m % 2]
        tl = slice(m * T, (m + 1) * T)
        P.dma("act", xt, C.xT[:, :, tl], [("xT", m)], [kxt])
        P.dma("sp", ot, C.OT.rearrange("c p s -> p c s")[:, :, tl], [("OT", c) for c in range(8)], [kot])
        for dc in range(NDC):
            ps, kps = psum_bank(C)
            for kc in range(8):
                P.mm(ps, wout[:, kc, dc * 128:(dc + 1) * 128], ot[:, kc, :], kc == 0, kc == 7, [kwout, kot], [kps])
            P.add("dve", lambda e, ps=ps, dc=dc, xt=xt: e.scalar_tensor_tensor(
                xt[:, dc, :], ps, mv["gate"][:, 8 + dc:8 + dc + 1], xt[:, dc, :], ALU.mult, ALU.add),
                [kps, mv["kgate"], kxt], [kxt])
        P.dma("act", C.xT[:, :, tl], xt, [kxt], [("xT", m)])
    A.release(m3)


def bc_last(ap, n):
    return ap.unsqueeze(2).to_broadcast([ap.shape[0], ap.shape[1], n])


def bc_mid(ap, n):
    return ap.unsqueeze(1).to_broadcast([ap.shape[0], n, ap.shape[1]])


def ssd_phase(C):
    P, A = C.P, C.arena
    l = 0
    mv = C.modv[l]
    NB = S // 128
    T = 512
    NT = S // T
    m_phase = A.mark()
    one_t, kone = const_tile(C, "one", 1.0)
    masks, kmask = A.alloc("masks", [6, 128], F32)
    P.dma("sp", masks, C.d_masks.ap().rearrange("p (a b) -> p a b", a=6), [], [kmask])
    ones_f, konesf = const_tile(C, "ones_f", 1.0, F32, 128)
    cw, kcw = C.vec["ssd_cw"]
    cb_, kcb = C.vec["ssd_cb"]
    dtb, kdtb = C.vec["ssd_dtb"]
    alog, kalog = C.vec["ssd_alog"]
    dsk, kdsk = C.vec["ssd_dskip"]
    Aneg, kAneg = A.alloc("Aneg", [64], F32)
    P.add("act", lambda e: e.activation(Aneg, alog, AF.Exp), [kalog], [kAneg])
    P.add("dve", lambda e: e.tensor_scalar(Aneg, Aneg, -1.0, None, ALU.mult), [kAneg], [kAneg])
    h_all, khall = A.alloc("h_all", [NDC, S], BF16)

    m0 = A.mark()
    xb = [A.alloc(f"sx{i}", [NDC, T], F32) for i in range(2)]
    sq, ksq = A.alloc("ssq", [NDC, T], BF16)
    rstd, krstd = A.alloc("srstd", [T], F32)
    tmp = [A.alloc(f"stmp{i}", [512], F32) for i in range(3)]
    for m in range(NT):
        xt, kxt = xb[m % 2]
        P.dma("act", xt, C.xT[:, :, m * T:(m + 1) * T], [("xT", m)], [kxt])
        norm_modulate(C, xt, kxt, h_all[:, :, m * T:(m + 1) * T], (khall, m), T, l, 1, (sq, ksq, rstd, krstd, tmp, None))
    A.release(m0)
    hkeys = [((khall, m), 0) for m in range(NT)]

    m0 = A.mark()
    wsrc = C.d_ssd_win.ap().rearrange("(kc p) n -> p kc n", p=128)
    wst = [A.alloc(f"swst{i}", [8, 128], F32) for i in range(2)]
    wcb = [A.alloc(f"swc{i}", [8, 128], BF16) for i in range(2)]
    pre = [A.alloc(f"spre{i}", [S + 4], F32) for i in range(2)]
    acc = [A.alloc(f"sacc{i}", [S], F32) for i in range(2)]
    xo = [A.alloc(f"sxo{i}", [S], BF16) for i in range(2)]
    for i in range(2):
        pr, kpr = pre[i]
        P.add("pool", lambda e, pr=pr: e.memset(pr, 0.0), [], [kpr])
    for c in range(32):
        ws, kws = wst[c % 2]
        wc, kwc = wcb[c % 2]
        pr, kpr = pre[c % 2]
        ac, kac = acc[c % 2]
        xo_, kxo = xo[c % 2]
        P.dma("sp", ws, wsrc[:, :, 2048 + c * 128:2048 + (c + 1) * 128], [], [kws])
        P.add("pool", lambda e, wc=wc, ws=ws: e.tensor_copy(wc, ws), [kws], [kwc])
        for m in range(NT):
            ps, kps = psum_bank(C)
            for kc in range(8):
                P.mm(ps, wc[:, kc, :], h_all[:, kc, m * T:(m + 1) * T], kc == 0, kc == 7, [kwc, hkeys[m]], [kps])
            P.add("act", lambda e, ps=ps, pr=pr, m=m: e.copy(pr[:, 2 + m * T:2 + (m + 1) * T], ps), [kps], [kpr])
        for hf in range(2):
            o0 = hf * (S // 2)
            n_ = S // 2
            P.add("dve", lambda e, ac=ac, pr=pr, c=c, o0=o0, n_=n_: e.tensor_scalar(
                ac[:, o0:o0 + n_], pr[:, o0:o0 + n_], cw[:, c * 5:c * 5 + 1], cb_[:, c:c + 1], ALU.mult, ALU.add),
                [kpr, kcw, kcb], [(kac, hf)])
            for j in range(1, 5):
                P.add("dve", lambda e, ac=ac, pr=pr, c=c, j=j, o0=o0, n_=n_: e.scalar_tensor_tensor(
                    ac[:, o0:o0 + n_], pr[:, o0 + j:o0 + j + n_], cw[:, c * 5 + j:c * 5 + j + 1], ac[:, o0:o0 + n_],
                    ALU.mult, ALU.add), [kpr, kcw, (kac, hf)], [(kac, hf)])
            P.add("act", lambda e, xo_=xo_, ac=ac, o0=o0, n_=n_: e.activation(xo_[:, o0:o0 + n_], ac[:, o0:o0 + n_], AF.Silu),
                  [(kac, hf)], [(kxo, hf)])
        P.dma("sp", C.XBC[c], xo_, [(kxo, 0), (kxo, 1)], [("XBC", c)])
    A.release(m0)

    wdt, kwdt = A.alloc("wdt", [8, 64], BF16)
    m_big = A.mark()
    wz, kwz = A.alloc("wz", [8, 2048], BF16)
    m0 = A.mark()
    stg = [A.alloc(f"szst{i}", [8, 256], F32) for i in range(2)]
    it = 0
    for c0 in range(0, 2048, 256):
        st, kst = stg[it % 2]
        it += 1
        P.dma("sp", st, wsrc[:, :, c0:c0 + 256], [], [kst])
        emit_cast(P, cast_engine(C), wz[:, :, c0:c0 + 256], st, [kst], [kwz])
    st, kst = stg[it % 2]
    it += 1
    P.dma("sp", st[:, :, 0:64], wsrc[:, :, 6144:6208], [], [kst])
    emit_cast(P, cast_engine(C), wdt, st[:, :, 0:64], [kst], [kwdt])
    A.release(m0)
    ng16, kng16 = C.vec["ssd_ng16"]

    xbcT = [A.alloc(f"xbcT{i}", [32, 128], BF16) for i in range(2)]
    steps = [(ck, 1) for ck in range(NB - 1, -1, -1)] + [(ck, 0) for ck in range(NB)]

    def load_xbc(i):
        ck_ = steps[i][0]
        xb_, kxb = xbcT[i % 2]
        P.dma("sp", xb_, C.XBC.rearrange("c p s -> p c s")[:, :, ck_ * 128:(ck_ + 1) * 128],
              [("XBC", c) for c in range(32)], [kxb])
    xs_tm, kxs = A.alloc("xs_tm", [32, 64], BF16)
    B_tm, kbt = A.alloc("B_tm", [8, 128], BF16)
    dt_, kdt = A.alloc("dt", [64], F32)
    a_, ka = A.alloc("a", [64], F32)
    dec, kdec = A.alloc("dec", [96], F32)
    dt2, kdt2 = A.alloc("dt2", [32], F32)
    xdt, kxdt = A.alloc("xdt", [32, 64], BF16)
    xdtE, kxdtE = A.alloc("xdtE", [32, 64], BF16)
    cbm, kcbm = A.alloc("cbm", [8, 128], F32)
    Lb = [A.alloc(f"Lb{i}", [4, 128], F32) for i in range(2)]
    ex = [A.alloc(f"ex{i}", [4, 128], F32) for i in range(2)]
    MT = [A.alloc(f"MT{i}", [4, 128], BF16) for i in range(2)]
    yo = [A.alloc(f"yo{i}", [256], F32) for i in range(2)]
    ydir, kydir = A.alloc("ydir", [2048], F32)
    H, kH = A.alloc("H", [2048], F32)
    Hb, kHb = A.alloc("Hb", [2048], BF16)
    yb_in, kybin = A.alloc("yb_in", [2048], F32)
    sz, ksz = A.alloc("sz", [2048], F32)
    gss, kgss = A.alloc("gss", [8], F32)
    ynb, kynb = A.alloc("ynb", [2048], BF16)
    yT, kyT = A.alloc("yT", [16, 128], BF16)
    xck, kxck = A.alloc("xck", [NDC, 128], F32)

    def chunk_step(ck, d, it_):
        tk = slice(ck * 128, (ck + 1) * 128)
        xb_, kxb = xbcT[it_ % 2]
        if it_ + 1 < len(steps):
            load_xbc(it_ + 1)
        Lm = masks[:, 0 + 2 * d, :]
        Rm = masks[:, 1 + 2 * d, :]
        Vm = masks[:, 4 + d, :]
        dc0 = d * 32
        for q in range(3):
            ps, kps = psum_bank(C)
            psb = ps.bitcast(BF16)
            for j in range(8):
                c = q * 8 + j
                P.add("pe", lambda e, psb=psb, j=j, c=c, xb_=xb_: e.transpose(
                    psb[:, j * 128:(j + 1) * 128], xb_[:, c, :], C.ident_b), [kxb, C.k_ident_b], [kps])
            if q < 2:
                P.add("act", lambda e, psb=psb, q=q: e.copy(
                    xs_tm.rearrange("p a b -> p (a b)")[:, q * 1024:(q + 1) * 1024], psb), [kps], [kxs])
            else:
                P.add("dve", lambda e, psb=psb: e.tensor_copy(B_tm.rearrange("p a b -> p (a b)"), psb), [kps], [kbt])
        ps, kps = psum_bank(C)
        for kc in range(8):
            P.mm(ps[:, 0:64], h_all[:, kc, tk], wdt[:, kc, :], kc == 0, kc == 7, [hkeys[ck // 4], kwdt], [kps])
        P.add("dve", lambda e, ps=ps: e.tensor_tensor(dt_, ps[:, 0:64], dtb, ALU.add), [kps, kdtb], [kdt])
        P.add("act", lambda e: e.activation(dt_, dt_, AF.Exp), [kdt], [kdt])
        P.add("act", lambda e: e.activation(dt_, dt_, AF.Ln, bias=one_t[:, 0:1]), [kdt, kone], [kdt])
        P.add("dve", lambda e: e.tensor_tensor(a_, dt_, Aneg, ALU.mult), [kdt, kAneg], [ka])
        ps, kps = psum_bank(C)
        P.mm(ps[:, 0:32], Rm, a_[:, dc0:dc0 + 32], True, True, [kmask, ka], [kps])
        P.mm(ps[:, 32:64], Lm, a_[:, dc0:dc0 + 32], True, True, [kmask, ka], [kps])
        P.mm(ps[:, 64:96], ones_f, a_[:, dc0:dc0 + 32], True, True, [konesf, ka], [kps])
        P.add("act", lambda e, ps=ps: e.activation(dec, ps[:, 0:96], AF.Exp), [kps], [kdec])
        P.add("dve", lambda e: e.tensor_tensor(dt2, dt_[:, dc0:dc0 + 32], dec[:, 32:64], ALU.mult), [kdt, kdec], [kdt2])
        P.add("dve", lambda e: e.tensor_tensor(xdt, xs_tm, bc_last(dt_[:, dc0:dc0 + 32], 64), ALU.mult),
              [kxs, kdt], [kxdt])
        P.add("dve", lambda e: e.tensor_tensor(xdtE, xs_tm, bc_last(dt2, 64), ALU.mult), [kxs, kdt2], [kxdtE])
        for half in range(2):
            ps, kps = psum_bank(C)
            for j in range(4):
                g = half * 4 + j
                P.mm(ps[:, j * 128:(j + 1) * 128], xb_[:, 16 + g, :], xb_[:, 24 + g, :], True, True, [kxb], [kps])
            P.add("dve", lambda e, ps=ps, half=half: e.tensor_tensor(
                cbm[:, half * 4:(half + 1) * 4, :], ps.rearrange("p (a b) -> p a b", a=4), bc_mid(Vm, 4), ALU.mult),
                [kps, kmask], [kcbm])
        for g in range(8):
            lb, klb = Lb[g % 2]
            ex_, kex = ex[g % 2]
            mt, kmt = MT[g % 2]
            yo_, kyo = yo[g % 2]
            P.add("dve", lambda e, lb=lb, g=g: e.tensor_tensor(
                lb, bc_mid(Lm, 4), bc_last(a_[:, dc0 + g * 4:dc0 + g * 4 + 4], 128), ALU.mult), [kmask, ka], [klb])
            ps, kps = psum_bank(C)
            for r in range(4):
                P.mm(ps[:, r * 128:(r + 1) * 128], lb[:, r, :], Rm, True, True, [klb, kmask], [kps])
            P.add("act", lambda e, ps=ps, ex_=ex_: e.activation(ex_.rearrange("p a b -> p (a b)"), ps, AF.Exp), [kps], [kex])
            P.add("dve", lambda e, mt=mt, ex_=ex_, g=g: e.tensor_tensor(mt, ex_, bc_mid(cbm[:, g, :], 4), ALU.mult),
                  [kex, kcbm], [kmt])
            psy, kpsy = psum_bank(C)
            for r in range(4):
                P.mm(psy[:, r * 64:(r + 1) * 64], mt[:, r, :], xdt[:, g * 4 + r, :], True, True, [kmt, kxdt], [kpsy])
            P.mm(psy[:, 256:512], xb_[:, 24 + g, :], Hb[:, g * 256:(g + 1) * 256], True, True, [kxb, kHb], [kpsy])
            P.add("dve", lambda e, psy=psy, yo_=yo_, g=g: e.tensor_tensor(
                yo_.rearrange("p (a b) -> p a b", a=4), psy[:, 256:512].rearrange("p (a b) -> p a b", a=4),
                bc_last(dec[:, g * 4:g * 4 + 4], 64), ALU.mult), [kpsy, kdec], [kyo])
            P.add("dve", lambda e, psy=psy, yo_=yo_, g=g: e.tensor_tensor(
                ydir[:, g * 256:(g + 1) * 256], psy[:, 0:256], yo_, ALU.add), [kpsy, kyo], [(kydir, g)])
            pss, kpss = psum_bank(C)
            P.mm(pss[:, 0:256], B_tm[:, g, :], xdtE[:, g * 4:g * 4 + 4, :].rearrange("p a b -> p (a b)"), True, True,
                 [kbt, kxdtE], [kpss])
            P.add("dve", lambda e, g=g: e.tensor_tensor(
                H[:, g * 256:(g + 1) * 256].rearrange("p (a b) -> p a b", a=4),
                H[:, g * 256:(g + 1) * 256].rearrange("p (a b) -> p a b", a=4),
                bc_last(dec[:, 64 + g * 4:64 + g * 4 + 4], 64), ALU.mult), [(kH, g), kdec], [(kH, g)])
            P.add("dve", lambda e, pss=pss, g=g: e.tensor_tensor(
                H[:, g * 256:(g + 1) * 256], H[:, g * 256:(g + 1) * 256], pss[:, 0:256], ALU.add),
                [(kH, g), kpss], [(kH, g)])
            P.add("act", lambda e, g=g: e.copy(Hb[:, g * 256:(g + 1) * 256], H[:, g * 256:(g + 1) * 256]),
                  [(kH, g)], [kHb])

    ykeys = [(kydir, g) for g in range(8)]
    P.add("pool", lambda e: e.memset(H, 0.0), [], [(kH, g) for g in range(8)])
    P.add("pool", lambda e: e.memset(Hb, 0.0), [], [kHb])
    it_ = 0
    load_xbc(0)
    for ck in range(NB - 1, -1, -1):
        chunk_step(ck, 1, it_)
        it_ += 1
        P.dma("sp", C.YB[ck], ydir, ykeys, [("YB", ck)])
        tk = slice(ck * 128, (ck + 1) * 128)
        for zc in range(4):
            ps, kps = psum_bank(C)
            for kc in range(8):
                P.mm(ps, h_all[:, kc, tk], wz[:, kc, zc * 512:(zc + 1) * 512], kc == 0, kc == 7,
                     [hkeys[ck // 4], kwz], [kps])
            P.add("act", lambda e, ps=ps, zc=zc: e.activation(sz[:, zc * 512:(zc + 1) * 512], ps, AF.Silu), [kps], [ksz])
        P.dma("sp", C.SZ[ck], sz, [ksz], [("SZ", ck)])
    wout = wz.rearrange("p a b -> p (a b)").rearrange("p (k n) -> p k n", k=16)
    kwout = kwz
    stg = [(yb_in.rearrange("p (k n) -> p k n", k=2), kybin), (sz.rearrange("p (k n) -> p k n", k=2), ksz)]
    osrc = C.d_ssd_wout.ap().rearrange("(kc p) n -> p kc n", p=128)
    for i_, c0 in enumerate(range(0, 16, 2)):
        st, kst = stg[i_ % 2]
        P.dma("sp", st, osrc[:, c0:c0 + 2, :], [], [kst])
        emit_cast(P, cast_engine(C), wout[:, c0:c0 + 2, :], st, [kst], [kwout])
    P.add("pool", lambda e: e.memset(H, 0.0), [], [(kH, g) for g in range(8)])
    P.add("pool", lambda e: e.memset(Hb, 0.0), [], [kHb])
    for ck in range(NB):
        tk = slice(ck * 128, (ck + 1) * 128)
        P.dma("act", yb_in, C.YB[ck], [("YB", ck)], [kybin])
        P.dma("act", xck, C.xT[:, :, tk], [("xT", ck // 4)], [kxck])
        chunk_step(ck, 0, it_)
        it_ += 1
        P.add("dve", lambda e: e.tensor_tensor(ydir, ydir, yb_in, ALU.add), ykeys + [kybin], ykeys)
        P.add("dve", lambda e: e.tensor_tensor(yb_in.rearrange("p (a b) -> p a b", a=32), xs_tm, bc_last(dsk, 64), ALU.mult),
              [kxs, kdsk], [kybin])
        P.add("dve", lambda e: e.tensor_tensor(ydir, ydir, yb_in, ALU.add), ykeys + [kybin], ykeys)
        P.dma("sp", sz, C.SZ[ck], [("SZ", ck)], [ksz])
        P.add("dve", lambda e: e.tensor_tensor(ydir, ydir, sz, ALU.mult), ykeys + [ksz], ykeys)
        P.add("act", lambda e: e.activation(sz, ydir, AF.Square), ykeys, [ksz])
        P.add("dve", lambda e: e.reduce_sum(gss, sz.rearrange("p (a b) -> p a b", a=8), AX.X), [ksz], [kgss])
        P.add("act", lambda e: e.activation(gss, gss, AF.Ln, bias=C.eps_t[:, 0:1], scale=1.0 / 256), [kgss, C.k_eps], [kgss])
        P.add("act", lambda e: e.activation(gss, gss, AF.Exp, scale=-0.5), [kgss], [kgss])
        P.add("dve", lambda e: e.tensor_tensor(ydir.rearrange("p (a b) -> p a b", a=8), ydir.rearrange("p (a b) -> p a b", a=8),
                                               bc_last(gss, 256), ALU.mult), ykeys + [kgss], ykeys)
        P.add("act", lambda e: e.copy(ynb, ydir), ykeys, [kynb])
        for q in range(2):
            ps, kps = psum_bank(C)
            psb = ps.bitcast(BF16)
            for j in range(8):
                c = q * 8 + j
                P.add("pe", lambda e, psb=psb, j=j, c=c: e.transpose(
                    psb[:, j * 128:(j + 1) * 128], ynb[:, c * 128:(c + 1) * 128], C.ident_b), [kynb, C.k_ident_b], [kps])
            for j in range(8):
                c = q * 8 + j
                P.add("act", lambda e, psb=psb, j=j, c=c: e.activation(
                    yT[:, c, :], psb[:, j * 128:(j + 1) * 128], AF.Identity, scale=ng16[:, c:c + 1]),
                    [kps, kng16], [kyT])
        for half in range(2):
            ps, kps = psum_bank(C)
            for j in range(4):
                dc = half * 4 + j
                for kc in range(16):
                    P.mm(ps[:, j * 128:(j + 1) * 128], wout[:, kc, dc * 128:(dc + 1) * 128], yT[:, kc, :],
                         kc == 0, kc == 15, [kwout, kyT], [kps])
            for j in range(4):
                dc = half * 4 + j
                P.add("dve", lambda e, ps=ps, j=j, dc=dc: e.scalar_tensor_tensor(
                    xck[:, dc, :], ps[:, j * 128:(j + 1) * 128], mv["gate"][:, 8 + dc:8 + dc + 1], xck[:, dc, :],
                    ALU.mult, ALU.add), [kps, mv["kgate"], kxck], [kxck])
        P.dma("act", C.xT[:, :, tk], xck, [kxck], [("xT", ck // 4)])
    A.release(m_phase)

def build_program(stages, seq=4096, debug=False):
    global S
    S = seq
    nc = bass.Bass("TRN2", target_bir_lowering=False)
    C = Ctx()
    C.debug = debug
    C.dbg_off = 0
    C.dbg_map = {}
    C.dbg_keys = []
    C.nc = nc
    C.P = Prog(nc)
    C.ps_i = 0
    C.bar_gidx = 0
    C.cast_i = 0
    dt = nc.dram_tensor
    C.d_x = dt("x", [S, D], F32, kind="ExternalInput")
    C.d_out = dt("out", [S, D], F32, kind="ExternalOutput")
    C.d_ident = dt("ident", [128, 128], F32, kind="ExternalInput")
    if debug:
        C.d_dbg = dt("dbg", [128, 8192], F32, kind="ExternalOutput")
    C.d_w_mod = dt("w_mod", [2, D, 9 * D], F32, kind="ExternalInput")
    C.d_wg = dt("ffn_w_gate", [2, 2, D, DFF], F32, kind="ExternalInput")
    C.d_wu = dt("ffn_w_up", [2, 2, D, DFF], F32, kind="ExternalInput")
    C.d_wd = dt("ffn_w_down", [2, 2, DFF, D], F32, kind="ExternalInput")
    C.d_pm = dt("pmT", [96, 96], F32, kind="ExternalInput")
    C.d_pos = dt("pos", [128, S], I32, kind="ExternalInput")
    C.d_mla_win = dt("mla_w_in", [D, 672], F32, kind="ExternalInput")
    C.d_mla_wuq = dt("mla_w_uq", [384, 1536], F32, kind="ExternalInput")
    C.d_mla_wukv = dt("mla_w_ukv", [256, 2048], F32, kind="ExternalInput")
    C.d_mla_wout = dt("mla_w_out", [D, D], F32, kind="ExternalInput")
    C.OT = dt("OT_scr", [8, 128, S], BF16, kind="Internal").ap()
    C.d_masks = dt("masks", [128, 768], F32, kind="ExternalInput")
    C.d_ssd_win = dt("ssd_w_in", [D, 6208], F32, kind="ExternalInput")
    C.d_ssd_wout = dt("ssd_w_out", [2048, D], F32, kind="ExternalInput")
    C.SZ = dt("SZ_scr", [S // 128, 128, 2048], F32, kind="Internal").ap()
    C.XBC = dt("XBC_scr", [32, 128, S], BF16, kind="Internal").ap()
    C.YB = dt("YB_scr", [S // 128, 128, 2048], F32, kind="Internal").ap()
    C.d_vecs = {}
    for name, n in VEC_SPECS:
        C.d_vecs[name] = dt("v_" + name, [128, n], F32, kind="ExternalInput")
    C.xT = dt("xT_scr", [128, NDC, S], F32, kind="Internal").ap()
    C.WGU = [dt(f"wgu_scr{i}", [NF, 128, 2, 8, 128], BF16, kind="Internal").ap() for i in range(4)]
    C.WD = [dt(f"wd_scr{i}", [NDC, 128, NF, 128], BF16, kind="Internal").ap() for i in range(4)]

    ARENA_BYTES = 207 * 1024
    with ExitStack() as es:
        ah = es.enter_context(nc.sbuf_tensor("arena", [128, ARENA_BYTES // 4], F32))
        C.arena = Arena(ah, ARENA_BYTES, C.P)
        C.ps = [es.enter_context(nc.psum_tensor(f"ps{i}", [128, 512], F32))[:] for i in range(8)]
        eng_sems = {e: es.enter_context(nc.semaphore(f"sem_{e}")) for e in ENGS}
        dma_sems = [es.enter_context(nc.semaphore(f"dsem{i}")) for i in range(N_DMA_SEMS)]

        setup_consts(C)
        if debug:
            C.dbg_stage, _ = C.arena.alloc('dbg_stage', [3584], F32)
        load_transpose_x(C)
        compute_mod(C)
        for (l, w) in stages.get("ffn", []):
            convert_ffn_weights(C, l, w)
        for st_ in stages.get("order", []):
            if BARRIERS:
                phase_barrier(C)
            if st_[0] == "ffn":
                ffn_phase(C, st_[1], st_[2])
            elif st_[0] == "mla":
                mla_phase(C)
            elif st_[0] == "ssd":
                ssd_phase(C)
        store_transpose_out(C)
        C.P.emit(eng_sems, dma_sems)
    C.nc = nc
    return C


VEC_SPECS = [("c", 8), ("b_mod0", 72), ("b_mod1", 72), ("norm_g0", 24), ("norm_g1", 24),
             ("mla_qg", 1), ("mla_kg", 1), ("mla_qng", 3), ("mla_kvng", 2), ("invf", 1),
             ("ssd_cw", 160), ("ssd_cb", 32), ("ssd_dtb", 64), ("ssd_alog", 64), ("ssd_dskip", 32), ("ssd_ng16", 16)]


def _consts():
    inv = (10000.0 ** (-np.arange(0, 32, 2, dtype=np.float32) / 32)).astype(np.float32)
    invf = np.zeros((128, 1), np.float32)
    invf[64:80, 0] = inv
    invf[80:96, 0] = inv
    pm = np.zeros((96, 96), np.float32)
    for i in range(16):
        pm[80 + i, 64 + i] = -1.0
        pm[64 + i, 80 + i] = 1.0
    return invf, pm


INVF, PMT = _consts()


def _masks():
    k = np.arange(128)[:, None]
    j = np.arange(128)[None, :]
    Lf = (k > j); Rf = (k <= j); Lb = (k < j); Rb = (k >= j)
    Vf = (j >= k)
    Vb = (j <= k)
    return np.concatenate([m.astype(np.float32) for m in (Lf, Rf, Lb, Rb, Vf, Vb)], axis=1)


MASKS = _masks()


def host_vecs(inputs, b):
    f = np.float32
    v = {}
    v["c"] = np.ascontiguousarray(inputs["c"][b].reshape(8, 128).T.astype(f))
    for l in range(2):
        v[f"b_mod{l}"] = np.ascontiguousarray(inputs["b_mod"][l].reshape(72, 128).T.astype(f))
        v[f"norm_g{l}"] = np.ascontiguousarray(inputs["norm_g"][l].reshape(24, 128).T.astype(f))
    def col(a, n=128):
        o = np.zeros((128, 1), f)
        o[:len(a), 0] = a
        return o
    v["mla_qg"] = col(inputs["mla_q_head_g"][0])
    v["mla_kg"] = col(inputs["mla_k_head_g"][0])
    v["mla_qng"] = np.ascontiguousarray(inputs["mla_q_norm_g"][0].reshape(3, 128).T.astype(f))
    v["mla_kvng"] = np.ascontiguousarray(inputs["mla_kv_norm_g"][0].reshape(2, 128).T.astype(f))
    v["invf"] = INVF
    cwt = inputs["ssd_conv_w"][0]
    v["ssd_cw"] = np.ascontiguousarray(cwt.reshape(5, 32, 128).transpose(2, 1, 0).reshape(128, 160).astype(f))
    v["ssd_cb"] = np.ascontiguousarray(inputs["ssd_conv_b"][0].reshape(32, 128).T.astype(f))
    v["ssd_dtb"] = np.ascontiguousarray(np.broadcast_to(inputs["ssd_dt_bias"][0].reshape(1, 64), (128, 64)).astype(f))
    v["ssd_alog"] = np.ascontiguousarray(np.broadcast_to(inputs["ssd_a_log"][0].reshape(1, 64), (128, 64)).astype(f))
    v["ssd_ng16"] = np.ascontiguousarray(inputs["ssd_norm_g"][0].reshape(16, 128).T.astype(f))
    v["ssd_dskip"] = np.ascontiguousarray(np.broadcast_to(inputs["ssd_d"][0].reshape(1, 32), (128, 32)).astype(f))
    return v


def run(inputs, stages, seq=4096, cores=8, debug=False):
    C = build_program(stages, seq, debug)
    nc = C.nc
    ident = np.eye(128, dtype=np.float32)
    in_maps = []
    for b in range(cores):
        m = {
            "x": np.ascontiguousarray(inputs["x"][b][:seq]),
            "ident": ident,
            "w_mod": inputs["w_mod"],
            "ffn_w_gate": inputs["ffn_w_gate"],
            "ffn_w_up": inputs["ffn_w_up"],
            "ffn_w_down": inputs["ffn_w_down"],
            "pmT": PMT,
            "masks": MASKS,
            "ssd_w_in": inputs["ssd_w_in"][0], "ssd_w_out": inputs["ssd_w_out"][0],

            "pos": np.ascontiguousarray(np.broadcast_to(inputs["positions"][b][None, :seq], (128, seq)).astype(np.int32)),
            "mla_w_in": inputs["mla_w_in"][0], "mla_w_uq": inputs["mla_w_uq"][0],
            "mla_w_ukv": inputs["mla_w_ukv"][0], "mla_w_out": inputs["mla_w_out"][0],
        }
        for k, a in host_vecs(inputs, b).items():
            m["v_" + k] = a
        in_maps.append(m)
    res = run_bass_kernel_spmd(nc, in_maps, core_ids=list(range(cores)))
    out = np.stack([r["out"] for r in res.results], axis=0)
    if debug:
        return out, {k: res.results[0]["dbg"][:, o:o + n] for k, (o, n) in C.dbg_map.items()}
    return out


def kernel(**inputs):
    inputs = {k: np.asarray(v) for k, v in inputs.items()}
    stages = {
        "ffn": [(0, 0), (0, 1), (1, 0), (1, 1)],
        "order": [("ffn", 0, 0), ("ssd",), ("ffn", 0, 1), ("ffn", 1, 0), ("mla",), ("ffn", 1, 1)],
    }
    return run(inputs, stages).astype(np.float32)
```

```python
import numpy as np
from contextlib import ExitStack
import concourse.bass as bass
import concourse.mybir as mybir
from concourse.bass_utils import run_bass_kernel_spmd

F32 = mybir.dt.float32
BF16 = mybir.dt.bfloat16
I32 = mybir.dt.int32
AF = mybir.ActivationFunctionType
ALU = mybir.AluOpType
AX = mybir.AxisListType

D = 1024
S = 4096
DFF = 2816
NF = DFF // 128
NDC = D // 128
EPS = 1e-6
N_DMA_SEMS = 40
DBG_M = 0
BARRIERS = False
SKIP_SAME_ENG_NONRAW = False
ENGS = ("pe", "act", "dve", "pool", "sp")


class Op:
    __slots__ = ("eng", "fn", "deps", "sig", "seq", "dma", "semid", "semval", "prev", "pos", "gidx", "raw")


class Prog:
    def __init__(self, nc):
        self.nc = nc
        self.ops = {e: [] for e in ENGS}
        self.lastw = {}
        self.readers = {}
        self.ndma = 0
        self.dma_last = [None] * N_DMA_SEMS
        self.dma_cnt = [0] * N_DMA_SEMS
        self.nops = 0
        self.bases = set()
        self.touched = {}
        self.inherit = {}
        self.seen = set()

    def base_of(self, k):
        for _ in range(4):
            if k in self.bases:
                return k
            if isinstance(k, tuple) and len(k):
                k = k[0]
            else:
                return None
        return None

    def add(self, eng, fn, reads=(), writes=(), dma=False):
        op = Op()
        op.eng = eng
        op.fn = fn
        op.dma = dma
        op.sig = False
        op.seq = 0
        op.gidx = self.nops
        self.nops += 1
        deps = set()
        raw = set()
        for k in reads:
            w = self.lastw.get(k)
            if w is not None:
                raw.add(w)
        op.raw = raw
        for k in list(reads) + list(writes):
            b = self.base_of(k)
            if b is None:
                continue
            if k not in self.seen:
                self.seen.add(k)
                deps.update(self.inherit.get(b, ()))
            t = self.touched.setdefault(b, {})
            if dma:
                t[("dma", op.gidx)] = op
            else:
                t[eng] = op
        for k in reads:
            w = self.lastw.get(k)
            if w is not None:
                deps.add(w)
        for k in writes:
            w = self.lastw.get(k)
            if w is not None:
                deps.add(w)
            for r in self.readers.get(k, ()):
                deps.add(r)
        for k in reads:
            self.readers.setdefault(k, []).append(op)
        for k in writes:
            self.lastw[k] = op
            self.readers[k] = []
        deps.discard(op)
        op.prev = None
        if dma:
            s = self.ndma % N_DMA_SEMS
            self.ndma += 1
            op.semid = s
            self.dma_cnt[s] += 16
            op.semval = self.dma_cnt[s]
            op.prev = self.dma_last[s]
            self.dma_last[s] = op
        op.deps = deps
        op.pos = len(self.ops[eng])
        self.ops[eng].append(op)
        return op

    def dma(self, eng, out, in_, reads, writes, **kw):
        return self.add(eng, lambda e: e.dma_start(out=out, in_=in_, **kw), reads, writes, dma=True)

    def mm(self, out, lhsT, rhs, start, stop, reads, writes):
        return self.add("pe", lambda e: e.matmul(out, lhsT, rhs, start=start, stop=stop), reads, writes)

    def emit(self, eng_sems, dma_sems):
        nc = self.nc

        def needs_sync(op, d):
            if d.dma:
                return True
            if d.eng == op.eng and not op.dma:
                if op.eng == "pe":
                    return False
                return (d in op.raw) or not SKIP_SAME_ENG_NONRAW
            if d.eng == op.eng and op.dma:
                return True
            return True

        for e in ENGS:
            for op in self.ops[e]:
                for d in op.deps:
                    if needs_sync(op, d) and not d.dma:
                        d.sig = True
        for e in ENGS:
            n = 0
            for op in self.ops[e]:
                if op.sig and not op.dma:
                    n += 1
                    op.seq = n

        def emit_engine(ename, eobj):
            waited = {}
            for op in self.ops[ename]:
                need = {}
                for d in op.deps:
                    if not needs_sync(op, d):
                        continue
                    if d.dma:
                        key = ("d", d.semid)
                        val = d.semval
                    else:
                        key = ("e", d.eng)
                        val = d.seq
                    if need.get(key, 0) < val:
                        need[key] = val
                if op.dma and op.prev is not None:
                    key = ("d", op.prev.semid)
                    if need.get(key, 0) < op.prev.semval:
                        need[key] = op.prev.semval
                pend = []
                for key, val in need.items():
                    if waited.get(key, 0) >= val:
                        continue
                    waited[key] = val
                    sem = dma_sems[key[1]] if key[0] == "d" else eng_sems[key[1]]
                    pend.append((key[0] == "d", sem, val))
                pend.sort(key=lambda t: t[0])
                if op.fn is None:
                    for _, sem, val in pend:
                        eobj.wait_ge(sem, val)
                    continue
                for _, sem, val in pend[:-1]:
                    eobj.wait_ge(sem, val)
                ins = op.fn(eobj)
                if pend:
                    ins._wait_ge(pend[-1][1], pend[-1][2])
                if op.dma:
                    ins.then_inc(dma_sems[op.semid], 16)
                elif op.sig:
                    ins.then_inc(eng_sems[ename], 1)

        with nc.Block() as block:
            @block.tensor
            def _(e):
                emit_engine("pe", e)

            @block.scalar
            def _(e):
                emit_engine("act", e)

            @block.vector
            def _(e):
                emit_engine("dve", e)

            @block.gpsimd
            def _(e):
                emit_engine("pool", e)

            @block.sync
            def _(e):
                emit_engine("sp", e)


class Arena:
    def __init__(self, handle, nbytes, prog):
        self.h = handle
        self.cap = nbytes
        self.top = 0
        self.gen = 0
        self.P = prog
        self.allocs = []

    def mark(self):
        return self.top

    def release(self, m):
        self.top = m

    def alloc(self, name, shape, dtype):
        esz = 2 if dtype == BF16 else 4
        n = 1
        for s_ in shape:
            n *= s_
        nbytes = (n * esz + 63) // 64 * 64
        off = self.top
        assert off + nbytes <= self.cap, f"SBUF arena overflow allocating {name}: {off}+{nbytes}>{self.cap}"
        self.top += nbytes
        ap = self.h[:, off // 4:(off + nbytes) // 4]
        if dtype != F32:
            ap = ap.bitcast(dtype)
        ap = ap[:, 0:n]
        if len(shape) == 2:
            ap = ap.rearrange("p (a b) -> p a b", a=shape[0])
        elif len(shape) == 3:
            ap = ap.rearrange("p (a b c) -> p a b c", a=shape[0], b=shape[1])
        elif len(shape) == 4:
            ap = ap.rearrange("p (a b c d) -> p a b c d", a=shape[0], b=shape[1], c=shape[2])
        self.gen += 1
        key = (name, self.gen)
        P = self.P
        P.bases.add(key)
        inh = set()
        for (a0, a1, ok) in self.allocs:
            if a0 < off + nbytes and off < a1:
                inh.update(P.touched.get(ok, {}).values())
                inh.update(P.inherit.get(ok, ()))
        P.inherit[key] = inh
        self.allocs = [(a0, a1, ok) for (a0, a1, ok) in self.allocs if not (a0 >= off and a1 <= off + nbytes)]
        self.allocs.append((off, off + nbytes, key))
        return ap, key


class Ctx:
    pass


def dbg(C, name, ap, key, n):
    if not C.debug:
        return
    P, A = C.P, C.arena
    st = C.dbg_stage[:, C.dbg_off:C.dbg_off + n]
    kst = ("dbgst", name)
    P.add("pool", lambda e: e.tensor_copy(st, ap), [key], [kst])
    off = C.dbg_off
    C.dbg_off += n
    C.dbg_map[name] = (off, n)
    P.dma("sp", C.d_dbg.ap()[:, off:off + n], st, [kst], [("DBG", name)])
    C.dbg_keys.append(("DBG", name))


def rr(lst, i):
    return lst[i % len(lst)]


def setup_consts(C):
    P, A = C.P, C.arena
    C.ident_f, C.k_ident_f = A.alloc("ident_f", [128], F32)
    C.ident_b, C.k_ident_b = A.alloc("ident_b", [128], BF16)
    C.onesD_b, C.k_onesD = A.alloc("onesD", [128], BF16)
    P.dma("sp", C.ident_f, C.d_ident.ap(), [], [C.k_ident_f])
    P.add("dve", lambda e: e.tensor_copy(C.ident_b, C.ident_f), [C.k_ident_f], [C.k_ident_b])
    P.add("pool", lambda e: e.memset(C.onesD_b, 1.0 / D), [], [C.k_onesD])
    C.eps_t, C.k_eps = A.alloc("eps_t", [1], F32)
    P.add("pool", lambda e: e.memset(C.eps_t, EPS), [], [C.k_eps])
    C.vec = {}
    for name, t in C.d_vecs.items():
        n = t.ap().shape[1]
        ap, k = A.alloc("v_" + name, [n], F32)
        P.dma("sp", ap, t.ap(), [], [k])
        C.vec[name] = (ap, k)


def phase_barrier(C):
    P = C.P
    last = [P.ops[e][-1] for e in ENGS if P.ops[e]]
    last = [o for o in last if o.fn is not None]
    dmas = [o for e in ENGS for o in P.ops[e] if o.dma and o.gidx >= C.bar_gidx]
    C.bar_gidx = P.nops
    for e in ENGS:
        op = P.add(e, None)
        op.deps.update(last)
        op.deps.update(dmas)
        op.deps.discard(op)


def psum_bank(C):
    i = C.ps_i % 8
    C.ps_i += 1
    return C.ps[i], ("ps", i)


def load_transpose_x(C):
    P, A = C.P, C.arena
    m0 = A.mark()
    xin = [A.alloc(f"xin{i}", [4, D], F32) for i in range(2)]
    xtt = [A.alloc(f"xtt{i}", [NDC, 512], F32) for i in range(2)]
    xd = C.d_x.ap().rearrange("(g j p) d -> g p j d", j=4, p=128)
    for g in range(S // 512):
        xi, kxi = xin[g % 2]
        xt, kxt = xtt[g % 2]
        P.dma("sp", xi, xd[g], [], [kxi])
        for dc in range(NDC):
            ps, kps = psum_bank(C)
            for j in range(4):
                P.add("pe", lambda e, ps=ps, xi=xi, j=j, dc=dc: e.transpose(
                    ps[:, j * 128:(j + 1) * 128], xi[:, j, dc * 128:(dc + 1) * 128], C.ident_f),
                    [kxi, C.k_ident_f], [kps])
            eng = "act" if dc % 2 == 0 else "dve"
            if eng == "act":
                P.add("act", lambda e, ps=ps, xt=xt, dc=dc: e.copy(xt[:, dc, :], ps), [kps], [kxt])
            else:
                P.add("dve", lambda e, ps=ps, xt=xt, dc=dc: e.tensor_copy(xt[:, dc, :], ps), [kps], [kxt])
        P.dma("sp", C.xT[:, :, g * 512:(g + 1) * 512], xt, [kxt], [("xT", g)])
    A.release(m0)


def store_transpose_out(C):
    P, A = C.P, C.arena
    m0 = A.mark()
    xtt = [A.alloc(f"oxt{i}", [NDC, 512], F32) for i in range(2)]
    xo = [A.alloc(f"oxo{i}", [4, D], F32) for i in range(2)]
    od = C.d_out.ap().rearrange("(g j p) d -> g p j d", j=4, p=128)
    for g in range(S // 512):
        xt, kxt = xtt[g % 2]
        xo_, kxo = xo[g % 2]
        P.dma("sp", xt, C.xT[:, :, g * 512:(g + 1) * 512], [("xT", g)], [kxt])
        for j in range(4):
            for half in range(2):
                ps, kps = psum_bank(C)
                for q in range(4):
                    dc = half * 4 + q
                    P.add("pe", lambda e, ps=ps, xt=xt, j=j, dc=dc, q=q: e.transpose(
                        ps[:, q * 128:(q + 1) * 128], xt[:, dc, j * 128:(j + 1) * 128], C.ident_f),
                        [kxt, C.k_ident_f], [kps])
                if half == 0:
                    P.add("act", lambda e, ps=ps, xo_=xo_, j=j: e.copy(xo_[:, j, 0:512], ps), [kps], [kxo])
                else:
                    P.add("dve", lambda e, ps=ps, xo_=xo_, j=j: e.tensor_copy(xo_[:, j, 512:1024], ps), [kps], [kxo])
        P.dma("sp", od[g], xo_, [kxo], [("OUT", g)])
    A.release(m0)
    P.add("sp", None, [("OUT", g) for g in range(S // 512)] + C.dbg_keys, [])


def compute_mod(C):
    P, A = C.P, C.arena
    cvec, kc_ = C.vec["c"]
    C.cond, C.k_cond = A.alloc("cond", [8], F32)
    P.add("act", lambda e: e.activation(C.cond, cvec, AF.Silu), [kc_], [C.k_cond])
    C.mod = []
    m0 = None
    for l in range(2):
        mod, kmod = A.alloc(f"mod{l}", [72], F32)
        C.mod.append((mod, kmod))
    m0 = A.mark()
    wb = [A.alloc(f"wmod{i}", [8, 1024], F32) for i in range(2)]
    it = 0
    for l in range(2):
        mod, kmod = C.mod[l]
        bm, kbm = C.vec[f"b_mod{l}"]
        wd = C.d_w_mod.ap()[l].rearrange("(kc p) n -> p kc n", p=128)
        ps, kps = psum_bank(C)
        for cb in range(9):
            w, kw = wb[it % 2]
            it += 1
            P.dma("sp", w, wd[:, :, cb * 1024:(cb + 1) * 1024], [], [kw])
            for j in range(8):
                col = cb * 8 + j
                for kc in range(8):
                    P.mm(ps[:, col:col + 1], w[:, kc, j * 128:(j + 1) * 128], C.cond[:, kc:kc + 1],
                         kc == 0, kc == 7, [kw, C.k_cond], [kps])
        P.add("dve", lambda e, mod=mod, ps=ps, bm=bm: e.tensor_tensor(mod, ps[:, 0:72], bm, ALU.add),
              [kps, kbm], [kmod])
    A.release(m0)
    C.modv = []
    for l in range(2):
        mod, kmod = C.mod[l]
        g, kg = C.vec[f"norm_g{l}"]
        a, ka = A.alloc(f"moda{l}", [24], F32)
        gt, kgt = A.alloc(f"modg{l}", [24], F32)
        for sub in range(3):
            sc = mod[:, (sub * 3 + 1) * 8:(sub * 3 + 2) * 8]
            P.add("dve", lambda e, a=a, sub=sub, sc=sc, g=g: e.scalar_tensor_tensor(
                a[:, sub * 8:(sub + 1) * 8], sc, 1.0, g[:, sub * 8:(sub + 1) * 8], ALU.add, ALU.mult),
                [kmod, kg], [ka])
            gsrc = mod[:, (sub * 3 + 2) * 8:(sub * 3 + 3) * 8]
            fac = 1.0 if sub == 1 else 0.5
            P.add("dve", lambda e, gt=gt, sub=sub, gsrc=gsrc, fac=fac: e.tensor_scalar(
                gt[:, sub * 8:(sub + 1) * 8], gsrc, fac, None, ALU.mult), [kmod], [kgt])
        C.modv.append(dict(a=a, ka=ka, gate=gt, kgate=kgt, mod=mod, kmod=kmod))
        if l == 0:
            dbg(C, 'mod0', mod, kmod, 72)
            dbg(C, 'a0', a, ka, 24)
            dbg(C, 'gate0', gt, kgt, 24)


def cast_engine(C):
    e = ("dve", "act")[C.cast_i % 2]
    C.cast_i += 1
    return e


def emit_cast(P, eng, out, in_, reads, writes):
    if eng == "act":
        P.add("act", lambda e: e.copy(out, in_), reads, writes)
    else:
        P.add(eng, lambda e: e.tensor_copy(out, in_), reads, writes)


def convert_ffn_weights(C, l, w):
    P, A = C.P, C.arena
    idx = l * 2 + w
    m0 = A.mark()
    stg = [A.alloc(f"cst{i}", [4096], F32) for i in range(3)]
    stb = [A.alloc(f"csb{i}", [4096], BF16) for i in range(3)]
    it = 0
    for gi, src in enumerate((C.d_wg, C.d_wu)):
        sd = src.ap()[l, w].rearrange("(kc p) n -> p kc n", p=128)
        for fb in range(0, NF, 4):
            nf = min(4, NF - fb)
            sf, ksf = stg[it % 3]
            sb, ksb = stb[it % 3]
            it += 1
            sfv = sf[:, 0:8 * nf * 128].rearrange("p (kc n) -> p kc n", kc=8)
            P.dma("sp", sfv, sd[:, :, fb * 128:(fb + nf) * 128], [], [ksf])
            sbv = sb[:, 0:nf * 8 * 128].rearrange("p (f kc m) -> p f kc m", f=nf, kc=8)
            emit_cast(P, cast_engine(C), sbv, sfv.rearrange("p kc (f m) -> p f kc m", f=nf), [ksf], [ksb])
            dst = C.WGU[idx][fb:fb + nf, :, gi, :, :].rearrange("f p kc m -> p f (kc m)")
            P.dma("sp", dst, sbv.rearrange("p f kc m -> p f (kc m)"), [ksb], [("WGU", idx, f_) for f_ in range(fb, fb + nf)])
    sd = C.d_wd.ap()[l, w].rearrange("(fc p) n -> p fc n", p=128)
    for fb in range(0, NF, 4):
        nf = min(4, NF - fb)
        sf, ksf = stg[it % 3]
        sb, ksb = stb[it % 3]
        it += 1
        sfv = sf[:, 0:nf * 1024].rearrange("p (fc n) -> p fc n", fc=nf)
        P.dma("sp", sfv, sd[:, fb:fb + nf, :], [], [ksf])
        sbv = sb[:, 0:8 * nf * 128].rearrange("p (dc fc m) -> p dc fc m", dc=8, fc=nf)
        emit_cast(P, cast_engine(C), sbv, sfv.rearrange("p fc (dc m) -> p dc fc m", dc=8), [ksf], [ksb])
        dst = C.WD[idx][:, :, fb:fb + nf, :].rearrange("dc p fc m -> p dc (fc m)")
        P.dma("sp", dst, sbv.rearrange("p dc fc m -> p dc (fc m)"), [ksb], [("WD", idx, dc) for dc in range(8)])
    A.release(m0)


def norm_modulate(C, xt, kxt, h, kh, T, l, sub, scratch):
    P = C.P
    mv = C.modv[l]
    sq, ksq, rstd, krstd, tmp, ktmp = scratch
    shift = mv["mod"][:, (sub * 3) * 8:(sub * 3 + 1) * 8]
    for st in range(T // 512):
        sl = slice(st * 512, (st + 1) * 512)
        for dc in range(NDC):
            P.add("act", lambda e, dc=dc, sl=sl: e.activation(sq[:, dc, sl], xt[:, dc, sl], AF.Square),
                  [kxt], [(ksq, st)])
        ps, kps = psum_bank(C)
        for dc in range(NDC):
            P.mm(ps, C.onesD_b, sq[:, dc, sl], dc == 0, dc == NDC - 1, [(ksq, st), C.k_onesD], [kps])
        P.add("act", lambda e, ps=ps, sl=sl: e.activation(rstd[:, sl], ps, AF.Ln, bias=C.eps_t[:, 0:1]),
              [kps, C.k_eps], [(krstd, st)])
        P.add("act", lambda e, sl=sl: e.activation(rstd[:, sl], rstd[:, sl], AF.Exp, scale=-0.5),
              [(krstd, st)], [(krstd, st)])
        for dc in range(NDC):
            tm, ktm = tmp[dc % len(tmp)]
            eng = "dve"
            P.add(eng, lambda e, tm=tm, dc=dc, sl=sl: e.tensor_tensor(tm, xt[:, dc, sl], rstd[:, sl], ALU.mult),
                  [kxt, (krstd, st)], [ktm])
            P.add("act", lambda e, tm=tm, dc=dc, sl=sl: e.activation(
                h[:, dc, sl], tm, AF.Identity, bias=shift[:, dc:dc + 1],
                scale=mv["a"][:, sub * 8 + dc:sub * 8 + dc + 1]),
                [ktm, mv["ka"], mv["kmod"]], [(kh, st)])


def ffn_phase(C, l, w):
    P, A = C.P, C.arena
    idx = l * 2 + w
    sub = 0 if w == 0 else 2
    mv = C.modv[l]
    T = 1024
    m0 = A.mark()
    xb = [A.alloc(f"fx{i}", [NDC, T], F32) for i in range(2)]
    h, kh = A.alloc("fh", [NDC, T], BF16)
    act, kact = A.alloc("fact", [NF, T], BF16)
    sq, ksq = A.alloc("fsq", [NDC, T], BF16)
    rstd, krstd = A.alloc("frstd", [T], F32)
    tmp = [A.alloc(f"ftmp{i}", [512], F32) for i in range(3)]
    sg = [A.alloc(f"fsg{i}", [512], F32) for i in range(3)]
    wgu = [A.alloc(f"fwgu{i}", [2, 8, 128], BF16) for i in range(4)]
    wdb = [A.alloc(f"fwd{i}", [NF, 128], BF16) for i in range(3)]
    NM = S // T
    items = []
    for m in range(NM):
        for f in range(NF):
            items.append(("g", f))
        for dc in range(NDC):
            items.append(("d", dc))
    issued = [0]
    cnt = {"g": 0, "d": 0}
    slot_of = {}

    def prefetch(upto):
        while issued[0] < min(upto, len(items)):
            kind, j = items[issued[0]]
            if kind == "g":
                buf, kb = wgu[cnt["g"] % 4]
                cnt["g"] += 1
                P.dma("sp", buf, C.WGU[idx][j].rearrange("p g kc m -> p g kc m"), [("WGU", idx, j)], [kb])
            else:
                buf, kb = wdb[cnt["d"] % 3]
                cnt["d"] += 1
                P.dma("sp", buf, C.WD[idx][j], [("WD", idx, j)], [kb])
            slot_of[issued[0]] = (buf, kb)
            issued[0] += 1

    def load_x(m):
        xt, kxt = xb[m % 2]
        P.dma("act", xt, C.xT[:, :, m * T:(m + 1) * T], [("xT", 2 * m), ("xT", 2 * m + 1)], [kxt])

    load_x(0)
    pos = 0
    for m in range(NM):
        xt, kxt = xb[m % 2]
        prefetch(pos + 3)
        norm_modulate(C, xt, kxt, h, kh, T, l, sub, (sq, ksq, rstd, krstd, tmp, None))
        if m + 1 < NM:
            load_x(m + 1)
        if m == DBG_M and idx == 0:
            dbg(C, 'rstd', rstd[:, 0:512], (krstd, 0), 512)
            dbg(C, 'h0', h[:, 0, 0:512], (kh, 0), 512)
            dbg(C, 'h7', h[:, 7, 0:512], (kh, 0), 512)
        for f in range(NF):
            prefetch(pos + 3)
            wbuf, kwb = slot_of.pop(pos)
            pos += 1
            for st in range(T // 512):
                sl = slice(st * 512, (st + 1) * 512)
                psg, kpsg = psum_bank(C)
                psu, kpsu = psum_bank(C)
                for kc in range(8):
                    P.mm(psg, wbuf[:, 0, kc, :], h[:, kc, sl], kc == 0, kc == 7, [kwb, (kh, st)], [kpsg])
                for kc in range(8):
                    P.mm(psu, wbuf[:, 1, kc, :], h[:, kc, sl], kc == 0, kc == 7, [kwb, (kh, st)], [kpsu])
                s_, ks_ = sg[(f * 2 + st) % 3]
                P.add("act", lambda e, s_=s_, psg=psg: e.activation(s_, psg, AF.Silu), [kpsg], [ks_])
                P.add("dve", lambda e, s_=s_, psu=psu, f=f, sl=sl: e.tensor_tensor(act[:, f, sl], s_, psu, ALU.mult),
                      [ks_, kpsu], [(kact, st)])
        for dc in range(NDC):
            prefetch(pos + 3)
            wbuf, kwb = slot_of.pop(pos)
            pos += 1
            for st in range(T // 512):
                sl = slice(st * 512, (st + 1) * 512)
                pso, kpso = psum_bank(C)
                for f in range(NF):
                    P.mm(pso, wbuf[:, f, :], act[:, f, sl], f == 0, f == NF - 1, [kwb, (kact, st)], [kpso])
                P.add("dve", lambda e, pso=pso, dc=dc, sl=sl, xt=xt: e.scalar_tensor_tensor(
                    xt[:, dc, sl], pso, mv["gate"][:, sub * 8 + dc:sub * 8 + dc + 1], xt[:, dc, sl],
                    ALU.mult, ALU.add), [kpso, mv["kgate"], kxt], [kxt])
        if m == DBG_M and idx == 0:
            dbg(C, 'act0', act[:, 0, 0:512], (kact, 0), 512)
            dbg(C, 'act21', act[:, 21, 0:512], (kact, 0), 512)
            dbg(C, 'xo0', xt[:, 0, 0:512], kxt, 512)
        P.dma("act", C.xT[:, :, m * T:(m + 1) * T], xt, [kxt], [("xT", 2 * m), ("xT", 2 * m + 1)])
    A.release(m0)


PI = float(np.pi)


def const_tile(C, name, val, dtype=F32, n=1):
    ap, k = C.arena.alloc("c_" + name, [n], dtype)
    C.P.add("pool", lambda e: e.memset(ap, val), [], [k])
    return ap, k


def load_cast_weight(C, name, src_ap, shape, eng="sp"):
    P, A = C.P, C.arena
    n = 1
    for s_ in shape:
        n *= s_
    wb, kwb = A.alloc(name, shape, BF16)
    m0 = A.mark()
    CH = 2048
    flat_b = wb
    stg = [A.alloc(f"{name}_st{i}", [CH], F32) for i in range(2)]
    A.release(m0)
    return wb, kwb, stg


def mla_phase(C):
    P, A = C.P, C.arena
    l = 1
    mv = C.modv[l]
    T = 512
    NT = S // T
    NB = S // 128
    SCALE = float(96 ** -0.5)
    m_phase = A.mark()
    ones384, k384 = const_tile(C, "o384", 1.0 / 384, BF16, 128)
    ones256, k256 = const_tile(C, "o256", 1.0 / 256, BF16, 128)
    ones96, k96 = const_tile(C, "o96", 1.0 / 96, BF16, 128)
    onesrow, krow = const_tile(C, "orow", 1.0, F32, 128)
    hpi, khpi = const_tile(C, "hpi", PI / 2)
    nhpi, knhpi = const_tile(C, "nhpi", -PI / 2)
    pmT_f, kpmf = A.alloc("pmT_f", [96], F32)
    pmT, kpm = A.alloc("pmT", [96], BF16)
    P.dma("sp", pmT_f[0:96, :], C.d_pm.ap(), [], [kpmf])
    P.add("dve", lambda e: e.tensor_copy(pmT[0:96, :], pmT_f[0:96, :]), [kpmf], [kpm])
    gq, kgq = C.vec["mla_qg"]
    gk, kgk = C.vec["mla_kg"]
    gql, kgql = C.vec["mla_qng"]
    gkvl, kgkvl = C.vec["mla_kvng"]
    invf, kinvf = C.vec["invf"]
    qn, kqn = A.alloc("qn", [3, S], BF16)
    kvn, kkvn = A.alloc("kvn", [2, S], BF16)
    kpe, kkpe = A.alloc("kpe", [S], F32)
    sqk, ksqk = A.alloc("sqk", [S], BF16)
    COS, kcos = A.alloc("COS", [S], F32)
    SIN, ksin = A.alloc("SIN", [S], F32)
    wuq, kwuq = A.alloc("wuq", [3, 1536], BF16)
    wukv, kwukv = A.alloc("wukv", [2, 2048], BF16)

    m0 = A.mark()
    posi, kposi = A.alloc("posi", [S], I32)
    ang, kang = A.alloc("ang", [S], F32)
    t1, kt1 = A.alloc("rt1", [S], F32)
    ti, kti = A.alloc("rti", [S], I32)
    P.dma("sp", posi, C.d_pos.ap(), [], [kposi])
    P.add("dve", lambda e: e.tensor_copy(ang, posi), [kposi], [kang])
    P.add("dve", lambda e: e.tensor_scalar(ang, ang, invf[:, 0:1], None, ALU.mult), [kang, kinvf], [kang])
    for (dst, kdst, shift) in ((SIN, ksin, 0.0), (COS, kcos, PI / 2)):
        P.add("dve", lambda e, shift=shift: e.tensor_scalar(t1, ang, shift, 1.0 / (2 * PI), ALU.add, ALU.mult),
              [kang], [kt1])
        P.add("dve", lambda e: e.tensor_copy(ti, t1), [kt1], [kti])
        P.add("dve", lambda e: e.tensor_copy(t1, ti), [kti], [kt1])
        P.add("dve", lambda e: e.scalar_tensor_tensor(t1, t1, -2 * PI, ang, ALU.mult, ALU.add), [kt1, kang], [kt1])
        bias_ap = nhpi if shift == 0.0 else None
        if shift == 0.0:
            P.add("act", lambda e: e.activation(t1, t1, AF.Abs, bias=nhpi[:, 0:1]), [kt1, knhpi], [kt1])
        else:
            P.add("act", lambda e: e.activation(t1, t1, AF.Abs), [kt1], [kt1])
        P.add("act", lambda e, dst=dst: e.activation(dst, t1, AF.Sin, bias=hpi[:, 0:1], scale=-1.0),
              [kt1, khpi], [kdst])
    A.release(m0)

    def load_cast(dst, kdst, src, nk, ncol, colchunk):
        stg = [A.alloc(f"wst{i}", [nk, colchunk], F32) for i in range(2)]
        it = 0
        for c0 in range(0, ncol, colchunk):
            cw = min(colchunk, ncol - c0)
            st, kst = stg[it % 2]
            it += 1
            P.dma("sp", st[:, :, 0:cw], src[:, :, c0:c0 + cw], [], [kst])
            emit_cast(P, cast_engine(C), dst[:, :, c0:c0 + cw], st[:, :, 0:cw], [kst], [kdst])

    m1 = A.mark()
    mm_ = A.mark()
    load_cast(wuq, kwuq, C.d_mla_wuq.ap().rearrange("(kc p) n -> p kc n", p=128), 3, 1536, 512)
    load_cast(wukv, kwukv, C.d_mla_wukv.ap().rearrange("(kc p) n -> p kc n", p=128), 2, 2048, 512)
    A.release(mm_)
    win, kwin = A.alloc("win", [8, 736], BF16)
    P.add("pool", lambda e: e.memset(win, 0.0), [], [kwin])
    wsrc = C.d_mla_win.ap().rearrange("(kc p) n -> p kc n", p=128)
    mm2 = A.mark()
    stg = [A.alloc(f"wst_in{i}", [8, 224], F32) for i in range(2)]
    for i, c0 in enumerate(range(0, 672, 224)):
        st, kst = stg[i % 2]
        P.dma("sp", st, wsrc[:, :, c0:c0 + 224], [], [kst])
        if c0 + 224 <= 640:
            emit_cast(P, cast_engine(C), win[:, :, c0:c0 + 224], st, [kst], [kwin])
        else:
            nl = 640 - c0
            emit_cast(P, cast_engine(C), win[:, :, c0:640], st[:, :, 0:nl], [kst], [kwin])
            emit_cast(P, cast_engine(C), win[:, :, 704:736], st[:, :, nl:nl + 32], [kst], [kwin])

    A.release(mm2)
    xb = [A.alloc(f"mx{i}", [NDC, T], F32) for i in range(2)]
    h, kh = A.alloc("mh", [NDC, T], BF16)
    sq, ksq = A.alloc("msq", [NDC, T], BF16)
    rstd, krstd = A.alloc("mrstd", [T], F32)
    tmp = [A.alloc(f"mtmp{i}", [512], F32) for i in range(3)]
    lsq, klsq = A.alloc("mlsq", [3, T], BF16)
    lrs, klrs = A.alloc("mlrs", [T], F32)

    def load_x(m):
        xt, kxt = xb[m % 2]
        P.dma("act", xt, C.xT[:, :, m * T:(m + 1) * T], [("xT", m)], [kxt])

    load_x(0)
    for m in range(NT):
        xt, kxt = xb[m % 2]
        sl = slice(m * T, (m + 1) * T)
        norm_modulate(C, xt, kxt, h, kh, T, l, 1, (sq, ksq, rstd, krstd, tmp, None))
        if m + 1 < NT:
            load_x(m + 1)
        for (c0, nch, ones, kones, dst, kdst, g) in ((0, 3, ones384, k384, qn, kqn, gql), (3, 2, ones256, k256, kvn, kkvn, gkvl)):
            banks = []
            for c in range(nch):
                ps, kps = psum_bank(C)
                banks.append((ps, kps))
                for kc in range(8):
                    P.mm(ps, win[:, kc, (c0 + c) * 128:(c0 + c + 1) * 128], h[:, kc, :], kc == 0, kc == 7,
                         [kwin, (kh, 0)], [kps])
                P.add("act", lambda e, ps=ps, c=c: e.activation(lsq[:, c, :], ps, AF.Square), [kps], [klsq])
            pss, kpss = psum_bank(C)
            for c in range(nch):
                P.mm(pss, ones, lsq[:, c, :], c == 0, c == nch - 1, [klsq, kones], [kpss])
            P.add("act", lambda e, pss=pss: e.activation(lrs, pss, AF.Ln, bias=C.eps_t[:, 0:1]), [kpss, C.k_eps], [klrs])
            P.add("act", lambda e: e.activation(lrs, lrs, AF.Exp, scale=-0.5), [klrs], [klrs])
            for c in range(nch):
                ps, kps = banks[c]
                tm, ktm = tmp[c % 3]
                P.add("dve", lambda e, tm=tm, ps=ps: e.tensor_tensor(tm, ps, lrs, ALU.mult), [kps, klrs], [ktm])
                P.add("act", lambda e, tm=tm, c=c, dst=dst, g=g, sl=sl: e.activation(
                    dst[:, c, sl], tm, AF.Identity, scale=g[:, c:c + 1]), [ktm], [kdst])
        ps, kps = psum_bank(C)
        for kc in range(8):
            P.mm(ps[0:96, :], win[:, kc, 640:736], h[:, kc, :], kc == 0, kc == 7, [kwin, (kh, 0)], [kps])
        P.add("act", lambda e, ps=ps, sl=sl: e.copy(kpe[64:96, sl], ps[64:96, :]), [kps], [kkpe])
        P.add("act", lambda e, ps=ps, sl=sl: e.activation(sqk[64:96, sl], ps[64:96, :], AF.Square), [kps], [(ksqk, "pe")])
    A.release(m1)

    kT = [A.alloc(f"kT{i}", [S], BF16) for i in range(2)]
    Vau = [A.alloc(f"Vau{i}", [NB, 128], BF16) for i in range(2)]
    OTs = [A.alloc(f"OTs{i}", [S], BF16) for i in range(2)]
    for par in range(2):
        va, kva = Vau[par]
        P.add("pool", lambda e, va=va: e.memset(va, 1.0), [], [kva])
    rsk, krsk = A.alloc("rsk", [T], F32)
    rt1 = [A.alloc(f"rp1_{i}", [T], F32) for i in range(2)]
    rt2 = [A.alloc(f"rp2_{i}", [T], F32) for i in range(2)]
    qsq, kqsq = A.alloc("qsq", [T], BF16)
    qT = [A.alloc(f"qT{i}", [T], BF16) for i in range(2)]
    PT = [A.alloc(f"PT{i}", [T], BF16) for i in range(4)]
    osb, kosb = A.alloc("osb", [T], F32)
    rl, krl = A.alloc("rl", [T], F32)
    pti = 0

    C.held = []

    def bank_excl(excl):
        while True:
            ps, kps = psum_bank(C)
            if all(ps is not x for x in excl) and all(ps is not x for x in C.held):
                return ps, kps

    def norm_rope(src_ps, ksrc_ps, src_sb, ksrc_sb, sqt, ksqt_keys, g, kg, dstT, kdstT, sl, tl):
        yield
        pss, kpss = bank_excl([])
        P.mm(pss[0:96, :], ones96[0:96, 0:96], sqt, True, True, ksqt_keys + [k96], [kpss])
        P.add("act", lambda e: e.activation(rsk[0:96, :], pss[0:96, :], AF.Ln, bias=C.eps_t[0:96, 0:1]),
              [kpss, C.k_eps], [krsk])
        P.add("act", lambda e: e.activation(rsk[0:96, :], rsk[0:96, :], AF.Exp, scale=-0.5), [krsk], [krsk])
        if src_sb is None:
            P.add("dve", lambda e: e.scalar_tensor_tensor(dstT[0:96, sl], src_ps[0:96, :], g[0:96, 0:1], rsk[0:96, :],
                                                          ALU.mult, ALU.mult), [ksrc_ps, kg, krsk], [kdstT])
        else:
            P.add("dve", lambda e: e.scalar_tensor_tensor(dstT[0:64, sl], src_ps[0:64, :], g[0:64, 0:1], rsk[0:64, :],
                                                          ALU.mult, ALU.mult), [ksrc_ps, kg, krsk], [kdstT])
            P.add("dve", lambda e: e.scalar_tensor_tensor(dstT[64:96, sl], src_sb[64:96, tl], g[64:96, 0:1],
                                                          rsk[64:96, :], ALU.mult, ALU.mult), [ksrc_sb, kg, krsk], [kdstT])
        C.held[:] = [x for x in C.held if x is not src_ps]
        yield
        psr, kpsr = bank_excl([])
        P.mm(psr[0:96, :], pmT[0:96, 0:96], dstT[0:96, sl], True, True, [kdstT, kpm], [kpsr])
        a1, ka1 = rt1[C.ps_i % 2]
        a2, ka2 = rt2[C.ps_i % 2]
        P.add("dve", lambda e: e.tensor_tensor(a1[64:96, :], dstT[64:96, sl], COS[64:96, tl], ALU.mult),
              [kdstT, kcos], [ka1])
        P.add("dve", lambda e: e.tensor_tensor(a2[64:96, :], psr[64:96, :], SIN[64:96, tl], ALU.mult),
              [kpsr, ksin], [ka2])
        P.add("dve", lambda e: e.tensor_tensor(dstT[64:96, sl], a1[64:96, :], a2[64:96, :], ALU.add),
              [ka1, ka2], [kdstT])

    LA = 3
    PTn = [A.alloc(f"PTn{i}", [T], BF16) for i in range(LA + 3)]

    def prepK(hd, m):
        kt_, kkt = kT[hd % 2]
        tl = slice(m * T, (m + 1) * T)
        ps, kps = bank_excl([])
        C.held.append(ps)
        for kc in range(2):
            P.mm(ps[0:64, :], wukv[:, kc, hd * 128:hd * 128 + 64], kvn[:, kc, tl], kc == 0, kc == 1,
                 [kwukv, kkvn], [kps])
        P.add("act", lambda e, ps=ps, tl=tl: e.activation(sqk[0:64, tl], ps[0:64, :], AF.Square),
              [kps], [(ksqk, "n", m)])
        yield from norm_rope(ps, kps, kpe, kkpe, sqk[0:96, tl], [(ksqk, "n", m), (ksqk, "pe")], gk, kgk, kt_, kkt, tl, tl)

    def prepV(hd, b0):
        va, kva = Vau[hd % 2]
        voff = 0 if hd % 2 == 0 else 64
        ps, kps = bank_excl([])
        for j in range(8):
            blk = b0 + j
            for kc in range(2):
                P.mm(ps[:, j * 64:(j + 1) * 64], kvn[:, kc, blk * 128:(blk + 1) * 128],
                     wukv[:, kc, hd * 128 + 64:hd * 128 + 128], kc == 0, kc == 1, [kwukv, kkvn], [kps])
        P.add("dve", lambda e, ps=ps, b0=b0, va=va, voff=voff: e.tensor_copy(
            va[:, b0:b0 + 8, voff:voff + 64], ps.rearrange("p (j v) -> p j v", j=8)), [kps], [kva])
        yield

    qcount = [0]

    def prepQ(hd, m, q_, kq_):
        tl = slice(m * T, (m + 1) * T)
        ps, kps = bank_excl([])
        C.held.append(ps)
        for kc in range(3):
            P.mm(ps[0:96, :], wuq[:, kc, hd * 96:(hd + 1) * 96], qn[:, kc, tl], kc == 0, kc == 2,
                 [kwuq, kqn], [kps])
        P.add("act", lambda e, ps=ps: e.activation(qsq[0:96, :], ps[0:96, :], AF.Square), [kps], [kqsq])
        yield from norm_rope(ps, kps, None, None, qsq[0:96, :], [kqsq], gq, kgq, q_, kq_, slice(0, T), tl)

    def run_all(gen):
        for _ in gen:
            pass

    def next_q():
        q = qT[qcount[0] % 2]
        qcount[0] += 1
        return q

    for m in range(NT):
        run_all(prepK(0, m))
    for b0 in range(0, NB, 8):
        run_all(prepV(0, b0))
    nextq = next_q()
    run_all(prepQ(0, 0, nextq[0], nextq[1]))
    vgroups = list(range(0, NB, 8))
    for hd in range(16):
        par = hd % 2
        kt_, kkt = kT[par]
        va, kva = Vau[par]
        ots, kots = OTs[(hd // 2) % 2]
        oh = 0 if par == 0 else 64
        lp = 64 if par == 0 else 0
        for m in range(NT):
            tl = slice(m * T, (m + 1) * T)
            q_, kq_ = nextq
            gens = []
            if m + 1 < NT:
                nextq = next_q()
                gens.append(prepQ(hd, m + 1, nextq[0], nextq[1]))
            elif hd + 1 < 16:
                nextq = next_q()
                gens.append(prepQ(hd + 1, 0, nextq[0], nextq[1]))
            if hd + 1 < 16:
                gens.append(prepK(hd + 1, m))
                if m < len(vgroups):
                    gens.append(prepV(hd + 1, vgroups[m]))
            pso, kpso = bank_excl([])
            C.held.append(pso)
            pend = []
            for kb in range(NB + LA):
                if kb < NB:
                    pss, kpss = bank_excl([])
                    P.mm(pss, kt_[0:96, kb * 128:(kb + 1) * 128], q_[0:96, :], True, True, [kkt, kq_], [kpss])
                    pt, kpt = PTn[pti % len(PTn)]
                    pti += 1
                    P.add("act", lambda e, pt=pt, pss=pss: e.activation(pt, pss, AF.Exp, scale=SCALE), [kpss], [kpt])
                    pend.append((kb, pt, kpt))
                if kb >= LA:
                    kb2, pt, kpt = pend.pop(0)
                    P.mm(pso, va[:, kb2, :], pt, kb2 == 0, kb2 == NB - 1, [kva, kpt], [kpso])
                if kb % 6 == 2 and gens:
                    gi = (kb // 6) % len(gens)
                    for g_ in list(gens):
                        try:
                            next(g_)
                        except StopIteration:
                            gens.remove(g_)
            for g_ in gens:
                run_all(g_)
            P.add("dve", lambda e, pso=pso, lp=lp: e.reciprocal(rl[lp:lp + 1, :], pso[lp:lp + 1, :]), [kpso], [krl])
            P.add("act", lambda e, pso=pso, oh=oh: e.copy(osb[oh:oh + 64, :], pso[oh:oh + 64, :]), [kpso], [kosb])
            psb, kpsb = bank_excl([])
            P.mm(psb, onesrow[lp:lp + 1, :], rl[lp:lp + 1, :], True, True, [krl, krow], [kpsb])
            P.add("dve", lambda e, psb=psb, oh=oh, ots=ots, tl=tl: e.tensor_tensor(
                ots[oh:oh + 64, tl], osb[oh:oh + 64, :], psb[oh:oh + 64, :], ALU.mult), [kosb, kpsb], [kots])
            C.held[:] = [x for x in C.held if x is not pso]
        if par == 1:
            c = hd // 2
            P.dma("sp", C.OT[c], ots, [kots], [("OT", c)])
    A.release(m_phase)

    m3 = A.mark()
    wout, kwout = A.alloc("wout", [8, 1024], BF16)
    stg = [A.alloc(f"wst_o{i}", [8, 256], F32) for i in range(2)]
    wsrc = C.d_mla_wout.ap().rearrange("(kc p) n -> p kc n", p=128)
    for i, c0 in enumerate(range(0, 1024, 256)):
        st, kst = stg[i % 2]
        P.dma("sp", st, wsrc[:, :, c0:c0 + 256], [], [kst])
        emit_cast(P, cast_engine(C), wout[:, :, c0:c0 + 256], st, [kst], [kwout])
    xb = [A.alloc(f"ox{i}", [NDC, T], F32) for i in range(2)]
    ob = [A.alloc(f"oo{i}", [NDC, T], BF16) for i in range(2)]
    for m in range(NT):
        xt, kxt = xb[m % 2]
        ot, kot = ob[m % 2]
        tl = slice(m * T, (m + 1) * T)
        P.dma("act", xt, C.xT[:, :, tl], [("xT", m)], [kxt])
        P.dma("sp", ot, C.OT.rearrange("c p s -> p c s")[:, :, tl], [("OT", c) for c in range(8)], [kot])
        for dc in range(NDC):
            ps, kps = psum_bank(C)
            for kc in range(8):
                P.mm(ps, wout[:, kc, dc * 128:(dc + 1) * 128], ot[:, kc, :], kc == 0, kc == 7, [kwout, kot], [kps])
            P.add("dve", lambda e, ps=ps, dc=dc, xt=xt: e.scalar_tensor_tensor(
                xt[:, dc, :], ps, mv["gate"][:, 8 + dc:8 + dc + 1], xt[:, dc, :], ALU.mult, ALU.add),
                [kps, mv["kgate"], kxt], [kxt])
        P.dma("act", C.xT[:, :, tl], xt, [kxt], [("xT", m)])
    A.release(m3)


def bc_last(ap, n):
    return ap.unsqueeze(2).to_broadcast([ap.shape[0], ap.shape[1], n])


def bc_mid(ap, n):
    return ap.unsqueeze(1).to_broadcast([ap.shape[0], n, ap.shape[1]])


def ssd_phase(C):
    P, A = C.P, C.arena
    l = 0
    mv = C.modv[l]
    NB = S // 128
    T = 512
    NT = S // T
    m_phase = A.mark()
    one_t, kone = const_tile(C, "one", 1.0)
    masks, kmask = A.alloc("masks", [6, 128], F32)
    P.dma("sp", masks, C.d_masks.ap().rearrange("p (a b) -> p a b", a=6), [], [kmask])
    ones_f, konesf = const_tile(C, "ones_f", 1.0, F32, 128)
    cw, kcw = C.vec["ssd_cw"]
    cb_, kcb = C.vec["ssd_cb"]
    dtb, kdtb = C.vec["ssd_dtb"]
    alog, kalog = C.vec["ssd_alog"]
    dsk, kdsk = C.vec["ssd_dskip"]
    Aneg, kAneg = A.alloc("Aneg", [64], F32)
    P.add("act", lambda e: e.activation(Aneg, alog, AF.Exp), [kalog], [kAneg])
    P.add("dve", lambda e: e.tensor_scalar(Aneg, Aneg, -1.0, None, ALU.mult), [kAneg], [kAneg])
    h_all, khall = A.alloc("h_all", [NDC, S], BF16)

    m0 = A.mark()
    xb = [A.alloc(f"sx{i}", [NDC, T], F32) for i in range(2)]
    sq, ksq = A.alloc("ssq", [NDC, T], BF16)
    rstd, krstd = A.alloc("srstd", [T], F32)
    tmp = [A.alloc(f"stmp{i}", [512], F32) for i in range(3)]
    for m in range(NT):
        xt, kxt = xb[m % 2]
        P.dma("act", xt, C.xT[:, :, m * T:(m + 1) * T], [("xT", m)], [kxt])
        norm_modulate(C, xt, kxt, h_all[:, :, m * T:(m + 1) * T], (khall, m), T, l, 1, (sq, ksq, rstd, krstd, tmp, None))
    A.release(m0)
    hkeys = [((khall, m), 0) for m in range(NT)]

    m0 = A.mark()
    wsrc = C.d_ssd_win.ap().rearrange("(kc p) n -> p kc n", p=128)
    wst = [A.alloc(f"swst{i}", [8, 128], F32) for i in range(2)]
    wcb = [A.alloc(f"swc{i}", [8, 128], BF16) for i in range(2)]
    pre = [A.alloc(f"spre{i}", [S + 4], F32) for i in range(2)]
    acc = [A.alloc(f"sacc{i}", [S], F32) for i in range(2)]
    xo = [A.alloc(f"sxo{i}", [S], BF16) for i in range(2)]
    for i in range(2):
        pr, kpr = pre[i]
        P.add("pool", lambda e, pr=pr: e.memset(pr, 0.0), [], [kpr])
    for c in range(32):
        ws, kws = wst[c % 2]
        wc, kwc = wcb[c % 2]
        pr, kpr = pre[c % 2]
        ac, kac = acc[c % 2]
        xo_, kxo = xo[c % 2]
        P.dma("sp", ws, wsrc[:, :, 2048 + c * 128:2048 + (c + 1) * 128], [], [kws])
        P.add("pool", lambda e, wc=wc, ws=ws: e.tensor_copy(wc, ws), [kws], [kwc])
        for m in range(NT):
            ps, kps = psum_bank(C)
            for kc in range(8):
                P.mm(ps, wc[:, kc, :], h_all[:, kc, m * T:(m + 1) * T], kc == 0, kc == 7, [kwc, hkeys[m]], [kps])
            P.add("act", lambda e, ps=ps, pr=pr, m=m: e.copy(pr[:, 2 + m * T:2 + (m + 1) * T], ps), [kps], [kpr])
        for hf in range(2):
            o0 = hf * (S // 2)
            n_ = S // 2
            P.add("dve", lambda e, ac=ac, pr=pr, c=c, o0=o0, n_=n_: e.tensor_scalar(
                ac[:, o0:o0 + n_], pr[:, o0:o0 + n_], cw[:, c * 5:c * 5 + 1], cb_[:, c:c + 1], ALU.mult, ALU.add),
                [kpr, kcw, kcb], [(kac, hf)])
            for j in range(1, 5):
                P.add("dve", lambda e, ac=ac, pr=pr, c=c, j=j, o0=o0, n_=n_: e.scalar_tensor_tensor(
                    ac[:, o0:o0 + n_], pr[:, o0 + j:o0 + j + n_], cw[:, c * 5 + j:c * 5 + j + 1], ac[:, o0:o0 + n_],
                    ALU.mult, ALU.add), [kpr, kcw, (kac, hf)], [(kac, hf)])
            P.add("act", lambda e, xo_=xo_, ac=ac, o0=o0, n_=n_: e.activation(xo_[:, o0:o0 + n_], ac[:, o0:o0 + n_], AF.Silu),
                  [(kac, hf)], [(kxo, hf)])
        P.dma("sp", C.XBC[c], xo_, [(kxo, 0), (kxo, 1)], [("XBC", c)])
    A.release(m0)

    wdt, kwdt = A.alloc("wdt", [8, 64], BF16)
    m_big = A.mark()
    wz, kwz = A.alloc("wz", [8, 2048], BF16)
    m0 = A.mark()
    stg = [A.alloc(f"szst{i}", [8, 256], F32) for i in range(2)]
    it = 0
    for c0 in range(0, 2048, 256):
        st, kst = stg[it % 2]
        it += 1
        P.dma("sp", st, wsrc[:, :, c0:c0 + 256], [], [kst])
        emit_cast(P, cast_engine(C), wz[:, :, c0:c0 + 256], st, [kst], [kwz])
    st, kst = stg[it % 2]
    it += 1
    P.dma("sp", st[:, :, 0:64], wsrc[:, :, 6144:6208], [], [kst])
    emit_cast(P, cast_engine(C), wdt, st[:, :, 0:64], [kst], [kwdt])
    A.release(m0)
    ng16, kng16 = C.vec["ssd_ng16"]

    xbcT = [A.alloc(f"xbcT{i}", [32, 128], BF16) for i in range(2)]
    steps = [(ck, 1) for ck in range(NB - 1, -1, -1)] + [(ck, 0) for ck in range(NB)]

    def load_xbc(i):
        ck_ = steps[i][0]
        xb_, kxb = xbcT[i % 2]
        P.dma("sp", xb_, C.XBC.rearrange("c p s -> p c s")[:, :, ck_ * 128:(ck_ + 1) * 128],
              [("XBC", c) for c in range(32)], [kxb])
    xs_tm, kxs = A.alloc("xs_tm", [32, 64], BF16)
    B_tm, kbt = A.alloc("B_tm", [8, 128], BF16)
    dt_, kdt = A.alloc("dt", [64], F32)
    a_, ka = A.alloc("a", [64], F32)
    dec, kdec = A.alloc("dec", [96], F32)
    dt2, kdt2 = A.alloc("dt2", [32], F32)
    xdt, kxdt = A.alloc("xdt", [32, 64], BF16)
    xdtE, kxdtE = A.alloc("xdtE", [32, 64], BF16)
    cbm, kcbm = A.alloc("cbm", [8, 128], F32)
    Lb = [A.alloc(f"Lb{i}", [4, 128], F32) for i in range(2)]
    ex = [A.alloc(f"ex{i}", [4, 128], F32) for i in range(2)]
    MT = [A.alloc(f"MT{i}", [4, 128], BF16) for i in range(2)]
    yo = [A.alloc(f"yo{i}", [256], F32) for i in range(2)]
    ydir, kydir = A.alloc("ydir", [2048], F32)
    H, kH = A.alloc("H", [2048], F32)
    Hb, kHb = A.alloc("Hb", [2048], BF16)
    yb_in, kybin = A.alloc("yb_in", [2048], F32)
    sz, ksz = A.alloc("sz", [2048], F32)
    gss, kgss = A.alloc("gss", [8], F32)
    ynb, kynb = A.alloc("ynb", [2048], BF16)
    yT, kyT = A.alloc("yT", [16, 128], BF16)
    xcks = [A.alloc(f"xck{i}", [NDC, 128], F32) for i in range(2)]

    def chunk_step(ck, d, it_, hook=None):
        tk = slice(ck * 128, (ck + 1) * 128)
        xb_, kxb = xbcT[it_ % 2]
        if it_ + 1 < len(steps):
            load_xbc(it_ + 1)
        Lm = masks[:, 0 + 2 * d, :]
        Rm = masks[:, 1 + 2 * d, :]
        Vm = masks[:, 4 + d, :]
        dc0 = d * 32
        for q in range(3):
            ps, kps = psum_bank(C)
            psb = ps.bitcast(BF16)
            for j in range(8):
                c = q * 8 + j
                P.add("pe", lambda e, psb=psb, j=j, c=c, xb_=xb_: e.transpose(
                    psb[:, j * 128:(j + 1) * 128], xb_[:, c, :], C.ident_b), [kxb, C.k_ident_b], [kps])
            if q < 2:
                P.add("act", lambda e, psb=psb, q=q: e.copy(
                    xs_tm.rearrange("p a b -> p (a b)")[:, q * 1024:(q + 1) * 1024], psb), [kps], [kxs])
            else:
                P.add("dve", lambda e, psb=psb: e.tensor_copy(B_tm.rearrange("p a b -> p (a b)"), psb), [kps], [kbt])
        ps, kps = psum_bank(C)
        for kc in range(8):
            P.mm(ps[:, 0:64], h_all[:, kc, tk], wdt[:, kc, :], kc == 0, kc == 7, [hkeys[ck // 4], kwdt], [kps])
        P.add("dve", lambda e, ps=ps: e.tensor_tensor(dt_, ps[:, 0:64], dtb, ALU.add), [kps, kdtb], [kdt])
        P.add("act", lambda e: e.activation(dt_, dt_, AF.Exp), [kdt], [kdt])
        P.add("act", lambda e: e.activation(dt_, dt_, AF.Ln, bias=one_t[:, 0:1]), [kdt, kone], [kdt])
        P.add("dve", lambda e: e.tensor_tensor(a_, dt_, Aneg, ALU.mult), [kdt, kAneg], [ka])
        ps, kps = psum_bank(C)
        P.mm(ps[:, 0:32], Rm, a_[:, dc0:dc0 + 32], True, True, [kmask, ka], [kps])
        P.mm(ps[:, 32:64], Lm, a_[:, dc0:dc0 + 32], True, True, [kmask, ka], [kps])
        P.mm(ps[:, 64:96], ones_f, a_[:, dc0:dc0 + 32], True, True, [konesf, ka], [kps])
        P.add("act", lambda e, ps=ps: e.activation(dec, ps[:, 0:96], AF.Exp), [kps], [kdec])
        P.add("dve", lambda e: e.tensor_tensor(dt2, dt_[:, dc0:dc0 + 32], dec[:, 32:64], ALU.mult), [kdt, kdec], [kdt2])
        P.add("dve", lambda e: e.tensor_tensor(xdt, xs_tm, bc_last(dt_[:, dc0:dc0 + 32], 64), ALU.mult),
              [kxs, kdt], [kxdt])
        P.add("dve", lambda e: e.tensor_tensor(xdtE, xs_tm, bc_last(dt2, 64), ALU.mult), [kxs, kdt2], [kxdtE])
        for half in range(2):
            ps, kps = psum_bank(C)
            for j in range(4):
                g = half * 4 + j
                P.mm(ps[:, j * 128:(j + 1) * 128], xb_[:, 16 + g, :], xb_[:, 24 + g, :], True, True, [kxb], [kps])
            P.add("dve", lambda e, ps=ps, half=half: e.tensor_tensor(
                cbm[:, half * 4:(half + 1) * 4, :], ps.rearrange("p (a b) -> p a b", a=4), bc_mid(Vm, 4), ALU.mult),
                [kps, kmask], [kcbm])
        if hook is not None:
            hook()
        for g in range(8):
            lb, klb = Lb[g % 2]
            ex_, kex = ex[g % 2]
            mt, kmt = MT[g % 2]
            yo_, kyo = yo[g % 2]
            P.add("dve", lambda e, lb=lb, g=g: e.tensor_tensor(
                lb, bc_mid(Lm, 4), bc_last(a_[:, dc0 + g * 4:dc0 + g * 4 + 4], 128), ALU.mult), [kmask, ka], [klb])
            ps, kps = psum_bank(C)
            for r in range(4):
                P.mm(ps[:, r * 128:(r + 1) * 128], lb[:, r, :], Rm, True, True, [klb, kmask], [kps])
            P.add("act", lambda e, ps=ps, ex_=ex_: e.activation(ex_.rearrange("p a b -> p (a b)"), ps, AF.Exp), [kps], [kex])
            P.add("dve", lambda e, mt=mt, ex_=ex_, g=g: e.tensor_tensor(mt, ex_, bc_mid(cbm[:, g, :], 4), ALU.mult),
                  [kex, kcbm], [kmt])
            psy, kpsy = psum_bank(C)
            for r in range(4):
                P.mm(psy[:, r * 64:(r + 1) * 64], mt[:, r, :], xdt[:, g * 4 + r, :], True, True, [kmt, kxdt], [kpsy])
            P.mm(psy[:, 256:512], xb_[:, 24 + g, :], Hb[:, g * 256:(g + 1) * 256], True, True, [kxb, kHb], [kpsy])
            P.add("dve", lambda e, psy=psy, yo_=yo_, g=g: e.tensor_tensor(
                yo_.rearrange("p (a b) -> p a b", a=4), psy[:, 256:512].rearrange("p (a b) -> p a b", a=4),
                bc_last(dec[:, g * 4:g * 4 + 4], 64), ALU.mult), [kpsy, kdec], [kyo])
            P.add("dve", lambda e, psy=psy, yo_=yo_, g=g: e.tensor_tensor(
                ydir[:, g * 256:(g + 1) * 256], psy[:, 0:256], yo_, ALU.add), [kpsy, kyo], [(kydir, g)])
            pss, kpss = psum_bank(C)
            P.mm(pss[:, 0:256], B_tm[:, g, :], xdtE[:, g * 4:g * 4 + 4, :].rearrange("p a b -> p (a b)"), True, True,
                 [kbt, kxdtE], [kpss])
            P.add("dve", lambda e, g=g: e.tensor_tensor(
                H[:, g * 256:(g + 1) * 256].rearrange("p (a b) -> p a b", a=4),
                H[:, g * 256:(g + 1) * 256].rearrange("p (a b) -> p a b", a=4),
                bc_last(dec[:, 64 + g * 4:64 + g * 4 + 4], 64), ALU.mult), [(kH, g), kdec], [(kH, g)])
            P.add("dve", lambda e, pss=pss, g=g: e.tensor_tensor(
                H[:, g * 256:(g + 1) * 256], H[:, g * 256:(g + 1) * 256], pss[:, 0:256], ALU.add),
                [(kH, g), kpss], [(kH, g)])
            P.add("act", lambda e, g=g: e.copy(Hb[:, g * 256:(g + 1) * 256], H[:, g * 256:(g + 1) * 256]),
                  [(kH, g)], [kHb])

    ykeys = [(kydir, g) for g in range(8)]
    P.add("pool", lambda e: e.memset(H, 0.0), [], [(kH, g) for g in range(8)])
    P.add("pool", lambda e: e.memset(Hb, 0.0), [], [kHb])
    it_ = 0
    load_xbc(0)
    for ck in range(NB - 1, -1, -1):
        chunk_step(ck, 1, it_)
        it_ += 1
        P.dma("sp", C.YB[ck], ydir, ykeys, [("YB", ck)])
        tk = slice(ck * 128, (ck + 1) * 128)
        for zc in range(4):
            ps, kps = psum_bank(C)
            for kc in range(8):
                P.mm(ps, h_all[:, kc, tk], wz[:, kc, zc * 512:(zc + 1) * 512], kc == 0, kc == 7,
                     [hkeys[ck // 4], kwz], [kps])
            P.add("act", lambda e, ps=ps, zc=zc: e.activation(sz[:, zc * 512:(zc + 1) * 512], ps, AF.Silu), [kps], [ksz])
        P.dma("sp", C.SZ[ck], sz, [ksz], [("SZ", ck)])
    wout = wz.rearrange("p a b -> p (a b)").rearrange("p (k n) -> p k n", k=16)
    kwout = kwz
    stg = [(yb_in.rearrange("p (k n) -> p k n", k=2), kybin), (sz.rearrange("p (k n) -> p k n", k=2), ksz)]
    osrc = C.d_ssd_wout.ap().rearrange("(kc p) n -> p kc n", p=128)
    for i_, c0 in enumerate(range(0, 16, 2)):
        st, kst = stg[i_ % 2]
        P.dma("sp", st, osrc[:, c0:c0 + 2, :], [], [kst])
        emit_cast(P, cast_engine(C), wout[:, c0:c0 + 2, :], st, [kst], [kwout])
    P.add("pool", lambda e: e.memset(H, 0.0), [], [(kH, g) for g in range(8)])
    P.add("pool", lambda e: e.memset(Hb, 0.0), [], [kHb])
    deferred = [None]
    for ck in range(NB):
        tk = slice(ck * 128, (ck + 1) * 128)
        P.dma("act", yb_in, C.YB[ck], [("YB", ck)], [kybin])
        xck, kxck = xcks[ck % 2]
        P.dma("act", xck, C.xT[:, :, tk], [("xT", ck // 4)], [kxck])
        chunk_step(ck, 0, it_, hook=deferred[0])
        it_ += 1
        P.add("dve", lambda e: e.tensor_tensor(ydir, ydir, yb_in, ALU.add), ykeys + [kybin], ykeys)
        P.add("dve", lambda e: e.tensor_tensor(yb_in.rearrange("p (a b) -> p a b", a=32), xs_tm, bc_last(dsk, 64), ALU.mult),
              [kxs, kdsk], [kybin])
        P.add("dve", lambda e: e.tensor_tensor(ydir, ydir, yb_in, ALU.add), ykeys + [kybin], ykeys)
        P.dma("sp", sz, C.SZ[ck], [("SZ", ck)], [ksz])
        P.add("dve", lambda e: e.tensor_tensor(ydir, ydir, sz, ALU.mult), ykeys + [ksz], ykeys)
        P.add("act", lambda e: e.activation(sz, ydir, AF.Square), ykeys, [ksz])
        P.add("dve", lambda e: e.reduce_sum(gss, sz.rearrange("p (a b) -> p a b", a=8), AX.X), [ksz], [kgss])
        P.add("act", lambda e: e.activation(gss, gss, AF.Ln, bias=C.eps_t[:, 0:1], scale=1.0 / 256), [kgss, C.k_eps], [kgss])
        P.add("act", lambda e: e.activation(gss, gss, AF.Exp, scale=-0.5), [kgss], [kgss])
        P.add("dve", lambda e: e.tensor_tensor(ydir.rearrange("p (a b) -> p a b", a=8), ydir.rearrange("p (a b) -> p a b", a=8),
                                               bc_last(gss, 256), ALU.mult), ykeys + [kgss], ykeys)
        P.add("act", lambda e: e.copy(ynb, ydir), ykeys, [kynb])
        def make_partB(ck=ck, tk=tk, xck=xck, kxck=kxck):
            def partB():
                for q in range(2):
                    ps, kps = psum_bank(C)
                    psb = ps.bitcast(BF16)
                    for j in range(8):
                        c = q * 8 + j
                        P.add("pe", lambda e, psb=psb, j=j, c=c: e.transpose(
                            psb[:, j * 128:(j + 1) * 128], ynb[:, c * 128:(c + 1) * 128], C.ident_b), [kynb, C.k_ident_b], [kps])
                    for j in range(8):
                        c = q * 8 + j
                        P.add("act", lambda e, psb=psb, j=j, c=c: e.activation(
                            yT[:, c, :], psb[:, j * 128:(j + 1) * 128], AF.Identity, scale=ng16[:, c:c + 1]),
                            [kps, kng16], [kyT])
                for half in range(2):
                    ps, kps = psum_bank(C)
                    for j in range(4):
                        dc = half * 4 + j
                        for kc in range(16):
                            P.mm(ps[:, j * 128:(j + 1) * 128], wout[:, kc, dc * 128:(dc + 1) * 128], yT[:, kc, :],
                                 kc == 0, kc == 15, [kwout, kyT], [kps])
                    for j in range(4):
                        dc = half * 4 + j
                        P.add("dve", lambda e, ps=ps, j=j, dc=dc: e.scalar_tensor_tensor(
                            xck[:, dc, :], ps[:, j * 128:(j + 1) * 128], mv["gate"][:, 8 + dc:8 + dc + 1], xck[:, dc, :],
                            ALU.mult, ALU.add), [kps, mv["kgate"], kxck], [kxck])
                P.dma("act", C.xT[:, :, tk], xck, [kxck], [("xT", ck // 4)])

            return partB
        deferred[0] = make_partB()
    deferred[0]()
    A.release(m_phase)

def build_program(stages, seq=4096, debug=False):
    global S
    S = seq
    nc = bass.Bass("TRN2", target_bir_lowering=False)
    C = Ctx()
    C.debug = debug
    C.dbg_off = 0
    C.dbg_map = {}
    C.dbg_keys = []
    C.nc = nc
    C.P = Prog(nc)
    C.ps_i = 0
    C.bar_gidx = 0
    C.cast_i = 0
    dt = nc.dram_tensor
    C.d_x = dt("x", [S, D], F32, kind="ExternalInput")
    C.d_out = dt("out", [S, D], F32, kind="ExternalOutput")
    C.d_ident = dt("ident", [128, 128], F32, kind="ExternalInput")
    if debug:
        C.d_dbg = dt("dbg", [128, 8192], F32, kind="ExternalOutput")
    C.d_w_mod = dt("w_mod", [2, D, 9 * D], F32, kind="ExternalInput")
    C.d_wg = dt("ffn_w_gate", [2, 2, D, DFF], F32, kind="ExternalInput")
    C.d_wu = dt("ffn_w_up", [2, 2, D, DFF], F32, kind="ExternalInput")
    C.d_wd = dt("ffn_w_down", [2, 2, DFF, D], F32, kind="ExternalInput")
    C.d_pm = dt("pmT", [96, 96], F32, kind="ExternalInput")
    C.d_pos = dt("pos", [128, S], I32, kind="ExternalInput")
    C.d_mla_win = dt("mla_w_in", [D, 672], F32, kind="ExternalInput")
    C.d_mla_wuq = dt("mla_w_uq", [384, 1536], F32, kind="ExternalInput")
    C.d_mla_wukv = dt("mla_w_ukv", [256, 2048], F32, kind="ExternalInput")
    C.d_mla_wout = dt("mla_w_out", [D, D], F32, kind="ExternalInput")
    C.OT = dt("OT_scr", [8, 128, S], BF16, kind="Internal").ap()
    C.d_masks = dt("masks", [128, 768], F32, kind="ExternalInput")
    C.d_ssd_win = dt("ssd_w_in", [D, 6208], F32, kind="ExternalInput")
    C.d_ssd_wout = dt("ssd_w_out", [2048, D], F32, kind="ExternalInput")
    C.SZ = dt("SZ_scr", [S // 128, 128, 2048], F32, kind="Internal").ap()
    C.XBC = dt("XBC_scr", [32, 128, S], BF16, kind="Internal").ap()
    C.YB = dt("YB_scr", [S // 128, 128, 2048], F32, kind="Internal").ap()
    C.d_vecs = {}
    for name, n in VEC_SPECS:
        C.d_vecs[name] = dt("v_" + name, [128, n], F32, kind="ExternalInput")
    C.xT = dt("xT_scr", [128, NDC, S], F32, kind="Internal").ap()
    C.WGU = [dt(f"wgu_scr{i}", [NF, 128, 2, 8, 128], BF16, kind="Internal").ap() for i in range(4)]
    C.WD = [dt(f"wd_scr{i}", [NDC, 128, NF, 128], BF16, kind="Internal").ap() for i in range(4)]

    ARENA_BYTES = 207 * 1024
    with ExitStack() as es:
        ah = es.enter_context(nc.sbuf_tensor("arena", [128, ARENA_BYTES // 4], F32))
        C.arena = Arena(ah, ARENA_BYTES, C.P)
        C.ps = [es.enter_context(nc.psum_tensor(f"ps{i}", [128, 512], F32))[:] for i in range(8)]
        eng_sems = {e: es.enter_context(nc.semaphore(f"sem_{e}")) for e in ENGS}
        dma_sems = [es.enter_context(nc.semaphore(f"dsem{i}")) for i in range(N_DMA_SEMS)]

        setup_consts(C)
        if debug:
            C.dbg_stage, _ = C.arena.alloc('dbg_stage', [3584], F32)
        load_transpose_x(C)
        compute_mod(C)
        for (l, w) in stages.get("ffn", []):
            convert_ffn_weights(C, l, w)
        for st_ in stages.get("order", []):
            if BARRIERS:
                phase_barrier(C)
            if st_[0] == "ffn":
                ffn_phase(C, st_[1], st_[2])
            elif st_[0] == "mla":
                mla_phase(C)
            elif st_[0] == "ssd":
                ssd_phase(C)
        store_transpose_out(C)
        C.P.emit(eng_sems, dma_sems)
    C.nc = nc
    return C


VEC_SPECS = [("c", 8), ("b_mod0", 72), ("b_mod1", 72), ("norm_g0", 24), ("norm_g1", 24),
             ("mla_qg", 1), ("mla_kg", 1), ("mla_qng", 3), ("mla_kvng", 2), ("invf", 1),
             ("ssd_cw", 160), ("ssd_cb", 32), ("ssd_dtb", 64), ("ssd_alog", 64), ("ssd_dskip", 32), ("ssd_ng16", 16)]


def _consts():
    inv = (10000.0 ** (-np.arange(0, 32, 2, dtype=np.float32) / 32)).astype(np.float32)
    invf = np.zeros((128, 1), np.float32)
    invf[64:80, 0] = inv
    invf[80:96, 0] = inv
    pm = np.zeros((96, 96), np.float32)
    for i in range(16):
        pm[80 + i, 64 + i] = -1.0
        pm[64 + i, 80 + i] = 1.0
    return invf, pm


INVF, PMT = _consts()


def _masks():
    k = np.arange(128)[:, None]
    j = np.arange(128)[None, :]
    Lf = (k > j); Rf = (k <= j); Lb = (k < j); Rb = (k >= j)
    Vf = (j >= k)
    Vb = (j <= k)
    return np.concatenate([m.astype(np.float32) for m in (Lf, Rf, Lb, Rb, Vf, Vb)], axis=1)


MASKS = _masks()


def host_vecs(inputs, b):
    f = np.float32
    v = {}
    v["c"] = np.ascontiguousarray(inputs["c"][b].reshape(8, 128).T.astype(f))
    for l in range(2):
        v[f"b_mod{l}"] = np.ascontiguousarray(inputs["b_mod"][l].reshape(72, 128).T.astype(f))
        v[f"norm_g{l}"] = np.ascontiguousarray(inputs["norm_g"][l].reshape(24, 128).T.astype(f))
    def col(a, n=128):
        o = np.zeros((128, 1), f)
        o[:len(a), 0] = a
        return o
    v["mla_qg"] = col(inputs["mla_q_head_g"][0])
    v["mla_kg"] = col(inputs["mla_k_head_g"][0])
    v["mla_qng"] = np.ascontiguousarray(inputs["mla_q_norm_g"][0].reshape(3, 128).T.astype(f))
    v["mla_kvng"] = np.ascontiguousarray(inputs["mla_kv_norm_g"][0].reshape(2, 128).T.astype(f))
    v["invf"] = INVF
    cwt = inputs["ssd_conv_w"][0]
    v["ssd_cw"] = np.ascontiguousarray(cwt.reshape(5, 32, 128).transpose(2, 1, 0).reshape(128, 160).astype(f))
    v["ssd_cb"] = np.ascontiguousarray(inputs["ssd_conv_b"][0].reshape(32, 128).T.astype(f))
    v["ssd_dtb"] = np.ascontiguousarray(np.broadcast_to(inputs["ssd_dt_bias"][0].reshape(1, 64), (128, 64)).astype(f))
    v["ssd_alog"] = np.ascontiguousarray(np.broadcast_to(inputs["ssd_a_log"][0].reshape(1, 64), (128, 64)).astype(f))
    v["ssd_ng16"] = np.ascontiguousarray(inputs["ssd_norm_g"][0].reshape(16, 128).T.astype(f))
    v["ssd_dskip"] = np.ascontiguousarray(np.broadcast_to(inputs["ssd_d"][0].reshape(1, 32), (128, 32)).astype(f))
    return v


def run(inputs, stages, seq=4096, cores=8, debug=False):
    C = build_program(stages, seq, debug)
    nc = C.nc
    ident = np.eye(128, dtype=np.float32)
    in_maps = []
    for b in range(cores):
        m = {
            "x": np.ascontiguousarray(inputs["x"][b][:seq]),
            "ident": ident,
            "w_mod": inputs["w_mod"],
            "ffn_w_gate": inputs["ffn_w_gate"],
            "ffn_w_up": inputs["ffn_w_up"],
            "ffn_w_down": inputs["ffn_w_down"],
            "pmT": PMT,
            "masks": MASKS,
            "ssd_w_in": inputs["ssd_w_in"][0], "ssd_w_out": inputs["ssd_w_out"][0],

            "pos": np.ascontiguousarray(np.broadcast_to(inputs["positions"][b][None, :seq], (128, seq)).astype(np.int32)),
            "mla_w_in": inputs["mla_w_in"][0], "mla_w_uq": inputs["mla_w_uq"][0],
            "mla_w_ukv": inputs["mla_w_ukv"][0], "mla_w_out": inputs["mla_w_out"][0],
        }
        for k, a in host_vecs(inputs, b).items():
            m["v_" + k] = a
        in_maps.append(m)
    res = run_bass_kernel_spmd(nc, in_maps, core_ids=list(range(cores)))
    out = np.stack([r["out"] for r in res.results], axis=0)
    if debug:
        return out, {k: res.results[0]["dbg"][:, o:o + n] for k, (o, n) in C.dbg_map.items()}
    return out


def kernel(**inputs):
    inputs = {k: np.asarray(v) for k, v in inputs.items()}
    stages = {
        "ffn": [(0, 0), (0, 1), (1, 0), (1, 1)],
        "order": [("ffn", 0, 0), ("ssd",), ("ffn", 0, 1), ("ffn", 1, 0), ("mla",), ("ffn", 1, 1)],
    }
    return run(inputs, stages).astype(np.float32)
```

```python
import numpy as np
from contextlib import ExitStack
import concourse.bass as bass
import concourse.mybir as mybir
from concourse.bass_utils import run_bass_kernel_spmd

F32 = mybir.dt.float32
BF16 = mybir.dt.bfloat16
I32 = mybir.dt.int32
AF = mybir.ActivationFunctionType
ALU = mybir.AluOpType
AX = mybir.AxisListType

D = 1024
S = 4096
DFF = 2816
NF = DFF // 128
NDC = D // 128
EPS = 1e-6
N_DMA_SEMS = 40
DBG_M = 0
BARRIERS = False
SKIP_SAME_ENG_NONRAW = False
ENGS = ("pe", "act", "dve", "pool", "sp")


class Op:
    __slots__ = ("eng", "fn", "deps", "sig", "seq", "dma", "semid", "semval", "prev", "pos", "gidx", "raw")


class Prog:
    def __init__(self, nc):
        self.nc = nc
        self.ops = {e: [] for e in ENGS}
        self.lastw = {}
        self.readers = {}
        self.ndma = 0
        self.dma_last = [None] * N_DMA_SEMS
        self.dma_cnt = [0] * N_DMA_SEMS
        self.nops = 0
        self.bases = set()
        self.touched = {}
        self.inherit = {}
        self.seen = set()

    def base_of(self, k):
        for _ in range(4):
            if k in self.bases:
                return k
            if isinstance(k, tuple) and len(k):
                k = k[0]
            else:
                return None
        return None

    def add(self, eng, fn, reads=(), writes=(), dma=False):
        op = Op()
        op.eng = eng
        op.fn = fn
        op.dma = dma
        op.sig = False
        op.seq = 0
        op.gidx = self.nops
        self.nops += 1
        deps = set()
        raw = set()
        for k in reads:
            w = self.lastw.get(k)
            if w is not None:
                raw.add(w)
        op.raw = raw
        for k in list(reads) + list(writes):
            b = self.base_of(k)
            if b is None:
                continue
            if k not in self.seen:
                self.seen.add(k)
                deps.update(self.inherit.get(b, ()))
            t = self.touched.setdefault(b, {})
            if dma:
                t[("dma", op.gidx)] = op
            else:
                t[eng] = op
        for k in reads:
            w = self.lastw.get(k)
            if w is not None:
                deps.add(w)
        for k in writes:
            w = self.lastw.get(k)
            if w is not None:
                deps.add(w)
            for r in self.readers.get(k, ()):
                deps.add(r)
        for k in reads:
            self.readers.setdefault(k, []).append(op)
        for k in writes:
            self.lastw[k] = op
            self.readers[k] = []
        deps.discard(op)
        op.prev = None
        if dma:
            s = self.ndma % N_DMA_SEMS
            self.ndma += 1
            op.semid = s
            self.dma_cnt[s] += 16
            op.semval = self.dma_cnt[s]
            op.prev = self.dma_last[s]
            self.dma_last[s] = op
        op.deps = deps
        op.pos = len(self.ops[eng])
        self.ops[eng].append(op)
        return op

    def dma(self, eng, out, in_, reads, writes, **kw):
        return self.add(eng, lambda e: e.dma_start(out=out, in_=in_, **kw), reads, writes, dma=True)

    def mm(self, out, lhsT, rhs, start, stop, reads, writes):
        return self.add("pe", lambda e: e.matmul(out, lhsT, rhs, start=start, stop=stop), reads, writes)

    def emit(self, eng_sems, dma_sems):
        nc = self.nc

        def needs_sync(op, d):
            if d.dma:
                return True
            if d.eng == op.eng and not op.dma:
                if op.eng == "pe":
                    return False
                return (d in op.raw) or not SKIP_SAME_ENG_NONRAW
            if d.eng == op.eng and op.dma:
                return True
            return True

        for e in ENGS:
            for op in self.ops[e]:
                for d in op.deps:
                    if needs_sync(op, d) and not d.dma:
                        d.sig = True
        for e in ENGS:
            n = 0
            for op in self.ops[e]:
                if op.sig and not op.dma:
                    n += 1
                    op.seq = n

        def emit_engine(ename, eobj):
            waited = {}
            for op in self.ops[ename]:
                need = {}
                for d in op.deps:
                    if not needs_sync(op, d):
                        continue
                    if d.dma:
                        key = ("d", d.semid)
                        val = d.semval
                    else:
                        key = ("e", d.eng)
                        val = d.seq
                    if need.get(key, 0) < val:
                        need[key] = val
                if op.dma and op.prev is not None:
                    key = ("d", op.prev.semid)
                    if need.get(key, 0) < op.prev.semval:
                        need[key] = op.prev.semval
                pend = []
                for key, val in need.items():
                    if waited.get(key, 0) >= val:
                        continue
                    waited[key] = val
                    sem = dma_sems[key[1]] if key[0] == "d" else eng_sems[key[1]]
                    pend.append((key[0] == "d", sem, val))
                pend.sort(key=lambda t: t[0])
                if op.fn is None:
                    for _, sem, val in pend:
                        eobj.wait_ge(sem, val)
                    continue
                for _, sem, val in pend[:-1]:
                    eobj.wait_ge(sem, val)
                ins = op.fn(eobj)
                if pend:
                    ins._wait_ge(pend[-1][1], pend[-1][2])
                if op.dma:
                    ins.then_inc(dma_sems[op.semid], 16)
                elif op.sig:
                    ins.then_inc(eng_sems[ename], 1)

        with nc.Block() as block:
            @block.tensor
            def _(e):
                emit_engine("pe", e)

            @block.scalar
            def _(e):
                emit_engine("act", e)

            @block.vector
            def _(e):
                emit_engine("dve", e)

            @block.gpsimd
            def _(e):
                emit_engine("pool", e)

            @block.sync
            def _(e):
                emit_engine("sp", e)


class Arena:
    def __init__(self, handle, nbytes, prog):
        self.h = handle
        self.cap = nbytes
        self.top = 0
        self.gen = 0
        self.P = prog
        self.allocs = []

    def mark(self):
        return self.top

    def release(self, m):
        self.top = m

    def alloc(self, name, shape, dtype):
        esz = 2 if dtype == BF16 else 4
        n = 1
        for s_ in shape:
            n *= s_
        nbytes = (n * esz + 63) // 64 * 64
        off = self.top
        assert off + nbytes <= self.cap, f"SBUF arena overflow allocating {name}: {off}+{nbytes}>{self.cap}"
        self.top += nbytes
        ap = self.h[:, off // 4:(off + nbytes) // 4]
        if dtype != F32:
            ap = ap.bitcast(dtype)
        ap = ap[:, 0:n]
        if len(shape) == 2:
            ap = ap.rearrange("p (a b) -> p a b", a=shape[0])
        elif len(shape) == 3:
            ap = ap.rearrange("p (a b c) -> p a b c", a=shape[0], b=shape[1])
        elif len(shape) == 4:
            ap = ap.rearrange("p (a b c d) -> p a b c d", a=shape[0], b=shape[1], c=shape[2])
        self.gen += 1
        key = (name, self.gen)
        P = self.P
        P.bases.add(key)
        inh = set()
        for (a0, a1, ok) in self.allocs:
            if a0 < off + nbytes and off < a1:
                inh.update(P.touched.get(ok, {}).values())
                inh.update(P.inherit.get(ok, ()))
        P.inherit[key] = inh
        self.allocs = [(a0, a1, ok) for (a0, a1, ok) in self.allocs if not (a0 >= off and a1 <= off + nbytes)]
        self.allocs.append((off, off + nbytes, key))
        return ap, key


class Ctx:
    pass


def dbg(C, name, ap, key, n):
    if not C.debug:
        return
    P, A = C.P, C.arena
    st = C.dbg_stage[:, C.dbg_off:C.dbg_off + n]
    kst = ("dbgst", name)
    P.add("pool", lambda e: e.tensor_copy(st, ap), [key], [kst])
    off = C.dbg_off
    C.dbg_off += n
    C.dbg_map[name] = (off, n)
    P.dma("sp", C.d_dbg.ap()[:, off:off + n], st, [kst], [("DBG", name)])
    C.dbg_keys.append(("DBG", name))


def rr(lst, i):
    return lst[i % len(lst)]


def setup_consts(C):
    P, A = C.P, C.arena
    C.ident_f, C.k_ident_f = A.alloc("ident_f", [128], F32)
    C.ident_b, C.k_ident_b = A.alloc("ident_b", [128], BF16)
    C.onesD_b, C.k_onesD = A.alloc("onesD", [128], BF16)
    P.dma("sp", C.ident_f, C.d_ident.ap(), [], [C.k_ident_f])
    P.add("dve", lambda e: e.tensor_copy(C.ident_b, C.ident_f), [C.k_ident_f], [C.k_ident_b])
    P.add("pool", lambda e: e.memset(C.onesD_b, 1.0 / D), [], [C.k_onesD])
    C.eps_t, C.k_eps = A.alloc("eps_t", [1], F32)
    P.add("pool", lambda e: e.memset(C.eps_t, EPS), [], [C.k_eps])
    C.vec = {}
    for name, t in C.d_vecs.items():
        n = t.ap().shape[1]
        ap, k = A.alloc("v_" + name, [n], F32)
        P.dma("sp", ap, t.ap(), [], [k])
        C.vec[name] = (ap, k)


def phase_barrier(C):
    P = C.P
    last = [P.ops[e][-1] for e in ENGS if P.ops[e]]
    last = [o for o in last if o.fn is not None]
    dmas = [o for e in ENGS for o in P.ops[e] if o.dma and o.gidx >= C.bar_gidx]
    C.bar_gidx = P.nops
    for e in ENGS:
        op = P.add(e, None)
        op.deps.update(last)
        op.deps.update(dmas)
        op.deps.discard(op)


def psum_bank(C):
    i = C.ps_i % 8
    C.ps_i += 1
    return C.ps[i], ("ps", i)


def load_transpose_x(C):
    P, A = C.P, C.arena
    m0 = A.mark()
    xin = [A.alloc(f"xin{i}", [4, D], F32) for i in range(2)]
    xtt = [A.alloc(f"xtt{i}", [NDC, 512], F32) for i in range(2)]
    xd = C.d_x.ap().rearrange("(g j p) d -> g p j d", j=4, p=128)
    for g in range(S // 512):
        xi, kxi = xin[g % 2]
        xt, kxt = xtt[g % 2]
        P.dma("sp", xi, xd[g], [], [kxi])
        for dc in range(NDC):
            ps, kps = psum_bank(C)
            for j in range(4):
                P.add("pe", lambda e, ps=ps, xi=xi, j=j, dc=dc: e.transpose(
                    ps[:, j * 128:(j + 1) * 128], xi[:, j, dc * 128:(dc + 1) * 128], C.ident_f),
                    [kxi, C.k_ident_f], [kps])
            eng = "act" if dc % 2 == 0 else "dve"
            if eng == "act":
                P.add("act", lambda e, ps=ps, xt=xt, dc=dc: e.copy(xt[:, dc, :], ps), [kps], [kxt])
            else:
                P.add("dve", lambda e, ps=ps, xt=xt, dc=dc: e.tensor_copy(xt[:, dc, :], ps), [kps], [kxt])
        P.dma("sp", C.xT[:, :, g * 512:(g + 1) * 512], xt, [kxt], [("xT", g)])
    A.release(m0)


def store_transpose_out(C):
    P, A = C.P, C.arena
    m0 = A.mark()
    xtt = [A.alloc(f"oxt{i}", [NDC, 512], F32) for i in range(2)]
    xo = [A.alloc(f"oxo{i}", [4, D], F32) for i in range(2)]
    od = C.d_out.ap().rearrange("(g j p) d -> g p j d", j=4, p=128)
    for g in range(S // 512):
        xt, kxt = xtt[g % 2]
        xo_, kxo = xo[g % 2]
        P.dma("sp", xt, C.xT[:, :, g * 512:(g + 1) * 512], [("xT", g)], [kxt])
        for j in range(4):
            for half in range(2):
                ps, kps = psum_bank(C)
                for q in range(4):
                    dc = half * 4 + q
                    P.add("pe", lambda e, ps=ps, xt=xt, j=j, dc=dc, q=q: e.transpose(
                        ps[:, q * 128:(q + 1) * 128], xt[:, dc, j * 128:(j + 1) * 128], C.ident_f),
                        [kxt, C.k_ident_f], [kps])
                if half == 0:
                    P.add("act", lambda e, ps=ps, xo_=xo_, j=j: e.copy(xo_[:, j, 0:512], ps), [kps], [kxo])
                else:
                    P.add("dve", lambda e, ps=ps, xo_=xo_, j=j: e.tensor_copy(xo_[:, j, 512:1024], ps), [kps], [kxo])
        P.dma("sp", od[g], xo_, [kxo], [("OUT", g)])
    A.release(m0)
    P.add("sp", None, [("OUT", g) for g in range(S // 512)] + C.dbg_keys, [])


def compute_mod(C):
    P, A = C.P, C.arena
    cvec, kc_ = C.vec["c"]
    C.cond, C.k_cond = A.alloc("cond", [8], F32)
    P.add("act", lambda e: e.activation(C.cond, cvec, AF.Silu), [kc_], [C.k_cond])
    C.mod = []
    m0 = None
    for l in range(2):
        mod, kmod = A.alloc(f"mod{l}", [72], F32)
        C.mod.append((mod, kmod))
    m0 = A.mark()
    wb = [A.alloc(f"wmod{i}", [8, 1024], F32) for i in range(2)]
    it = 0
    for l in range(2):
        mod, kmod = C.mod[l]
        bm, kbm = C.vec[f"b_mod{l}"]
        wd = C.d_w_mod.ap()[l].rearrange("(kc p) n -> p kc n", p=128)
        ps, kps = psum_bank(C)
        for cb in range(9):
            w, kw = wb[it % 2]
            it += 1
            P.dma("sp", w, wd[:, :, cb * 1024:(cb + 1) * 1024], [], [kw])
            for j in range(8):
                col = cb * 8 + j
                for kc in range(8):
                    P.mm(ps[:, col:col + 1], w[:, kc, j * 128:(j + 1) * 128], C.cond[:, kc:kc + 1],
                         kc == 0, kc == 7, [kw, C.k_cond], [kps])
        P.add("dve", lambda e, mod=mod, ps=ps, bm=bm: e.tensor_tensor(mod, ps[:, 0:72], bm, ALU.add),
              [kps, kbm], [kmod])
    A.release(m0)
    C.modv = []
    for l in range(2):
        mod, kmod = C.mod[l]
        g, kg = C.vec[f"norm_g{l}"]
        a, ka = A.alloc(f"moda{l}", [24], F32)
        gt, kgt = A.alloc(f"modg{l}", [24], F32)
        for sub in range(3):
            sc = mod[:, (sub * 3 + 1) * 8:(sub * 3 + 2) * 8]
            P.add("dve", lambda e, a=a, sub=sub, sc=sc, g=g: e.scalar_tensor_tensor(
                a[:, sub * 8:(sub + 1) * 8], sc, 1.0, g[:, sub * 8:(sub + 1) * 8], ALU.add, ALU.mult),
                [kmod, kg], [ka])
            gsrc = mod[:, (sub * 3 + 2) * 8:(sub * 3 + 3) * 8]
            fac = 1.0 if sub == 1 else 0.5
            P.add("dve", lambda e, gt=gt, sub=sub, gsrc=gsrc, fac=fac: e.tensor_scalar(
                gt[:, sub * 8:(sub + 1) * 8], gsrc, fac, None, ALU.mult), [kmod], [kgt])
        C.modv.append(dict(a=a, ka=ka, gate=gt, kgate=kgt, mod=mod, kmod=kmod))
        if l == 0:
            dbg(C, 'mod0', mod, kmod, 72)
            dbg(C, 'a0', a, ka, 24)
            dbg(C, 'gate0', gt, kgt, 24)


def cast_engine(C):
    e = ("dve", "act")[C.cast_i % 2]
    C.cast_i += 1
    return e


def emit_cast(P, eng, out, in_, reads, writes):
    if eng == "act":
        P.add("act", lambda e: e.copy(out, in_), reads, writes)
    else:
        P.add(eng, lambda e: e.tensor_copy(out, in_), reads, writes)


def convert_ffn_gen(C, l, w, stg, stb, nfmax, engs):
    P = C.P
    idx = l * 2 + w
    it = 0
    for gi, src in enumerate((C.d_wg, C.d_wu)):
        sd = src.ap()[l, w].rearrange("(kc p) n -> p kc n", p=128)
        for fb in range(0, NF, nfmax):
            nf = min(nfmax, NF - fb)
            sf, ksf = stg[it % len(stg)]
            sb, ksb = stb[it % len(stb)]
            eng = engs[it % len(engs)]
            it += 1
            sfv = sf[:, 0:8 * nf * 128].rearrange("p (kc n) -> p kc n", kc=8)
            P.dma("sp", sfv, sd[:, :, fb * 128:(fb + nf) * 128], [], [ksf])
            sbv = sb[:, 0:nf * 8 * 128].rearrange("p (f kc m) -> p f kc m", f=nf, kc=8)
            emit_cast(P, eng, sbv, sfv.rearrange("p kc (f m) -> p f kc m", f=nf), [ksf], [ksb])
            dst = C.WGU[idx][fb:fb + nf, :, gi, :, :].rearrange("f p kc m -> p f (kc m)")
            P.dma("sp", dst, sbv.rearrange("p f kc m -> p f (kc m)"), [ksb], [("WGU", idx, f_) for f_ in range(fb, fb + nf)])
            yield
    sd = C.d_wd.ap()[l, w].rearrange("(fc p) n -> p fc n", p=128)
    for fb in range(0, NF, nfmax):
        nf = min(nfmax, NF - fb)
        sf, ksf = stg[it % len(stg)]
        sb, ksb = stb[it % len(stb)]
        eng = engs[it % len(engs)]
        it += 1
        sfv = sf[:, 0:nf * 1024].rearrange("p (fc n) -> p fc n", fc=nf)
        P.dma("sp", sfv, sd[:, fb:fb + nf, :], [], [ksf])
        sbv = sb[:, 0:8 * nf * 128].rearrange("p (dc fc m) -> p dc fc m", dc=8, fc=nf)
        emit_cast(P, eng, sbv, sfv.rearrange("p fc (dc m) -> p dc fc m", dc=8), [ksf], [ksb])
        dst = C.WD[idx][:, :, fb:fb + nf, :].rearrange("dc p fc m -> p dc (fc m)")
        P.dma("sp", dst, sbv.rearrange("p dc fc m -> p dc (fc m)"), [ksb], [("WD", idx, dc) for dc in range(8)])
        yield
    C.converted.add((l, w))


def convert_ffn_weights(C, l, w):
    if (l, w) in C.converted:
        return
    A = C.arena
    m0 = A.mark()
    stg = [A.alloc(f"cst{i}", [4096], F32) for i in range(3)]
    stb = [A.alloc(f"csb{i}", [4096], BF16) for i in range(3)]
    for _ in convert_ffn_gen(C, l, w, stg, stb, 4, ("dve", "act")):
        pass
    A.release(m0)


def norm_modulate(C, xt, kxt, h, kh, T, l, sub, scratch):
    P = C.P
    mv = C.modv[l]
    sq, ksq, rstd, krstd, tmp, ktmp = scratch
    shift = mv["mod"][:, (sub * 3) * 8:(sub * 3 + 1) * 8]
    for st in range(T // 512):
        sl = slice(st * 512, (st + 1) * 512)
        for dc in range(NDC):
            P.add("act", lambda e, dc=dc, sl=sl: e.activation(sq[:, dc, sl], xt[:, dc, sl], AF.Square),
                  [kxt], [(ksq, st)])
        ps, kps = psum_bank(C)
        for dc in range(NDC):
            P.mm(ps, C.onesD_b, sq[:, dc, sl], dc == 0, dc == NDC - 1, [(ksq, st), C.k_onesD], [kps])
        P.add("act", lambda e, ps=ps, sl=sl: e.activation(rstd[:, sl], ps, AF.Ln, bias=C.eps_t[:, 0:1]),
              [kps, C.k_eps], [(krstd, st)])
        P.add("act", lambda e, sl=sl: e.activation(rstd[:, sl], rstd[:, sl], AF.Exp, scale=-0.5),
              [(krstd, st)], [(krstd, st)])
        for dc in range(NDC):
            tm, ktm = tmp[dc % len(tmp)]
            eng = "dve"
            P.add(eng, lambda e, tm=tm, dc=dc, sl=sl: e.tensor_tensor(tm, xt[:, dc, sl], rstd[:, sl], ALU.mult),
                  [kxt, (krstd, st)], [ktm])
            P.add("act", lambda e, tm=tm, dc=dc, sl=sl: e.activation(
                h[:, dc, sl], tm, AF.Identity, bias=shift[:, dc:dc + 1],
                scale=mv["a"][:, sub * 8 + dc:sub * 8 + dc + 1]),
                [ktm, mv["ka"], mv["kmod"]], [(kh, st)])


def ffn_phase(C, l, w):
    P, A = C.P, C.arena
    convert_ffn_weights(C, l, w)
    idx = l * 2 + w
    sub = 0 if w == 0 else 2
    mv = C.modv[l]
    T = 1024
    m0 = A.mark()
    xb = [A.alloc(f"fx{i}", [NDC, T], F32) for i in range(2)]
    h, kh = A.alloc("fh", [NDC, T], BF16)
    act, kact = A.alloc("fact", [NF, T], BF16)
    sq, ksq = A.alloc("fsq", [NDC, T], BF16)
    rstd, krstd = A.alloc("frstd", [T], F32)
    tmp = [A.alloc(f"ftmp{i}", [512], F32) for i in range(3)]
    sg = [A.alloc(f"fsg{i}", [512], F32) for i in range(3)]
    wgu = [A.alloc(f"fwgu{i}", [2, 8, 128], BF16) for i in range(4)]
    wdb = [A.alloc(f"fwd{i}", [NF, 128], BF16) for i in range(3)]
    NM = S // T
    items = []
    for m in range(NM):
        for f in range(NF):
            items.append(("g", f))
        for dc in range(NDC):
            items.append(("d", dc))
    issued = [0]
    cnt = {"g": 0, "d": 0}
    slot_of = {}

    def prefetch(upto):
        while issued[0] < min(upto, len(items)):
            kind, j = items[issued[0]]
            if kind == "g":
                buf, kb = wgu[cnt["g"] % 4]
                cnt["g"] += 1
                P.dma("sp", buf, C.WGU[idx][j].rearrange("p g kc m -> p g kc m"), [("WGU", idx, j)], [kb])
            else:
                buf, kb = wdb[cnt["d"] % 3]
                cnt["d"] += 1
                P.dma("sp", buf, C.WD[idx][j], [("WD", idx, j)], [kb])
            slot_of[issued[0]] = (buf, kb)
            issued[0] += 1

    def load_x(m):
        xt, kxt = xb[m % 2]
        P.dma("act", xt, C.xT[:, :, m * T:(m + 1) * T], [("xT", 2 * m), ("xT", 2 * m + 1)], [kxt])

    load_x(0)
    pos = 0
    for m in range(NM):
        xt, kxt = xb[m % 2]
        prefetch(pos + 3)
        norm_modulate(C, xt, kxt, h, kh, T, l, sub, (sq, ksq, rstd, krstd, tmp, None))
        if m + 1 < NM:
            load_x(m + 1)
        if m == DBG_M and idx == 0:
            dbg(C, 'rstd', rstd[:, 0:512], (krstd, 0), 512)
            dbg(C, 'h0', h[:, 0, 0:512], (kh, 0), 512)
            dbg(C, 'h7', h[:, 7, 0:512], (kh, 0), 512)
        for f in range(NF):
            prefetch(pos + 3)
            wbuf, kwb = slot_of.pop(pos)
            pos += 1
            for st in range(T // 512):
                sl = slice(st * 512, (st + 1) * 512)
                psg, kpsg = psum_bank(C)
                psu, kpsu = psum_bank(C)
                for kc in range(8):
                    P.mm(psg, wbuf[:, 0, kc, :], h[:, kc, sl], kc == 0, kc == 7, [kwb, (kh, st)], [kpsg])
                for kc in range(8):
                    P.mm(psu, wbuf[:, 1, kc, :], h[:, kc, sl], kc == 0, kc == 7, [kwb, (kh, st)], [kpsu])
                s_, ks_ = sg[(f * 2 + st) % 3]
                P.add("act", lambda e, s_=s_, psg=psg: e.activation(s_, psg, AF.Silu), [kpsg], [ks_])
                P.add("dve", lambda e, s_=s_, psu=psu, f=f, sl=sl: e.tensor_tensor(act[:, f, sl], s_, psu, ALU.mult),
                      [ks_, kpsu], [(kact, st)])
        for dc in range(NDC):
            prefetch(pos + 3)
            wbuf, kwb = slot_of.pop(pos)
            pos += 1
            for st in range(T // 512):
                sl = slice(st * 512, (st + 1) * 512)
                pso, kpso = psum_bank(C)
                for f in range(NF):
                    P.mm(pso, wbuf[:, f, :], act[:, f, sl], f == 0, f == NF - 1, [kwb, (kact, st)], [kpso])
                P.add("dve", lambda e, pso=pso, dc=dc, sl=sl, xt=xt: e.scalar_tensor_tensor(
                    xt[:, dc, sl], pso, mv["gate"][:, sub * 8 + dc:sub * 8 + dc + 1], xt[:, dc, sl],
                    ALU.mult, ALU.add), [kpso, mv["kgate"], kxt], [kxt])
        if m == DBG_M and idx == 0:
            dbg(C, 'act0', act[:, 0, 0:512], (kact, 0), 512)
            dbg(C, 'act21', act[:, 21, 0:512], (kact, 0), 512)
            dbg(C, 'xo0', xt[:, 0, 0:512], kxt, 512)
        P.dma("act", C.xT[:, :, m * T:(m + 1) * T], xt, [kxt], [("xT", 2 * m), ("xT", 2 * m + 1)])
    A.release(m0)


PI = float(np.pi)


def const_tile(C, name, val, dtype=F32, n=1):
    ap, k = C.arena.alloc("c_" + name, [n], dtype)
    C.P.add("pool", lambda e: e.memset(ap, val), [], [k])
    return ap, k


def load_cast_weight(C, name, src_ap, shape, eng="sp"):
    P, A = C.P, C.arena
    n = 1
    for s_ in shape:
        n *= s_
    wb, kwb = A.alloc(name, shape, BF16)
    m0 = A.mark()
    CH = 2048
    flat_b = wb
    stg = [A.alloc(f"{name}_st{i}", [CH], F32) for i in range(2)]
    A.release(m0)
    return wb, kwb, stg


def mla_phase(C):
    P, A = C.P, C.arena
    l = 1
    mv = C.modv[l]
    T = 512
    NT = S // T
    NB = S // 128
    SCALE = float(96 ** -0.5)
    m_phase = A.mark()
    ones384, k384 = const_tile(C, "o384", 1.0 / 384, BF16, 128)
    ones256, k256 = const_tile(C, "o256", 1.0 / 256, BF16, 128)
    ones96, k96 = const_tile(C, "o96", 1.0 / 96, BF16, 128)
    onesrow, krow = const_tile(C, "orow", 1.0, F32, 128)
    hpi, khpi = const_tile(C, "hpi", PI / 2)
    nhpi, knhpi = const_tile(C, "nhpi", -PI / 2)
    pmT_f, kpmf = A.alloc("pmT_f", [96], F32)
    pmT, kpm = A.alloc("pmT", [96], BF16)
    P.dma("sp", pmT_f[0:96, :], C.d_pm.ap(), [], [kpmf])
    P.add("dve", lambda e: e.tensor_copy(pmT[0:96, :], pmT_f[0:96, :]), [kpmf], [kpm])
    gq, kgq = C.vec["mla_qg"]
    gk, kgk = C.vec["mla_kg"]
    gql, kgql = C.vec["mla_qng"]
    gkvl, kgkvl = C.vec["mla_kvng"]
    invf, kinvf = C.vec["invf"]
    qn, kqn = A.alloc("qn", [3, S], BF16)
    kvn, kkvn = A.alloc("kvn", [2, S], BF16)
    kpe, kkpe = A.alloc("kpe", [S], F32)
    sqk, ksqk = A.alloc("sqk", [S], BF16)
    COS, kcos = A.alloc("COS", [S], F32)
    SIN, ksin = A.alloc("SIN", [S], F32)
    wuq, kwuq = A.alloc("wuq", [3, 1536], BF16)
    wukv, kwukv = A.alloc("wukv", [2, 2048], BF16)

    m0 = A.mark()
    posi, kposi = A.alloc("posi", [S], I32)
    ang, kang = A.alloc("ang", [S], F32)
    t1, kt1 = A.alloc("rt1", [S], F32)
    ti, kti = A.alloc("rti", [S], I32)
    P.dma("sp", posi, C.d_pos.ap(), [], [kposi])
    P.add("dve", lambda e: e.tensor_copy(ang, posi), [kposi], [kang])
    P.add("dve", lambda e: e.tensor_scalar(ang, ang, invf[:, 0:1], None, ALU.mult), [kang, kinvf], [kang])
    for (dst, kdst, shift) in ((SIN, ksin, 0.0), (COS, kcos, PI / 2)):
        P.add("dve", lambda e, shift=shift: e.tensor_scalar(t1, ang, shift, 1.0 / (2 * PI), ALU.add, ALU.mult),
              [kang], [kt1])
        P.add("dve", lambda e: e.tensor_copy(ti, t1), [kt1], [kti])
        P.add("dve", lambda e: e.tensor_copy(t1, ti), [kti], [kt1])
        P.add("dve", lambda e: e.scalar_tensor_tensor(t1, t1, -2 * PI, ang, ALU.mult, ALU.add), [kt1, kang], [kt1])
        bias_ap = nhpi if shift == 0.0 else None
        if shift == 0.0:
            P.add("act", lambda e: e.activation(t1, t1, AF.Abs, bias=nhpi[:, 0:1]), [kt1, knhpi], [kt1])
        else:
            P.add("act", lambda e: e.activation(t1, t1, AF.Abs), [kt1], [kt1])
        P.add("act", lambda e, dst=dst: e.activation(dst, t1, AF.Sin, bias=hpi[:, 0:1], scale=-1.0),
              [kt1, khpi], [kdst])
    A.release(m0)

    def load_cast(dst, kdst, src, nk, ncol, colchunk):
        stg = [A.alloc(f"wst{i}", [nk, colchunk], F32) for i in range(2)]
        it = 0
        for c0 in range(0, ncol, colchunk):
            cw = min(colchunk, ncol - c0)
            st, kst = stg[it % 2]
            it += 1
            P.dma("sp", st[:, :, 0:cw], src[:, :, c0:c0 + cw], [], [kst])
            emit_cast(P, cast_engine(C), dst[:, :, c0:c0 + cw], st[:, :, 0:cw], [kst], [kdst])

    m1 = A.mark()
    mm_ = A.mark()
    load_cast(wuq, kwuq, C.d_mla_wuq.ap().rearrange("(kc p) n -> p kc n", p=128), 3, 1536, 512)
    load_cast(wukv, kwukv, C.d_mla_wukv.ap().rearrange("(kc p) n -> p kc n", p=128), 2, 2048, 512)
    A.release(mm_)
    win, kwin = A.alloc("win", [8, 736], BF16)
    P.add("pool", lambda e: e.memset(win, 0.0), [], [kwin])
    wsrc = C.d_mla_win.ap().rearrange("(kc p) n -> p kc n", p=128)
    mm2 = A.mark()
    stg = [A.alloc(f"wst_in{i}", [8, 224], F32) for i in range(2)]
    for i, c0 in enumerate(range(0, 672, 224)):
        st, kst = stg[i % 2]
        P.dma("sp", st, wsrc[:, :, c0:c0 + 224], [], [kst])
        if c0 + 224 <= 640:
            emit_cast(P, cast_engine(C), win[:, :, c0:c0 + 224], st, [kst], [kwin])
        else:
            nl = 640 - c0
            emit_cast(P, cast_engine(C), win[:, :, c0:640], st[:, :, 0:nl], [kst], [kwin])
            emit_cast(P, cast_engine(C), win[:, :, 704:736], st[:, :, nl:nl + 32], [kst], [kwin])

    A.release(mm2)
    xb = [A.alloc(f"mx{i}", [NDC, T], F32) for i in range(2)]
    h, kh = A.alloc("mh", [NDC, T], BF16)
    sq, ksq = A.alloc("msq", [NDC, T], BF16)
    rstd, krstd = A.alloc("mrstd", [T], F32)
    tmp = [A.alloc(f"mtmp{i}", [512], F32) for i in range(3)]
    lsq, klsq = A.alloc("mlsq", [3, T], BF16)
    lrs, klrs = A.alloc("mlrs", [T], F32)

    def load_x(m):
        xt, kxt = xb[m % 2]
        P.dma("act", xt, C.xT[:, :, m * T:(m + 1) * T], [("xT", m)], [kxt])

    load_x(0)
    for m in range(NT):
        xt, kxt = xb[m % 2]
        sl = slice(m * T, (m + 1) * T)
        norm_modulate(C, xt, kxt, h, kh, T, l, 1, (sq, ksq, rstd, krstd, tmp, None))
        if m + 1 < NT:
            load_x(m + 1)
        for (c0, nch, ones, kones, dst, kdst, g) in ((0, 3, ones384, k384, qn, kqn, gql), (3, 2, ones256, k256, kvn, kkvn, gkvl)):
            banks = []
            for c in range(nch):
                ps, kps = psum_bank(C)
                banks.append((ps, kps))
                for kc in range(8):
                    P.mm(ps, win[:, kc, (c0 + c) * 128:(c0 + c + 1) * 128], h[:, kc, :], kc == 0, kc == 7,
                         [kwin, (kh, 0)], [kps])
                P.add("act", lambda e, ps=ps, c=c: e.activation(lsq[:, c, :], ps, AF.Square), [kps], [klsq])
            pss, kpss = psum_bank(C)
            for c in range(nch):
                P.mm(pss, ones, lsq[:, c, :], c == 0, c == nch - 1, [klsq, kones], [kpss])
            P.add("act", lambda e, pss=pss: e.activation(lrs, pss, AF.Ln, bias=C.eps_t[:, 0:1]), [kpss, C.k_eps], [klrs])
            P.add("act", lambda e: e.activation(lrs, lrs, AF.Exp, scale=-0.5), [klrs], [klrs])
            for c in range(nch):
                ps, kps = banks[c]
                tm, ktm = tmp[c % 3]
                P.add("dve", lambda e, tm=tm, ps=ps: e.tensor_tensor(tm, ps, lrs, ALU.mult), [kps, klrs], [ktm])
                P.add("act", lambda e, tm=tm, c=c, dst=dst, g=g, sl=sl: e.activation(
                    dst[:, c, sl], tm, AF.Identity, scale=g[:, c:c + 1]), [ktm], [kdst])
        ps, kps = psum_bank(C)
        for kc in range(8):
            P.mm(ps[0:96, :], win[:, kc, 640:736], h[:, kc, :], kc == 0, kc == 7, [kwin, (kh, 0)], [kps])
        P.add("act", lambda e, ps=ps, sl=sl: e.copy(kpe[64:96, sl], ps[64:96, :]), [kps], [kkpe])
        P.add("act", lambda e, ps=ps, sl=sl: e.activation(sqk[64:96, sl], ps[64:96, :], AF.Square), [kps], [(ksqk, "pe")])
    A.release(m1)

    kT = [A.alloc(f"kT{i}", [S], BF16) for i in range(2)]
    Vau = [A.alloc(f"Vau{i}", [NB, 128], BF16) for i in range(2)]
    OTs = [A.alloc(f"OTs{i}", [S], BF16) for i in range(2)]
    for par in range(2):
        va, kva = Vau[par]
        P.add("pool", lambda e, va=va: e.memset(va, 1.0), [], [kva])
    rsk, krsk = A.alloc("rsk", [T], F32)
    rt1 = [A.alloc(f"rp1_{i}", [T], F32) for i in range(2)]
    rt2 = [A.alloc(f"rp2_{i}", [T], F32) for i in range(2)]
    qsq, kqsq = A.alloc("qsq", [T], BF16)
    qT = [A.alloc(f"qT{i}", [T], BF16) for i in range(2)]
    PT = [A.alloc(f"PT{i}", [T], BF16) for i in range(4)]
    osb, kosb = A.alloc("osb", [T], F32)
    rl, krl = A.alloc("rl", [T], F32)
    pti = 0

    C.held = []

    def bank_excl(excl):
        while True:
            ps, kps = psum_bank(C)
            if all(ps is not x for x in excl) and all(ps is not x for x in C.held):
                return ps, kps

    def norm_rope(src_ps, ksrc_ps, src_sb, ksrc_sb, sqt, ksqt_keys, g, kg, dstT, kdstT, sl, tl):
        yield
        pss, kpss = bank_excl([])
        P.mm(pss[0:96, :], ones96[0:96, 0:96], sqt, True, True, ksqt_keys + [k96], [kpss])
        P.add("act", lambda e: e.activation(rsk[0:96, :], pss[0:96, :], AF.Ln, bias=C.eps_t[0:96, 0:1]),
              [kpss, C.k_eps], [krsk])
        P.add("act", lambda e: e.activation(rsk[0:96, :], rsk[0:96, :], AF.Exp, scale=-0.5), [krsk], [krsk])
        if src_sb is None:
            P.add("dve", lambda e: e.scalar_tensor_tensor(dstT[0:96, sl], src_ps[0:96, :], g[0:96, 0:1], rsk[0:96, :],
                                                          ALU.mult, ALU.mult), [ksrc_ps, kg, krsk], [kdstT])
        else:
            P.add("dve", lambda e: e.scalar_tensor_tensor(dstT[0:64, sl], src_ps[0:64, :], g[0:64, 0:1], rsk[0:64, :],
                                                          ALU.mult, ALU.mult), [ksrc_ps, kg, krsk], [kdstT])
            P.add("dve", lambda e: e.scalar_tensor_tensor(dstT[64:96, sl], src_sb[64:96, tl], g[64:96, 0:1],
                                                          rsk[64:96, :], ALU.mult, ALU.mult), [ksrc_sb, kg, krsk], [kdstT])
        C.held[:] = [x for x in C.held if x is not src_ps]
        yield
        psr, kpsr = bank_excl([])
        P.mm(psr[0:96, :], pmT[0:96, 0:96], dstT[0:96, sl], True, True, [kdstT, kpm], [kpsr])
        a1, ka1 = rt1[C.ps_i % 2]
        a2, ka2 = rt2[C.ps_i % 2]
        P.add("dve", lambda e: e.tensor_tensor(a1[64:96, :], dstT[64:96, sl], COS[64:96, tl], ALU.mult),
              [kdstT, kcos], [ka1])
        P.add("dve", lambda e: e.tensor_tensor(a2[64:96, :], psr[64:96, :], SIN[64:96, tl], ALU.mult),
              [kpsr, ksin], [ka2])
        P.add("dve", lambda e: e.tensor_tensor(dstT[64:96, sl], a1[64:96, :], a2[64:96, :], ALU.add),
              [ka1, ka2], [kdstT])

    LA = 3
    PTn = [A.alloc(f"PTn{i}", [T], BF16) for i in range(LA + 3)]

    def prepK(hd, m):
        kt_, kkt = kT[hd % 2]
        tl = slice(m * T, (m + 1) * T)
        ps, kps = bank_excl([])
        C.held.append(ps)
        for kc in range(2):
            P.mm(ps[0:64, :], wukv[:, kc, hd * 128:hd * 128 + 64], kvn[:, kc, tl], kc == 0, kc == 1,
                 [kwukv, kkvn], [kps])
        P.add("act", lambda e, ps=ps, tl=tl: e.activation(sqk[0:64, tl], ps[0:64, :], AF.Square),
              [kps], [(ksqk, "n", m)])
        yield from norm_rope(ps, kps, kpe, kkpe, sqk[0:96, tl], [(ksqk, "n", m), (ksqk, "pe")], gk, kgk, kt_, kkt, tl, tl)

    def prepV(hd, b0):
        va, kva = Vau[hd % 2]
        voff = 0 if hd % 2 == 0 else 64
        ps, kps = bank_excl([])
        for j in range(8):
            blk = b0 + j
            for kc in range(2):
                P.mm(ps[:, j * 64:(j + 1) * 64], kvn[:, kc, blk * 128:(blk + 1) * 128],
                     wukv[:, kc, hd * 128 + 64:hd * 128 + 128], kc == 0, kc == 1, [kwukv, kkvn], [kps])
        P.add("dve", lambda e, ps=ps, b0=b0, va=va, voff=voff: e.tensor_copy(
            va[:, b0:b0 + 8, voff:voff + 64], ps.rearrange("p (j v) -> p j v", j=8)), [kps], [kva])
        yield

    qcount = [0]

    def prepQ(hd, m, q_, kq_):
        tl = slice(m * T, (m + 1) * T)
        ps, kps = bank_excl([])
        C.held.append(ps)
        for kc in range(3):
            P.mm(ps[0:96, :], wuq[:, kc, hd * 96:(hd + 1) * 96], qn[:, kc, tl], kc == 0, kc == 2,
                 [kwuq, kqn], [kps])
        P.add("act", lambda e, ps=ps: e.activation(qsq[0:96, :], ps[0:96, :], AF.Square), [kps], [kqsq])
        yield from norm_rope(ps, kps, None, None, qsq[0:96, :], [kqsq], gq, kgq, q_, kq_, slice(0, T), tl)

    def run_all(gen):
        for _ in gen:
            pass

    def next_q():
        q = qT[qcount[0] % 2]
        qcount[0] += 1
        return q

    for m in range(NT):
        run_all(prepK(0, m))
    for b0 in range(0, NB, 8):
        run_all(prepV(0, b0))
    nextq = next_q()
    run_all(prepQ(0, 0, nextq[0], nextq[1]))
    vgroups = list(range(0, NB, 8))
    for hd in range(16):
        par = hd % 2
        kt_, kkt = kT[par]
        va, kva = Vau[par]
        ots, kots = OTs[(hd // 2) % 2]
        oh = 0 if par == 0 else 64
        lp = 64 if par == 0 else 0
        for m in range(NT):
            tl = slice(m * T, (m + 1) * T)
            q_, kq_ = nextq
            gens = []
            if m + 1 < NT:
                nextq = next_q()
                gens.append(prepQ(hd, m + 1, nextq[0], nextq[1]))
            elif hd + 1 < 16:
                nextq = next_q()
                gens.append(prepQ(hd + 1, 0, nextq[0], nextq[1]))
            if hd + 1 < 16:
                gens.append(prepK(hd + 1, m))
                if m < len(vgroups):
                    gens.append(prepV(hd + 1, vgroups[m]))
            pso, kpso = bank_excl([])
            C.held.append(pso)
            pend = []
            for kb in range(NB + LA):
                if kb < NB:
                    pss, kpss = bank_excl([])
                    P.mm(pss, kt_[0:96, kb * 128:(kb + 1) * 128], q_[0:96, :], True, True, [kkt, kq_], [kpss])
                    pt, kpt = PTn[pti % len(PTn)]
                    pti += 1
                    P.add("act", lambda e, pt=pt, pss=pss: e.activation(pt, pss, AF.Exp, scale=SCALE), [kpss], [kpt])
                    pend.append((kb, pt, kpt))
                if kb >= LA:
                    kb2, pt, kpt = pend.pop(0)
                    P.mm(pso, va[:, kb2, :], pt, kb2 == 0, kb2 == NB - 1, [kva, kpt], [kpso])
                if kb % 6 == 2 and gens:
                    gi = (kb // 6) % len(gens)
                    for g_ in list(gens):
                        try:
                            next(g_)
                        except StopIteration:
                            gens.remove(g_)
            for g_ in gens:
                run_all(g_)
            P.add("dve", lambda e, pso=pso, lp=lp: e.reciprocal(rl[lp:lp + 1, :], pso[lp:lp + 1, :]), [kpso], [krl])
            P.add("act", lambda e, pso=pso, oh=oh: e.copy(osb[oh:oh + 64, :], pso[oh:oh + 64, :]), [kpso], [kosb])
            psb, kpsb = bank_excl([])
            P.mm(psb, onesrow[lp:lp + 1, :], rl[lp:lp + 1, :], True, True, [krl, krow], [kpsb])
            P.add("dve", lambda e, psb=psb, oh=oh, ots=ots, tl=tl: e.tensor_tensor(
                ots[oh:oh + 64, tl], osb[oh:oh + 64, :], psb[oh:oh + 64, :], ALU.mult), [kosb, kpsb], [kots])
            C.held[:] = [x for x in C.held if x is not pso]
        if par == 1:
            c = hd // 2
            P.dma("sp", C.OT[c], ots, [kots], [("OT", c)])
    A.release(m_phase)

    m3 = A.mark()
    wout, kwout = A.alloc("wout", [8, 1024], BF16)
    stg = [A.alloc(f"wst_o{i}", [8, 256], F32) for i in range(2)]
    wsrc = C.d_mla_wout.ap().rearrange("(kc p) n -> p kc n", p=128)
    for i, c0 in enumerate(range(0, 1024, 256)):
        st, kst = stg[i % 2]
        P.dma("sp", st, wsrc[:, :, c0:c0 + 256], [], [kst])
        emit_cast(P, cast_engine(C), wout[:, :, c0:c0 + 256], st, [kst], [kwout])
    xb = [A.alloc(f"ox{i}", [NDC, T], F32) for i in range(2)]
    ob = [A.alloc(f"oo{i}", [NDC, T], BF16) for i in range(2)]
    for m in range(NT):
        xt, kxt = xb[m % 2]
        ot, kot = ob[m % 2]
        tl = slice(m * T, (m + 1) * T)
        P.dma("act", xt, C.xT[:, :, tl], [("xT", m)], [kxt])
        P.dma("sp", ot, C.OT.rearrange("c p s -> p c s")[:, :, tl], [("OT", c) for c in range(8)], [kot])
        for dc in range(NDC):
            ps, kps = psum_bank(C)
            for kc in range(8):
                P.mm(ps, wout[:, kc, dc * 128:(dc + 1) * 128], ot[:, kc, :], kc == 0, kc == 7, [kwout, kot], [kps])
            P.add("dve", lambda e, ps=ps, dc=dc, xt=xt: e.scalar_tensor_tensor(
                xt[:, dc, :], ps, mv["gate"][:, 8 + dc:8 + dc + 1], xt[:, dc, :], ALU.mult, ALU.add),
                [kps, mv["kgate"], kxt], [kxt])
        P.dma("act", C.xT[:, :, tl], xt, [kxt], [("xT", m)])
    A.release(m3)


def bc_last(ap, n):
    return ap.unsqueeze(2).to_broadcast([ap.shape[0], ap.shape[1], n])


def bc_mid(ap, n):
    return ap.unsqueeze(1).to_broadcast([ap.shape[0], n, ap.shape[1]])


def ssd_phase(C):
    P, A = C.P, C.arena
    l = 0
    mv = C.modv[l]
    NB = S // 128
    T = 512
    NT = S // T
    m_phase = A.mark()
    one_t, kone = const_tile(C, "one", 1.0)
    masks, kmask = A.alloc("masks", [6, 128], F32)
    P.dma("sp", masks, C.d_masks.ap().rearrange("p (a b) -> p a b", a=6), [], [kmask])
    ones_f, konesf = const_tile(C, "ones_f", 1.0, F32, 128)
    cw, kcw = C.vec["ssd_cw"]
    cb_, kcb = C.vec["ssd_cb"]
    dtb, kdtb = C.vec["ssd_dtb"]
    alog, kalog = C.vec["ssd_alog"]
    dsk, kdsk = C.vec["ssd_dskip"]
    Aneg, kAneg = A.alloc("Aneg", [64], F32)
    P.add("act", lambda e: e.activation(Aneg, alog, AF.Exp), [kalog], [kAneg])
    P.add("dve", lambda e: e.tensor_scalar(Aneg, Aneg, -1.0, None, ALU.mult), [kAneg], [kAneg])
    h_all, khall = A.alloc("h_all", [NDC, S], BF16)

    m0 = A.mark()
    xb = [A.alloc(f"sx{i}", [NDC, T], F32) for i in range(2)]
    sq, ksq = A.alloc("ssq", [NDC, T], BF16)
    rstd, krstd = A.alloc("srstd", [T], F32)
    tmp = [A.alloc(f"stmp{i}", [512], F32) for i in range(3)]
    for m in range(NT):
        xt, kxt = xb[m % 2]
        P.dma("act", xt, C.xT[:, :, m * T:(m + 1) * T], [("xT", m)], [kxt])
        norm_modulate(C, xt, kxt, h_all[:, :, m * T:(m + 1) * T], (khall, m), T, l, 1, (sq, ksq, rstd, krstd, tmp, None))
    A.release(m0)
    hkeys = [((khall, m), 0) for m in range(NT)]

    m0 = A.mark()
    wsrc = C.d_ssd_win.ap().rearrange("(kc p) n -> p kc n", p=128)
    wst = [A.alloc(f"swst{i}", [8, 128], F32) for i in range(2)]
    wcb = [A.alloc(f"swc{i}", [8, 128], BF16) for i in range(2)]
    pre = [A.alloc(f"spre{i}", [S + 4], F32) for i in range(2)]
    acc = [A.alloc(f"sacc{i}", [S], F32) for i in range(2)]
    xo = [A.alloc(f"sxo{i}", [S], BF16) for i in range(2)]
    for i in range(2):
        pr, kpr = pre[i]
        P.add("pool", lambda e, pr=pr: e.memset(pr, 0.0), [], [kpr])
    conv_gens = []
    if C.pending_conv:
        cstg = [A.alloc(f"ccst{i}", [2048], F32) for i in range(2)]
        cstb = [A.alloc(f"ccsb{i}", [2048], BF16) for i in range(2)]
        for (l_, w_) in C.pending_conv:
            if (l_, w_) not in C.converted:
                conv_gens.append(convert_ffn_gen(C, l_, w_, cstg, cstb, 2, ("act",)))
        C.pending_conv = []
    n_conv_steps = len(conv_gens) * 33
    per_chunk = (n_conv_steps + 31) // 32

    def conv_advance(n):
        for _ in range(n):
            while conv_gens:
                try:
                    next(conv_gens[0])
                    break
                except StopIteration:
                    conv_gens.pop(0)

    for c in range(32):
        conv_advance(per_chunk)
        ws, kws = wst[c % 2]
        wc, kwc = wcb[c % 2]
        pr, kpr = pre[c % 2]
        ac, kac = acc[c % 2]
        xo_, kxo = xo[c % 2]
        P.dma("sp", ws, wsrc[:, :, 2048 + c * 128:2048 + (c + 1) * 128], [], [kws])
        P.add("pool", lambda e, wc=wc, ws=ws: e.tensor_copy(wc, ws), [kws], [kwc])
        for m in range(NT):
            ps, kps = psum_bank(C)
            for kc in range(8):
                P.mm(ps, wc[:, kc, :], h_all[:, kc, m * T:(m + 1) * T], kc == 0, kc == 7, [kwc, hkeys[m]], [kps])
            P.add("act", lambda e, ps=ps, pr=pr, m=m: e.copy(pr[:, 2 + m * T:2 + (m + 1) * T], ps), [kps], [kpr])
        for hf in range(2):
            o0 = hf * (S // 2)
            n_ = S // 2
            P.add("dve", lambda e, ac=ac, pr=pr, c=c, o0=o0, n_=n_: e.tensor_scalar(
                ac[:, o0:o0 + n_], pr[:, o0:o0 + n_], cw[:, c * 5:c * 5 + 1], cb_[:, c:c + 1], ALU.mult, ALU.add),
                [kpr, kcw, kcb], [(kac, hf)])
            for j in range(1, 5):
                P.add("dve", lambda e, ac=ac, pr=pr, c=c, j=j, o0=o0, n_=n_: e.scalar_tensor_tensor(
                    ac[:, o0:o0 + n_], pr[:, o0 + j:o0 + j + n_], cw[:, c * 5 + j:c * 5 + j + 1], ac[:, o0:o0 + n_],
                    ALU.mult, ALU.add), [kpr, kcw, (kac, hf)], [(kac, hf)])
            P.add("act", lambda e, xo_=xo_, ac=ac, o0=o0, n_=n_: e.activation(xo_[:, o0:o0 + n_], ac[:, o0:o0 + n_], AF.Silu),
                  [(kac, hf)], [(kxo, hf)])
        P.dma("sp", C.XBC[c], xo_, [(kxo, 0), (kxo, 1)], [("XBC", c)])
    conv_advance(10000)
    A.release(m0)

    wdt, kwdt = A.alloc("wdt", [8, 64], BF16)
    m_big = A.mark()
    wz, kwz = A.alloc("wz", [8, 2048], BF16)
    m0 = A.mark()
    stg = [A.alloc(f"szst{i}", [8, 256], F32) for i in range(2)]
    it = 0
    for c0 in range(0, 2048, 256):
        st, kst = stg[it % 2]
        it += 1
        P.dma("sp", st, wsrc[:, :, c0:c0 + 256], [], [kst])
        emit_cast(P, cast_engine(C), wz[:, :, c0:c0 + 256], st, [kst], [kwz])
    st, kst = stg[it % 2]
    it += 1
    P.dma("sp", st[:, :, 0:64], wsrc[:, :, 6144:6208], [], [kst])
    emit_cast(P, cast_engine(C), wdt, st[:, :, 0:64], [kst], [kwdt])
    A.release(m0)
    ng16, kng16 = C.vec["ssd_ng16"]

    xbcT = [A.alloc(f"xbcT{i}", [32, 128], BF16) for i in range(2)]
    steps = [(ck, 1) for ck in range(NB - 1, -1, -1)] + [(ck, 0) for ck in range(NB)]

    def load_xbc(i):
        ck_ = steps[i][0]
        xb_, kxb = xbcT[i % 2]
        P.dma("sp", xb_, C.XBC.rearrange("c p s -> p c s")[:, :, ck_ * 128:(ck_ + 1) * 128],
              [("XBC", c) for c in range(32)], [kxb])
    xs_tm, kxs = A.alloc("xs_tm", [32, 64], BF16)
    B_tm, kbt = A.alloc("B_tm", [8, 128], BF16)
    dt_, kdt = A.alloc("dt", [64], F32)
    a_, ka = A.alloc("a", [64], F32)
    dec, kdec = A.alloc("dec", [96], F32)
    dt2, kdt2 = A.alloc("dt2", [32], F32)
    xdt, kxdt = A.alloc("xdt", [32, 64], BF16)
    xdtE, kxdtE = A.alloc("xdtE", [32, 64], BF16)
    cbm, kcbm = A.alloc("cbm", [8, 128], F32)
    Lb = [A.alloc(f"Lb{i}", [4, 128], F32) for i in range(2)]
    ex = [A.alloc(f"ex{i}", [4, 128], F32) for i in range(2)]
    MT = [A.alloc(f"MT{i}", [4, 128], BF16) for i in range(2)]
    yo = [A.alloc(f"yo{i}", [256], F32) for i in range(2)]
    ydir, kydir = A.alloc("ydir", [2048], F32)
    H, kH = A.alloc("H", [2048], F32)
    Hb, kHb = A.alloc("Hb", [2048], BF16)
    yb_in, kybin = A.alloc("yb_in", [2048], F32)
    sz, ksz = A.alloc("sz", [2048], F32)
    gss, kgss = A.alloc("gss", [8], F32)
    ynb, kynb = A.alloc("ynb", [2048], BF16)
    yT, kyT = A.alloc("yT", [16, 128], BF16)
    xcks = [A.alloc(f"xck{i}", [NDC, 128], F32) for i in range(2)]

    def chunk_step(ck, d, it_, hook=None):
        tk = slice(ck * 128, (ck + 1) * 128)
        xb_, kxb = xbcT[it_ % 2]
        if it_ + 1 < len(steps):
            load_xbc(it_ + 1)
        Lm = masks[:, 0 + 2 * d, :]
        Rm = masks[:, 1 + 2 * d, :]
        Vm = masks[:, 4 + d, :]
        dc0 = d * 32
        for q in range(3):
            ps, kps = psum_bank(C)
            psb = ps.bitcast(BF16)
            for j in range(8):
                c = q * 8 + j
                P.add("pe", lambda e, psb=psb, j=j, c=c, xb_=xb_: e.transpose(
                    psb[:, j * 128:(j + 1) * 128], xb_[:, c, :], C.ident_b), [kxb, C.k_ident_b], [kps])
            if q < 2:
                P.add("act", lambda e, psb=psb, q=q: e.copy(
                    xs_tm.rearrange("p a b -> p (a b)")[:, q * 1024:(q + 1) * 1024], psb), [kps], [kxs])
            else:
                P.add("dve", lambda e, psb=psb: e.tensor_copy(B_tm.rearrange("p a b -> p (a b)"), psb), [kps], [kbt])
        ps, kps = psum_bank(C)
        for kc in range(8):
            P.mm(ps[:, 0:64], h_all[:, kc, tk], wdt[:, kc, :], kc == 0, kc == 7, [hkeys[ck // 4], kwdt], [kps])
        P.add("dve", lambda e, ps=ps: e.tensor_tensor(dt_, ps[:, 0:64], dtb, ALU.add), [kps, kdtb], [kdt])
        P.add("act", lambda e: e.activation(dt_, dt_, AF.Exp), [kdt], [kdt])
        P.add("act", lambda e: e.activation(dt_, dt_, AF.Ln, bias=one_t[:, 0:1]), [kdt, kone], [kdt])
        P.add("dve", lambda e: e.tensor_tensor(a_, dt_, Aneg, ALU.mult), [kdt, kAneg], [ka])
        ps, kps = psum_bank(C)
        P.mm(ps[:, 0:32], Rm, a_[:, dc0:dc0 + 32], True, True, [kmask, ka], [kps])
        P.mm(ps[:, 32:64], Lm, a_[:, dc0:dc0 + 32], True, True, [kmask, ka], [kps])
        P.mm(ps[:, 64:96], ones_f, a_[:, dc0:dc0 + 32], True, True, [konesf, ka], [kps])
        P.add("act", lambda e, ps=ps: e.activation(dec, ps[:, 0:96], AF.Exp), [kps], [kdec])
        P.add("dve", lambda e: e.tensor_tensor(dt2, dt_[:, dc0:dc0 + 32], dec[:, 32:64], ALU.mult), [kdt, kdec], [kdt2])
        P.add("dve", lambda e: e.tensor_tensor(xdt, xs_tm, bc_last(dt_[:, dc0:dc0 + 32], 64), ALU.mult),
              [kxs, kdt], [kxdt])
        P.add("dve", lambda e: e.tensor_tensor(xdtE, xs_tm, bc_last(dt2, 64), ALU.mult), [kxs, kdt2], [kxdtE])
        for half in range(2):
            ps, kps = psum_bank(C)
            for j in range(4):
                g = half * 4 + j
                P.mm(ps[:, j * 128:(j + 1) * 128], xb_[:, 16 + g, :], xb_[:, 24 + g, :], True, True, [kxb], [kps])
            P.add("dve", lambda e, ps=ps, half=half: e.tensor_tensor(
                cbm[:, half * 4:(half + 1) * 4, :], ps.rearrange("p (a b) -> p a b", a=4), bc_mid(Vm, 4), ALU.mult),
                [kps, kmask], [kcbm])
        if hook is not None:
            hook()
        for g in range(8):
            lb, klb = Lb[g % 2]
            ex_, kex = ex[g % 2]
            mt, kmt = MT[g % 2]
            yo_, kyo = yo[g % 2]
            P.add("dve", lambda e, lb=lb, g=g: e.tensor_tensor(
                lb, bc_mid(Lm, 4), bc_last(a_[:, dc0 + g * 4:dc0 + g * 4 + 4], 128), ALU.mult), [kmask, ka], [klb])
            ps, kps = psum_bank(C)
            for r in range(4):
                P.mm(ps[:, r * 128:(r + 1) * 128], lb[:, r, :], Rm, True, True, [klb, kmask], [kps])
            P.add("act", lambda e, ps=ps, ex_=ex_: e.activation(ex_.rearrange("p a b -> p (a b)"), ps, AF.Exp), [kps], [kex])
            P.add("dve", lambda e, mt=mt, ex_=ex_, g=g: e.tensor_tensor(mt, ex_, bc_mid(cbm[:, g, :], 4), ALU.mult),
                  [kex, kcbm], [kmt])
            psy, kpsy = psum_bank(C)
            for r in range(4):
                P.mm(psy[:, r * 64:(r + 1) * 64], mt[:, r, :], xdt[:, g * 4 + r, :], True, True, [kmt, kxdt], [kpsy])
            P.mm(psy[:, 256:512], xb_[:, 24 + g, :], Hb[:, g * 256:(g + 1) * 256], True, True, [kxb, kHb], [kpsy])
            P.add("dve", lambda e, psy=psy, yo_=yo_, g=g: e.tensor_tensor(
                yo_.rearrange("p (a b) -> p a b", a=4), psy[:, 256:512].rearrange("p (a b) -> p a b", a=4),
                bc_last(dec[:, g * 4:g * 4 + 4], 64), ALU.mult), [kpsy, kdec], [kyo])
            P.add("dve", lambda e, psy=psy, yo_=yo_, g=g: e.tensor_tensor(
                ydir[:, g * 256:(g + 1) * 256], psy[:, 0:256], yo_, ALU.add), [kpsy, kyo], [(kydir, g)])
            pss, kpss = psum_bank(C)
            P.mm(pss[:, 0:256], B_tm[:, g, :], xdtE[:, g * 4:g * 4 + 4, :].rearrange("p a b -> p (a b)"), True, True,
                 [kbt, kxdtE], [kpss])
            P.add("dve", lambda e, g=g: e.tensor_tensor(
                H[:, g * 256:(g + 1) * 256].rearrange("p (a b) -> p a b", a=4),
                H[:, g * 256:(g + 1) * 256].rearrange("p (a b) -> p a b", a=4),
                bc_last(dec[:, 64 + g * 4:64 + g * 4 + 4], 64), ALU.mult), [(kH, g), kdec], [(kH, g)])
            P.add("dve", lambda e, pss=pss, g=g: e.tensor_tensor(
                H[:, g * 256:(g + 1) * 256], H[:, g * 256:(g + 1) * 256], pss[:, 0:256], ALU.add),
                [(kH, g), kpss], [(kH, g)])
            P.add("act", lambda e, g=g: e.copy(Hb[:, g * 256:(g + 1) * 256], H[:, g * 256:(g + 1) * 256]),
                  [(kH, g)], [kHb])

    ykeys = [(kydir, g) for g in range(8)]
    P.add("pool", lambda e: e.memset(H, 0.0), [], [(kH, g) for g in range(8)])
    P.add("pool", lambda e: e.memset(Hb, 0.0), [], [kHb])
    it_ = 0
    load_xbc(0)
    for ck in range(NB - 1, -1, -1):
        chunk_step(ck, 1, it_)
        it_ += 1
        P.dma("sp", C.YB[ck], ydir, ykeys, [("YB", ck)])
        tk = slice(ck * 128, (ck + 1) * 128)
        for zc in range(4):
            ps, kps = psum_bank(C)
            for kc in range(8):
                P.mm(ps, h_all[:, kc, tk], wz[:, kc, zc * 512:(zc + 1) * 512], kc == 0, kc == 7,
                     [hkeys[ck // 4], kwz], [kps])
            P.add("act", lambda e, ps=ps, zc=zc: e.activation(sz[:, zc * 512:(zc + 1) * 512], ps, AF.Silu), [kps], [ksz])
        P.dma("sp", C.SZ[ck], sz, [ksz], [("SZ", ck)])
    wout = wz.rearrange("p a b -> p (a b)").rearrange("p (k n) -> p k n", k=16)
    kwout = kwz
    stg = [(yb_in.rearrange("p (k n) -> p k n", k=2), kybin), (sz.rearrange("p (k n) -> p k n", k=2), ksz)]
    osrc = C.d_ssd_wout.ap().rearrange("(kc p) n -> p kc n", p=128)
    for i_, c0 in enumerate(range(0, 16, 2)):
        st, kst = stg[i_ % 2]
        P.dma("sp", st, osrc[:, c0:c0 + 2, :], [], [kst])
        emit_cast(P, cast_engine(C), wout[:, c0:c0 + 2, :], st, [kst], [kwout])
    P.add("pool", lambda e: e.memset(H, 0.0), [], [(kH, g) for g in range(8)])
    P.add("pool", lambda e: e.memset(Hb, 0.0), [], [kHb])
    deferred = [None]
    for ck in range(NB):
        tk = slice(ck * 128, (ck + 1) * 128)
        P.dma("act", yb_in, C.YB[ck], [("YB", ck)], [kybin])
        xck, kxck = xcks[ck % 2]
        P.dma("act", xck, C.xT[:, :, tk], [("xT", ck // 4)], [kxck])
        chunk_step(ck, 0, it_, hook=deferred[0])
        it_ += 1
        P.add("dve", lambda e: e.tensor_tensor(ydir, ydir, yb_in, ALU.add), ykeys + [kybin], ykeys)
        P.add("dve", lambda e: e.tensor_tensor(yb_in.rearrange("p (a b) -> p a b", a=32), xs_tm, bc_last(dsk, 64), ALU.mult),
              [kxs, kdsk], [kybin])
        P.add("dve", lambda e: e.tensor_tensor(ydir, ydir, yb_in, ALU.add), ykeys + [kybin], ykeys)
        P.dma("sp", sz, C.SZ[ck], [("SZ", ck)], [ksz])
        P.add("dve", lambda e: e.tensor_tensor(ydir, ydir, sz, ALU.mult), ykeys + [ksz], ykeys)
        P.add("act", lambda e: e.activation(sz, ydir, AF.Square), ykeys, [ksz])
        P.add("dve", lambda e: e.reduce_sum(gss, sz.rearrange("p (a b) -> p a b", a=8), AX.X), [ksz], [kgss])
        P.add("act", lambda e: e.activation(gss, gss, AF.Ln, bias=C.eps_t[:, 0:1], scale=1.0 / 256), [kgss, C.k_eps], [kgss])
        P.add("act", lambda e: e.activation(gss, gss, AF.Exp, scale=-0.5), [kgss], [kgss])
        P.add("dve", lambda e: e.tensor_tensor(ydir.rearrange("p (a b) -> p a b", a=8), ydir.rearrange("p (a b) -> p a b", a=8),
                                               bc_last(gss, 256), ALU.mult), ykeys + [kgss], ykeys)
        P.add("act", lambda e: e.copy(ynb, ydir), ykeys, [kynb])
        def make_partB(ck=ck, tk=tk, xck=xck, kxck=kxck):
            def partB():
                for q in range(2):
                    ps, kps = psum_bank(C)
                    psb = ps.bitcast(BF16)
                    for j in range(8):
                        c = q * 8 + j
                        P.add("pe", lambda e, psb=psb, j=j, c=c: e.transpose(
                            psb[:, j * 128:(j + 1) * 128], ynb[:, c * 128:(c + 1) * 128], C.ident_b), [kynb, C.k_ident_b], [kps])
                    for j in range(8):
                        c = q * 8 + j
                        P.add("act", lambda e, psb=psb, j=j, c=c: e.activation(
                            yT[:, c, :], psb[:, j * 128:(j + 1) * 128], AF.Identity, scale=ng16[:, c:c + 1]),
                            [kps, kng16], [kyT])
                for half in range(2):
                    ps, kps = psum_bank(C)
                    for j in range(4):
                        dc = half * 4 + j
                        for kc in range(16):
                            P.mm(ps[:, j * 128:(j + 1) * 128], wout[:, kc, dc * 128:(dc + 1) * 128], yT[:, kc, :],
                                 kc == 0, kc == 15, [kwout, kyT], [kps])
                    for j in range(4):
                        dc = half * 4 + j
                        P.add("dve", lambda e, ps=ps, j=j, dc=dc: e.scalar_tensor_tensor(
                            xck[:, dc, :], ps[:, j * 128:(j + 1) * 128], mv["gate"][:, 8 + dc:8 + dc + 1], xck[:, dc, :],
                            ALU.mult, ALU.add), [kps, mv["kgate"], kxck], [kxck])
                P.dma("act", C.xT[:, :, tk], xck, [kxck], [("xT", ck // 4)])

            return partB
        deferred[0] = make_partB()
    deferred[0]()
    A.release(m_phase)

def build_program(stages, seq=4096, debug=False):
    global S
    S = seq
    nc = bass.Bass("TRN2", target_bir_lowering=False)
    C = Ctx()
    C.debug = debug
    C.dbg_off = 0
    C.dbg_map = {}
    C.dbg_keys = []
    C.nc = nc
    C.P = Prog(nc)
    C.ps_i = 0
    C.bar_gidx = 0
    C.cast_i = 0
    dt = nc.dram_tensor
    C.d_x = dt("x", [S, D], F32, kind="ExternalInput")
    C.d_out = dt("out", [S, D], F32, kind="ExternalOutput")
    C.d_ident = dt("ident", [128, 128], F32, kind="ExternalInput")
    if debug:
        C.d_dbg = dt("dbg", [128, 8192], F32, kind="ExternalOutput")
    C.d_w_mod = dt("w_mod", [2, D, 9 * D], F32, kind="ExternalInput")
    C.d_wg = dt("ffn_w_gate", [2, 2, D, DFF], F32, kind="ExternalInput")
    C.d_wu = dt("ffn_w_up", [2, 2, D, DFF], F32, kind="ExternalInput")
    C.d_wd = dt("ffn_w_down", [2, 2, DFF, D], F32, kind="ExternalInput")
    C.d_pm = dt("pmT", [96, 96], F32, kind="ExternalInput")
    C.d_pos = dt("pos", [128, S], I32, kind="ExternalInput")
    C.d_mla_win = dt("mla_w_in", [D, 672], F32, kind="ExternalInput")
    C.d_mla_wuq = dt("mla_w_uq", [384, 1536], F32, kind="ExternalInput")
    C.d_mla_wukv = dt("mla_w_ukv", [256, 2048], F32, kind="ExternalInput")
    C.d_mla_wout = dt("mla_w_out", [D, D], F32, kind="ExternalInput")
    C.OT = dt("OT_scr", [8, 128, S], BF16, kind="Internal").ap()
    C.d_masks = dt("masks", [128, 768], F32, kind="ExternalInput")
    C.d_ssd_win = dt("ssd_w_in", [D, 6208], F32, kind="ExternalInput")
    C.d_ssd_wout = dt("ssd_w_out", [2048, D], F32, kind="ExternalInput")
    C.SZ = dt("SZ_scr", [S // 128, 128, 2048], F32, kind="Internal").ap()
    C.XBC = dt("XBC_scr", [32, 128, S], BF16, kind="Internal").ap()
    C.YB = dt("YB_scr", [S // 128, 128, 2048], F32, kind="Internal").ap()
    C.d_vecs = {}
    for name, n in VEC_SPECS:
        C.d_vecs[name] = dt("v_" + name, [128, n], F32, kind="ExternalInput")
    C.xT = dt("xT_scr", [128, NDC, S], F32, kind="Internal").ap()
    C.WGU = [dt(f"wgu_scr{i}", [NF, 128, 2, 8, 128], BF16, kind="Internal").ap() for i in range(4)]
    C.WD = [dt(f"wd_scr{i}", [NDC, 128, NF, 128], BF16, kind="Internal").ap() for i in range(4)]

    ARENA_BYTES = 207 * 1024
    with ExitStack() as es:
        ah = es.enter_context(nc.sbuf_tensor("arena", [128, ARENA_BYTES // 4], F32))
        C.arena = Arena(ah, ARENA_BYTES, C.P)
        C.ps = [es.enter_context(nc.psum_tensor(f"ps{i}", [128, 512], F32))[:] for i in range(8)]
        eng_sems = {e: es.enter_context(nc.semaphore(f"sem_{e}")) for e in ENGS}
        dma_sems = [es.enter_context(nc.semaphore(f"dsem{i}")) for i in range(N_DMA_SEMS)]

        setup_consts(C)
        if debug:
            C.dbg_stage, _ = C.arena.alloc('dbg_stage', [3584], F32)
        load_transpose_x(C)
        compute_mod(C)
        C.converted = set()
        ffn_list = [(o[1], o[2]) for o in stages.get("order", []) if o[0] == "ffn"]
        C.pending_conv = ffn_list[1:] if any(o[0] == "ssd" for o in stages.get("order", [])) else []
        if ffn_list and stages.get("order", [])[0][0] == "ffn":
            convert_ffn_weights(C, *ffn_list[0])
        for st_ in stages.get("order", []):
            if BARRIERS:
                phase_barrier(C)
            if st_[0] == "ffn":
                ffn_phase(C, st_[1], st_[2])
            elif st_[0] == "mla":
                mla_phase(C)
            elif st_[0] == "ssd":
                ssd_phase(C)
        store_transpose_out(C)
        C.P.emit(eng_sems, dma_sems)
    C.nc = nc
    return C


VEC_SPECS = [("c", 8), ("b_mod0", 72), ("b_mod1", 72), ("norm_g0", 24), ("norm_g1", 24),
             ("mla_qg", 1), ("mla_kg", 1), ("mla_qng", 3), ("mla_kvng", 2), ("invf", 1),
             ("ssd_cw", 160), ("ssd_cb", 32), ("ssd_dtb", 64), ("ssd_alog", 64), ("ssd_dskip", 32), ("ssd_ng16", 16)]


def _consts():
    inv = (10000.0 ** (-np.arange(0, 32, 2, dtype=np.float32) / 32)).astype(np.float32)
    invf = np.zeros((128, 1), np.float32)
    invf[64:80, 0] = inv
    invf[80:96, 0] = inv
    pm = np.zeros((96, 96), np.float32)
    for i in range(16):
        pm[80 + i, 64 + i] = -1.0
        pm[64 + i, 80 + i] = 1.0
    return invf, pm


INVF, PMT = _consts()


def _masks():
    k = np.arange(128)[:, None]
    j = np.arange(128)[None, :]
    Lf = (k > j); Rf = (k <= j); Lb = (k < j); Rb = (k >= j)
    Vf = (j >= k)
    Vb = (j <= k)
    return np.concatenate([m.astype(np.float32) for m in (Lf, Rf, Lb, Rb, Vf, Vb)], axis=1)


MASKS = _masks()


def host_vecs(inputs, b):
    f = np.float32
    v = {}
    v["c"] = np.ascontiguousarray(inputs["c"][b].reshape(8, 128).T.astype(f))
    for l in range(2):
        v[f"b_mod{l}"] = np.ascontiguousarray(inputs["b_mod"][l].reshape(72, 128).T.astype(f))
        v[f"norm_g{l}"] = np.ascontiguousarray(inputs["norm_g"][l].reshape(24, 128).T.astype(f))
    def col(a, n=128):
        o = np.zeros((128, 1), f)
        o[:len(a), 0] = a
        return o
    v["mla_qg"] = col(inputs["mla_q_head_g"][0])
    v["mla_kg"] = col(inputs["mla_k_head_g"][0])
    v["mla_qng"] = np.ascontiguousarray(inputs["mla_q_norm_g"][0].reshape(3, 128).T.astype(f))
    v["mla_kvng"] = np.ascontiguousarray(inputs["mla_kv_norm_g"][0].reshape(2, 128).T.astype(f))
    v["invf"] = INVF
    cwt = inputs["ssd_conv_w"][0]
    v["ssd_cw"] = np.ascontiguousarray(cwt.reshape(5, 32, 128).transpose(2, 1, 0).reshape(128, 160).astype(f))
    v["ssd_cb"] = np.ascontiguousarray(inputs["ssd_conv_b"][0].reshape(32, 128).T.astype(f))
    v["ssd_dtb"] = np.ascontiguousarray(np.broadcast_to(inputs["ssd_dt_bias"][0].reshape(1, 64), (128, 64)).astype(f))
    v["ssd_alog"] = np.ascontiguousarray(np.broadcast_to(inputs["ssd_a_log"][0].reshape(1, 64), (128, 64)).astype(f))
    v["ssd_ng16"] = np.ascontiguousarray(inputs["ssd_norm_g"][0].reshape(16, 128).T.astype(f))
    v["ssd_dskip"] = np.ascontiguousarray(np.broadcast_to(inputs["ssd_d"][0].reshape(1, 32), (128, 32)).astype(f))
    return v


def run(inputs, stages, seq=4096, cores=8, debug=False):
    C = build_program(stages, seq, debug)
    nc = C.nc
    ident = np.eye(128, dtype=np.float32)
    in_maps = []
    for b in range(cores):
        m = {
            "x": np.ascontiguousarray(inputs["x"][b][:seq]),
            "ident": ident,
            "w_mod": inputs["w_mod"],
            "ffn_w_gate": inputs["ffn_w_gate"],
            "ffn_w_up": inputs["ffn_w_up"],
            "ffn_w_down": inputs["ffn_w_down"],
            "pmT": PMT,
            "masks": MASKS,
            "ssd_w_in": inputs["ssd_w_in"][0], "ssd_w_out": inputs["ssd_w_out"][0],

            "pos": np.ascontiguousarray(np.broadcast_to(inputs["positions"][b][None, :seq], (128, seq)).astype(np.int32)),
            "mla_w_in": inputs["mla_w_in"][0], "mla_w_uq": inputs["mla_w_uq"][0],
            "mla_w_ukv": inputs["mla_w_ukv"][0], "mla_w_out": inputs["mla_w_out"][0],
        }
        for k, a in host_vecs(inputs, b).items():
            m["v_" + k] = a
        in_maps.append(m)
    res = run_bass_kernel_spmd(nc, in_maps, core_ids=list(range(cores)))
    out = np.stack([r["out"] for r in res.results], axis=0)
    if debug:
        return out, {k: res.results[0]["dbg"][:, o:o + n] for k, (o, n) in C.dbg_map.items()}
    return out


def kernel(**inputs):
    inputs = {k: np.asarray(v) for k, v in inputs.items()}
    stages = {
        "ffn": [(0, 0), (0, 1), (1, 0), (1, 1)],
        "order": [("ffn", 0, 0), ("ssd",), ("ffn", 0, 1), ("ffn", 1, 0), ("mla",), ("ffn", 1, 1)],
    }
    return run(inputs, stages).astype(np.float32)
```
